# Optimizing a Trainium2 kernel written in Bass

```python
import math
import jax, jax.numpy as jnp
from jax import lax
import numpy as np

D_MODEL = 1024
BATCH = 2
SEQ = 8192
DEPTH = 4

NSA_WIDTH = D_MODEL // 2
POOL_WIDTH = D_MODEL // 4
SGU_WIDTH = D_MODEL // 4
MIX_WIDTH = NSA_WIDTH + POOL_WIDTH + SGU_WIDTH

NSA_HEADS = 8
NSA_KV_GROUPS = 2
HEADS_PER_GROUP = NSA_HEADS // NSA_KV_GROUPS
HEAD_DIM = NSA_WIDTH // NSA_HEADS
KV_WIDTH = NSA_KV_GROUPS * HEAD_DIM
GATE_WIDTH = NSA_HEADS * 3
CMP_STRIDE = 16
CMP_LEN = 2 * CMP_STRIDE
CMP_HIDDEN = 2 * HEAD_DIM
SEL_BLOCK = 64
SEL_TOPK = 16
N_LOCAL_SEL = 2
SEL_RATIO = SEL_BLOCK // CMP_STRIDE
SEL_AGG = (1.0, 2.0, 2.0, 2.0, 1.0)
FORCE_SCORE = 1e9
WINDOW = 512
Q_BLOCK = 128

REL_BUCKETS = 32
REL_MAX_DIST = 1024

POOL_WINDOWS = (2, 4, 8, 16)
POOL_GROUPS = len(POOL_WINDOWS)
POOL_CH = POOL_WIDTH // POOL_GROUPS

SGU_GROUPS = 4
SGU_CH = SGU_WIDTH // SGU_GROUPS
SGU_CHUNK = 128

IN_WIDTHS = (NSA_WIDTH, KV_WIDTH, KV_WIDTH, KV_WIDTH, KV_WIDTH, KV_WIDTH, KV_WIDTH,
             GATE_WIDTH, POOL_WIDTH, SGU_WIDTH, SGU_WIDTH)
IN_WIDTH = sum(IN_WIDTHS)
IN_SPLITS = tuple(int(s) for s in np.cumsum(IN_WIDTHS)[:-1])

MEM_LEN = 256
MEM_HEADS = 4
MEM_HEAD_DIM = D_MODEL // MEM_HEADS

FFN_HIDDEN = ((11 * D_MODEL // 4) // 128) * 128
CONV_WIDTH = 3

EPS = 1e-6
F32 = jnp.float32

kernel_name = 'hymba_nsa_pool_sgu_hybrid'


def rms_norm(x, g):
    xf = x.astype(F32)
    y = xf * lax.rsqrt(jnp.mean(xf * xf, axis=-1, keepdims=True) + EPS)
    return (y * g.astype(F32)).astype(x.dtype)


def layer_norm(x, g, b):
    xf = x.astype(F32)
    mu = jnp.mean(xf, axis=-1, keepdims=True)
    var = jnp.mean(jnp.square(xf - mu), axis=-1, keepdims=True)
    return ((xf - mu) * lax.rsqrt(var + EPS) * g.astype(F32) + b.astype(F32)).astype(x.dtype)


def t5_bucket(dist):
    n = jnp.maximum(dist, 0)
    max_exact = REL_BUCKETS // 2
    nf = jnp.maximum(n, 1).astype(F32)
    large = max_exact + (jnp.log(nf / max_exact) / math.log(REL_MAX_DIST / max_exact)
                         * (REL_BUCKETS - max_exact)).astype(jnp.int32)
    large = jnp.minimum(large, REL_BUCKETS - 1)
    return jnp.where(n < max_exact, n, large)


def masked_softmax(s, mask):
    s = jnp.where(mask, s.astype(F32), -jnp.inf)
    m = jnp.max(s, axis=-1, keepdims=True)
    m = jnp.where(jnp.isfinite(m), m, 0.0)
    e = jnp.where(mask, jnp.exp(s - m), 0.0)
    return e / jnp.maximum(jnp.sum(e, axis=-1, keepdims=True), 1.0)


def compress(kv, pos, w1, b1, w2, b2):
    B, S, G, dk = kv.shape
    r = kv.reshape(B, S // CMP_STRIDE, CMP_STRIDE, G, dk)
    blocks = jnp.concatenate([r[:, :-1], r[:, 1:]], axis=2) + pos[None, None, :, None, :]
    h = jax.nn.gelu(jnp.einsum('bclgd,ldh->bcgh', blocks, w1) + b1)
    return jnp.einsum('bcgh,hd->bcgd', h, w2) + b2


def nsa_mixer(q, k_cmp, v_cmp, k_sel, v_sel, k_win, v_win, gates, rel_bias,
              cmp_pos, cmp_w1, cmp_b1, cmp_w2, cmp_b2):
    B, S, G, Hg, dk = q.shape
    scale = dk ** -0.5
    kc = compress(k_cmp, cmp_pos[0], cmp_w1[0], cmp_b1[0], cmp_w2[0], cmp_b2[0])
    vc = compress(v_cmp, cmp_pos[1], cmp_w1[1], cmp_b1[1], cmp_w2[1], cmp_b2[1])
    NC = kc.shape[1]
    NS = S // SEL_BLOCK
    topk = min(SEL_TOPK, NS)
    ksb = jnp.transpose(k_sel.reshape(B, NS, SEL_BLOCK, G, dk), (0, 3, 1, 2, 4))
    vsb = jnp.transpose(v_sel.reshape(B, NS, SEL_BLOCK, G, dk), (0, 3, 1, 2, 4))
    kwp = jnp.pad(k_win, ((0, 0), (WINDOW, 0), (0, 0), (0, 0)))
    vwp = jnp.pad(v_win, ((0, 0), (WINDOW, 0), (0, 0), (0, 0)))
    cmp_end = jnp.arange(NC) * CMP_STRIDE + CMP_LEN - 1
    sel_start = jnp.arange(NS) * SEL_BLOCK
    jj = jnp.arange(NS)
    bias_g = jnp.transpose(rel_bias.reshape(REL_BUCKETS, G, Hg), (1, 0, 2))
    gidx = jnp.arange(G)
    bidx = jnp.arange(B)
    ti = jnp.arange(Q_BLOCK)
    wj = jnp.arange(Q_BLOCK + WINDOW)
    wdist = ti[:, None] + WINDOW - wj[None, :]
    win_band = (wdist >= 0) & (wdist < WINDOW)
    win_bias = jnp.transpose(rel_bias[t5_bucket(wdist)].reshape(Q_BLOCK, Q_BLOCK + WINDOW, G, Hg), (2, 3, 0, 1))
    nblk = S // Q_BLOCK
    qb = jnp.moveaxis(q.reshape(B, nblk, Q_BLOCK, G, Hg, dk), 1, 0)
    gb = jnp.moveaxis(gates.reshape(B, nblk, Q_BLOCK, G, Hg, 3), 1, 0)

    def block(args):
        qi, gi, i = args
        s0 = i * Q_BLOCK
        t = s0 + ti
        qs = qi * scale
        sc = jnp.einsum('btghd,bcgd->bghtc', qs, kc)
        cdist = t[:, None] - cmp_end[None, :]
        cbias = jnp.transpose(rel_bias[t5_bucket(cdist)].reshape(Q_BLOCK, NC, G, Hg), (2, 3, 0, 1))
        pc = masked_softmax(sc + cbias, cdist >= 0)
        o_cmp = jnp.einsum('bghtc,bcgd->btghd', pc.astype(vc.dtype), vc)
        imp = jnp.pad(jnp.sum(pc, axis=2), ((0, 0), (0, 0), (0, 0), (1, 1)))
        p_sel = sum(w * imp[..., m:m + SEL_RATIO * NS:SEL_RATIO] for m, w in enumerate(SEL_AGG))
        blk_q = t // SEL_BLOCK
        rel_blk = blk_q[:, None] - jj[None, :]
        svalid = sel_start[None, :] <= t[:, None]
        forced = (jj[None, :] == 0) | ((rel_blk >= 0) & (rel_blk < N_LOCAL_SEL))
        score = jnp.where(forced, FORCE_SCORE, jnp.where(svalid, p_sel, -1.0))
        _, sidx = lax.top_k(score, topk)
        ks = ksb[bidx[:, None, None, None], gidx[None, :, None, None], sidx]
        vs = vsb[bidx[:, None, None, None], gidx[None, :, None, None], sidx]
        spos = sidx[..., None] * SEL_BLOCK + jnp.arange(SEL_BLOCK)
        sdist = t[None, None, :, None, None] - spos
        sbias = bias_g[gidx[None, :, None, None, None], t5_bucket(sdist)]
        ss = jnp.einsum('btghd,bgtkld->bghtkl', qs, ks) + jnp.transpose(sbias, (0, 1, 5, 2, 3, 4))
        nsel = topk * SEL_BLOCK
        ps = masked_softmax(ss.reshape(B, G, Hg, Q_BLOCK, nsel), (sdist >= 0).reshape(B, G, 1, Q_BLOCK, nsel))
        o_sel = jnp.einsum('bghtn,bgtnd->btghd', ps.astype(vs.dtype), vs.reshape(B, G, Q_BLOCK, nsel, dk))
        kw = lax.dynamic_slice_in_dim(kwp, s0, Q_BLOCK + WINDOW, axis=1)
        vw = lax.dynamic_slice_in_dim(vwp, s0, Q_BLOCK + WINDOW, axis=1)
        wpos = s0 - WINDOW + wj
        sw = jnp.einsum('btghd,bsgd->bghts', qs, kw) + win_bias
        pw = masked_softmax(sw, win_band & (wpos >= 0)[None, :])
        o_win = jnp.einsum('bghts,bsgd->btghd', pw.astype(vw.dtype), vw)
        return gi[..., 0:1] * o_cmp + gi[..., 1:2] * o_sel + gi[..., 2:3] * o_win

    out = lax.map(block, (qb, gb, jnp.arange(nblk)))
    return jnp.moveaxis(out, 0, 1).reshape(B, S, G * Hg * dk)


def pool_mixer(p, w, scale):
    B, S, C = p.shape
    pf = p.astype(F32)
    cs = jnp.pad(jnp.cumsum(pf, axis=1), ((0, 0), (1, 0), (0, 0)))
    t = jnp.arange(S)
    outs = []
    for gi, win in enumerate(POOL_WINDOWS):
        seg = cs[:, :, gi * POOL_CH:(gi + 1) * POOL_CH]
        lo = jnp.maximum(t + 1 - win, 0)
        cnt = jnp.minimum(t + 1, win).astype(F32)
        outs.append((seg[:, 1:] - seg[:, lo]) / cnt[None, :, None] - pf[:, :, gi * POOL_CH:(gi + 1) * POOL_CH])
    d = jnp.stack(outs, axis=2).astype(p.dtype)
    return jnp.einsum('bsgc,gcd->bsgd', d, w).reshape(B, S, C) * scale


def sgu_mixer(u, v, norm_g, norm_b, w_s, b_s):
    B, S, C = u.shape
    u = jax.nn.gelu(u)
    v = layer_norm(jax.nn.gelu(v), norm_g, norm_b)
    vc = v.reshape(B, S // SGU_CHUNK, SGU_CHUNK, SGU_GROUPS, SGU_CH)
    causal = jnp.tril(jnp.ones((SGU_CHUNK, SGU_CHUNK), dtype=bool))
    ws = jnp.where(causal, w_s, 0.0)
    mixed = jnp.einsum('hts,bnshc->bnthc', ws, vc) + jnp.transpose(b_s)[:, :, None]
    return u * mixed.reshape(B, S, C)


def memory_attention(h, mem_n, wq, wk, wv, wo):
    B, S, D = h.shape
    M = mem_n.shape[1]
    q = (h @ wq).reshape(B, S, MEM_HEADS, MEM_HEAD_DIM)
    k = (mem_n @ wk).reshape(B, M, MEM_HEADS, MEM_HEAD_DIM)
    v = (mem_n @ wv).reshape(B, M, MEM_HEADS, MEM_HEAD_DIM)
    s = jnp.einsum('bshd,bmhd->bhsm', q, k).astype(F32) * (MEM_HEAD_DIM ** -0.5)
    p = jax.nn.softmax(s, axis=-1).astype(v.dtype)
    return jnp.einsum('bhsm,bmhd->bshd', p, v).reshape(B, S, D) @ wo


def conv_ffn(h, w_up, conv_w, conv_b, w_down):
    a = h @ w_up
    a = lax.conv_general_dilated(a, conv_w[:, None, :], window_strides=(1,),
                                 padding=[(CONV_WIDTH - 1, 0)],
                                 dimension_numbers=('NWC', 'WIO', 'NWC'),
                                 feature_group_count=a.shape[-1]) + conv_b
    gate, val = jnp.split(a, 2, axis=-1)
    return (jax.nn.gelu(gate) * val) @ w_down


def setup_inputs(seed: int = 0) -> dict:
    key = jax.random.key(seed)
    ks = iter(jax.random.split(key, 40))

    def nrm(shape, scale):
        return jax.random.normal(next(ks), shape, F32) * scale

    def gain(shape):
        return 1.0 + 0.1 * jax.random.normal(next(ks), shape, F32)

    L, D, H2 = DEPTH, D_MODEL, 2 * FFN_HIDDEN
    return {
        'x': nrm((BATCH, SEQ, D), 1.0),
        'mem': nrm((BATCH, MEM_LEN, D), 1.0),
        'rel_bias': nrm((REL_BUCKETS, NSA_HEADS), 0.2),
        'mix_norm_pre': gain((L, D)),
        'mix_norm_post': gain((L, D)),
        'w_in': nrm((L, D, IN_WIDTH), D ** -0.5),
        'cmp_pos': nrm((L, 2, CMP_LEN, HEAD_DIM), 0.1),
        'cmp_w1': nrm((L, 2, CMP_LEN, HEAD_DIM, CMP_HIDDEN), (CMP_LEN * HEAD_DIM) ** -0.5),
        'cmp_b1': nrm((L, 2, CMP_HIDDEN), 0.01),
        'cmp_w2': nrm((L, 2, CMP_HIDDEN, HEAD_DIM), CMP_HIDDEN ** -0.5),
        'cmp_b2': nrm((L, 2, HEAD_DIM), 0.01),
        'pool_w': nrm((L, POOL_GROUPS, POOL_CH, POOL_CH), POOL_CH ** -0.5),
        'pool_scale': gain((L, POOL_WIDTH)),
        'sgu_norm_g': gain((L, SGU_WIDTH)),
        'sgu_norm_b': nrm((L, SGU_WIDTH), 0.01),
        'sgu_w': nrm((L, SGU_GROUPS, SGU_CHUNK, SGU_CHUNK), SGU_CHUNK ** -0.5),
        'sgu_b': gain((L, SGU_GROUPS, SGU_CHUNK)),
        'w_out': nrm((L, MIX_WIDTH, D), MIX_WIDTH ** -0.5),
        'mem_norm_pre': gain((L, D)),
        'mem_norm_kv': gain((L, D)),
        'mem_norm_post': gain((L, D)),
        'w_mq': nrm((L, D, D), D ** -0.5),
        'w_mk': nrm((L, D, D), D ** -0.5),
        'w_mv': nrm((L, D, D), D ** -0.5),
        'w_mo': nrm((L, D, D), D ** -0.5),
        'ffn_norm_pre': gain((L, D)),
        'ffn_norm_post': gain((L, D)),
        'w_up': nrm((L, D, H2), D ** -0.5),
        'conv_w': nrm((L, CONV_WIDTH, H2), CONV_WIDTH ** -0.5),
        'conv_b': nrm((L, H2), 0.01),
        'w_down': nrm((L, FFN_HIDDEN, D), FFN_HIDDEN ** -0.5),
    }


def reference(x, mem, rel_bias, mix_norm_pre, mix_norm_post, w_in, cmp_pos, cmp_w1, cmp_b1,
              cmp_w2, cmp_b2, pool_w, pool_scale, sgu_norm_g, sgu_norm_b, sgu_w, sgu_b, w_out,
              mem_norm_pre, mem_norm_kv, mem_norm_post, w_mq, w_mk, w_mv, w_mo,
              ffn_norm_pre, ffn_norm_post, w_up, conv_w, conv_b, w_down):
    B, S, D = x.shape
    G, Hg, dk = NSA_KV_GROUPS, HEADS_PER_GROUP, HEAD_DIM
    for l in range(DEPTH):
        h = rms_norm(x, mix_norm_pre[l])
        proj = h @ w_in[l]
        q, kc, vc, ksl, vsl, kw, vw, gt, pin, u, v = jnp.split(proj, IN_SPLITS, axis=-1)
        gates = jax.nn.sigmoid(gt.astype(F32)).astype(x.dtype).reshape(B, S, G, Hg, 3)
        kv = lambda a: a.reshape(B, S, G, dk)
        out_a = nsa_mixer(q.reshape(B, S, G, Hg, dk), kv(kc), kv(vc), kv(ksl), kv(vsl), kv(kw), kv(vw),
                          gates, rel_bias, cmp_pos[l], cmp_w1[l], cmp_b1[l], cmp_w2[l], cmp_b2[l])
        out_b = pool_mixer(pin, pool_w[l], pool_scale[l])
        out_c = sgu_mixer(u, v, sgu_norm_g[l], sgu_norm_b[l], sgu_w[l], sgu_b[l])
        mix = jnp.concatenate([out_a, out_b, out_c], axis=-1) @ w_out[l]
        x = x + rms_norm(mix, mix_norm_post[l])
        h = rms_norm(x, mem_norm_pre[l])
        m = rms_norm(mem, mem_norm_kv[l])
        x = x + rms_norm(memory_attention(h, m, w_mq[l], w_mk[l], w_mv[l], w_mo[l]), mem_norm_post[l])
        h = rms_norm(x, ffn_norm_pre[l])
        x = x + rms_norm(conv_ffn(h, w_up[l], conv_w[l], conv_b[l], w_down[l]), ffn_norm_post[l])
    return x
```

```python
from contextlib import ExitStack
import numpy as np
import concourse.bass as bass
import concourse.mybir as mybir

F32 = mybir.dt.float32
BF16 = mybir.dt.bfloat16
AF = mybir.ActivationFunctionType
ALU = mybir.AluOpType
AX = mybir.AxisListType

COMPUTE = ("pe", "act", "dve", "pool")
DMAQ = ("sp", "actq", "poolq")
QENG = {"sp": "sp", "actq": "act", "poolq": "pool"}
NSEM_DMA = 12
SAME_ENGINE_SYNC = True


class Buf:
    __slots__ = ("name", "writers", "readers")

    def __init__(self, name):
        self.name = name
        self.writers = []
        self.readers = []


class Op:
    __slots__ = ("stream", "emit", "deps", "signal", "dma", "idx")


class Sched:
    def __init__(self, nc, es):
        self.nc = nc
        self.es = es
        self.streams = {s: [] for s in ("pe", "act", "dve", "pool", "sp")}
        self.sems = {}
        for s in COMPUTE:
            self.sems[s] = es.enter_context(nc.semaphore("c_" + s))
        self.dsems = {}
        self.dcount = {}
        self.dnext = {}
        for q in DMAQ:
            self.dsems[q] = [es.enter_context(nc.semaphore(f"d_{q}_{i}")) for i in range(NSEM_DMA)]
            self.dcount[q] = [0] * NSEM_DMA
            self.dnext[q] = 0
        self.out_events = []

    def _collect(self, reads, writes, nowaw):
        deps = []
        for b in reads:
            deps.extend(b.writers)
        for b in writes:
            if not nowaw:
                deps.extend(b.writers)
            deps.extend(b.readers)
        return deps

    def _commit(self, ev, reads, writes, nowaw):
        for b in reads:
            if ev[0] == "c":
                b.readers = [r for r in b.readers if not (r[0] == "c" and r[1] == ev[1])]
            b.readers.append(ev)
        for b in writes:
            if nowaw:
                if ev[0] == "c":
                    b.writers = [w for w in b.writers if not (w[0] == "c" and w[1] == ev[1])]
                b.writers.append(ev)
            else:
                b.writers = [ev]
            b.readers = []

    def op(self, eng, emit, reads=(), writes=(), nowaw=False):
        o = Op()
        o.stream = eng
        o.emit = emit
        o.deps = self._collect(reads, writes, nowaw)
        o.signal = False
        o.dma = None
        o.idx = len([x for x in self.streams[eng] if x.dma is None and x.emit is not None]) if False else None
        lst = self.streams[eng]
        o.idx = len(lst)
        lst.append(o)
        ev = ("c", eng, o.idx)
        self._commit(ev, reads, writes, nowaw)
        return ev

    def dma(self, q, out, in_, reads=(), writes=(), nowaw=False, is_output=False, **kw):
        o = Op()
        stream = QENG[q]
        o.stream = stream
        k = self.dnext[q]
        self.dnext[q] = (k + 1) % NSEM_DMA
        self.dcount[q][k] += 1
        val = 16 * self.dcount[q][k]
        ev = ("d", q, k, val)
        o.deps = self._collect(reads, writes, nowaw)
        if self.dcount[q][k] > 1:
            o.deps.append(("d", q, k, val - 16))
        o.dma = (q, k, val)
        o.signal = False
        o.emit = lambda e: e.dma_start(out=out, in_=in_, **kw)
        lst = self.streams[stream]
        o.idx = len(lst)
        lst.append(o)
        self._commit(ev, reads, writes, nowaw)
        if is_output:
            self.out_events.append(ev)
        return ev

    def finalize(self):
        nc = self.nc
        fin = Op()
        fin.stream = "sp"; fin.emit = None; fin.deps = list(self.out_events); fin.signal = False; fin.dma = None
        fin.idx = len(self.streams["sp"])
        self.streams["sp"].append(fin)
        for s, lst in self.streams.items():
            for o in lst:
                for d in o.deps:
                    if d[0] == "c":
                        if d[1] == s and (s == "pe" or not SAME_ENGINE_SYNC):
                            continue
                        self.streams[d[1]][d[2]].signal = True
        sigcount = {}
        for s in COMPUTE:
            c = 0
            arr = []
            for o in self.streams[s]:
                if o.signal:
                    c += 1
                arr.append(c)
            sigcount[s] = arr
        block = self.es.enter_context(nc.Block())
        engs = {"pe": block.tensor, "act": block.scalar, "dve": block.vector, "pool": block.gpsimd, "sp": block.sync}

        def make(sname):
            lst = self.streams[sname]

            def body(e):
                waited_c = {s: 0 for s in COMPUTE}
                waited_d = {}
                for o in lst:
                    for d in o.deps:
                        if d[0] == "c":
                            if d[1] == sname and (sname == "pe" or not SAME_ENGINE_SYNC):
                                continue
                            v = sigcount[d[1]][d[2]]
                            if v > waited_c[d[1]]:
                                e.wait_ge(self.sems[d[1]], v)
                                waited_c[d[1]] = v
                        else:
                            key = (d[1], d[2])
                            if d[3] > waited_d.get(key, 0):
                                e.wait_ge(self.dsems[d[1]][d[2]], d[3])
                                waited_d[key] = d[3]
                    if o.emit is None:
                        continue
                    ins = o.emit(e)
                    if o.dma is not None:
                        q, k, val = o.dma
                        ins.then_inc(self.dsems[q][k], 16)
                    elif o.signal:
                        ins.then_inc(self.sems[sname], 1)
            return body

        for sname in ("sp", "pe", "act", "dve", "pool"):
            engs[sname](make(sname))


class TB:
    def __init__(self, t, name):
        self.t = t
        self.b = Buf(name)

    def __getitem__(self, k):
        return self.t[k]


class Ctx:
    def __init__(self, name):
        self.es = ExitStack()
        self.nc = bass.Bass("TRN2", target_bir_lowering=False)
        self.S = Sched(self.nc, self.es)
        self.n = 0

    def sb(self, shape, dt, name=None):
        self.n += 1
        name = name or f"sb{self.n}"
        return TB(self.es.enter_context(self.nc.sbuf_tensor(name, list(shape), dt)), name)

    def ps(self, shape, dt, name=None):
        self.n += 1
        name = name or f"ps{self.n}"
        return TB(self.es.enter_context(self.nc.psum_tensor(name, list(shape), dt)), name)

    def din(self, name, shape, dt):
        return TB(self.nc.dram_tensor(name, list(shape), dt, kind="ExternalInput").ap(), name)

    def dout(self, name, shape, dt):
        return TB(self.nc.dram_tensor(name, list(shape), dt, kind="ExternalOutput").ap(), name)

    def dscr(self, name, shape, dt):
        return TB(self.nc.dram_tensor(name, list(shape), dt, kind="Internal").ap(), name)

    def finish(self):
        self.S.finalize()
        self.es.close()
        return self.nc


def bcast_rows(ap1d_tb, n, parts=128):
    a = ap1d_tb.t
    return bass.AP(tensor=a.tensor, offset=a.offset, ap=[[0, parts], [1, n]])


D = 1024
NT = 16
TL = NT * 128
EPS = 1e-6
INW = 2072
FMW = 1536
TMW = 536


def build_A():
    C = Ctx("A"); S = C.S
    x = C.din("x", [TL, D], F32)
    gpre = C.din("gpre", [128, 8], F32)
    w = C.din("w", [D, INW], F32)
    ident = C.din("ident", [128, 128], F32)
    sgug = C.din("sgug", [256], F32)
    sgub = C.din("sgub", [256], F32)
    sguwT = C.din("sguwT", [4, 128, 128], F32)
    tril = C.din("tril", [128, 128], F32)
    sgubs = C.din("sgubs", [512], F32)
    qT = C.dout("qT", [512, TL], BF16)
    kvT = C.dout("kvT", [512, TL], BF16)
    pinT = C.dout("pinT", [256, TL], F32)
    vtm = C.dout("vtm", [TL, 256], BF16)
    gate = C.dout("gate", [TL, 24], F32)
    ocT = C.dout("ocT", [256, TL], BF16)

    Wb = C.sb([128, 8, INW], BF16, "Wb")
    stage = [C.sb([128, INW], F32, f"stage{i}") for i in range(2)]
    g_sb = C.sb([128, 8], F32, "g_sb")
    id_b = C.sb([128, 128], BF16, "id_b")
    sg_g = C.sb([128, 256], F32, "sg_g")
    sg_b = C.sb([128, 256], F32, "sg_b")
    bsb = C.sb([128, 4, 128], F32, "bsb")
    wst_f = C.sb([128, 4, 128], F32, "wst_f")
    tril_s = C.sb([128, 128], F32, "tril_s")
    wsT = C.sb([128, 4, 128], BF16, "wsT")

    eps_c = C.sb([128, 1], F32, "eps_c")
    S.op("dve", lambda e: e.memset(eps_c[:, :], EPS), writes=[eps_c.b])
    S.dma("sp", g_sb[:, :], gpre[:, :], writes=[g_sb.b])
    S.dma("poolq", id_b[:, :], ident[:, :], writes=[id_b.b])
    S.dma("sp", sg_g[:, :], bcast_rows(sgug, 256), writes=[sg_g.b])
    S.dma("sp", sg_b[:, :], bcast_rows(sgub, 256), writes=[sg_b.b])
    S.dma("sp", bsb[:, :, :].rearrange("p h t -> p (h t)"), bcast_rows(sgubs, 512), writes=[bsb.b])
    S.dma("sp", wst_f[:, :, :], sguwT[:, :, :].rearrange("h s t -> s h t"), writes=[wst_f.b])
    S.dma("sp", tril_s[:, :], tril[:, :], writes=[tril_s.b])
    for h in range(4):
        S.op("dve", lambda e, h=h: e.tensor_tensor(out=wsT[:, h, :], in0=wst_f[:, h, :], in1=tril_s[:, :], op=ALU.mult),
             reads=[wst_f.b, tril_s.b], writes=[wsT.b], nowaw=True)
    for kc in range(8):
        st = stage[kc % 2]
        S.dma("sp", st[:, :], w[kc * 128:(kc + 1) * 128, :], writes=[st.b])
        S.op("dve", lambda e, kc=kc, st=st: e.tensor_scalar(out=Wb[:, kc, :], in0=st[:, :], scalar1=g_sb[:, kc:kc + 1],
                                                           scalar2=None, op0=ALU.mult),
             reads=[st.b, g_sb.b], writes=[Wb.b], nowaw=True)

    xt = [C.sb([128, D], F32, f"xt{i}") for i in range(2)]
    sq = C.sb([128, D], BF16, "sq")
    ss = [C.sb([128, 1], F32, f"ss{i}") for i in range(2)]
    rs = [C.sb([128, 1], F32, f"rs{i}") for i in range(2)]
    hn = [C.sb([128, D], BF16, f"hn{i}") for i in range(2)]
    hT = [C.sb([128, 8, 512], BF16, f"hT{i}") for i in range(2)]
    psT = [C.ps([128, 8, 128], BF16, f"psT{i}") for i in range(2)]
    psF = [C.ps([128, 512], F32, f"psF{i}") for i in range(3)]
    psA = C.ps([128, 512], F32, "psA")
    psB = C.ps([128, 512], F32, "psB")
    psC = C.ps([128, 512], F32, "psC")
    fm_b = [C.sb([128, 512], BF16, f"fm_b{i}") for i in range(3)]
    fm_f = [C.sb([128, 512], F32, f"fm_f{i}") for i in range(2)]
    gu = [C.sb([128, 2, 512], F32, f"gu{i}") for i in range(2)]
    vt_b = [C.sb([128, 256], BF16, f"vt_b{i}") for i in range(2)]
    gt_f = [C.sb([128, 24], F32, f"gt_f{i}") for i in range(2)]
    vg = [C.sb([128, 256], F32, f"vg{i}") for i in range(2)]
    vc_ = [C.sb([128, 256], F32, f"vc{i}") for i in range(2)]
    vsq = C.sb([128, 256], F32, "vsq")
    st1 = [C.sb([128, 4], F32, f"st1{i}") for i in range(2)]
    vln = [C.sb([128, 256], BF16, f"vln{i}") for i in range(2)]
    tmp = [C.sb([128, 2, 128], F32, f"tmp{i}") for i in range(2)]
    oc = [C.sb([128, 2, 512], BF16, f"oc{i}") for i in range(2)]

    nfm = 0
    for grp in range(NT // 4):
        hTg = hT[grp % 2]
        gug = gu[grp % 2]
        ocg = oc[grp % 2]
        for tt in range(4):
            t = grp * 4 + tt
            xx = xt[t % 2]; s_ = ss[t % 2]; r_ = rs[t % 2]; hh = hn[t % 2]; pT = psT[t % 2]
            S.dma("sp", xx[:, :], x[t * 128:(t + 1) * 128, :], writes=[xx.b])
            S.op("act", lambda e, xx=xx, s_=s_: e.activation(out=sq[:, :], in_=xx[:, :], func=AF.Square, accum_out=s_[:, :]),
                 reads=[xx.b], writes=[sq.b, s_.b])
            S.op("act", lambda e, s_=s_, r_=r_: e.activation(out=r_[:, :], in_=s_[:, :], func=AF.Sqrt, scale=1.0 / D, bias=eps_c[:, :]),
                 reads=[s_.b, eps_c.b], writes=[r_.b])
            S.op("dve", lambda e, r_=r_: e.reciprocal(out=r_[:, :], in_=r_[:, :]), reads=[r_.b], writes=[r_.b])
            S.op("dve", lambda e, xx=xx, r_=r_, hh=hh: e.tensor_scalar(out=hh[:, :], in0=xx[:, :], scalar1=r_[:, :], scalar2=None,
                                                                     op0=ALU.mult), reads=[xx.b, r_.b], writes=[hh.b])
            for kc in range(8):
                S.op("pe", lambda e, kc=kc, hh=hh, pT=pT: e.transpose(out=pT[:, kc, :], in_=hh[:, kc * 128:(kc + 1) * 128], identity=id_b[:, :]),
                     reads=[hh.b, id_b.b], writes=[pT.b], nowaw=(kc > 0))
            S.op("dve", lambda e, pT=pT, tt=tt, hTg=hTg: e.tensor_copy(out=hTg[:, :, tt * 128:(tt + 1) * 128], in_=pT[:, :, :]),
                 reads=[pT.b], writes=[hTg.b], nowaw=True)
        for ocn in range(12):
            pf = psF[nfm % 3]
            for kc in range(8):
                S.op("pe", lambda e, pf=pf, kc=kc, ocn=ocn, hTg=hTg: e.matmul(pf[:, :], lhsT=Wb[:, kc, ocn * 128:(ocn + 1) * 128], rhs=hTg[:, kc, :],
                                                                          start=(kc == 0), stop=(kc == 7)),
                     reads=[Wb.b, hTg.b], writes=[pf.b])
            cols = slice(grp * 512, (grp + 1) * 512)
            if ocn < 4:
                fb = fm_b[nfm % 3]
                S.op("act", lambda e, pf=pf, fb=fb: e.activation(out=fb[:, :], in_=pf[:, :], func=AF.Copy, scale=0.125),
                     reads=[pf.b], writes=[fb.b])
                S.dma("poolq", qT[ocn * 128:(ocn + 1) * 128, cols], fb[:, :], reads=[fb.b], writes=[qT.b], nowaw=True, is_output=True)
            elif ocn < 8:
                fb = fm_b[nfm % 3]
                S.op("dve", lambda e, pf=pf, fb=fb: e.tensor_copy(out=fb[:, :], in_=pf[:, :]), reads=[pf.b], writes=[fb.b])
                S.dma("poolq", kvT[(ocn - 4) * 128:(ocn - 3) * 128, cols], fb[:, :], reads=[fb.b], writes=[kvT.b], nowaw=True, is_output=True)
            elif ocn < 10:
                ff = fm_f[nfm % 2]
                S.op("act", lambda e, pf=pf, ff=ff: e.activation(out=ff[:, :], in_=pf[:, :], func=AF.Copy),
                     reads=[pf.b], writes=[ff.b])
                S.dma("poolq", pinT[(ocn - 8) * 128:(ocn - 7) * 128, cols], ff[:, :], reads=[ff.b], writes=[pinT.b], nowaw=True, is_output=True)
            else:
                j = ocn - 10
                S.op("act", lambda e, pf=pf, j=j, gug=gug: e.activation(out=gug[:, j, :], in_=pf[:, :], func=AF.Gelu_apprx_tanh),
                     reads=[pf.b], writes=[gug.b], nowaw=(j > 0))
            nfm += 1
        for tt in range(4):
            t = grp * 4 + tt
            for kc in range(8):
                S.op("pe", lambda e, kc=kc, tt=tt, hTg=hTg: e.matmul(psA[:, :], lhsT=hTg[:, kc, tt * 128:(tt + 1) * 128], rhs=Wb[:, kc, FMW:FMW + 512],
                                                                  start=(kc == 0), stop=(kc == 7)),
                     reads=[Wb.b, hTg.b], writes=[psA.b])
            for kc in range(8):
                S.op("pe", lambda e, kc=kc, tt=tt, hTg=hTg: e.matmul(psB[:, 0:24], lhsT=hTg[:, kc, tt * 128:(tt + 1) * 128], rhs=Wb[:, kc, FMW + 512:FMW + 536],
                                                                  start=(kc == 0), stop=(kc == 7)),
                     reads=[Wb.b, hTg.b], writes=[psB.b])
            vb = vt_b[t % 2]; gf = gt_f[t % 2]; vg_ = vg[t % 2]; vcc = vc_[t % 2]; s1 = st1[t % 2]; vl = vln[t % 2]; tm = tmp[t % 2]
            S.op("dve", lambda e, vb=vb: e.tensor_copy(out=vb[:, :], in_=psA[:, 0:256]), reads=[psA.b], writes=[vb.b])
            S.dma("poolq", vtm[t * 128:(t + 1) * 128, :], vb[:, :], reads=[vb.b], writes=[vtm.b], nowaw=True, is_output=True)
            S.op("act", lambda e, gf=gf: e.activation(out=gf[:, :], in_=psB[:, 0:24], func=AF.Sigmoid), reads=[psB.b], writes=[gf.b])
            S.dma("poolq", gate[t * 128:(t + 1) * 128, :], gf[:, :], reads=[gf.b], writes=[gate.b], nowaw=True, is_output=True)
            S.op("act", lambda e, vg_=vg_, s1=s1: e.activation(out=vg_[:, :], in_=psA[:, 256:512], func=AF.Gelu_apprx_tanh, accum_out=s1[:, 0:1]),
                 reads=[psA.b], writes=[vg_.b, s1.b])
            S.op("dve", lambda e, s1=s1: e.tensor_scalar(out=s1[:, 1:2], in0=s1[:, 0:1], scalar1=-1.0 / 256, scalar2=None, op0=ALU.mult),
                 reads=[s1.b], writes=[s1.b])
            S.op("dve", lambda e, vg_=vg_, vcc=vcc, s1=s1: e.tensor_scalar(out=vcc[:, :], in0=vg_[:, :], scalar1=s1[:, 1:2], scalar2=None, op0=ALU.add),
                 reads=[vg_.b, s1.b], writes=[vcc.b])
            S.op("act", lambda e, vcc=vcc, s1=s1: e.activation(out=vsq[:, :], in_=vcc[:, :], func=AF.Square, accum_out=s1[:, 2:3]),
                 reads=[vcc.b], writes=[vsq.b, s1.b])
            S.op("act", lambda e, s1=s1: e.activation(out=s1[:, 3:4], in_=s1[:, 2:3], func=AF.Sqrt, scale=1.0 / 256, bias=eps_c[:, :]),
                 reads=[s1.b, eps_c.b], writes=[s1.b])
            S.op("dve", lambda e, s1=s1: e.reciprocal(out=s1[:, 3:4], in_=s1[:, 3:4]), reads=[s1.b], writes=[s1.b])
            S.op("dve", lambda e, vcc=vcc, s1=s1: e.tensor_scalar(out=vcc[:, :], in0=vcc[:, :], scalar1=s1[:, 3:4], scalar2=None, op0=ALU.mult),
                 reads=[vcc.b, s1.b], writes=[vcc.b])
            S.op("dve", lambda e, vcc=vcc: e.tensor_tensor(out=vcc[:, :], in0=vcc[:, :], in1=sg_g[:, :], op=ALU.mult),
                 reads=[vcc.b, sg_g.b], writes=[vcc.b])
            S.op("dve", lambda e, vcc=vcc, vl=vl: e.tensor_tensor(out=vl[:, :], in0=vcc[:, :], in1=sg_b[:, :], op=ALU.add),
                 reads=[vcc.b, sg_b.b], writes=[vl.b])
            for h in range(4):
                j = h // 2
                S.op("pe", lambda e, h=h, j=j, vl=vl: e.matmul(psC[:, h * 128:(h + 1) * 128], lhsT=vl[:, j * 128:(j + 1) * 128], rhs=wsT[:, h, :],
                                                            start=True, stop=True),
                     reads=[vl.b, wsT.b], writes=[psC.b], nowaw=(h > 0))
            for h in range(4):
                j = h // 2; r = h % 2
                P = slice(r * 64, r * 64 + 64)
                S.op("dve", lambda e, h=h, j=j, P=P, tm=tm: e.tensor_tensor(out=tm[P, j, :], in0=psC[P, h * 128:(h + 1) * 128], in1=bsb[P, h, :], op=ALU.add),
                     reads=[psC.b, bsb.b], writes=[tm.b], nowaw=True)
                S.op("dve", lambda e, j=j, P=P, tm=tm, tt=tt, gug=gug, ocg=ocg: e.tensor_tensor(out=ocg[P, j, tt * 128:(tt + 1) * 128], in0=tm[P, j, :],
                                                                                           in1=gug[P, j, tt * 128:(tt + 1) * 128], op=ALU.mult),
                     reads=[tm.b, gug.b], writes=[ocg.b], nowaw=True)
        for j in range(2):
            S.dma("poolq", ocT[j * 128:(j + 1) * 128, grp * 512:(grp + 1) * 512], ocg[:, j, :], reads=[ocg.b], writes=[ocT.b], nowaw=True, is_output=True)
    return C.finish()


_SPL = np.cumsum([0, 512, 128, 128, 128, 128, 128, 128, 24, 256, 256, 256])
_NAMES = ["q", "kc", "vc", "ksl", "vsl", "kw", "vw", "gt", "pin", "u", "v"]
_COL = {n: np.arange(_SPL[i], _SPL[i + 1]) for i, n in enumerate(_NAMES)}
_PERM = np.concatenate([_COL[n] for n in ["q", "kc", "vc", "ksl", "kw", "pin", "u", "vsl", "vw", "v", "gt"]])
_IDENT = np.eye(128, dtype=np.float32)
_TRIL_ST = np.triu(np.ones((128, 128), np.float32))


def local_rows(a, core):
    b, cp = divmod(core, 4)
    t = a[b].reshape((64, 128) + a.shape[2:])[cp::4]
    return np.ascontiguousarray(t.reshape((TL,) + a.shape[2:]))


def gcol(g):
    return np.ascontiguousarray(g.reshape(-1, 128).T)


def prep_A(inp, l, x, core):
    return {
        "x": local_rows(x, core),
        "gpre": gcol(inp["mix_norm_pre"][l]),
        "w": np.ascontiguousarray(inp["w_in"][l][:, _PERM]),
        "ident": _IDENT,
        "sgug": np.ascontiguousarray(inp["sgu_norm_g"][l]),
        "sgub": np.ascontiguousarray(inp["sgu_norm_b"][l]),
        "sguwT": np.ascontiguousarray(np.transpose(inp["sgu_w"][l], (0, 2, 1))),
        "tril": _TRIL_ST,
        "sgubs": np.ascontiguousarray(inp["sgu_b"][l].reshape(-1)),
    }


NEG = 30000.0


def build_B1():
    C = Ctx("B1"); S = C.S
    qT_in = C.din("qT_in", [128, 4, TL], BF16)
    gate_in = C.din("gate_in", [TL, 24], F32)
    kc_in = C.din("kc_in", [64, 2, 8192], BF16)
    vc_in = C.din("vc_in", [64, 2, 8192], BF16)
    ksl_in = C.din("ksl_in", [128, 8192], BF16)
    vsl_in = C.din("vsl_in", [128, 64, 2, 64], BF16)
    kw_in = C.din("kw_in", [128, NT, 640], BF16)
    vw_in = C.din("vw_in", [128, NT, 5, 2, 64], BF16)
    pin_in = C.din("pin_in", [128, 2, NT, 144], F32)
    oc_in = C.din("oc_in", [128, 2, TL], BF16)
    x_in = C.din("x_in", [TL, D], F32)
    relb = C.din("relb", [32, 8], F32)
    ohs = C.din("ohs", [33, 1536], F32)
    ohw = C.din("ohw", [33, 1280], F32)
    ohw0 = C.din("ohw0", [33, 1280], F32)
    ohc = C.din("ohc", [33, 88, 128], F32)
    expand = C.din("expand", [128, 8192], F32)
    vfc = C.din("vfc", [128, 2, 9], F32)
    ident = C.din("ident", [128, 128], F32)
    invc0 = C.din("invc0", [128, 2, 128], F32)
    cpos = C.din("cpos", [64, 2, 32], F32)
    cw1 = C.din("cw1", [64, 2, 32, 128], F32)
    cb1 = C.din("cb1", [128, 2], F32)
    cw2 = C.din("cw2", [128, 2, 64], F32)
    cb2 = C.din("cb2", [2, 64], F32)
    poolw = C.din("poolw", [4, 64, 64], F32)
    pools = C.din("pools", [128, 2], F32)
    wo = C.din("wo", [D, D], F32)
    gpost = C.din("gpost", [D], F32)
    x_out = C.dout("x_out", [TL, D], F32)
    ebs_d = C.dscr("ebs_d", [8, 1536], BF16)
    ebw_d = C.dscr("ebw_d", [8, 1280], BF16)
    ebw0_d = C.dscr("ebw0_d", [8, 1280], BF16)

    rot = [C.ps([128, 512], F32, f"rot{i}") for i in range(4)]
    acc = [C.ps([128, 512], F32, f"acc{i}") for i in range(2)]
    psT = C.ps([128, 8, 128], BF16, "psT")
    psM = C.ps([128, 512], F32, "psM")
    rc = [0]

    def nrot():
        rc[0] += 1
        return rot[rc[0] % 4]

    eps_c = C.sb([128, 1], F32, "eps_c")
    S.op("dve", lambda e: e.memset(eps_c[:, :], EPS), writes=[eps_c.b])
    id_f = C.sb([128, 128], F32, "id_f")
    id_b = C.sb([128, 128], BF16, "id_b")
    S.dma("sp", id_f[:, :], ident[:, :], writes=[id_f.b])
    S.dma("poolq", id_b[:, :], ident[:, :], writes=[id_b.b])
    rb = C.sb([33, 8], F32, "rb")
    rbx = C.sb([33, 8], F32, "rbx")
    b31 = C.sb([32, 8], F32, "b31")
    S.dma("sp", rb[0:32, :], relb[:, :], writes=[rb.b])
    S.op("dve", lambda e: e.memset(rb[32:33, :], -NEG), writes=[rb.b], nowaw=True)
    S.dma("sp", b31[:, :], bass.AP(tensor=relb.t.tensor, offset=31 * 8, ap=[[0, 32], [1, 8]]), writes=[b31.b])
    S.op("dve", lambda e: e.tensor_tensor(out=rbx[0:32, :], in0=rb[0:32, :], in1=b31[:, :], op=ALU.subtract), reads=[rb.b, b31.b], writes=[rbx.b])
    S.op("dve", lambda e: e.memset(rbx[32:33, :], -NEG), writes=[rbx.b], nowaw=True)

    Tsel = C.sb([128, 11, 8, 128], BF16, "Tsel")
    Twin = C.sb([128, 5, 8, 128], BF16, "Twin")
    Tband = C.sb([128, 88, 8], F32, "Tband")
    oh_sb = C.sb([33, 512], F32, "oh_sb")
    eb_sb = C.sb([8, 1536], BF16, "eb_sb")

    def build_row(oh_d, L, lhs, scr):
        for j in range(0, L, 512):
            wd = min(512, L - j)
            S.dma("sp", oh_sb[:, 0:wd], oh_d[:, j:j + wd], writes=[oh_sb.b])
            p = nrot()
            S.op("pe", lambda e, p=p, j=j, wd=wd: e.matmul(p[0:8, 0:wd], lhsT=lhs[:, :], rhs=oh_sb[:, 0:wd], start=True, stop=True),
                 reads=[lhs.b, oh_sb.b], writes=[p.b])
            S.op("act", lambda e, p=p, j=j, wd=wd: e.activation(out=eb_sb[:, j:j + wd], in_=p[0:8, 0:wd], func=AF.Exp),
                 reads=[p.b], writes=[eb_sb.b], nowaw=True)
        S.dma("sp", scr[:, :], eb_sb[:, 0:L], reads=[eb_sb.b], writes=[scr.b])

    def toep(dst, ntile, scr, L, step):
        for i in range(ntile):
            S.dma("sp", dst[:, i, :, :], bass.AP(tensor=scr.t.tensor, offset=step * i, ap=[[1, 128], [L, 8], [1, 128]]),
                  reads=[scr.b], writes=[dst.b], nowaw=True)

    build_row(ohs, 1536, rbx, ebs_d)
    toep(Tsel, 11, ebs_d, 1536, 128)
    build_row(ohw0, 1280, rbx, ebw0_d)
    toep(Twin, 5, ebw0_d, 1280, 256)
    build_row(ohw, 1280, rbx, ebw_d)
    ohc_sb = [C.sb([33, 4, 128], F32, f"ohc_sb{i}") for i in range(2)]
    pb = [nrot(), nrot()]
    for ch in range(22):
        o_ = ohc_sb[ch % 2]
        S.dma("sp", o_[:, :, :], ohc[:, ch * 4:(ch + 1) * 4, :], writes=[o_.b])
        for k in range(4):
            c3 = ch * 4 + k
            p = pb[c3 // 64]
            S.op("pe", lambda e, p=p, o_=o_, k=k, c3=c3: e.matmul(p[:, (c3 % 64) * 8:(c3 % 64) * 8 + 8], lhsT=o_[:, k, :], rhs=rbx[:, :], start=True, stop=True),
                 reads=[o_.b, rbx.b], writes=[p.b], nowaw=True)
    S.op("act", lambda e: e.activation(out=Tband[:, 0:64, :].rearrange("p c h -> p (c h)"), in_=pb[0][:, 0:512], func=AF.Exp),
         reads=[pb[0].b], writes=[Tband.b])
    S.op("act", lambda e: e.activation(out=Tband[:, 64:88, :].rearrange("p c h -> p (c h)"), in_=pb[1][:, 0:192], func=AF.Exp),
         reads=[pb[1].b], writes=[Tband.b], nowaw=True)

    k_sb = C.sb([128, 8192], BF16, "k_sb")
    v_sb = C.sb([128, 64, 2, 65], BF16, "v_sb")
    S.op("pool", lambda e: e.memset(v_sb[:, :, :, 64:65], 1.0), writes=[v_sb.b])
    for hf in range(4):
        S.dma("sp", k_sb[:, hf * 2048:(hf + 1) * 2048], ksl_in[:, hf * 2048:(hf + 1) * 2048], writes=[k_sb.b], nowaw=True)
    for q4 in range(4):
        S.dma("sp", v_sb[:, q4 * 16:(q4 + 1) * 16, :, 0:64], vsl_in[:, q4 * 16:(q4 + 1) * 16, :, :], writes=[v_sb.b], nowaw=True)
    exp_sb = C.sb([128, 64, 128], BF16, "exp_sb")
    for q4 in range(4):
        S.dma("poolq", exp_sb[:, q4 * 16:(q4 + 1) * 16, :].rearrange("p a b -> p (a b)"), expand[:, q4 * 2048:(q4 + 1) * 2048], writes=[exp_sb.b], nowaw=True)
    vf_sb = C.sb([128, 2, 9], F32, "vf_sb")
    S.dma("sp", vf_sb[:, :, :], vfc[:, :, :], writes=[vf_sb.b])
    invc_sb = C.sb([128, 2, 128], F32, "invc_sb")
    S.dma("sp", invc_sb[:, :, :], invc0[:, :, :], writes=[invc_sb.b])
    ps_sb = C.sb([128, 2], F32, "ps_sb")
    S.dma("sp", ps_sb[:, :], pools[:, :], writes=[ps_sb.b])
    gp_sb = C.sb([128, D], F32, "gp_sb")
    S.dma("sp", gp_sb[:, :], bcast_rows(gpost, D), writes=[gp_sb.b])
    pwf = C.sb([128, 2, 128], F32, "pwf")
    pwb = C.sb([128, 2, 128], BF16, "pwb")
    S.op("dve", lambda e: e.memset(pwf[:, :, :], 0.0), writes=[pwf.b])
    for gi in range(4):
        r = gi % 2
        S.dma("sp", pwf[r * 64:(r + 1) * 64, gi // 2, r * 64:(r + 1) * 64], poolw[gi, :, :], writes=[pwf.b], nowaw=True)
    S.op("dve", lambda e: e.tensor_copy(out=pwb[:, :, :], in_=pwf[:, :, :]), reads=[pwf.b], writes=[pwb.b])
    wo_sb = C.sb([128, 8, D], BF16, "wo_sb")
    for kc in range(8):
        S.dma("poolq", wo_sb[:, kc, :], wo[kc * 128:(kc + 1) * 128, :], writes=[wo_sb.b], nowaw=True)

    w1_sb = C.sb([64, 32, 128], BF16, "w1_sb")
    pos_sb = C.sb([64, 2, 32], BF16, "pos_sb")
    w2_sb = C.sb([128, 2, 64], BF16, "w2_sb")
    b1_sb = C.sb([128, 2], F32, "b1_sb")
    b2c = C.sb([128, 1], F32, "b2c")
    w2f = C.sb([128, 64], F32, "w2f")
    w2pad = C.sb([128, 2, 128], BF16, "w2pad")
    b2bc = C.sb([128, 64], F32, "b2bc")
    bias_c = C.sb([128, 2], F32, "bias_c")
    S.dma("poolq", pos_sb[:, :, :], cpos[:, :, :], writes=[pos_sb.b])
    S.dma("poolq", w2_sb[:, :, :], cw2[:, :, :], writes=[w2_sb.b])
    S.dma("sp", b1_sb[:, :], cb1[:, :], writes=[b1_sb.b])
    for g in range(2):
        S.dma("sp", b2c[g * 64:(g + 1) * 64, :], bass.AP(tensor=cb2.t.tensor, offset=0, ap=[[1, 64], [1, 1]]), writes=[b2c.b], nowaw=True)
    S.dma("sp", w2f[:, :], cw2[:, 0, :], writes=[w2f.b])
    S.op("dve", lambda e: e.memset(w2pad[:, :, :], 0.0), writes=[w2pad.b])
    for g in range(2):
        S.op("dve", lambda e, g=g: e.tensor_copy(out=w2pad[:, g, g * 64:(g + 1) * 64], in_=w2f[:, :]), reads=[w2f.b], writes=[w2pad.b], nowaw=True)
    S.dma("sp", b2bc[:, :], bass.AP(tensor=cb2.t.tensor, offset=64, ap=[[0, 128], [1, 64]]), writes=[b2bc.b])
    kcmp = C.sb([128, 512], BF16, "kcmp")
    vcmp = C.sb([128, 4, 2, 64], BF16, "vcmp")
    cst = [C.sb([64, 4112], BF16, "cst0")]
    hTs = [C.sb([128, 512], BF16, f"hTs{i}") for i in range(2)]
    for i in range(2):
        S.op("dve", lambda e, i=i: e.memset(hTs[i][:, :], 0.0), writes=[hTs[i].b])
    ci = 0
    for a in range(2):
        S.dma("poolq", w1_sb[:, :, :], cw1[:, a, :, :], writes=[w1_sb.b])
        p = nrot()
        for l in range(32):
            S.op("pe", lambda e, p=p, a=a, l=l: e.matmul(p[:, 0:1], lhsT=w1_sb[:, l, :], rhs=pos_sb[:, a, l:l + 1], start=(l == 0), stop=(l == 31)),
                 reads=[w1_sb.b, pos_sb.b], writes=[p.b])
        S.op("dve", lambda e, p=p, a=a: e.tensor_tensor(out=bias_c[:, a:a + 1], in0=p[:, 0:1], in1=b1_sb[:, a:a + 1], op=ALU.add),
             reads=[p.b, b1_sb.b], writes=[bias_c.b], nowaw=True)
        src = kc_in if a == 0 else vc_in
        for g in range(2):
            st = cst[0]; hT_ = hTs[ci % 2]; ci += 1
            p = nrot()
            for hf in range(2):
                ntok = 4112 if hf == 0 else 4096
                ncb = 256 if hf == 0 else 255
                S.dma("sp", st[:, 0:ntok], src[:, g, hf * 4096:hf * 4096 + ntok], writes=[st.b])
                for l in range(32):
                    S.op("pe", lambda e, p=p, a=a, l=l, st=st, hf=hf, ncb=ncb: e.matmul(p[:, hf * 256:hf * 256 + ncb], lhsT=w1_sb[:, l, :], rhs=st[:, l:l + 16 * (ncb - 1) + 1:16],
                                                                                 start=(l == 0), stop=(l == 31)),
                         reads=[w1_sb.b, st.b], writes=[p.b], nowaw=(hf > 0))
            S.op("act", lambda e, p=p, a=a, hT_=hT_: e.activation(out=hT_[:, 0:511], in_=p[:, 0:511], func=AF.Gelu_apprx_tanh, bias=bias_c[:, a:a + 1]),
                 reads=[p.b, bias_c.b], writes=[hT_.b])
            if a == 0:
                p2 = nrot()
                S.op("pe", lambda e, p2=p2, hT_=hT_, g=g: e.matmul(p2[:, 0:512], lhsT=w2pad[:, g, :], rhs=hT_[:, :], start=True, stop=True),
                     reads=[w2pad.b, hT_.b], writes=[p2.b])
                S.op("act", lambda e, p2=p2, g=g: e.activation(out=kcmp[g * 64:(g + 1) * 64, :], in_=p2[g * 64:(g + 1) * 64, 0:512], func=AF.Identity, bias=b2c[g * 64:(g + 1) * 64, 0:1]),
                     reads=[p2.b, b2c.b], writes=[kcmp.b], nowaw=True)
            else:
                p2 = nrot()
                for ct in range(4):
                    S.op("pe", lambda e, p2=p2, hT_=hT_, ct=ct: e.matmul(p2[:, ct * 64:(ct + 1) * 64], lhsT=hT_[:, ct * 128:(ct + 1) * 128], rhs=w2_sb[:, 1, :], start=True, stop=True),
                         reads=[w2_sb.b, hT_.b], writes=[p2.b], nowaw=(ct > 0))
                for ct in range(4):
                    S.op("dve", lambda e, p2=p2, ct=ct, g=g: e.tensor_tensor(out=vcmp[:, ct, g, :], in0=p2[:, ct * 64:(ct + 1) * 64], in1=b2bc[:, :], op=ALU.add),
                         reads=[p2.b, b2bc.b], writes=[vcmp.b], nowaw=True)

    q_sb = [C.sb([128, 4, 128], BF16, f"q_sb{i}") for i in range(2)]
    q64 = Buf("q64")
    kw_sb = [C.sb([128, 640], BF16, f"kw_sb{i}") for i in range(2)]
    vw_sb = [C.sb([128, 5, 2, 65], BF16, f"vw_sb{i}") for i in range(2)]
    kvw64 = Buf("kvw64")
    for i in range(2):
        S.op("pool", lambda e, i=i: e.memset(vw_sb[i][:, :, :, 64:65], 1.0), writes=[kvw64], nowaw=True)
    g_sb = [C.sb([128, 24], F32, f"g_sb{i}") for i in range(2)]
    pin_sb = [C.sb([128, 2, 144], F32, f"pin_sb{i}") for i in range(2)]
    x_sb = [C.sb([128, D], F32, "x_sb0")] * 2
    mfT = [C.sb([128, 8, 128], BF16, f"mfT{i}") for i in range(2)]
    mfT_oc = [Buf(f"mfT_oc{i}") for i in range(2)]
    Ec = [C.sb([128, 4, 512], F32, "Ec0")] * 2
    pnb = [C.sb([128, 4, 512], BF16, "pnb0")] * 2
    impad = [C.sb([128, 520], F32, f"impad{i}") for i in range(2)]
    S.op("pool", lambda e: e.memset(pnb[0][:, :, :], 0.0), writes=[pnb[0].b])
    for i in range(2):
        S.op("pool", lambda e, i=i: e.memset(impad[i][:, :], 0.0), writes=[impad[i].b])
    dc = [C.sb([128, 8], F32, f"dc{i}") for i in range(2)]
    pT_sb = [C.sb([128, 4, 4, 128], BF16, "pT_sb0")] * 2
    sc = [C.sb([128, 128], F32, f"sc{i}") for i in range(2)]
    sc2 = C.sb([128, 128], F32, "sc2")
    t1 = C.sb([128, 128], F32, "t1")
    m8 = [C.sb([128, 16], F32, f"m8{i}") for i in range(2)]
    nsb = [C.sb([128, 128], BF16, f"nsb{i}") for i in range(2)]
    negT = [C.sb([128, 4, 128], BF16, f"negT{i}") for i in range(2)]
    Et = [C.sb([128, 512], BF16, f"Et{i}") for i in range(4)]
    ec = [0]
    oT_sb = [C.sb([65, 512], F32, f"oT_sb{i}") for i in range(2)]
    mix = [C.sb([128, 512], F32, f"mix{i}") for i in range(2)]
    mixb = [C.sb([128, 512], BF16, f"mixb{i}") for i in range(2)]
    tmpm = [C.sb([128, 4, 64], F32, f"tmpm{i}") for i in range(2)]
    fsc = [C.sb([128, 8], F32, f"fsc{i}") for i in range(2)]
    pl = [C.sb([128, 2, 144], F32, f"pl{i}") for i in range(4)]
    dd = [C.sb([128, 2, 128], BF16, f"dd{i}") for i in range(2)]
    ysq = C.sb([128, 512], BF16, "ysq")
    yss = [C.sb([128, 4], F32, f"yss{i}") for i in range(2)]
    yt = [C.sb([128, D], F32, "yt0")] * 2
    acn = [0]
    pc_ = [0]

    def tile(n):
        k2 = n % 2
        qs = q_sb[k2]; kws = kw_sb[k2]; vws = vw_sb[k2]; gs = g_sb[k2]; pins = pin_sb[k2]; xs = x_sb[k2]; mf = mfT[k2]
        mx = mix[k2]
        S.dma("sp", qs[:, :, :], qT_in[:, :, n * 128:(n + 1) * 128], writes=[qs.b])
        S.dma("sp", kws[:, :], kw_in[:, n, :], writes=[kws.b])
        S.dma("sp", vws[:, :, :, 0:64], vw_in[:, n, :, :, :], writes=[vws.b])
        S.dma("sp", gs[:, :], gate_in[n * 128:(n + 1) * 128, :], writes=[gs.b])
        S.dma("sp", pins[:, :, :], pin_in[:, :, n, :], writes=[pins.b])
        S.dma("sp", xs[:, :], x_in[n * 128:(n + 1) * 128, :], writes=[xs.b])
        S.dma("sp", mf[:, 6:8, :], oc_in[:, :, n * 128:(n + 1) * 128], writes=[mfT_oc[k2]])
        if n == 1:
            toep(Twin, 5, ebw_d, 1280, 256)
        Nc = 32 * n + 32
        nct = (Nc + 127) // 128
        lo = max(0, 32 * n - 56)
        c30 = lo - (32 * n - 56)
        def grp(g):
            E_ = Ec[g]; pb_ = pnb[g]; im = impad[g]; d_ = dc[g]; pTs = pT_sb[g]
            gv = gs[:, g * 12:(g + 1) * 12].rearrange("p (h r) -> p h r", r=3)
            for hh in range(4):
                h = 4 * g + hh
                p = nrot()
                S.op("pe", lambda e, p=p, hh=hh, g=g, qs=qs: e.matmul(p[:, 0:Nc], lhsT=qs[g * 64:(g + 1) * 64, hh, :], rhs=kcmp[g * 64:(g + 1) * 64, 0:Nc], start=True, stop=True),
                     reads=[qs.b, q64, kcmp.b], writes=[p.b])
                S.op("act", lambda e, p=p, hh=hh, E_=E_: e.activation(out=E_[:, hh, 0:Nc], in_=p[:, 0:Nc], func=AF.Exp),
                     reads=[p.b], writes=[E_.b], nowaw=(hh > 0))
            S.op("dve", lambda e, E_=E_, g=g: e.tensor_tensor(out=E_[:, :, lo:Nc], in0=E_[:, :, lo:Nc],
                                                           in1=Tband[:, c30:c30 + Nc - lo, 4 * g:4 * g + 4].rearrange("p c h -> p h c"), op=ALU.mult),
                 reads=[E_.b, Tband.b], writes=[E_.b])
            S.op("dve", lambda e, E_=E_, d_=d_: e.tensor_reduce(out=d_[:, 0:4], in_=E_[:, :, 0:Nc], axis=AX.X, op=ALU.add),
                 reads=[E_.b], writes=[d_.b])
            S.op("dve", lambda e, d_=d_: e.tensor_scalar(out=d_[:, 0:4], in0=d_[:, 0:4], scalar1=1e-30, scalar2=None, op0=ALU.max), reads=[d_.b], writes=[d_.b])
            S.op("dve", lambda e, d_=d_: e.reciprocal(out=d_[:, 4:8], in_=d_[:, 0:4]), reads=[d_.b], writes=[d_.b])
            S.op("dve", lambda e, E_=E_, d_=d_: e.tensor_tensor(out=E_[:, :, 0:Nc], in0=E_[:, :, 0:Nc],
                                                             in1=d_[:, 4:8].unsqueeze(2).to_broadcast([128, 4, Nc]), op=ALU.mult),
                 reads=[E_.b, d_.b], writes=[E_.b])
            S.op("dve", lambda e, E_=E_, im=im: e.tensor_reduce(out=im[:, 1:1 + Nc], in_=E_[:, :, 0:Nc].rearrange("p h c -> p c h"), axis=AX.X, op=ALU.add),
                 reads=[E_.b], writes=[im.b])
            S.op("act", lambda e, E_=E_, pb_=pb_: e.activation(out=pb_[:, :, 0:Nc], in_=E_[:, :, 0:Nc], func=AF.Copy),
                 reads=[E_.b], writes=[pb_.b])
            for hh in range(4):
                for ct in range(nct):
                    S.op("pe", lambda e, hh=hh, ct=ct, pb_=pb_: e.transpose(out=psT[:, hh * 2 + ct % 2, :], in_=pb_[:, hh, ct * 128:(ct + 1) * 128], identity=id_b[:, :]),
                         reads=[pb_.b, id_b.b], writes=[psT.b], nowaw=not (hh == 0 and ct % 2 == 0 and ct == 0))
                    if ct % 2 == 1 or ct == nct - 1:
                        c0 = ct - (ct % 2)
                        w_ = ct - c0 + 1
                        S.op("act", lambda e, hh=hh, c0=c0, w_=w_, pTs=pTs: e.activation(out=pTs[:, hh, c0:c0 + w_, :], in_=psT[:, hh * 2:hh * 2 + w_, :], func=AF.Copy),
                             reads=[psT.b], writes=[pTs.b], nowaw=True)
            po = nrot()
            for hh in range(4):
                for ct in range(nct):
                    S.op("pe", lambda e, po=po, hh=hh, ct=ct, pTs=pTs, g=g: e.matmul(po[:, hh * 64:(hh + 1) * 64], lhsT=pTs[:, hh, ct, :], rhs=vcmp[:, ct, g, :],
                                                                              start=(ct == 0), stop=(ct == nct - 1)),
                         reads=[pTs.b, vcmp.b], writes=[po.b], nowaw=not (hh == 0 and ct == 0))
            S.op("dve", lambda e, po=po, mx=mx, g=g, gv=gv: e.tensor_tensor(out=mx[:, g * 256:(g + 1) * 256].rearrange("p (h d) -> p h d", d=64),
                                                                         in0=po[:, 0:256].rearrange("p (h d) -> p h d", d=64),
                                                                         in1=gv[:, :, 0:1].to_broadcast([128, 4, 64]), op=ALU.mult),
                 reads=[po.b, gs.b], writes=[mx.b], nowaw=(g > 0))
            s_ = sc[g]; m_ = m8[g]; ns_ = nsb[g]; nT = negT[g]
            A = im[:, 0:512].rearrange("p (j m) -> p j m", m=4)
            Bv = im[:, 4:516].rearrange("p (j m) -> p j m", m=4)
            S.op("dve", lambda e, A=A: e.tensor_tensor(out=t1[:, :], in0=A[:, :, 1], in1=A[:, :, 2], op=ALU.add), reads=[im.b], writes=[t1.b])
            S.op("dve", lambda e, A=A: e.tensor_tensor(out=t1[:, :], in0=t1[:, :], in1=A[:, :, 3], op=ALU.add), reads=[im.b, t1.b], writes=[t1.b])
            S.op("dve", lambda e, A=A: e.scalar_tensor_tensor(out=t1[:, :], in0=t1[:, :], scalar=2.0, in1=A[:, :, 0], op0=ALU.mult, op1=ALU.add),
                 reads=[im.b, t1.b], writes=[t1.b])
            S.op("dve", lambda e, Bv=Bv, s_=s_: e.tensor_tensor(out=s_[:, :], in0=t1[:, :], in1=Bv[:, :, 0], op=ALU.add), reads=[im.b, t1.b], writes=[s_.b])
            j0 = max(0, 8 * n - 1)
            f0 = j0 - (8 * n - 1)
            wloc = 8 * n + 8 - j0
            S.op("dve", lambda e, s_=s_: e.tensor_tensor(out=s_[:, j0:j0 + wloc], in0=s_[:, j0:j0 + wloc], in1=vf_sb[:, 0, f0:f0 + wloc], op=ALU.mult),
                 reads=[s_.b, vf_sb.b], writes=[s_.b])
            S.op("dve", lambda e, s_=s_: e.tensor_tensor(out=s_[:, j0:j0 + wloc], in0=s_[:, j0:j0 + wloc], in1=vf_sb[:, 1, f0:f0 + wloc], op=ALU.add),
                 reads=[s_.b, vf_sb.b], writes=[s_.b])
            if 8 * n + 8 < 128:
                S.op("dve", lambda e, s_=s_: e.memset(s_[:, 8 * n + 8:128], -1.0), writes=[s_.b], reads=[s_.b])
            S.op("dve", lambda e, s_=s_: e.memset(s_[:, 0:1], 1e9), writes=[s_.b], reads=[s_.b])
            S.op("dve", lambda e, s_=s_, m_=m_: e.max(out=m_[:, 0:8], in_=s_[:, :]), reads=[s_.b], writes=[m_.b])
            S.op("dve", lambda e, s_=s_, m_=m_: e.match_replace(out=sc2[:, :], in_to_replace=m_[:, 0:8], in_values=s_[:, :], imm_value=-2.0),
                 reads=[s_.b, m_.b], writes=[sc2.b])
            S.op("dve", lambda e, m_=m_: e.max(out=m_[:, 8:16], in_=sc2[:, :]), reads=[sc2.b], writes=[m_.b])
            S.op("dve", lambda e, s_=s_, m_=m_, ns_=ns_: e.tensor_scalar(out=ns_[:, :], in0=s_[:, :], scalar1=m_[:, 15:16], scalar2=1.0, op0=ALU.is_ge, op1=ALU.subtract),
                 reads=[s_.b, m_.b], writes=[ns_.b])
            S.op("pe", lambda e, ns_=ns_: e.transpose(out=psT[:, 0, :], in_=ns_[:, :], identity=id_b[:, :]), reads=[ns_.b, id_b.b], writes=[psT.b])
            S.op("dve", lambda e, nT=nT: e.tensor_scalar(out=nT[:, :, :], in0=psT[:, 0:1, :].to_broadcast([128, 4, 128]), scalar1=NEG, scalar2=None, op0=ALU.mult),
                 reads=[psT.b], writes=[nT.b])
            def branch(br):
                a_ = acc[acn[0] % 2]; acn[0] += 1
                nk = 4 * n + 4 if br == 1 else 5
                def ktb(kt):
                    p = nrot()
                    if br == 1:
                        S.op("pe", lambda e, p=p, kt=kt, g=g, qs=qs: e.matmul(p[:, :], lhsT=k_sb[g * 64:(g + 1) * 64, kt * 128:(kt + 1) * 128],
                                                                          rhs=qs[g * 64:(g + 1) * 64, :, :].rearrange("p h q -> p (h q)"), start=True, stop=False),
                             reads=[k_sb.b, qs.b, q64], writes=[p.b])
                        S.op("pe", lambda e, p=p, kt=kt, nT=nT: e.matmul(p[:, :], lhsT=exp_sb[:, kt, :], rhs=nT[:, :, :].rearrange("p h q -> p (h q)"), start=False, stop=True),
                             reads=[exp_sb.b, nT.b], writes=[p.b])
                        di = 4 * n - kt + 3
                        tbl = Tsel[:, di, 4 * g:4 * g + 4, :] if di <= 10 else None
                        tb_b = Tsel.b
                        vl = v_sb[:, kt, g, :]
                        vb = v_sb.b
                    else:
                        S.op("pe", lambda e, p=p, kt=kt, g=g, qs=qs, kws=kws: e.matmul(p[:, :], lhsT=kws[g * 64:(g + 1) * 64, kt * 128:(kt + 1) * 128],
                                                                                   rhs=qs[g * 64:(g + 1) * 64, :, :].rearrange("p h q -> p (h q)"), start=True, stop=True),
                             reads=[kws.b, kvw64, qs.b, q64], writes=[p.b])
                        tbl = Twin[:, kt, 4 * g:4 * g + 4, :]
                        tb_b = Twin.b
                        vl = vws[:, kt, g, :]
                        vb = vws.b
                    et = Et[ec[0] % 4]; ec[0] += 1
                    S.op("act", lambda e, p=p, et=et: e.activation(out=et[:, :], in_=p[:, :], func=AF.Exp), reads=[p.b], writes=[et.b])
                    if tbl is not None:
                        S.op("dve", lambda e, et=et, tbl=tbl: e.tensor_tensor(out=et[:, :].rearrange("p (h q) -> p h q", h=4), in0=et[:, :].rearrange("p (h q) -> p h q", h=4),
                                                                          in1=tbl, op=ALU.mult), reads=[et.b, tb_b], writes=[et.b])
                    S.op("pe", lambda e, a_=a_, vl=vl, et=et, kt=kt, nk=nk: e.matmul(a_[0:65, :], lhsT=vl, rhs=et[:, :], start=(kt == 0), stop=(kt == nk - 1)),
                         reads=[vb, kvw64, et.b], writes=[a_.b])
                for kt in range(nk):
                    ktb(kt)
                ot = oT_sb[pc_[0] % 2]; tm_ = tmpm[pc_[0] % 2]; fs = fsc[pc_[0] % 2]; pc_[0] += 1
                S.op("act", lambda e, a_=a_, ot=ot: e.activation(out=ot[:, :], in_=a_[0:65, :], func=AF.Copy), reads=[a_.b], writes=[ot.b])
                for hh in range(4):
                    S.op("pe", lambda e, hh=hh, ot=ot: e.transpose(out=psM[:, hh * 65:(hh + 1) * 65], in_=ot[0:65, hh * 128:(hh + 1) * 128], identity=id_f[0:65, 0:65]),
                         reads=[ot.b, id_f.b], writes=[psM.b], nowaw=(hh > 0))
                pv = psM[:, 0:260].rearrange("p (h d) -> p h d", d=65)
                S.op("dve", lambda e, pv=pv, fs=fs: e.tensor_scalar(out=fs[:, 0:4], in0=pv[:, :, 64], scalar1=1e-30, scalar2=None, op0=ALU.max), reads=[psM.b], writes=[fs.b])
                S.op("dve", lambda e, fs=fs: e.reciprocal(out=fs[:, 4:8], in_=fs[:, 0:4]), reads=[fs.b], writes=[fs.b])
                S.op("dve", lambda e, fs=fs, gv=gv, br=br: e.tensor_tensor(out=fs[:, 0:4], in0=fs[:, 4:8], in1=gv[:, :, br], op=ALU.mult), reads=[fs.b, gs.b], writes=[fs.b])
                S.op("dve", lambda e, pv=pv, fs=fs, tm_=tm_: e.tensor_tensor(out=tm_[:, :, :], in0=pv[:, :, 0:64], in1=fs[:, 0:4].unsqueeze(2).to_broadcast([128, 4, 64]), op=ALU.mult),
                     reads=[psM.b, fs.b], writes=[tm_.b])
                S.op("dve", lambda e, mx=mx, tm_=tm_, g=g: e.tensor_tensor(out=mx[:, g * 256:(g + 1) * 256].rearrange("p (h d) -> p h d", d=64),
                                                                        in0=mx[:, g * 256:(g + 1) * 256].rearrange("p (h d) -> p h d", d=64), in1=tm_[:, :, :], op=ALU.add),
                     reads=[mx.b, tm_.b], writes=[mx.b])
            for br in (1, 2):
                branch(br)
        for g in range(2):
            grp(g)
        mb = mixb[k2]
        S.op("act", lambda e, mb=mb, mx=mx: e.activation(out=mb[:, :], in_=mx[:, :], func=AF.Copy), reads=[mx.b], writes=[mb.b])
        for j in range(4):
            S.op("pe", lambda e, j=j, mb=mb: e.transpose(out=psT[:, j, :], in_=mb[:, j * 128:(j + 1) * 128], identity=id_b[:, :]),
                 reads=[mb.b, id_b.b], writes=[psT.b], nowaw=(j > 0))
        S.op("act", lambda e, mf=mf: e.activation(out=mf[:, 0:4, :], in_=psT[:, 0:4, :], func=AF.Copy), reads=[psT.b], writes=[mf.b])
        p2_, p4_, p8_, p16_ = pl
        S.op("pool", lambda e, pins=pins: e.tensor_tensor(out=p2_[:, :, 1:144], in0=pins[:, :, 1:144], in1=pins[:, :, 0:143], op=ALU.add), reads=[pins.b], writes=[p2_.b])
        S.op("pool", lambda e: e.tensor_tensor(out=p4_[:, :, 3:144], in0=p2_[:, :, 3:144], in1=p2_[:, :, 1:142], op=ALU.add), reads=[p2_.b], writes=[p4_.b])
        S.op("pool", lambda e: e.tensor_tensor(out=p8_[:, :, 7:144], in0=p4_[:, :, 7:144], in1=p4_[:, :, 3:140], op=ALU.add), reads=[p4_.b], writes=[p8_.b])
        S.op("pool", lambda e: e.tensor_tensor(out=p16_[:, :, 15:144], in0=p8_[:, :, 15:144], in1=p8_[:, :, 7:136], op=ALU.add), reads=[p8_.b], writes=[p16_.b])
        d_t = dd[k2]
        for gi, (srcw, win) in enumerate(((p2_, 2), (p4_, 4), (p8_, 8), (p16_, 16))):
            ch = gi // 2; P = slice((gi % 2) * 64, (gi % 2) * 64 + 64)
            if n == 0:
                S.op("pool", lambda e, srcw=srcw, ch=ch, P=P: e.tensor_tensor(out=srcw[P, ch, 16:144], in0=srcw[P, ch, 16:144], in1=invc_sb[P, ch, :], op=ALU.mult),
                     reads=[srcw.b, invc_sb.b], writes=[srcw.b])
                S.op("pool", lambda e, srcw=srcw, ch=ch, P=P, d_t=d_t, pins=pins: e.tensor_tensor(out=d_t[P, ch, :], in0=srcw[P, ch, 16:144], in1=pins[P, ch, 16:144], op=ALU.subtract),
                     reads=[srcw.b, pins.b], writes=[d_t.b], nowaw=True)
            else:
                S.op("dve", lambda e, srcw=srcw, ch=ch, P=P, d_t=d_t, pins=pins, win=win: e.scalar_tensor_tensor(out=d_t[P, ch, :], in0=srcw[P, ch, 16:144], scalar=1.0 / win,
                                                                                                         in1=pins[P, ch, 16:144], op0=ALU.mult, op1=ALU.subtract),
                     reads=[srcw.b, pins.b], writes=[d_t.b], nowaw=True)
        pp = nrot()
        for ch in range(2):
            S.op("pe", lambda e, pp=pp, ch=ch, d_t=d_t: e.matmul(pp[:, ch * 128:(ch + 1) * 128], lhsT=pwb[:, ch, :], rhs=d_t[:, ch, :], start=True, stop=True),
                 reads=[pwb.b, d_t.b], writes=[pp.b], nowaw=(ch > 0))
        for ch in range(2):
            S.op("act", lambda e, pp=pp, ch=ch, mf=mf: e.activation(out=mf[:, 4 + ch, :], in_=pp[:, ch * 128:(ch + 1) * 128], func=AF.Copy, scale=ps_sb[:, ch:ch + 1]),
                 reads=[pp.b, ps_sb.b], writes=[mf.b], nowaw=True)
        ys = yss[k2]; y_ = yt[k2]; xo_ = y_
        pys = [nrot(), nrot()]
        for hf in range(2):
            for kc in range(8):
                S.op("pe", lambda e, hf=hf, kc=kc, mf=mf: e.matmul(pys[hf][:, :], lhsT=mf[:, kc, :], rhs=wo_sb[:, kc, hf * 512:(hf + 1) * 512], start=(kc == 0), stop=(kc == 7)),
                     reads=[mf.b, mfT_oc[k2], wo_sb.b], writes=[pys[hf].b])
            S.op("act", lambda e, hf=hf, ys=ys: e.activation(out=ysq[:, :], in_=pys[hf][:, :], func=AF.Square, accum_out=ys[:, hf:hf + 1]),
                 reads=[pys[hf].b], writes=[ysq.b, ys.b])
        S.op("dve", lambda e, ys=ys: e.tensor_tensor(out=ys[:, 2:3], in0=ys[:, 0:1], in1=ys[:, 1:2], op=ALU.add), reads=[ys.b], writes=[ys.b])
        S.op("act", lambda e, ys=ys: e.activation(out=ys[:, 3:4], in_=ys[:, 2:3], func=AF.Sqrt, scale=1.0 / D, bias=eps_c[:, :]), reads=[ys.b, eps_c.b], writes=[ys.b])
        S.op("dve", lambda e, ys=ys: e.reciprocal(out=ys[:, 3:4], in_=ys[:, 3:4]), reads=[ys.b], writes=[ys.b])
        for hf in range(2):
            S.op("dve", lambda e, hf=hf, ys=ys, y_=y_: e.scalar_tensor_tensor(out=y_[:, hf * 512:(hf + 1) * 512], in0=pys[hf][:, :], scalar=ys[:, 3:4],
                                                                        in1=gp_sb[:, hf * 512:(hf + 1) * 512], op0=ALU.mult, op1=ALU.mult),
                 reads=[pys[hf].b, ys.b, gp_sb.b], writes=[y_.b], nowaw=(hf > 0))
        S.op("pool", lambda e, y_=y_, xs=xs, xo_=xo_: e.tensor_tensor(out=xo_[:, :], in0=y_[:, :], in1=xs[:, :], op=ALU.add), reads=[y_.b, xs.b], writes=[xo_.b])
        S.dma("poolq", x_out[n * 128:(n + 1) * 128, :], xo_[:, :], reads=[xo_.b], writes=[x_out.b], nowaw=True, is_output=True)
    for n in range(NT):
        tile(n)
    return C.finish()


def _t5_bucket(n):
    n = np.maximum(np.asarray(n, np.int64), 0)
    nf = np.maximum(n, 1).astype(np.float32)
    lg = (np.log(nf / np.float32(16.0)) / np.float32(np.log(64.0)) * np.float32(16.0)).astype(np.float32)
    large = 16 + lg.astype(np.int32)
    large = np.minimum(large, 31)
    return np.where(n < 16, n, large).astype(np.int64)


def _onehot(dist, valid):
    dist = np.asarray(dist); valid = np.asarray(valid, bool)
    b = _t5_bucket(np.where(valid, dist, 0))
    rows = np.where(valid, b, 32)
    oh = np.zeros((33, dist.size), np.float32)
    oh[rows.reshape(-1), np.arange(dist.size)] = 1.0
    return oh


_CONST_CACHE = {}


def consts_B1(cp):
    if cp in _CONST_CACHE:
        return _CONST_CACHE[cp]
    m = np.arange(1536)
    d = m - 511 + 128 * cp
    ohs = _onehot(d, d >= 0)
    w = np.repeat(np.arange(5), 256); mp = np.tile(np.arange(256), 5)
    d = 128 * (4 - w) + mp - 127
    okw = (d >= 0) & (d < 512) & (mp <= 254)
    ohw = _onehot(d, okw)
    ohw0 = _onehot(d, okw & (w >= 4 - cp))
    c3 = np.repeat(np.arange(88), 128); tq = np.tile(np.arange(128), 88)
    d = 128 * cp + tq + 865 - 16 * c3
    ohc = _onehot(d, d >= 0).reshape(33, 88, 128)
    expand = np.zeros((128, 64, 128), np.float32)
    for kt in range(64):
        expand[2 * kt + 1, kt, 0:64] = 1.0
        expand[2 * kt, kt, 64:128] = 1.0
    vfc = np.zeros((128, 2, 9), np.float32)
    for t in range(128):
        bq = 2 * cp + (1 if t >= 64 else 0)
        for jj in range(9):
            jp = jj - 1
            valid = jp <= bq
            forced = (jp == bq) or (jp == bq - 1)
            vfc[t, 0, jj] = 1.0 if (valid and not forced) else 0.0
            vfc[t, 1, jj] = 1e9 if forced else (0.0 if valid else -1.0)
    invc0 = np.zeros((128, 2, 128), np.float32)
    for p in range(128):
        for ch in range(2):
            win = (2, 4, 8, 16)[2 * ch + (1 if p >= 64 else 0)]
            t = 128 * cp + np.arange(128)
            invc0[p, ch] = 1.0 / np.minimum(t + 1, win)
    r = dict(ohs=ohs, ohw=ohw, ohw0=ohw0, ohc=ohc, expand=expand.reshape(128, 8192), vfc=vfc, invc0=invc0, ident=_IDENT)
    _CONST_CACHE[cp] = r
    return r


def gather_full(A, b, key, rows):
    r0, r1 = rows
    out = np.empty((r1 - r0, 64, 128), A[0][key].dtype)
    for cp in range(4):
        out[:, cp::4, :] = A[b * 4 + cp][key][r0:r1].reshape(r1 - r0, NT, 128)
    return out.reshape(r1 - r0, 8192)


def gather_full_tm(A, b, key, cols):
    c0, c1 = cols
    out = np.empty((64, 128, c1 - c0), A[0][key].dtype)
    for cp in range(4):
        out[cp::4] = A[b * 4 + cp][key][:, c0:c1].reshape(NT, 128, c1 - c0)
    return out.reshape(8192, c1 - c0)


def prep_B1_batch(A, b):
    kc = gather_full(A, b, "kvT", (0, 128)); vc = gather_full(A, b, "kvT", (128, 256))
    ksl = gather_full(A, b, "kvT", (256, 384)); kw = gather_full(A, b, "kvT", (384, 512))
    vsl = gather_full_tm(A, b, "vtm", (0, 128)); vw = gather_full_tm(A, b, "vtm", (128, 256))
    pin = gather_full(A, b, "pinT", (0, 256))
    r = {}
    r["kc_in"] = np.ascontiguousarray(kc.reshape(2, 64, 8192).transpose(1, 0, 2))
    r["vc_in"] = np.ascontiguousarray(vc.reshape(2, 64, 8192).transpose(1, 0, 2))
    r["ksl_in"] = np.ascontiguousarray(ksl.reshape(128, 64, 128)[:, :, ::-1].reshape(128, 8192))
    r["vsl_in"] = np.ascontiguousarray(vsl.reshape(64, 128, 2, 64)[:, ::-1].transpose(1, 0, 2, 3))
    r["kwp"] = np.concatenate([np.zeros((128, 512), kw.dtype), kw], axis=1)
    r["vwp"] = np.concatenate([np.zeros((512, 128), vw.dtype), vw], axis=0)
    r["pinp"] = np.concatenate([np.zeros((256, 16), pin.dtype), pin], axis=1)
    return r


def prep_B1(inp, l, x, core, A, bb):
    b, cp = divmod(core, 4)
    r = dict(consts_B1(cp))
    a = A[core]
    r["qT_in"] = np.ascontiguousarray(a["qT"].reshape(2, 4, 64, TL).transpose(0, 2, 1, 3).reshape(128, 4, TL))
    r["gate_in"] = a["gate"]
    for k in ("kc_in", "vc_in", "ksl_in", "vsl_in"):
        r[k] = bb[k]
    kw_in = np.empty((128, NT, 640), bb["kwp"].dtype)
    vw_in = np.empty((128, NT, 5, 2, 64), bb["vwp"].dtype)
    pin_in = np.empty((128, 2, NT, 144), np.float32)
    for n in range(NT):
        i = 4 * n + cp
        kw_in[:, n, :] = bb["kwp"][:, 128 * i:128 * i + 640].reshape(128, 5, 128)[:, :, ::-1].reshape(128, 640)
        vw_in[:, n] = bb["vwp"][128 * i:128 * i + 640].reshape(5, 128, 2, 64)[:, ::-1].transpose(1, 0, 2, 3)
        pin_in[:, :, n, :] = bb["pinp"][:, 128 * i:128 * i + 144].reshape(2, 128, 144).transpose(1, 0, 2)
    r["kw_in"] = kw_in; r["vw_in"] = vw_in; r["pin_in"] = pin_in
    r["oc_in"] = np.ascontiguousarray(a["ocT"].reshape(2, 128, TL).transpose(1, 0, 2))
    r["x_in"] = local_rows(x, core)
    r["relb"] = np.ascontiguousarray(inp["rel_bias"])
    r["cpos"] = np.ascontiguousarray(inp["cmp_pos"][l].transpose(2, 0, 1))
    r["cw1"] = np.ascontiguousarray(inp["cmp_w1"][l].transpose(2, 0, 1, 3))
    r["cb1"] = np.ascontiguousarray(inp["cmp_b1"][l].T)
    r["cw2"] = np.ascontiguousarray(inp["cmp_w2"][l].transpose(1, 0, 2))
    r["cb2"] = np.ascontiguousarray(inp["cmp_b2"][l])
    r["poolw"] = np.ascontiguousarray(inp["pool_w"][l])
    r["pools"] = gcol(inp["pool_scale"][l])
    r["wo"] = np.ascontiguousarray(inp["w_out"][l])
    r["gpost"] = np.ascontiguousarray(inp["mix_norm_post"][l])
    return r


def load_w_scaled(C, S, w_d, rows, cols, g_sb, Wb, stage, q="sp"):
    for kc in range(rows // 128):
        st = stage[kc % 2]
        S.dma(q, st[:, 0:cols], w_d[kc * 128:(kc + 1) * 128, :], writes=[st.b])
        if g_sb is None:
            S.op("dve", lambda e, kc=kc, st=st: e.tensor_copy(out=Wb[:, kc, :], in_=st[:, 0:cols]), reads=[st.b], writes=[Wb.b], nowaw=True)
        else:
            S.op("dve", lambda e, kc=kc, st=st: e.tensor_scalar(out=Wb[:, kc, :], in0=st[:, 0:cols], scalar1=g_sb[:, kc:kc + 1], scalar2=None, op0=ALU.mult),
                 reads=[st.b, g_sb.b], writes=[Wb.b], nowaw=True)


def norm_rows(S, xin, nrows, sq, ss, eps_c, out_b):
    S.op("act", lambda e: e.activation(out=sq[0:nrows, :], in_=xin[0:nrows, :], func=AF.Square, accum_out=ss[0:nrows, 0:1]),
         reads=[xin.b], writes=[sq.b, ss.b])
    S.op("act", lambda e: e.activation(out=ss[0:nrows, 1:2], in_=ss[0:nrows, 0:1], func=AF.Sqrt, scale=1.0 / D, bias=eps_c[0:nrows, :]),
         reads=[ss.b, eps_c.b], writes=[ss.b])
    S.op("dve", lambda e: e.reciprocal(out=ss[0:nrows, 1:2], in_=ss[0:nrows, 1:2]), reads=[ss.b], writes=[ss.b])
    S.op("dve", lambda e: e.tensor_scalar(out=out_b[0:nrows, :], in0=xin[0:nrows, :], scalar1=ss[0:nrows, 1:2], scalar2=None, op0=ALU.mult),
         reads=[xin.b, ss.b], writes=[out_b.b])


def post_norm_residual(S, pys, ys, ysq, eps_c, gp_sb, y_, xs, x_out, row0):
    for hf in range(2):
        S.op("act", lambda e, hf=hf: e.activation(out=ysq[:, :], in_=pys[hf][:, :], func=AF.Square, accum_out=ys[:, hf:hf + 1]),
             reads=[pys[hf].b], writes=[ysq.b, ys.b])
    S.op("dve", lambda e: e.tensor_tensor(out=ys[:, 2:3], in0=ys[:, 0:1], in1=ys[:, 1:2], op=ALU.add), reads=[ys.b], writes=[ys.b])
    S.op("act", lambda e: e.activation(out=ys[:, 3:4], in_=ys[:, 2:3], func=AF.Sqrt, scale=1.0 / D, bias=eps_c[:, :]), reads=[ys.b, eps_c.b], writes=[ys.b])
    S.op("dve", lambda e: e.reciprocal(out=ys[:, 3:4], in_=ys[:, 3:4]), reads=[ys.b], writes=[ys.b])
    for hf in range(2):
        S.op("dve", lambda e, hf=hf: e.scalar_tensor_tensor(out=y_[:, hf * 512:(hf + 1) * 512], in0=pys[hf][:, :], scalar=ys[:, 3:4],
                                                            in1=gp_sb[:, hf * 512:(hf + 1) * 512], op0=ALU.mult, op1=ALU.mult),
             reads=[pys[hf].b, ys.b, gp_sb.b], writes=[y_.b], nowaw=(hf > 0))
    S.op("pool", lambda e: e.tensor_tensor(out=y_[:, :], in0=y_[:, :], in1=xs[:, :], op=ALU.add), reads=[y_.b, xs.b], writes=[y_.b])
    S.dma("poolq", x_out[row0:row0 + 128, :], y_[:, :], reads=[y_.b], writes=[x_out.b], nowaw=True, is_output=True)


def build_B2():
    C = Ctx("B2"); S = C.S
    x_in = C.din("x_in", [TL, D], F32)
    mem = C.din("mem", [256, D], F32)
    gq = C.din("gq", [128, 8], F32)
    gkv = C.din("gkv", [128, 8], F32)
    gpost = C.din("gpost", [D], F32)
    wq = C.din("wq", [D, D], F32); wk = C.din("wk", [D, D], F32); wv = C.din("wv", [D, D], F32); wo = C.din("wo", [D, D], F32)
    ident = C.din("ident", [128, 128], F32)
    x_out = C.dout("x_out", [TL, D], F32)
    rot = [C.ps([128, 512], F32, f"rot{i}") for i in range(5)]
    psT = [C.ps([128, 8, 128], BF16, f"psT{i}") for i in range(2)]
    psd = C.ps([128, 512], F32, "psd")
    rc = [0]

    def nrot():
        rc[0] += 1
        return rot[rc[0] % 5]
    eps_c = C.sb([128, 1], F32, "eps_c")
    S.op("dve", lambda e: e.memset(eps_c[:, :], EPS), writes=[eps_c.b])
    id_b = C.sb([128, 128], BF16, "id_b")
    S.dma("poolq", id_b[:, :], ident[:, :], writes=[id_b.b])
    ones = C.sb([128, 128], BF16, "ones")
    S.op("dve", lambda e: e.memset(ones[:, :], 1.0), writes=[ones.b])
    gq_sb = C.sb([128, 8], F32, "gq_sb"); gkv_sb = C.sb([128, 8], F32, "gkv_sb")
    S.dma("sp", gq_sb[:, :], gq[:, :], writes=[gq_sb.b])
    S.dma("sp", gkv_sb[:, :], gkv[:, :], writes=[gkv_sb.b])
    gp_sb = C.sb([128, D], F32, "gp_sb")
    S.dma("sp", gp_sb[:, :], bcast_rows(gpost, D), writes=[gp_sb.b])
    stage = [C.sb([128, D], F32, f"stage{i}") for i in range(2)]
    Wq = C.sb([128, 8, D], BF16, "Wq"); Wk = C.sb([128, 8, D], BF16, "Wk"); Wv = C.sb([128, 8, D], BF16, "Wv"); Wo = C.sb([128, 8, D], BF16, "Wo")
    load_w_scaled(C, S, wk, D, D, gkv_sb, Wk, stage)
    load_w_scaled(C, S, wv, D, D, gkv_sb, Wv, stage)
    load_w_scaled(C, S, wq, D, D, gq_sb, Wq, stage)
    load_w_scaled(C, S, wo, D, D, None, Wo, stage)
    xt = [C.sb([128, D], F32, f"xt{i}") for i in range(2)]
    sq = C.sb([128, D], BF16, "sq")
    ss = [C.sb([128, 2], F32, f"ss{i}") for i in range(2)]
    hn = [C.sb([128, D], BF16, f"hn{i}") for i in range(2)]
    mT = C.sb([128, 8, 256], BF16, "mT")
    for mt in range(2):
        xx = xt[mt]; hh = hn[mt]; pT = psT[mt]
        S.dma("sp", xx[:, :], mem[mt * 128:(mt + 1) * 128, :], writes=[xx.b])
        norm_rows(S, xx, 128, sq, ss[mt], eps_c, hh)
        for kc in range(8):
            S.op("pe", lambda e, kc=kc, hh=hh, pT=pT: e.transpose(out=pT[:, kc, :], in_=hh[:, kc * 128:(kc + 1) * 128], identity=id_b[:, :]),
                 reads=[hh.b, id_b.b], writes=[pT.b], nowaw=(kc > 0))
        S.op("dve", lambda e, pT=pT, mt=mt: e.tensor_copy(out=mT[:, :, mt * 128:(mt + 1) * 128], in_=pT[:, :, :]), reads=[pT.b], writes=[mT.b], nowaw=True)
    kT = C.sb([128, 8, 256], BF16, "kT")
    v_sb = C.sb([128, 2, D], BF16, "v_sb")
    for oc in range(8):
        p = nrot()
        for kc in range(8):
            S.op("pe", lambda e, p=p, kc=kc, oc=oc: e.matmul(p[:, 0:256], lhsT=Wk[:, kc, oc * 128:(oc + 1) * 128], rhs=mT[:, kc, :], start=(kc == 0), stop=(kc == 7)),
                 reads=[Wk.b, mT.b], writes=[p.b])
        S.op("act", lambda e, p=p, oc=oc: e.activation(out=kT[:, oc, :], in_=p[:, 0:256], func=AF.Copy), reads=[p.b], writes=[kT.b], nowaw=True)
    for mt in range(2):
        for hf in range(2):
            p = nrot()
            for kc in range(8):
                S.op("pe", lambda e, p=p, kc=kc, mt=mt, hf=hf: e.matmul(p[:, :], lhsT=mT[:, kc, mt * 128:(mt + 1) * 128], rhs=Wv[:, kc, hf * 512:(hf + 1) * 512], start=(kc == 0), stop=(kc == 7)),
                     reads=[Wv.b, mT.b], writes=[p.b])
            S.op("act", lambda e, p=p, mt=mt, hf=hf: e.activation(out=v_sb[:, mt, hf * 512:(hf + 1) * 512], in_=p[:, :], func=AF.Copy), reads=[p.b], writes=[v_sb.b], nowaw=True)
    hT = [C.sb([128, 8, 512], BF16, f"hT{i}") for i in range(2)]
    qT = [C.sb([128, 8, 512], BF16, f"qT{i}") for i in range(2)]
    oT = [C.sb([128, 8, 512], BF16, f"oT{i}") for i in range(2)]
    Pt = [C.sb([128, 512], BF16, f"Pt{i}") for i in range(4)]
    rden = [C.sb([128, 512], F32, f"rden{i}") for i in range(2)]
    xk = [C.sb([128, D], F32, f"xk{i}") for i in range(4)]
    ysq = C.sb([128, 512], BF16, "ysq")
    yss = [C.sb([128, 4], F32, f"yss{i}") for i in range(2)]
    yt = [C.sb([128, D], F32, f"yt{i}") for i in range(2)]
    pcnt = [0]

    def group(grp):
        hTg = hT[grp % 2]; qTg = qT[grp % 2]; oTg = oT[grp % 2]
        for tt in range(4):
            t = grp * 4 + tt
            xx = xk[tt]; hh = hn[t % 2]; pT = psT[t % 2]
            S.dma("sp", xx[:, :], x_in[t * 128:(t + 1) * 128, :], writes=[xx.b])
            norm_rows(S, xx, 128, sq, ss[t % 2], eps_c, hh)
            for kc in range(8):
                S.op("pe", lambda e, kc=kc, hh=hh, pT=pT: e.transpose(out=pT[:, kc, :], in_=hh[:, kc * 128:(kc + 1) * 128], identity=id_b[:, :]),
                     reads=[hh.b, id_b.b], writes=[pT.b], nowaw=(kc > 0))
            S.op("dve", lambda e, pT=pT, tt=tt: e.tensor_copy(out=hTg[:, :, tt * 128:(tt + 1) * 128], in_=pT[:, :, :]), reads=[pT.b], writes=[hTg.b], nowaw=True)
        for oc in range(8):
            p = nrot()
            for kc in range(8):
                S.op("pe", lambda e, p=p, kc=kc, oc=oc: e.matmul(p[:, :], lhsT=Wq[:, kc, oc * 128:(oc + 1) * 128], rhs=hTg[:, kc, :], start=(kc == 0), stop=(kc == 7)),
                     reads=[Wq.b, hTg.b], writes=[p.b])
            S.op("act", lambda e, p=p, oc=oc: e.activation(out=qTg[:, oc, :], in_=p[:, :], func=AF.Copy, scale=1.0 / 16), reads=[p.b], writes=[qTg.b], nowaw=True)
        for hd in range(4):
            pts = []
            for mt in range(2):
                p = nrot()
                for dc in range(2):
                    S.op("pe", lambda e, p=p, dc=dc, mt=mt, hd=hd: e.matmul(p[:, :], lhsT=kT[:, hd * 2 + dc, mt * 128:(mt + 1) * 128], rhs=qTg[:, hd * 2 + dc, :], start=(dc == 0), stop=(dc == 1)),
                         reads=[kT.b, qTg.b], writes=[p.b])
                pt = Pt[pcnt[0] % 4]; pcnt[0] += 1
                S.op("act", lambda e, p=p, pt=pt: e.activation(out=pt[:, :], in_=p[:, :], func=AF.Exp), reads=[p.b], writes=[pt.b])
                pts.append(pt)
            for mt in range(2):
                S.op("pe", lambda e, mt=mt, pts=pts: e.matmul(psd[:, :], lhsT=ones[:, :], rhs=pts[mt][:, :], start=(mt == 0), stop=(mt == 1)),
                     reads=[ones.b, pts[mt].b], writes=[psd.b])
            rd = rden[hd % 2]
            S.op("dve", lambda e, rd=rd: e.reciprocal(out=rd[:, :], in_=psd[:, :]), reads=[psd.b], writes=[rd.b])
            for dvc in range(2):
                p = nrot()
                for mt in range(2):
                    S.op("pe", lambda e, p=p, mt=mt, hd=hd, dvc=dvc, pts=pts: e.matmul(p[:, :], lhsT=v_sb[:, mt, hd * 256 + dvc * 128:hd * 256 + dvc * 128 + 128], rhs=pts[mt][:, :],
                                                                                  start=(mt == 0), stop=(mt == 1)),
                         reads=[v_sb.b, pts[mt].b], writes=[p.b])
                S.op("dve", lambda e, p=p, hd=hd, dvc=dvc, rd=rd: e.tensor_tensor(out=oTg[:, hd * 2 + dvc, :], in0=p[:, :], in1=rd[:, :], op=ALU.mult),
                     reads=[p.b, rd.b], writes=[oTg.b], nowaw=True)
        for tt in range(4):
            t = grp * 4 + tt
            pys = [nrot(), nrot()]
            for hf in range(2):
                for kc in range(8):
                    S.op("pe", lambda e, hf=hf, kc=kc, tt=tt, pys=pys: e.matmul(pys[hf][:, :], lhsT=oTg[:, kc, tt * 128:(tt + 1) * 128], rhs=Wo[:, kc, hf * 512:(hf + 1) * 512], start=(kc == 0), stop=(kc == 7)),
                         reads=[oTg.b, Wo.b], writes=[pys[hf].b])
            post_norm_residual(S, pys, yss[t % 2], ysq, eps_c, gp_sb, yt[t % 2], xk[tt], x_out, t * 128)
    for grp in range(NT // 4):
        group(grp)
    return C.finish()


def prep_B2(inp, l, core, x1_local):
    b = core // 4
    return {"x_in": x1_local, "mem": np.ascontiguousarray(inp["mem"][b]), "gq": gcol(inp["mem_norm_pre"][l]), "gkv": gcol(inp["mem_norm_kv"][l]),
            "gpost": np.ascontiguousarray(inp["mem_norm_post"][l]), "wq": np.ascontiguousarray(inp["w_mq"][l]), "wk": np.ascontiguousarray(inp["w_mk"][l]),
            "wv": np.ascontiguousarray(inp["w_mv"][l]), "wo": np.ascontiguousarray(inp["w_mo"][l]), "ident": _IDENT}


FH = 2816
NCH = 44


def build_C():
    C = Ctx("C"); S = C.S
    x_in = C.din("x_in", [TL, D], F32)
    xh = C.din("xh", [NT * 2, D], F32)
    gpre = C.din("gpre", [128, 8], F32)
    gpost = C.din("gpost", [D], F32)
    wu = C.din("wu", [D, 2 * FH], F32)
    wd = C.din("wd", [FH, D], F32)
    cwc = C.din("cwc", [128, NCH, 3], F32)
    cbc = C.din("cbc", [128, NCH], F32)
    ident = C.din("ident", [128, 128], F32)
    x_out = C.dout("x_out", [TL, D], F32)
    rot = [C.ps([128, 512], F32, f"rot{i}") for i in range(4)]
    psh = [C.ps([128, 512], F32, f"psh{i}") for i in range(2)]
    psT = [C.ps([128, 8, 128], BF16, f"psT{i}") for i in range(2)]
    rc = [0]

    def nrot():
        rc[0] += 1
        return rot[rc[0] % 4]
    eps_c = C.sb([128, 1], F32, "eps_c")
    S.op("dve", lambda e: e.memset(eps_c[:, :], EPS), writes=[eps_c.b])
    id_b = C.sb([128, 128], BF16, "id_b")
    S.dma("poolq", id_b[:, :], ident[:, :], writes=[id_b.b])
    g_sb = C.sb([128, 8], F32, "g_sb")
    S.dma("sp", g_sb[:, :], gpre[:, :], writes=[g_sb.b])
    gp_sb = C.sb([128, D], F32, "gp_sb")
    S.dma("sp", gp_sb[:, :], bcast_rows(gpost, D), writes=[gp_sb.b])
    cw_sb = C.sb([128, NCH, 3], F32, "cw_sb"); cb_sb = C.sb([128, NCH], F32, "cb_sb")
    S.dma("sp", cw_sb[:, :, :], cwc[:, :, :], writes=[cw_sb.b])
    S.dma("sp", cb_sb[:, :], cbc[:, :], writes=[cb_sb.b])
    Wu = C.sb([128, 8, 2 * FH], BF16, "Wu")
    Wd = C.sb([128, 22, D], BF16, "Wd")
    stage = [C.sb([128, 704], F32, f"stage{i}") for i in range(2)]
    si = 0
    for kc in range(8):
        for cbk in range(8):
            st = stage[si % 2]; si += 1
            S.dma("sp", st[:, :], wu[kc * 128:(kc + 1) * 128, cbk * 704:(cbk + 1) * 704], writes=[st.b])
            S.op("dve", lambda e, kc=kc, cbk=cbk, st=st: e.tensor_scalar(out=Wu[:, kc, cbk * 704:(cbk + 1) * 704], in0=st[:, :], scalar1=g_sb[:, kc:kc + 1], scalar2=None, op0=ALU.mult),
                 reads=[st.b, g_sb.b], writes=[Wu.b], nowaw=True)
    for j in range(22):
        S.dma("poolq", Wd[:, j, :], wd[j * 128:(j + 1) * 128, :], writes=[Wd.b], nowaw=True)
    xk = [C.sb([128, D], F32, f"xk{i}") for i in range(2)]
    xhs = C.sb([8, D], F32, "xhs")
    sq = C.sb([128, D], BF16, "sq")
    ss = [C.sb([128, 2], F32, f"ss{i}") for i in range(2)]
    hn = [C.sb([128, D], BF16, f"hn{i}") for i in range(2)]
    hT = [C.sb([128, 8, 520], BF16, "hT0")] * 2
    ab = [C.sb([128, 4, 130], F32, f"ab{i}") for i in range(2)]
    yy = [C.sb([128, 4, 128], F32, f"yy{i}") for i in range(2)]
    gg = [C.sb([128, 4, 128], F32, f"gg{i}") for i in range(2)]
    uT = C.sb([128, 22, 512], BF16, "uT")
    ysq = C.sb([128, 512], BF16, "ysq")
    yss = [C.sb([128, 4], F32, f"yss{i}") for i in range(2)]
    yt = [C.sb([128, D], F32, "yt0")] * 2
    cnt = [0]

    def group(grp):
        hTg = hT[grp % 2]
        for tt in range(4):
            t = grp * 4 + tt
            xx = xk[tt % 2]; hh = hn[t % 2]; pT = psT[t % 2]
            S.dma("sp", xx[:, :], x_in[t * 128:(t + 1) * 128, :], writes=[xx.b])
            norm_rows(S, xx, 128, sq, ss[t % 2], eps_c, hh)
            for kc in range(8):
                S.op("pe", lambda e, kc=kc, hh=hh, pT=pT: e.transpose(out=pT[:, kc, :], in_=hh[:, kc * 128:(kc + 1) * 128], identity=id_b[:, :]),
                     reads=[hh.b, id_b.b], writes=[pT.b], nowaw=(kc > 0))
            S.op("dve", lambda e, pT=pT, tt=tt: e.tensor_copy(out=hTg[:, :, tt * 128:(tt + 1) * 128], in_=pT[:, :, :]), reads=[pT.b], writes=[hTg.b], nowaw=True)
        hh = hn[0]; pT = psT[0]
        S.dma("sp", xhs[:, :], xh[grp * 8:(grp + 1) * 8, :], writes=[xhs.b])
        norm_rows(S, xhs, 8, sq, ss[0], eps_c, hh)
        for kc in range(8):
            S.op("pe", lambda e, kc=kc: e.transpose(out=pT[:, kc, 0:8], in_=hh[0:8, kc * 128:(kc + 1) * 128], identity=id_b[0:8, 0:8]),
                 reads=[hh.b, id_b.b], writes=[pT.b], nowaw=(kc > 0))
        S.op("dve", lambda e: e.tensor_copy(out=hTg[:, :, 512:520], in_=pT[:, :, 0:8]), reads=[pT.b], writes=[hTg.b], nowaw=True)

        def chunk(oc):
            k = cnt[0]; cnt[0] += 1
            a_ = ab[k % 2]; y_ = yy[k % 2]
            pm = nrot(); ph = psh[k % 2]
            for kc in range(8):
                S.op("pe", lambda e, kc=kc: e.matmul(pm[:, :], lhsT=Wu[:, kc, oc * 128:(oc + 1) * 128], rhs=hTg[:, kc, 0:512], start=(kc == 0), stop=(kc == 7)),
                     reads=[Wu.b, hTg.b], writes=[pm.b])
            for kc in range(8):
                S.op("pe", lambda e, kc=kc: e.matmul(ph[:, 0:8], lhsT=Wu[:, kc, oc * 128:(oc + 1) * 128], rhs=hTg[:, kc, 512:520], start=(kc == 0), stop=(kc == 7)),
                     reads=[Wu.b, hTg.b], writes=[ph.b])
            S.op("act", lambda e: e.activation(out=a_[:, :, 2:130], in_=pm[:, :].rearrange("p (t q) -> p t q", q=128), func=AF.Copy), reads=[pm.b], writes=[a_.b])
            S.op("act", lambda e: e.activation(out=a_[:, :, 0:2], in_=ph[:, 0:8].rearrange("p (t q) -> p t q", q=2), func=AF.Copy), reads=[ph.b], writes=[a_.b], nowaw=True)
            S.op("act", lambda e: e.activation(out=y_[:, :, :], in_=pm[:, :].rearrange("p (t q) -> p t q", q=128), func=AF.Identity,
                                               scale=cw_sb[:, oc, 2:3], bias=cb_sb[:, oc:oc + 1]), reads=[pm.b, cw_sb.b, cb_sb.b], writes=[y_.b])
            S.op("dve", lambda e: e.scalar_tensor_tensor(out=y_[:, :, :], in0=a_[:, :, 1:129], scalar=cw_sb[:, oc, 1:2], in1=y_[:, :, :], op0=ALU.mult, op1=ALU.add),
                 reads=[a_.b, y_.b, cw_sb.b], writes=[y_.b])
            S.op("dve", lambda e: e.scalar_tensor_tensor(out=y_[:, :, :], in0=a_[:, :, 0:128], scalar=cw_sb[:, oc, 0:1], in1=y_[:, :, :], op0=ALU.mult, op1=ALU.add),
                 reads=[a_.b, y_.b, cw_sb.b], writes=[y_.b])
            return y_
        for j in range(22):
            yg = chunk(j)
            g_ = gg[j % 2]
            S.op("act", lambda e, yg=yg, g_=g_: e.activation(out=g_[:, :, :], in_=yg[:, :, :], func=AF.Gelu_apprx_tanh), reads=[yg.b], writes=[g_.b])
            yv = chunk(22 + j)
            S.op("dve", lambda e, yv=yv, g_=g_, j=j: e.tensor_tensor(out=uT[:, j, :].rearrange("p (t q) -> p t q", q=128), in0=yv[:, :, :], in1=g_[:, :, :], op=ALU.mult),
                 reads=[yv.b, g_.b], writes=[uT.b], nowaw=True)
        for tt in range(4):
            t = grp * 4 + tt
            pys = [nrot(), nrot()]
            for hf in range(2):
                for j in range(22):
                    S.op("pe", lambda e, hf=hf, j=j, tt=tt, pys=pys: e.matmul(pys[hf][:, :], lhsT=uT[:, j, tt * 128:(tt + 1) * 128], rhs=Wd[:, j, hf * 512:(hf + 1) * 512], start=(j == 0), stop=(j == 21)),
                         reads=[uT.b, Wd.b], writes=[pys[hf].b])
            xx = xk[tt % 2]
            S.dma("sp", xx[:, :], x_in[t * 128:(t + 1) * 128, :], writes=[xx.b])
            post_norm_residual(S, pys, yss[t % 2], ysq, eps_c, gp_sb, yt[t % 2], xx, x_out, t * 128)
    for grp in range(NT // 4):
        group(grp)
    return C.finish()


def halo_rows(x_full, core, nh):
    b, cp = divmod(core, 4)
    out = np.zeros((NT, nh, x_full.shape[2]), x_full.dtype)
    for n in range(NT):
        i = 4 * n + cp
        if i > 0:
            out[n] = x_full[b, 128 * i - nh:128 * i]
    return out.reshape(NT * nh, -1)


def prep_C(inp, l, core, x2_local, x2_full):
    return {"x_in": x2_local, "xh": halo_rows(x2_full, core, 2), "gpre": gcol(inp["ffn_norm_pre"][l]), "gpost": np.ascontiguousarray(inp["ffn_norm_post"][l]),
            "wu": np.ascontiguousarray(inp["w_up"][l]), "wd": np.ascontiguousarray(inp["w_down"][l]),
            "cwc": np.ascontiguousarray(inp["conv_w"][l].reshape(3, NCH, 128).transpose(2, 1, 0)),
            "cbc": np.ascontiguousarray(inp["conv_b"][l].reshape(NCH, 128).T), "ident": _IDENT}


def scatter_local(outs, shape):
    full = np.empty(shape, np.float32)
    v = full.reshape(shape[0], 64, 128, shape[2])
    for core in range(8):
        b, cp = divmod(core, 4)
        v[b, cp::4] = outs[core].reshape(NT, 128, shape[2])
    return full


from concourse.bass_utils import run_bass_kernel_spmd

_PROGS = {}


def _prog(name):
    if name not in _PROGS:
        _PROGS[name] = {"A": build_A, "B1": build_B1, "B2": build_B2, "C": build_C}[name]()
    return _PROGS[name]


def _run(name, maps):
    res = run_bass_kernel_spmd(_prog(name), maps, core_ids=list(range(8)))
    return [{k: np.asarray(v) for k, v in r.items()} for r in res.results]


def kernel(**inputs):
    inp = {k: np.asarray(v) for k, v in inputs.items()}
    x = np.ascontiguousarray(inp["x"], dtype=np.float32)
    shape = x.shape
    for l in range(4):
        A = _run("A", [prep_A(inp, l, x, c) for c in range(8)])
        bbs = [prep_B1_batch(A, b) for b in range(2)]
        r1 = _run("B1", [prep_B1(inp, l, x, c, A, bbs[c // 4]) for c in range(8)])
        del A, bbs
        r2 = _run("B2", [prep_B2(inp, l, c, r1[c]["x_out"]) for c in range(8)])
        x2_full = scatter_local([r2[c]["x_out"] for c in range(8)], shape)
        r3 = _run("C", [prep_C(inp, l, c, r2[c]["x_out"], x2_full) for c in range(8)])
        x = scatter_local([r3[c]["x_out"] for c in range(8)], shape)
    return x
```

```python
from contextlib import ExitStack
import numpy as np
import concourse.bass as bass
import concourse.mybir as mybir

F32 = mybir.dt.float32
BF16 = mybir.dt.bfloat16
AF = mybir.ActivationFunctionType
ALU = mybir.AluOpType
AX = mybir.AxisListType

COMPUTE = ("pe", "act", "dve", "pool")
DMAQ = ("sp", "actq", "poolq")
QENG = {"sp": "sp", "actq": "act", "poolq": "pool"}
NSEM_DMA = 12
SAME_ENGINE_SYNC = True


class Buf:
    __slots__ = ("name", "writers", "readers")

    def __init__(self, name):
        self.name = name
        self.writers = []
        self.readers = []


class Op:
    __slots__ = ("stream", "emit", "deps", "signal", "dma", "idx")


class Sched:
    def __init__(self, nc, es):
        self.nc = nc
        self.es = es
        self.streams = {s: [] for s in ("pe", "act", "dve", "pool", "sp")}
        self.sems = {}
        for s in COMPUTE:
            self.sems[s] = es.enter_context(nc.semaphore("c_" + s))
        self.dsems = {}
        self.dcount = {}
        self.dnext = {}
        for q in DMAQ:
            self.dsems[q] = [es.enter_context(nc.semaphore(f"d_{q}_{i}")) for i in range(NSEM_DMA)]
            self.dcount[q] = [0] * NSEM_DMA
            self.dnext[q] = 0
        self.out_events = []

    def _collect(self, reads, writes, nowaw):
        deps = []
        for b in reads:
            deps.extend(b.writers)
        for b in writes:
            if not nowaw:
                deps.extend(b.writers)
            deps.extend(b.readers)
        return deps

    def _commit(self, ev, reads, writes, nowaw):
        for b in reads:
            if ev[0] == "c":
                b.readers = [r for r in b.readers if not (r[0] == "c" and r[1] == ev[1])]
            b.readers.append(ev)
        for b in writes:
            if nowaw:
                if ev[0] == "c":
                    b.writers = [w for w in b.writers if not (w[0] == "c" and w[1] == ev[1])]
                b.writers.append(ev)
            else:
                b.writers = [ev]
            b.readers = []

    def op(self, eng, emit, reads=(), writes=(), nowaw=False):
        o = Op()
        o.stream = eng
        o.emit = emit
        o.deps = self._collect(reads, writes, nowaw)
        o.signal = False
        o.dma = None
        o.idx = len([x for x in self.streams[eng] if x.dma is None and x.emit is not None]) if False else None
        lst = self.streams[eng]
        o.idx = len(lst)
        lst.append(o)
        ev = ("c", eng, o.idx)
        self._commit(ev, reads, writes, nowaw)
        return ev

    def dma(self, q, out, in_, reads=(), writes=(), nowaw=False, is_output=False, _emit=None, **kw):
        o = Op()
        stream = QENG[q]
        o.stream = stream
        k = self.dnext[q]
        self.dnext[q] = (k + 1) % NSEM_DMA
        self.dcount[q][k] += 1
        val = 16 * self.dcount[q][k]
        ev = ("d", q, k, val)
        o.deps = self._collect(reads, writes, nowaw)
        if self.dcount[q][k] > 1:
            o.deps.append(("d", q, k, val - 16))
        o.dma = (q, k, val)
        o.signal = False
        o.emit = _emit if _emit is not None else (lambda e: e.dma_start(out=out, in_=in_, **kw))
        lst = self.streams[stream]
        o.idx = len(lst)
        lst.append(o)
        self._commit(ev, reads, writes, nowaw)
        if is_output:
            self.out_events.append(ev)
        return ev

    def coll(self, kind, op, groups, src, dst):
        return self.dma("poolq", None, None, reads=[src.b], writes=[dst.b],
                        _emit=lambda e: e.collective_compute(kind, op, replica_groups=groups, ins=[src.t], outs=[dst.t]))

    def barrier(self):
        evs = []
        for s in COMPUTE:
            if self.streams[s]:
                idx = None
                for o in reversed(self.streams[s]):
                    if o.emit is not None and o.dma is None:
                        idx = o.idx
                        break
                if idx is not None:
                    evs.append(("c", s, idx))
        for q in DMAQ:
            for k in range(NSEM_DMA):
                if self.dcount[q][k] > 0:
                    evs.append(("d", q, k, 16 * self.dcount[q][k]))
        for s in ("pe", "act", "dve", "pool", "sp"):
            o = Op()
            o.stream = s; o.emit = None; o.deps = list(evs); o.signal = False; o.dma = None
            o.idx = len(self.streams[s])
            self.streams[s].append(o)

    def finalize(self):
        nc = self.nc
        fin = Op()
        fin.stream = "sp"; fin.emit = None; fin.deps = list(self.out_events); fin.signal = False; fin.dma = None
        fin.idx = len(self.streams["sp"])
        self.streams["sp"].append(fin)
        for s, lst in self.streams.items():
            for o in lst:
                for d in o.deps:
                    if d[0] == "c":
                        if d[1] == s and (s == "pe" or not SAME_ENGINE_SYNC):
                            continue
                        self.streams[d[1]][d[2]].signal = True
        sigcount = {}
        for s in COMPUTE:
            c = 0
            arr = []
            for o in self.streams[s]:
                if o.signal:
                    c += 1
                arr.append(c)
            sigcount[s] = arr
        block = self.es.enter_context(nc.Block())
        engs = {"pe": block.tensor, "act": block.scalar, "dve": block.vector, "pool": block.gpsimd, "sp": block.sync}

        def make(sname):
            lst = self.streams[sname]

            def body(e):
                waited_c = {s: 0 for s in COMPUTE}
                waited_d = {}
                for o in lst:
                    for d in o.deps:
                        if d[0] == "c":
                            if d[1] == sname and (sname == "pe" or not SAME_ENGINE_SYNC):
                                continue
                            v = sigcount[d[1]][d[2]]
                            if v > waited_c[d[1]]:
                                e.wait_ge(self.sems[d[1]], v)
                                waited_c[d[1]] = v
                        else:
                            key = (d[1], d[2])
                            if d[3] > waited_d.get(key, 0):
                                e.wait_ge(self.dsems[d[1]][d[2]], d[3])
                                waited_d[key] = d[3]
                    if o.emit is None:
                        continue
                    ins = o.emit(e)
                    if o.dma is not None:
                        q, k, val = o.dma
                        ins.then_inc(self.dsems[q][k], 16)
                    elif o.signal:
                        ins.then_inc(self.sems[sname], 1)
            return body

        for sname in ("sp", "pe", "act", "dve", "pool"):
            engs[sname](make(sname))


class TB:
    def __init__(self, t, name):
        self.t = t
        self.b = Buf(name)

    def __getitem__(self, k):
        return self.t[k]


class Ctx:
    def __init__(self, name):
        self.es = ExitStack()
        self.nc = bass.Bass("TRN2", target_bir_lowering=False)
        self.S = Sched(self.nc, self.es)
        self.n = 0

    def sb(self, shape, dt, name=None):
        self.n += 1
        name = name or f"sb{self.n}"
        return TB(self.es.enter_context(self.nc.sbuf_tensor(name, list(shape), dt)), name)

    def ps(self, shape, dt, name=None):
        self.n += 1
        name = name or f"ps{self.n}"
        return TB(self.es.enter_context(self.nc.psum_tensor(name, list(shape), dt)), name)

    def din(self, name, shape, dt):
        return TB(self.nc.dram_tensor(name, list(shape), dt, kind="ExternalInput").ap(), name)

    def dout(self, name, shape, dt):
        return TB(self.nc.dram_tensor(name, list(shape), dt, kind="ExternalOutput").ap(), name)

    def dscr(self, name, shape, dt):
        return TB(self.nc.dram_tensor(name, list(shape), dt, kind="Internal").ap(), name)

    def finish(self):
        self.S.finalize()
        self.es.close()
        return self.nc


def bcast_rows(ap1d_tb, n, parts=128):
    a = ap1d_tb.t
    return bass.AP(tensor=a.tensor, offset=a.offset, ap=[[0, parts], [1, n]])


D = 1024
NT = 16
TL = NT * 128
EPS = 1e-6
INW = 2072
FMW = 1536
TMW = 536


def build_A():
    C = Ctx("A"); S = C.S
    x = C.din("x", [TL, D], F32)
    gpre = C.din("gpre", [128, 8], F32)
    w = C.din("w", [D, INW], F32)
    ident = C.din("ident", [128, 128], F32)
    sgug = C.din("sgug", [256], F32)
    sgub = C.din("sgub", [256], F32)
    sguwT = C.din("sguwT", [4, 128, 128], F32)
    tril = C.din("tril", [128, 128], F32)
    sgubs = C.din("sgubs", [512], F32)
    qT = C.dout("qT", [512, TL], BF16)
    kvT = C.dout("kvT", [512, TL], BF16)
    pinT = C.dout("pinT", [256, TL], F32)
    vtm = C.dout("vtm", [TL, 256], BF16)
    gate = C.dout("gate", [TL, 24], F32)
    ocT = C.dout("ocT", [256, TL], BF16)

    Wb = C.sb([128, 8, INW], BF16, "Wb")
    stage = [C.sb([128, INW], F32, f"stage{i}") for i in range(2)]
    g_sb = C.sb([128, 8], F32, "g_sb")
    id_b = C.sb([128, 128], BF16, "id_b")
    sg_g = C.sb([128, 256], F32, "sg_g")
    sg_b = C.sb([128, 256], F32, "sg_b")
    bsb = C.sb([128, 4, 128], F32, "bsb")
    wst_f = C.sb([128, 4, 128], F32, "wst_f")
    tril_s = C.sb([128, 128], F32, "tril_s")
    wsT = C.sb([128, 4, 128], BF16, "wsT")

    eps_c = C.sb([128, 1], F32, "eps_c")
    S.op("dve", lambda e: e.memset(eps_c[:, :], EPS), writes=[eps_c.b])
    S.dma("sp", g_sb[:, :], gpre[:, :], writes=[g_sb.b])
    S.dma("poolq", id_b[:, :], ident[:, :], writes=[id_b.b])
    S.dma("sp", sg_g[:, :], bcast_rows(sgug, 256), writes=[sg_g.b])
    S.dma("sp", sg_b[:, :], bcast_rows(sgub, 256), writes=[sg_b.b])
    S.dma("sp", bsb[:, :, :].rearrange("p h t -> p (h t)"), bcast_rows(sgubs, 512), writes=[bsb.b])
    S.dma("sp", wst_f[:, :, :], sguwT[:, :, :].rearrange("h s t -> s h t"), writes=[wst_f.b])
    S.dma("sp", tril_s[:, :], tril[:, :], writes=[tril_s.b])
    for h in range(4):
        S.op("dve", lambda e, h=h: e.tensor_tensor(out=wsT[:, h, :], in0=wst_f[:, h, :], in1=tril_s[:, :], op=ALU.mult),
             reads=[wst_f.b, tril_s.b], writes=[wsT.b], nowaw=True)
    for kc in range(8):
        st = stage[kc % 2]
        S.dma("sp", st[:, :], w[kc * 128:(kc + 1) * 128, :], writes=[st.b])
        S.op("dve", lambda e, kc=kc, st=st: e.tensor_scalar(out=Wb[:, kc, :], in0=st[:, :], scalar1=g_sb[:, kc:kc + 1],
                                                           scalar2=None, op0=ALU.mult),
             reads=[st.b, g_sb.b], writes=[Wb.b], nowaw=True)

    xt = [C.sb([128, D], F32, f"xt{i}") for i in range(2)]
    sq = C.sb([128, D], BF16, "sq")
    ss = [C.sb([128, 1], F32, f"ss{i}") for i in range(2)]
    rs = [C.sb([128, 1], F32, f"rs{i}") for i in range(2)]
    hn = [C.sb([128, D], BF16, f"hn{i}") for i in range(2)]
    hT = [C.sb([128, 8, 512], BF16, f"hT{i}") for i in range(2)]
    psT = [C.ps([128, 8, 128], BF16, f"psT{i}") for i in range(2)]
    psF = [C.ps([128, 512], F32, f"psF{i}") for i in range(3)]
    psA = C.ps([128, 512], F32, "psA")
    psB = C.ps([128, 512], F32, "psB")
    psC = C.ps([128, 512], F32, "psC")
    fm_b = [C.sb([128, 512], BF16, f"fm_b{i}") for i in range(3)]
    fm_f = [C.sb([128, 512], F32, f"fm_f{i}") for i in range(2)]
    gu = [C.sb([128, 2, 512], F32, f"gu{i}") for i in range(2)]
    vt_b = [C.sb([128, 256], BF16, f"vt_b{i}") for i in range(2)]
    gt_f = [C.sb([128, 24], F32, f"gt_f{i}") for i in range(2)]
    vg = [C.sb([128, 256], F32, f"vg{i}") for i in range(2)]
    vc_ = [C.sb([128, 256], F32, f"vc{i}") for i in range(2)]
    vsq = C.sb([128, 256], F32, "vsq")
    st1 = [C.sb([128, 4], F32, f"st1{i}") for i in range(2)]
    vln = [C.sb([128, 256], BF16, f"vln{i}") for i in range(2)]
    tmp = [C.sb([128, 2, 128], F32, f"tmp{i}") for i in range(2)]
    oc = [C.sb([128, 2, 512], BF16, f"oc{i}") for i in range(2)]

    nfm = 0
    for grp in range(NT // 4):
        hTg = hT[grp % 2]
        gug = gu[grp % 2]
        ocg = oc[grp % 2]
        for tt in range(4):
            t = grp * 4 + tt
            xx = xt[t % 2]; s_ = ss[t % 2]; r_ = rs[t % 2]; hh = hn[t % 2]; pT = psT[t % 2]
            S.dma("sp", xx[:, :], x[t * 128:(t + 1) * 128, :], writes=[xx.b])
            S.op("act", lambda e, xx=xx, s_=s_: e.activation(out=sq[:, :], in_=xx[:, :], func=AF.Square, accum_out=s_[:, :]),
                 reads=[xx.b], writes=[sq.b, s_.b])
            S.op("act", lambda e, s_=s_, r_=r_: e.activation(out=r_[:, :], in_=s_[:, :], func=AF.Sqrt, scale=1.0 / D, bias=eps_c[:, :]),
                 reads=[s_.b, eps_c.b], writes=[r_.b])
            S.op("dve", lambda e, r_=r_: e.reciprocal(out=r_[:, :], in_=r_[:, :]), reads=[r_.b], writes=[r_.b])
            S.op("dve", lambda e, xx=xx, r_=r_, hh=hh: e.tensor_scalar(out=hh[:, :], in0=xx[:, :], scalar1=r_[:, :], scalar2=None,
                                                                     op0=ALU.mult), reads=[xx.b, r_.b], writes=[hh.b])
            for kc in range(8):
                S.op("pe", lambda e, kc=kc, hh=hh, pT=pT: e.transpose(out=pT[:, kc, :], in_=hh[:, kc * 128:(kc + 1) * 128], identity=id_b[:, :]),
                     reads=[hh.b, id_b.b], writes=[pT.b], nowaw=(kc > 0))
            S.op("dve", lambda e, pT=pT, tt=tt, hTg=hTg: e.tensor_copy(out=hTg[:, :, tt * 128:(tt + 1) * 128], in_=pT[:, :, :]),
                 reads=[pT.b], writes=[hTg.b], nowaw=True)
        for ocn in range(12):
            pf = psF[nfm % 3]
            for kc in range(8):
                S.op("pe", lambda e, pf=pf, kc=kc, ocn=ocn, hTg=hTg: e.matmul(pf[:, :], lhsT=Wb[:, kc, ocn * 128:(ocn + 1) * 128], rhs=hTg[:, kc, :],
                                                                          start=(kc == 0), stop=(kc == 7)),
                     reads=[Wb.b, hTg.b], writes=[pf.b])
            cols = slice(grp * 512, (grp + 1) * 512)
            if ocn < 4:
                fb = fm_b[nfm % 3]
                S.op("act", lambda e, pf=pf, fb=fb: e.activation(out=fb[:, :], in_=pf[:, :], func=AF.Copy, scale=0.125),
                     reads=[pf.b], writes=[fb.b])
                S.dma("poolq", qT[ocn * 128:(ocn + 1) * 128, cols], fb[:, :], reads=[fb.b], writes=[qT.b], nowaw=True, is_output=True)
            elif ocn < 8:
                fb = fm_b[nfm % 3]
                S.op("dve", lambda e, pf=pf, fb=fb: e.tensor_copy(out=fb[:, :], in_=pf[:, :]), reads=[pf.b], writes=[fb.b])
                S.dma("poolq", kvT[(ocn - 4) * 128:(ocn - 3) * 128, cols], fb[:, :], reads=[fb.b], writes=[kvT.b], nowaw=True, is_output=True)
            elif ocn < 10:
                ff = fm_f[nfm % 2]
                S.op("act", lambda e, pf=pf, ff=ff: e.activation(out=ff[:, :], in_=pf[:, :], func=AF.Copy),
                     reads=[pf.b], writes=[ff.b])
                S.dma("poolq", pinT[(ocn - 8) * 128:(ocn - 7) * 128, cols], ff[:, :], reads=[ff.b], writes=[pinT.b], nowaw=True, is_output=True)
            else:
                j = ocn - 10
                S.op("act", lambda e, pf=pf, j=j, gug=gug: e.activation(out=gug[:, j, :], in_=pf[:, :], func=AF.Gelu_apprx_tanh),
                     reads=[pf.b], writes=[gug.b], nowaw=(j > 0))
            nfm += 1
        for tt in range(4):
            t = grp * 4 + tt
            for kc in range(8):
                S.op("pe", lambda e, kc=kc, tt=tt, hTg=hTg: e.matmul(psA[:, :], lhsT=hTg[:, kc, tt * 128:(tt + 1) * 128], rhs=Wb[:, kc, FMW:FMW + 512],
                                                                  start=(kc == 0), stop=(kc == 7)),
                     reads=[Wb.b, hTg.b], writes=[psA.b])
            for kc in range(8):
                S.op("pe", lambda e, kc=kc, tt=tt, hTg=hTg: e.matmul(psB[:, 0:24], lhsT=hTg[:, kc, tt * 128:(tt + 1) * 128], rhs=Wb[:, kc, FMW + 512:FMW + 536],
                                                                  start=(kc == 0), stop=(kc == 7)),
                     reads=[Wb.b, hTg.b], writes=[psB.b])
            vb = vt_b[t % 2]; gf = gt_f[t % 2]; vg_ = vg[t % 2]; vcc = vc_[t % 2]; s1 = st1[t % 2]; vl = vln[t % 2]; tm = tmp[t % 2]
            S.op("dve", lambda e, vb=vb: e.tensor_copy(out=vb[:, :], in_=psA[:, 0:256]), reads=[psA.b], writes=[vb.b])
            S.dma("poolq", vtm[t * 128:(t + 1) * 128, :], vb[:, :], reads=[vb.b], writes=[vtm.b], nowaw=True, is_output=True)
            S.op("act", lambda e, gf=gf: e.activation(out=gf[:, :], in_=psB[:, 0:24], func=AF.Sigmoid), reads=[psB.b], writes=[gf.b])
            S.dma("poolq", gate[t * 128:(t + 1) * 128, :], gf[:, :], reads=[gf.b], writes=[gate.b], nowaw=True, is_output=True)
            S.op("act", lambda e, vg_=vg_, s1=s1: e.activation(out=vg_[:, :], in_=psA[:, 256:512], func=AF.Gelu_apprx_tanh, accum_out=s1[:, 0:1]),
                 reads=[psA.b], writes=[vg_.b, s1.b])
            S.op("dve", lambda e, s1=s1: e.tensor_scalar(out=s1[:, 1:2], in0=s1[:, 0:1], scalar1=-1.0 / 256, scalar2=None, op0=ALU.mult),
                 reads=[s1.b], writes=[s1.b])
            S.op("dve", lambda e, vg_=vg_, vcc=vcc, s1=s1: e.tensor_scalar(out=vcc[:, :], in0=vg_[:, :], scalar1=s1[:, 1:2], scalar2=None, op0=ALU.add),
                 reads=[vg_.b, s1.b], writes=[vcc.b])
            S.op("act", lambda e, vcc=vcc, s1=s1: e.activation(out=vsq[:, :], in_=vcc[:, :], func=AF.Square, accum_out=s1[:, 2:3]),
                 reads=[vcc.b], writes=[vsq.b, s1.b])
            S.op("act", lambda e, s1=s1: e.activation(out=s1[:, 3:4], in_=s1[:, 2:3], func=AF.Sqrt, scale=1.0 / 256, bias=eps_c[:, :]),
                 reads=[s1.b, eps_c.b], writes=[s1.b])
            S.op("dve", lambda e, s1=s1: e.reciprocal(out=s1[:, 3:4], in_=s1[:, 3:4]), reads=[s1.b], writes=[s1.b])
            S.op("dve", lambda e, vcc=vcc, s1=s1: e.tensor_scalar(out=vcc[:, :], in0=vcc[:, :], scalar1=s1[:, 3:4], scalar2=None, op0=ALU.mult),
                 reads=[vcc.b, s1.b], writes=[vcc.b])
            S.op("dve", lambda e, vcc=vcc: e.tensor_tensor(out=vcc[:, :], in0=vcc[:, :], in1=sg_g[:, :], op=ALU.mult),
                 reads=[vcc.b, sg_g.b], writes=[vcc.b])
            S.op("dve", lambda e, vcc=vcc, vl=vl: e.tensor_tensor(out=vl[:, :], in0=vcc[:, :], in1=sg_b[:, :], op=ALU.add),
                 reads=[vcc.b, sg_b.b], writes=[vl.b])
            for h in range(4):
                j = h // 2
                S.op("pe", lambda e, h=h, j=j, vl=vl: e.matmul(psC[:, h * 128:(h + 1) * 128], lhsT=vl[:, j * 128:(j + 1) * 128], rhs=wsT[:, h, :],
                                                            start=True, stop=True),
                     reads=[vl.b, wsT.b], writes=[psC.b], nowaw=(h > 0))
            for h in range(4):
                j = h // 2; r = h % 2
                P = slice(r * 64, r * 64 + 64)
                S.op("dve", lambda e, h=h, j=j, P=P, tm=tm: e.tensor_tensor(out=tm[P, j, :], in0=psC[P, h * 128:(h + 1) * 128], in1=bsb[P, h, :], op=ALU.add),
                     reads=[psC.b, bsb.b], writes=[tm.b], nowaw=True)
                S.op("dve", lambda e, j=j, P=P, tm=tm, tt=tt, gug=gug, ocg=ocg: e.tensor_tensor(out=ocg[P, j, tt * 128:(tt + 1) * 128], in0=tm[P, j, :],
                                                                                           in1=gug[P, j, tt * 128:(tt + 1) * 128], op=ALU.mult),
                     reads=[tm.b, gug.b], writes=[ocg.b], nowaw=True)
        for j in range(2):
            S.dma("poolq", ocT[j * 128:(j + 1) * 128, grp * 512:(grp + 1) * 512], ocg[:, j, :], reads=[ocg.b], writes=[ocT.b], nowaw=True, is_output=True)
    return C.finish()


_SPL = np.cumsum([0, 512, 128, 128, 128, 128, 128, 128, 24, 256, 256, 256])
_NAMES = ["q", "kc", "vc", "ksl", "vsl", "kw", "vw", "gt", "pin", "u", "v"]
_COL = {n: np.arange(_SPL[i], _SPL[i + 1]) for i, n in enumerate(_NAMES)}
_PERM = np.concatenate([_COL[n] for n in ["q", "kc", "vc", "ksl", "kw", "pin", "u", "vsl", "vw", "v", "gt"]])
_IDENT = np.eye(128, dtype=np.float32)
_TRIL_ST = np.triu(np.ones((128, 128), np.float32))


def local_rows(a, core):
    b, cp = divmod(core, 4)
    t = a[b].reshape((64, 128) + a.shape[2:])[cp::4]
    return np.ascontiguousarray(t.reshape((TL,) + a.shape[2:]))


def gcol(g):
    return np.ascontiguousarray(g.reshape(-1, 128).T)


def prep_A(inp, l, x, core):
    return {
        "x": local_rows(x, core),
        "gpre": gcol(inp["mix_norm_pre"][l]),
        "w": np.ascontiguousarray(inp["w_in"][l][:, _PERM]),
        "ident": _IDENT,
        "sgug": np.ascontiguousarray(inp["sgu_norm_g"][l]),
        "sgub": np.ascontiguousarray(inp["sgu_norm_b"][l]),
        "sguwT": np.ascontiguousarray(np.transpose(inp["sgu_w"][l], (0, 2, 1))),
        "tril": _TRIL_ST,
        "sgubs": np.ascontiguousarray(inp["sgu_b"][l].reshape(-1)),
    }


NEG = 30000.0


def build_B1(ST=4, NTL=NT):
    C = Ctx("B1"); S = C.S
    qT_in = C.din("qT_in", [128, 4, NTL * 128], BF16)
    gate_in = C.din("gate_in", [NTL * 128, 24], F32)
    kc_in = C.din("kc_in", [64, 2, 8192], BF16)
    vc_in = C.din("vc_in", [64, 2, 8192], BF16)
    ksl_in = C.din("ksl_in", [128, 8192], BF16)
    vsl_in = C.din("vsl_in", [128, 64, 2, 64], BF16)
    kw_in = C.din("kw_in", [128, NTL, 640], BF16)
    vw_in = C.din("vw_in", [128, NTL, 5, 2, 64], BF16)
    pin_in = C.din("pin_in", [128, 2, NTL, 144], F32)
    oc_in = C.din("oc_in", [128, 2, NTL * 128], BF16)
    x_in = C.din("x_in", [NTL * 128, D], F32)
    relb = C.din("relb", [32, 8], F32)
    ohs = C.din("ohs", [33, 1536], F32)
    ohw = C.din("ohw", [33, 1280], F32)
    ohw0 = C.din("ohw0", [33, 1280], F32)
    ohc = C.din("ohc", [33, 88, 128], F32)
    expand = C.din("expand", [128, 8192], F32)
    vfc = C.din("vfc", [128, 2, 9], F32)
    ident = C.din("ident", [128, 128], F32)
    invc0 = C.din("invc0", [128, 2, 128], F32)
    cpos = C.din("cpos", [64, 2, 32], F32)
    cw1 = C.din("cw1", [64, 2, 32, 128], F32)
    cb1 = C.din("cb1", [128, 2], F32)
    cw2 = C.din("cw2", [128, 2, 64], F32)
    cb2 = C.din("cb2", [2, 64], F32)
    poolw = C.din("poolw", [4, 64, 64], F32)
    pools = C.din("pools", [128, 2], F32)
    wo = C.din("wo", [D, D], F32)
    gpost = C.din("gpost", [D], F32)
    x_out = C.dout("x_out", [NTL * 128, D], F32)
    ebs_d = C.dscr("ebs_d", [8, 1536], BF16)
    ebw_d = C.dscr("ebw_d", [8, 1280], BF16)
    ebw0_d = C.dscr("ebw0_d", [8, 1280], BF16)

    rot = [C.ps([128, 512], F32, f"rot{i}") for i in range(4)]
    acc = [C.ps([128, 512], F32, f"acc{i}") for i in range(2)]
    psT = C.ps([128, 8, 128], BF16, "psT")
    psM = C.ps([128, 512], F32, "psM")
    rc = [0]

    def nrot():
        rc[0] += 1
        return rot[rc[0] % 4]

    eps_c = C.sb([128, 1], F32, "eps_c")
    S.op("dve", lambda e: e.memset(eps_c[:, :], EPS), writes=[eps_c.b])
    id_f = C.sb([128, 128], F32, "id_f")
    id_b = C.sb([128, 128], BF16, "id_b")
    S.dma("sp", id_f[:, :], ident[:, :], writes=[id_f.b])
    S.dma("poolq", id_b[:, :], ident[:, :], writes=[id_b.b])
    rb = C.sb([33, 8], F32, "rb")
    rbx = C.sb([33, 8], F32, "rbx")
    b31 = C.sb([32, 8], F32, "b31")
    S.dma("sp", rb[0:32, :], relb[:, :], writes=[rb.b])
    S.op("dve", lambda e: e.memset(rb[32:33, :], -NEG), writes=[rb.b], nowaw=True)
    S.dma("sp", b31[:, :], bass.AP(tensor=relb.t.tensor, offset=31 * 8, ap=[[0, 32], [1, 8]]), writes=[b31.b])
    S.op("dve", lambda e: e.tensor_tensor(out=rbx[0:32, :], in0=rb[0:32, :], in1=b31[:, :], op=ALU.subtract), reads=[rb.b, b31.b], writes=[rbx.b])
    S.op("dve", lambda e: e.memset(rbx[32:33, :], -NEG), writes=[rbx.b], nowaw=True)

    Tsel = C.sb([128, 11, 8, 128], BF16, "Tsel")
    Twin = C.sb([128, 5, 8, 128], BF16, "Twin")
    Tband = C.sb([128, 88, 8], F32, "Tband")
    oh_sb = C.sb([33, 512], F32, "oh_sb")
    eb_sb = C.sb([8, 1536], BF16, "eb_sb")

    def build_row(oh_d, L, lhs, scr):
        for j in range(0, L, 512):
            wd = min(512, L - j)
            S.dma("sp", oh_sb[:, 0:wd], oh_d[:, j:j + wd], writes=[oh_sb.b])
            p = nrot()
            S.op("pe", lambda e, p=p, j=j, wd=wd: e.matmul(p[0:8, 0:wd], lhsT=lhs[:, :], rhs=oh_sb[:, 0:wd], start=True, stop=True),
                 reads=[lhs.b, oh_sb.b], writes=[p.b])
            S.op("act", lambda e, p=p, j=j, wd=wd: e.activation(out=eb_sb[:, j:j + wd], in_=p[0:8, 0:wd], func=AF.Exp),
                 reads=[p.b], writes=[eb_sb.b], nowaw=True)
        S.dma("sp", scr[:, :], eb_sb[:, 0:L], reads=[eb_sb.b], writes=[scr.b])

    def toep(dst, ntile, scr, L, step):
        for i in range(ntile):
            S.dma("sp", dst[:, i, :, :], bass.AP(tensor=scr.t.tensor, offset=step * i, ap=[[1, 128], [L, 8], [1, 128]]),
                  reads=[scr.b], writes=[dst.b], nowaw=True)

    build_row(ohs, 1536, rbx, ebs_d)
    toep(Tsel, 11, ebs_d, 1536, 128)
    build_row(ohw0, 1280, rbx, ebw0_d)
    toep(Twin, 5, ebw0_d, 1280, 256)
    build_row(ohw, 1280, rbx, ebw_d)
    ohc_sb = [C.sb([33, 4, 128], F32, f"ohc_sb{i}") for i in range(2)]
    pb = [nrot(), nrot()]
    for ch in range(22):
        o_ = ohc_sb[ch % 2]
        S.dma("sp", o_[:, :, :], ohc[:, ch * 4:(ch + 1) * 4, :], writes=[o_.b])
        for k in range(4):
            c3 = ch * 4 + k
            p = pb[c3 // 64]
            S.op("pe", lambda e, p=p, o_=o_, k=k, c3=c3: e.matmul(p[:, (c3 % 64) * 8:(c3 % 64) * 8 + 8], lhsT=o_[:, k, :], rhs=rbx[:, :], start=True, stop=True),
                 reads=[o_.b, rbx.b], writes=[p.b], nowaw=True)
    S.op("act", lambda e: e.activation(out=Tband[:, 0:64, :].rearrange("p c h -> p (c h)"), in_=pb[0][:, 0:512], func=AF.Exp),
         reads=[pb[0].b], writes=[Tband.b])
    S.op("act", lambda e: e.activation(out=Tband[:, 64:88, :].rearrange("p c h -> p (c h)"), in_=pb[1][:, 0:192], func=AF.Exp),
         reads=[pb[1].b], writes=[Tband.b], nowaw=True)

    k_sb = C.sb([128, 8192], BF16, "k_sb")
    v_sb = C.sb([128, 64, 2, 65], BF16, "v_sb")
    S.op("pool", lambda e: e.memset(v_sb[:, :, :, 64:65], 1.0), writes=[v_sb.b])
    for hf in range(4):
        S.dma("sp", k_sb[:, hf * 2048:(hf + 1) * 2048], ksl_in[:, hf * 2048:(hf + 1) * 2048], writes=[k_sb.b], nowaw=True)
    for q4 in range(4):
        S.dma("sp", v_sb[:, q4 * 16:(q4 + 1) * 16, :, 0:64], vsl_in[:, q4 * 16:(q4 + 1) * 16, :, :], writes=[v_sb.b], nowaw=True)
    exp_sb = C.sb([128, 64, 128], BF16, "exp_sb")
    for q4 in range(4):
        S.dma("poolq", exp_sb[:, q4 * 16:(q4 + 1) * 16, :].rearrange("p a b -> p (a b)"), expand[:, q4 * 2048:(q4 + 1) * 2048], writes=[exp_sb.b], nowaw=True)
    vf_sb = C.sb([128, 2, 9], F32, "vf_sb")
    S.dma("sp", vf_sb[:, :, :], vfc[:, :, :], writes=[vf_sb.b])
    invc_sb = C.sb([128, 2, 128], F32, "invc_sb")
    S.dma("sp", invc_sb[:, :, :], invc0[:, :, :], writes=[invc_sb.b])
    ps_sb = C.sb([128, 2], F32, "ps_sb")
    S.dma("sp", ps_sb[:, :], pools[:, :], writes=[ps_sb.b])
    gp_sb = C.sb([128, D], F32, "gp_sb")
    S.dma("sp", gp_sb[:, :], bcast_rows(gpost, D), writes=[gp_sb.b])
    pwf = C.sb([128, 2, 128], F32, "pwf")
    pwb = C.sb([128, 2, 128], BF16, "pwb")
    S.op("dve", lambda e: e.memset(pwf[:, :, :], 0.0), writes=[pwf.b])
    for gi in range(4):
        r = gi % 2
        S.dma("sp", pwf[r * 64:(r + 1) * 64, gi // 2, r * 64:(r + 1) * 64], poolw[gi, :, :], writes=[pwf.b], nowaw=True)
    S.op("dve", lambda e: e.tensor_copy(out=pwb[:, :, :], in_=pwf[:, :, :]), reads=[pwf.b], writes=[pwb.b])
    wo_sb = C.sb([128, 8, D], BF16, "wo_sb")
    for kc in range(8):
        S.dma("poolq", wo_sb[:, kc, :], wo[kc * 128:(kc + 1) * 128, :], writes=[wo_sb.b], nowaw=True)

    w1_sb = C.sb([64, 32, 128], BF16, "w1_sb")
    pos_sb = C.sb([64, 2, 32], BF16, "pos_sb")
    w2_sb = C.sb([128, 2, 64], BF16, "w2_sb")
    b1_sb = C.sb([128, 2], F32, "b1_sb")
    b2c = C.sb([128, 1], F32, "b2c")
    w2f = C.sb([128, 64], F32, "w2f")
    w2pad = C.sb([128, 2, 128], BF16, "w2pad")
    b2bc = C.sb([128, 64], F32, "b2bc")
    bias_c = C.sb([128, 2], F32, "bias_c")
    S.dma("poolq", pos_sb[:, :, :], cpos[:, :, :], writes=[pos_sb.b])
    S.dma("poolq", w2_sb[:, :, :], cw2[:, :, :], writes=[w2_sb.b])
    S.dma("sp", b1_sb[:, :], cb1[:, :], writes=[b1_sb.b])
    for g in range(2):
        S.dma("sp", b2c[g * 64:(g + 1) * 64, :], bass.AP(tensor=cb2.t.tensor, offset=0, ap=[[1, 64], [1, 1]]), writes=[b2c.b], nowaw=True)
    S.dma("sp", w2f[:, :], cw2[:, 0, :], writes=[w2f.b])
    S.op("dve", lambda e: e.memset(w2pad[:, :, :], 0.0), writes=[w2pad.b])
    for g in range(2):
        S.op("dve", lambda e, g=g: e.tensor_copy(out=w2pad[:, g, g * 64:(g + 1) * 64], in_=w2f[:, :]), reads=[w2f.b], writes=[w2pad.b], nowaw=True)
    S.dma("sp", b2bc[:, :], bass.AP(tensor=cb2.t.tensor, offset=64, ap=[[0, 128], [1, 64]]), writes=[b2bc.b])
    kcmp = C.sb([128, 512], BF16, "kcmp")
    vcmp = C.sb([128, 4, 2, 64], BF16, "vcmp")
    cst = [C.sb([64, 4112], BF16, "cst0")]
    hTs = [C.sb([128, 512], BF16, f"hTs{i}") for i in range(2)]
    for i in range(2):
        S.op("dve", lambda e, i=i: e.memset(hTs[i][:, :], 0.0), writes=[hTs[i].b])
    ci = 0
    for a in range(2):
        S.dma("poolq", w1_sb[:, :, :], cw1[:, a, :, :], writes=[w1_sb.b])
        p = nrot()
        for l in range(32):
            S.op("pe", lambda e, p=p, a=a, l=l: e.matmul(p[:, 0:1], lhsT=w1_sb[:, l, :], rhs=pos_sb[:, a, l:l + 1], start=(l == 0), stop=(l == 31)),
                 reads=[w1_sb.b, pos_sb.b], writes=[p.b])
        S.op("dve", lambda e, p=p, a=a: e.tensor_tensor(out=bias_c[:, a:a + 1], in0=p[:, 0:1], in1=b1_sb[:, a:a + 1], op=ALU.add),
             reads=[p.b, b1_sb.b], writes=[bias_c.b], nowaw=True)
        src = kc_in if a == 0 else vc_in
        for g in range(2):
            st = cst[0]; hT_ = hTs[ci % 2]; ci += 1
            p = nrot()
            for hf in range(2):
                ntok = 4112 if hf == 0 else 4096
                ncb = 256 if hf == 0 else 255
                S.dma("sp", st[:, 0:ntok], src[:, g, hf * 4096:hf * 4096 + ntok], writes=[st.b])
                for l in range(32):
                    S.op("pe", lambda e, p=p, a=a, l=l, st=st, hf=hf, ncb=ncb: e.matmul(p[:, hf * 256:hf * 256 + ncb], lhsT=w1_sb[:, l, :], rhs=st[:, l:l + 16 * (ncb - 1) + 1:16],
                                                                                 start=(l == 0), stop=(l == 31)),
                         reads=[w1_sb.b, st.b], writes=[p.b], nowaw=(hf > 0))
            S.op("act", lambda e, p=p, a=a, hT_=hT_: e.activation(out=hT_[:, 0:511], in_=p[:, 0:511], func=AF.Gelu_apprx_tanh, bias=bias_c[:, a:a + 1]),
                 reads=[p.b, bias_c.b], writes=[hT_.b])
            if a == 0:
                p2 = nrot()
                S.op("pe", lambda e, p2=p2, hT_=hT_, g=g: e.matmul(p2[:, 0:512], lhsT=w2pad[:, g, :], rhs=hT_[:, :], start=True, stop=True),
                     reads=[w2pad.b, hT_.b], writes=[p2.b])
                S.op("act", lambda e, p2=p2, g=g: e.activation(out=kcmp[g * 64:(g + 1) * 64, :], in_=p2[g * 64:(g + 1) * 64, 0:512], func=AF.Identity, bias=b2c[g * 64:(g + 1) * 64, 0:1]),
                     reads=[p2.b, b2c.b], writes=[kcmp.b], nowaw=True)
            else:
                p2 = nrot()
                for ct in range(4):
                    S.op("pe", lambda e, p2=p2, hT_=hT_, ct=ct: e.matmul(p2[:, ct * 64:(ct + 1) * 64], lhsT=hT_[:, ct * 128:(ct + 1) * 128], rhs=w2_sb[:, 1, :], start=True, stop=True),
                         reads=[w2_sb.b, hT_.b], writes=[p2.b], nowaw=(ct > 0))
                for ct in range(4):
                    S.op("dve", lambda e, p2=p2, ct=ct, g=g: e.tensor_tensor(out=vcmp[:, ct, g, :], in0=p2[:, ct * 64:(ct + 1) * 64], in1=b2bc[:, :], op=ALU.add),
                         reads=[p2.b, b2bc.b], writes=[vcmp.b], nowaw=True)

    q_sb = [C.sb([128, 4, 128], BF16, f"q_sb{i}") for i in range(2)]
    q64 = Buf("q64")
    kw_sb = [C.sb([128, 640], BF16, f"kw_sb{i}") for i in range(2)]
    vw_sb = [C.sb([128, 5, 2, 65], BF16, f"vw_sb{i}") for i in range(2)]
    kvw64 = Buf("kvw64")
    for i in range(2):
        S.op("pool", lambda e, i=i: e.memset(vw_sb[i][:, :, :, 64:65], 1.0), writes=[kvw64], nowaw=True)
    g_sb = [C.sb([128, 24], F32, f"g_sb{i}") for i in range(2)]
    pin_sb = [C.sb([128, 2, 144], F32, f"pin_sb{i}") for i in range(2)]
    x_sb = [C.sb([128, D], F32, "x_sb0")] * 2
    mfT = [C.sb([128, 8, 128], BF16, f"mfT{i}") for i in range(2)]
    mfT_oc = [Buf(f"mfT_oc{i}") for i in range(2)]
    Ec = [C.sb([128, 4, 512], F32, "Ec0")] * 2
    pnb = [C.sb([128, 4, 512], BF16, "pnb0")] * 2
    impad = [C.sb([128, 520], F32, f"impad{i}") for i in range(2)]
    S.op("pool", lambda e: e.memset(pnb[0][:, :, :], 0.0), writes=[pnb[0].b])
    for i in range(2):
        S.op("pool", lambda e, i=i: e.memset(impad[i][:, :], 0.0), writes=[impad[i].b])
    dc = [C.sb([128, 8], F32, f"dc{i}") for i in range(2)]
    pT_sb = [C.sb([128, 4, 4, 128], BF16, "pT_sb0")] * 2
    sc = [C.sb([128, 128], F32, f"sc{i}") for i in range(2)]
    sc2 = C.sb([128, 128], F32, "sc2")
    t1 = C.sb([128, 128], F32, "t1")
    m8 = [C.sb([128, 16], F32, f"m8{i}") for i in range(2)]
    nsb = [C.sb([128, 128], BF16, f"nsb{i}") for i in range(2)]
    negT = [C.sb([128, 4, 128], BF16, f"negT{i}") for i in range(2)]
    Et = [C.sb([128, 512], BF16, f"Et{i}") for i in range(4)]
    ec = [0]
    oT_sb = [C.sb([65, 512], F32, f"oT_sb{i}") for i in range(2)]
    mix = [C.sb([128, 512], F32, f"mix{i}") for i in range(2)]
    mixb = [C.sb([128, 512], BF16, f"mixb{i}") for i in range(2)]
    tmpm = [C.sb([128, 4, 64], F32, f"tmpm{i}") for i in range(2)]
    fsc = [C.sb([128, 8], F32, f"fsc{i}") for i in range(2)]
    pl = [C.sb([128, 2, 144], F32, f"pl{i}") for i in range(4)]
    dd = [C.sb([128, 2, 128], BF16, f"dd{i}") for i in range(2)]
    ysq = C.sb([128, 512], BF16, "ysq")
    yss = [C.sb([128, 4], F32, f"yss{i}") for i in range(2)]
    yt = [C.sb([128, D], F32, "yt0")] * 2
    acn = [0]
    pc_ = [0]

    def tile(n):
        k2 = n % 2
        qs = q_sb[k2]; kws = kw_sb[k2]; vws = vw_sb[k2]; gs = g_sb[k2]; pins = pin_sb[k2]; xs = x_sb[k2]; mf = mfT[k2]
        mx = mix[k2]
        S.dma("sp", qs[:, :, :], qT_in[:, :, n * 128:(n + 1) * 128], writes=[qs.b])
        S.dma("sp", kws[:, :], kw_in[:, n, :], writes=[kws.b])
        S.dma("sp", vws[:, :, :, 0:64], vw_in[:, n, :, :, :], writes=[vws.b])
        S.dma("sp", gs[:, :], gate_in[n * 128:(n + 1) * 128, :], writes=[gs.b])
        S.dma("sp", pins[:, :, :], pin_in[:, :, n, :], writes=[pins.b])
        S.dma("sp", xs[:, :], x_in[n * 128:(n + 1) * 128, :], writes=[xs.b])
        S.dma("sp", mf[:, 6:8, :], oc_in[:, :, n * 128:(n + 1) * 128], writes=[mfT_oc[k2]])
        if n == 1:
            toep(Twin, 5, ebw_d, 1280, 256)
        Nc = 8 * ST * (n + 1)
        nct = (Nc + 127) // 128
        lo = max(0, 8 * ST * n - 56)
        c30 = lo - (8 * ST * n - 56)
        def grp(g):
            E_ = Ec[g]; pb_ = pnb[g]; im = impad[g]; d_ = dc[g]; pTs = pT_sb[g]
            gv = gs[:, g * 12:(g + 1) * 12].rearrange("p (h r) -> p h r", r=3)
            for hh in range(4):
                h = 4 * g + hh
                p = nrot()
                S.op("pe", lambda e, p=p, hh=hh, g=g, qs=qs: e.matmul(p[:, 0:Nc], lhsT=qs[g * 64:(g + 1) * 64, hh, :], rhs=kcmp[g * 64:(g + 1) * 64, 0:Nc], start=True, stop=True),
                     reads=[qs.b, q64, kcmp.b], writes=[p.b])
                S.op("act", lambda e, p=p, hh=hh, E_=E_: e.activation(out=E_[:, hh, 0:Nc], in_=p[:, 0:Nc], func=AF.Exp),
                     reads=[p.b], writes=[E_.b], nowaw=(hh > 0))
            S.op("dve", lambda e, E_=E_, g=g: e.tensor_tensor(out=E_[:, :, lo:Nc], in0=E_[:, :, lo:Nc],
                                                           in1=Tband[:, c30:c30 + Nc - lo, 4 * g:4 * g + 4].rearrange("p c h -> p h c"), op=ALU.mult),
                 reads=[E_.b, Tband.b], writes=[E_.b])
            S.op("dve", lambda e, E_=E_, d_=d_: e.tensor_reduce(out=d_[:, 0:4], in_=E_[:, :, 0:Nc], axis=AX.X, op=ALU.add),
                 reads=[E_.b], writes=[d_.b])
            S.op("dve", lambda e, d_=d_: e.tensor_scalar(out=d_[:, 0:4], in0=d_[:, 0:4], scalar1=1e-30, scalar2=None, op0=ALU.max), reads=[d_.b], writes=[d_.b])
            S.op("dve", lambda e, d_=d_: e.reciprocal(out=d_[:, 4:8], in_=d_[:, 0:4]), reads=[d_.b], writes=[d_.b])
            S.op("dve", lambda e, E_=E_, d_=d_: e.tensor_tensor(out=E_[:, :, 0:Nc], in0=E_[:, :, 0:Nc],
                                                             in1=d_[:, 4:8].unsqueeze(2).to_broadcast([128, 4, Nc]), op=ALU.mult),
                 reads=[E_.b, d_.b], writes=[E_.b])
            S.op("dve", lambda e, E_=E_, im=im: e.tensor_reduce(out=im[:, 1:1 + Nc], in_=E_[:, :, 0:Nc].rearrange("p h c -> p c h"), axis=AX.X, op=ALU.add),
                 reads=[E_.b], writes=[im.b])
            S.op("act", lambda e, E_=E_, pb_=pb_: e.activation(out=pb_[:, :, 0:Nc], in_=E_[:, :, 0:Nc], func=AF.Copy),
                 reads=[E_.b], writes=[pb_.b])
            for hh in range(4):
                for ct in range(nct):
                    S.op("pe", lambda e, hh=hh, ct=ct, pb_=pb_: e.transpose(out=psT[:, hh * 2 + ct % 2, :], in_=pb_[:, hh, ct * 128:(ct + 1) * 128], identity=id_b[:, :]),
                         reads=[pb_.b, id_b.b], writes=[psT.b], nowaw=not (hh == 0 and ct % 2 == 0 and ct == 0))
                    if ct % 2 == 1 or ct == nct - 1:
                        c0 = ct - (ct % 2)
                        w_ = ct - c0 + 1
                        S.op("act", lambda e, hh=hh, c0=c0, w_=w_, pTs=pTs: e.activation(out=pTs[:, hh, c0:c0 + w_, :], in_=psT[:, hh * 2:hh * 2 + w_, :], func=AF.Copy),
                             reads=[psT.b], writes=[pTs.b], nowaw=True)
            po = nrot()
            for hh in range(4):
                for ct in range(nct):
                    S.op("pe", lambda e, po=po, hh=hh, ct=ct, pTs=pTs, g=g: e.matmul(po[:, hh * 64:(hh + 1) * 64], lhsT=pTs[:, hh, ct, :], rhs=vcmp[:, ct, g, :],
                                                                              start=(ct == 0), stop=(ct == nct - 1)),
                         reads=[pTs.b, vcmp.b], writes=[po.b], nowaw=not (hh == 0 and ct == 0))
            S.op("dve", lambda e, po=po, mx=mx, g=g, gv=gv: e.tensor_tensor(out=mx[:, g * 256:(g + 1) * 256].rearrange("p (h d) -> p h d", d=64),
                                                                         in0=po[:, 0:256].rearrange("p (h d) -> p h d", d=64),
                                                                         in1=gv[:, :, 0:1].to_broadcast([128, 4, 64]), op=ALU.mult),
                 reads=[po.b, gs.b], writes=[mx.b], nowaw=(g > 0))
            s_ = sc[g]; m_ = m8[g]; ns_ = nsb[g]; nT = negT[g]
            A = im[:, 0:512].rearrange("p (j m) -> p j m", m=4)
            Bv = im[:, 4:516].rearrange("p (j m) -> p j m", m=4)
            S.op("dve", lambda e, A=A: e.tensor_tensor(out=t1[:, :], in0=A[:, :, 1], in1=A[:, :, 2], op=ALU.add), reads=[im.b], writes=[t1.b])
            S.op("dve", lambda e, A=A: e.tensor_tensor(out=t1[:, :], in0=t1[:, :], in1=A[:, :, 3], op=ALU.add), reads=[im.b, t1.b], writes=[t1.b])
            S.op("dve", lambda e, A=A: e.scalar_tensor_tensor(out=t1[:, :], in0=t1[:, :], scalar=2.0, in1=A[:, :, 0], op0=ALU.mult, op1=ALU.add),
                 reads=[im.b, t1.b], writes=[t1.b])
            S.op("dve", lambda e, Bv=Bv, s_=s_: e.tensor_tensor(out=s_[:, :], in0=t1[:, :], in1=Bv[:, :, 0], op=ALU.add), reads=[im.b, t1.b], writes=[s_.b])
            j0 = max(0, 2 * ST * n - 1)
            f0 = j0 - (2 * ST * n - 1)
            wloc = 2 * ST * (n + 1) - j0
            S.op("dve", lambda e, s_=s_: e.tensor_tensor(out=s_[:, j0:j0 + wloc], in0=s_[:, j0:j0 + wloc], in1=vf_sb[:, 0, f0:f0 + wloc], op=ALU.mult),
                 reads=[s_.b, vf_sb.b], writes=[s_.b])
            S.op("dve", lambda e, s_=s_: e.tensor_tensor(out=s_[:, j0:j0 + wloc], in0=s_[:, j0:j0 + wloc], in1=vf_sb[:, 1, f0:f0 + wloc], op=ALU.add),
                 reads=[s_.b, vf_sb.b], writes=[s_.b])
            if 2 * ST * (n + 1) < 128:
                S.op("dve", lambda e, s_=s_: e.memset(s_[:, 2 * ST * (n + 1):128], -1.0), writes=[s_.b], reads=[s_.b])
            S.op("dve", lambda e, s_=s_: e.memset(s_[:, 0:1], 1e9), writes=[s_.b], reads=[s_.b])
            S.op("dve", lambda e, s_=s_, m_=m_: e.max(out=m_[:, 0:8], in_=s_[:, :]), reads=[s_.b], writes=[m_.b])
            S.op("dve", lambda e, s_=s_, m_=m_: e.match_replace(out=sc2[:, :], in_to_replace=m_[:, 0:8], in_values=s_[:, :], imm_value=-2.0),
                 reads=[s_.b, m_.b], writes=[sc2.b])
            S.op("dve", lambda e, m_=m_: e.max(out=m_[:, 8:16], in_=sc2[:, :]), reads=[sc2.b], writes=[m_.b])
            S.op("dve", lambda e, s_=s_, m_=m_, ns_=ns_: e.tensor_scalar(out=ns_[:, :], in0=s_[:, :], scalar1=m_[:, 15:16], scalar2=1.0, op0=ALU.is_ge, op1=ALU.subtract),
                 reads=[s_.b, m_.b], writes=[ns_.b])
            S.op("pe", lambda e, ns_=ns_: e.transpose(out=psT[:, 0, :], in_=ns_[:, :], identity=id_b[:, :]), reads=[ns_.b, id_b.b], writes=[psT.b])
            S.op("dve", lambda e, nT=nT: e.tensor_scalar(out=nT[:, :, :], in0=psT[:, 0:1, :].to_broadcast([128, 4, 128]), scalar1=NEG, scalar2=None, op0=ALU.mult),
                 reads=[psT.b], writes=[nT.b])
            def branch(br):
                a_ = acc[acn[0] % 2]; acn[0] += 1
                nk = ST * (n + 1) if br == 1 else 5
                LA = 2
                staged = {}

                def stage1(kt):
                    p = nrot()
                    if br == 1:
                        S.op("pe", lambda e, p=p, kt=kt: e.matmul(p[:, :], lhsT=k_sb[g * 64:(g + 1) * 64, kt * 128:(kt + 1) * 128],
                                                                   rhs=qs[g * 64:(g + 1) * 64, :, :].rearrange("p h q -> p (h q)"), start=True, stop=False),
                             reads=[k_sb.b, qs.b, q64], writes=[p.b])
                        S.op("pe", lambda e, p=p, kt=kt: e.matmul(p[:, :], lhsT=exp_sb[:, kt, :], rhs=nT[:, :, :].rearrange("p h q -> p (h q)"), start=False, stop=True),
                             reads=[exp_sb.b, nT.b], writes=[p.b])
                        di = ST * n - kt + 3
                        tbl = Tsel[:, di, 4 * g:4 * g + 4, :] if di <= 10 else None
                        tb_b = Tsel.b
                        vl = v_sb[:, kt, g, :]
                        vb = v_sb.b
                    else:
                        S.op("pe", lambda e, p=p, kt=kt: e.matmul(p[:, :], lhsT=kws[g * 64:(g + 1) * 64, kt * 128:(kt + 1) * 128],
                                                                   rhs=qs[g * 64:(g + 1) * 64, :, :].rearrange("p h q -> p (h q)"), start=True, stop=True),
                             reads=[kws.b, kvw64, qs.b, q64], writes=[p.b])
                        tbl = Twin[:, kt, 4 * g:4 * g + 4, :]
                        tb_b = Twin.b
                        vl = vws[:, kt, g, :]
                        vb = vws.b
                    et = Et[ec[0] % 4]; ec[0] += 1
                    S.op("act", lambda e, p=p, et=et: e.activation(out=et[:, :], in_=p[:, :], func=AF.Exp), reads=[p.b], writes=[et.b])
                    if tbl is not None:
                        S.op("dve", lambda e, et=et, tbl=tbl: e.tensor_tensor(out=et[:, :].rearrange("p (h q) -> p h q", h=4), in0=et[:, :].rearrange("p (h q) -> p h q", h=4),
                                                                          in1=tbl, op=ALU.mult), reads=[et.b, tb_b], writes=[et.b])
                    staged[kt] = (et, vl, vb)

                def stage2(kt):
                    et, vl, vb = staged.pop(kt)
                    S.op("pe", lambda e, vl=vl, et=et, kt=kt: e.matmul(a_[0:65, :], lhsT=vl, rhs=et[:, :], start=(kt == 0), stop=(kt == nk - 1)),
                         reads=[vb, kvw64, et.b], writes=[a_.b])
                for kt in range(min(LA, nk)):
                    stage1(kt)
                for kt in range(nk):
                    if kt + LA < nk:
                        stage1(kt + LA)
                    stage2(kt)
                ot = oT_sb[pc_[0] % 2]; tm_ = tmpm[pc_[0] % 2]; fs = fsc[pc_[0] % 2]; pc_[0] += 1
                S.op("act", lambda e, a_=a_, ot=ot: e.activation(out=ot[:, :], in_=a_[0:65, :], func=AF.Copy), reads=[a_.b], writes=[ot.b])
                for hh in range(4):
                    S.op("pe", lambda e, hh=hh, ot=ot: e.transpose(out=psM[:, hh * 65:(hh + 1) * 65], in_=ot[0:65, hh * 128:(hh + 1) * 128], identity=id_f[0:65, 0:65]),
                         reads=[ot.b, id_f.b], writes=[psM.b], nowaw=(hh > 0))
                pv = psM[:, 0:260].rearrange("p (h d) -> p h d", d=65)
                S.op("dve", lambda e, pv=pv, fs=fs: e.tensor_scalar(out=fs[:, 0:4], in0=pv[:, :, 64], scalar1=1e-30, scalar2=None, op0=ALU.max), reads=[psM.b], writes=[fs.b])
                S.op("dve", lambda e, fs=fs: e.reciprocal(out=fs[:, 4:8], in_=fs[:, 0:4]), reads=[fs.b], writes=[fs.b])
                S.op("dve", lambda e, fs=fs, gv=gv, br=br: e.tensor_tensor(out=fs[:, 0:4], in0=fs[:, 4:8], in1=gv[:, :, br], op=ALU.mult), reads=[fs.b, gs.b], writes=[fs.b])
                S.op("dve", lambda e, pv=pv, fs=fs, tm_=tm_: e.tensor_tensor(out=tm_[:, :, :], in0=pv[:, :, 0:64], in1=fs[:, 0:4].unsqueeze(2).to_broadcast([128, 4, 64]), op=ALU.mult),
                     reads=[psM.b, fs.b], writes=[tm_.b])
                S.op("dve", lambda e, mx=mx, tm_=tm_, g=g: e.tensor_tensor(out=mx[:, g * 256:(g + 1) * 256].rearrange("p (h d) -> p h d", d=64),
                                                                        in0=mx[:, g * 256:(g + 1) * 256].rearrange("p (h d) -> p h d", d=64), in1=tm_[:, :, :], op=ALU.add),
                     reads=[mx.b, tm_.b], writes=[mx.b])
            for br in (1, 2):
                branch(br)
        for g in range(2):
            grp(g)
        mb = mixb[k2]
        S.op("act", lambda e, mb=mb, mx=mx: e.activation(out=mb[:, :], in_=mx[:, :], func=AF.Copy), reads=[mx.b], writes=[mb.b])
        for j in range(4):
            S.op("pe", lambda e, j=j, mb=mb: e.transpose(out=psT[:, j, :], in_=mb[:, j * 128:(j + 1) * 128], identity=id_b[:, :]),
                 reads=[mb.b, id_b.b], writes=[psT.b], nowaw=(j > 0))
        S.op("act", lambda e, mf=mf: e.activation(out=mf[:, 0:4, :], in_=psT[:, 0:4, :], func=AF.Copy), reads=[psT.b], writes=[mf.b])
        p2_, p4_, p8_, p16_ = pl
        S.op("pool", lambda e, pins=pins: e.tensor_tensor(out=p2_[:, :, 1:144], in0=pins[:, :, 1:144], in1=pins[:, :, 0:143], op=ALU.add), reads=[pins.b], writes=[p2_.b])
        S.op("pool", lambda e: e.tensor_tensor(out=p4_[:, :, 3:144], in0=p2_[:, :, 3:144], in1=p2_[:, :, 1:142], op=ALU.add), reads=[p2_.b], writes=[p4_.b])
        S.op("pool", lambda e: e.tensor_tensor(out=p8_[:, :, 7:144], in0=p4_[:, :, 7:144], in1=p4_[:, :, 3:140], op=ALU.add), reads=[p4_.b], writes=[p8_.b])
        S.op("pool", lambda e: e.tensor_tensor(out=p16_[:, :, 15:144], in0=p8_[:, :, 15:144], in1=p8_[:, :, 7:136], op=ALU.add), reads=[p8_.b], writes=[p16_.b])
        d_t = dd[k2]
        for gi, (srcw, win) in enumerate(((p2_, 2), (p4_, 4), (p8_, 8), (p16_, 16))):
            ch = gi // 2; P = slice((gi % 2) * 64, (gi % 2) * 64 + 64)
            if n == 0:
                S.op("pool", lambda e, srcw=srcw, ch=ch, P=P: e.tensor_tensor(out=srcw[P, ch, 16:144], in0=srcw[P, ch, 16:144], in1=invc_sb[P, ch, :], op=ALU.mult),
                     reads=[srcw.b, invc_sb.b], writes=[srcw.b])
                S.op("pool", lambda e, srcw=srcw, ch=ch, P=P, d_t=d_t, pins=pins: e.tensor_tensor(out=d_t[P, ch, :], in0=srcw[P, ch, 16:144], in1=pins[P, ch, 16:144], op=ALU.subtract),
                     reads=[srcw.b, pins.b], writes=[d_t.b], nowaw=True)
            else:
                S.op("dve", lambda e, srcw=srcw, ch=ch, P=P, d_t=d_t, pins=pins, win=win: e.scalar_tensor_tensor(out=d_t[P, ch, :], in0=srcw[P, ch, 16:144], scalar=1.0 / win,
                                                                                                         in1=pins[P, ch, 16:144], op0=ALU.mult, op1=ALU.subtract),
                     reads=[srcw.b, pins.b], writes=[d_t.b], nowaw=True)
        pp = nrot()
        for ch in range(2):
            S.op("pe", lambda e, pp=pp, ch=ch, d_t=d_t: e.matmul(pp[:, ch * 128:(ch + 1) * 128], lhsT=pwb[:, ch, :], rhs=d_t[:, ch, :], start=True, stop=True),
                 reads=[pwb.b, d_t.b], writes=[pp.b], nowaw=(ch > 0))
        for ch in range(2):
            S.op("act", lambda e, pp=pp, ch=ch, mf=mf: e.activation(out=mf[:, 4 + ch, :], in_=pp[:, ch * 128:(ch + 1) * 128], func=AF.Copy, scale=ps_sb[:, ch:ch + 1]),
                 reads=[pp.b, ps_sb.b], writes=[mf.b], nowaw=True)
        ys = yss[k2]; y_ = yt[k2]; xo_ = y_
        pys = [nrot(), nrot()]
        for hf in range(2):
            for kc in range(8):
                S.op("pe", lambda e, hf=hf, kc=kc, mf=mf: e.matmul(pys[hf][:, :], lhsT=mf[:, kc, :], rhs=wo_sb[:, kc, hf * 512:(hf + 1) * 512], start=(kc == 0), stop=(kc == 7)),
                     reads=[mf.b, mfT_oc[k2], wo_sb.b], writes=[pys[hf].b])
            S.op("act", lambda e, hf=hf, ys=ys: e.activation(out=ysq[:, :], in_=pys[hf][:, :], func=AF.Square, accum_out=ys[:, hf:hf + 1]),
                 reads=[pys[hf].b], writes=[ysq.b, ys.b])
        S.op("dve", lambda e, ys=ys: e.tensor_tensor(out=ys[:, 2:3], in0=ys[:, 0:1], in1=ys[:, 1:2], op=ALU.add), reads=[ys.b], writes=[ys.b])
        S.op("act", lambda e, ys=ys: e.activation(out=ys[:, 3:4], in_=ys[:, 2:3], func=AF.Sqrt, scale=1.0 / D, bias=eps_c[:, :]), reads=[ys.b, eps_c.b], writes=[ys.b])
        S.op("dve", lambda e, ys=ys: e.reciprocal(out=ys[:, 3:4], in_=ys[:, 3:4]), reads=[ys.b], writes=[ys.b])
        for hf in range(2):
            S.op("dve", lambda e, hf=hf, ys=ys, y_=y_: e.scalar_tensor_tensor(out=y_[:, hf * 512:(hf + 1) * 512], in0=pys[hf][:, :], scalar=ys[:, 3:4],
                                                                        in1=gp_sb[:, hf * 512:(hf + 1) * 512], op0=ALU.mult, op1=ALU.mult),
                 reads=[pys[hf].b, ys.b, gp_sb.b], writes=[y_.b], nowaw=(hf > 0))
        S.op("pool", lambda e, y_=y_, xs=xs, xo_=xo_: e.tensor_tensor(out=xo_[:, :], in0=y_[:, :], in1=xs[:, :], op=ALU.add), reads=[y_.b, xs.b], writes=[xo_.b])
        S.dma("poolq", x_out[n * 128:(n + 1) * 128, :], xo_[:, :], reads=[xo_.b], writes=[x_out.b], nowaw=True, is_output=True)
    for n in range(NTL):
        tile(n)
    return C.finish()


def _t5_bucket(n):
    n = np.maximum(np.asarray(n, np.int64), 0)
    nf = np.maximum(n, 1).astype(np.float32)
    lg = (np.log(nf / np.float32(16.0)) / np.float32(np.log(64.0)) * np.float32(16.0)).astype(np.float32)
    large = 16 + lg.astype(np.int32)
    large = np.minimum(large, 31)
    return np.where(n < 16, n, large).astype(np.int64)


def _onehot(dist, valid):
    dist = np.asarray(dist); valid = np.asarray(valid, bool)
    b = _t5_bucket(np.where(valid, dist, 0))
    rows = np.where(valid, b, 32)
    oh = np.zeros((33, dist.size), np.float32)
    oh[rows.reshape(-1), np.arange(dist.size)] = 1.0
    return oh


_CONST_CACHE = {}


def consts_B1(cp):
    if cp in _CONST_CACHE:
        return _CONST_CACHE[cp]
    m = np.arange(1536)
    d = m - 511 + 128 * cp
    ohs = _onehot(d, d >= 0)
    w = np.repeat(np.arange(5), 256); mp = np.tile(np.arange(256), 5)
    d = 128 * (4 - w) + mp - 127
    okw = (d >= 0) & (d < 512) & (mp <= 254)
    ohw = _onehot(d, okw)
    ohw0 = _onehot(d, okw & (w >= 4 - cp))
    c3 = np.repeat(np.arange(88), 128); tq = np.tile(np.arange(128), 88)
    d = 128 * cp + tq + 865 - 16 * c3
    ohc = _onehot(d, d >= 0).reshape(33, 88, 128)
    expand = np.zeros((128, 64, 128), np.float32)
    for kt in range(64):
        expand[2 * kt + 1, kt, 0:64] = 1.0
        expand[2 * kt, kt, 64:128] = 1.0
    vfc = np.zeros((128, 2, 9), np.float32)
    for t in range(128):
        bq = 2 * cp + (1 if t >= 64 else 0)
        for jj in range(9):
            jp = jj - 1
            valid = jp <= bq
            forced = (jp == bq) or (jp == bq - 1)
            vfc[t, 0, jj] = 1.0 if (valid and not forced) else 0.0
            vfc[t, 1, jj] = 1e9 if forced else (0.0 if valid else -1.0)
    invc0 = np.zeros((128, 2, 128), np.float32)
    for p in range(128):
        for ch in range(2):
            win = (2, 4, 8, 16)[2 * ch + (1 if p >= 64 else 0)]
            t = 128 * cp + np.arange(128)
            invc0[p, ch] = 1.0 / np.minimum(t + 1, win)
    r = dict(ohs=ohs, ohw=ohw, ohw0=ohw0, ohc=ohc, expand=expand.reshape(128, 8192), vfc=vfc, invc0=invc0, ident=_IDENT)
    _CONST_CACHE[cp] = r
    return r


def gather_full(A, b, key, rows):
    r0, r1 = rows
    out = np.empty((r1 - r0, 64, 128), A[0][key].dtype)
    for cp in range(4):
        out[:, cp::4, :] = A[b * 4 + cp][key][r0:r1].reshape(r1 - r0, NT, 128)
    return out.reshape(r1 - r0, 8192)


def gather_full_tm(A, b, key, cols):
    c0, c1 = cols
    out = np.empty((64, 128, c1 - c0), A[0][key].dtype)
    for cp in range(4):
        out[cp::4] = A[b * 4 + cp][key][:, c0:c1].reshape(NT, 128, c1 - c0)
    return out.reshape(8192, c1 - c0)


def prep_B1_batch(A, b):
    kc = gather_full(A, b, "kvT", (0, 128)); vc = gather_full(A, b, "kvT", (128, 256))
    ksl = gather_full(A, b, "kvT", (256, 384)); kw = gather_full(A, b, "kvT", (384, 512))
    vsl = gather_full_tm(A, b, "vtm", (0, 128)); vw = gather_full_tm(A, b, "vtm", (128, 256))
    pin = gather_full(A, b, "pinT", (0, 256))
    r = {}
    r["kc_in"] = np.ascontiguousarray(kc.reshape(2, 64, 8192).transpose(1, 0, 2))
    r["vc_in"] = np.ascontiguousarray(vc.reshape(2, 64, 8192).transpose(1, 0, 2))
    r["ksl_in"] = np.ascontiguousarray(ksl.reshape(128, 64, 128)[:, :, ::-1].reshape(128, 8192))
    r["vsl_in"] = np.ascontiguousarray(vsl.reshape(64, 128, 2, 64)[:, ::-1].transpose(1, 0, 2, 3))
    r["kwp"] = np.concatenate([np.zeros((128, 512), kw.dtype), kw], axis=1)
    r["vwp"] = np.concatenate([np.zeros((512, 128), vw.dtype), vw], axis=0)
    r["pinp"] = np.concatenate([np.zeros((256, 16), pin.dtype), pin], axis=1)
    return r


def prep_B1(inp, l, x, core, A, bb):
    b, cp = divmod(core, 4)
    r = dict(consts_B1(cp))
    a = A[core]
    r["qT_in"] = np.ascontiguousarray(a["qT"].reshape(2, 4, 64, TL).transpose(0, 2, 1, 3).reshape(128, 4, TL))
    r["gate_in"] = a["gate"]
    for k in ("kc_in", "vc_in", "ksl_in", "vsl_in"):
        r[k] = bb[k]
    kw_in = np.empty((128, NT, 640), bb["kwp"].dtype)
    vw_in = np.empty((128, NT, 5, 2, 64), bb["vwp"].dtype)
    pin_in = np.empty((128, 2, NT, 144), np.float32)
    for n in range(NT):
        i = 4 * n + cp
        kw_in[:, n, :] = bb["kwp"][:, 128 * i:128 * i + 640].reshape(128, 5, 128)[:, :, ::-1].reshape(128, 640)
        vw_in[:, n] = bb["vwp"][128 * i:128 * i + 640].reshape(5, 128, 2, 64)[:, ::-1].transpose(1, 0, 2, 3)
        pin_in[:, :, n, :] = bb["pinp"][:, 128 * i:128 * i + 144].reshape(2, 128, 144).transpose(1, 0, 2)
    r["kw_in"] = kw_in; r["vw_in"] = vw_in; r["pin_in"] = pin_in
    r["oc_in"] = np.ascontiguousarray(a["ocT"].reshape(2, 128, TL).transpose(1, 0, 2))
    r["x_in"] = local_rows(x, core)
    r["relb"] = np.ascontiguousarray(inp["rel_bias"])
    r["cpos"] = np.ascontiguousarray(inp["cmp_pos"][l].transpose(2, 0, 1))
    r["cw1"] = np.ascontiguousarray(inp["cmp_w1"][l].transpose(2, 0, 1, 3))
    r["cb1"] = np.ascontiguousarray(inp["cmp_b1"][l].T)
    r["cw2"] = np.ascontiguousarray(inp["cmp_w2"][l].transpose(1, 0, 2))
    r["cb2"] = np.ascontiguousarray(inp["cmp_b2"][l])
    r["poolw"] = np.ascontiguousarray(inp["pool_w"][l])
    r["pools"] = gcol(inp["pool_scale"][l])
    r["wo"] = np.ascontiguousarray(inp["w_out"][l])
    r["gpost"] = np.ascontiguousarray(inp["mix_norm_post"][l])
    return r


def load_w_scaled(C, S, w_d, rows, cols, g_sb, Wb, stage, q="sp"):
    for kc in range(rows // 128):
        st = stage[kc % 2]
        S.dma(q, st[:, 0:cols], w_d[kc * 128:(kc + 1) * 128, :], writes=[st.b])
        if g_sb is None:
            S.op("dve", lambda e, kc=kc, st=st: e.tensor_copy(out=Wb[:, kc, :], in_=st[:, 0:cols]), reads=[st.b], writes=[Wb.b], nowaw=True)
        else:
            S.op("dve", lambda e, kc=kc, st=st: e.tensor_scalar(out=Wb[:, kc, :], in0=st[:, 0:cols], scalar1=g_sb[:, kc:kc + 1], scalar2=None, op0=ALU.mult),
                 reads=[st.b, g_sb.b], writes=[Wb.b], nowaw=True)


def norm_rows(S, xin, nrows, sq, ss, eps_c, out_b):
    S.op("act", lambda e: e.activation(out=sq[0:nrows, :], in_=xin[0:nrows, :], func=AF.Square, accum_out=ss[0:nrows, 0:1]),
         reads=[xin.b], writes=[sq.b, ss.b])
    S.op("act", lambda e: e.activation(out=ss[0:nrows, 1:2], in_=ss[0:nrows, 0:1], func=AF.Sqrt, scale=1.0 / D, bias=eps_c[0:nrows, :]),
         reads=[ss.b, eps_c.b], writes=[ss.b])
    S.op("dve", lambda e: e.reciprocal(out=ss[0:nrows, 1:2], in_=ss[0:nrows, 1:2]), reads=[ss.b], writes=[ss.b])
    S.op("dve", lambda e: e.tensor_scalar(out=out_b[0:nrows, :], in0=xin[0:nrows, :], scalar1=ss[0:nrows, 1:2], scalar2=None, op0=ALU.mult),
         reads=[xin.b, ss.b], writes=[out_b.b])


def post_norm_residual(S, pys, ys, ysq, eps_c, gp_sb, y_, xs, x_out, row0):
    for hf in range(2):
        S.op("act", lambda e, hf=hf: e.activation(out=ysq[:, :], in_=pys[hf][:, :], func=AF.Square, accum_out=ys[:, hf:hf + 1]),
             reads=[pys[hf].b], writes=[ysq.b, ys.b])
    S.op("dve", lambda e: e.tensor_tensor(out=ys[:, 2:3], in0=ys[:, 0:1], in1=ys[:, 1:2], op=ALU.add), reads=[ys.b], writes=[ys.b])
    S.op("act", lambda e: e.activation(out=ys[:, 3:4], in_=ys[:, 2:3], func=AF.Sqrt, scale=1.0 / D, bias=eps_c[:, :]), reads=[ys.b, eps_c.b], writes=[ys.b])
    S.op("dve", lambda e: e.reciprocal(out=ys[:, 3:4], in_=ys[:, 3:4]), reads=[ys.b], writes=[ys.b])
    for hf in range(2):
        S.op("dve", lambda e, hf=hf: e.scalar_tensor_tensor(out=y_[:, hf * 512:(hf + 1) * 512], in0=pys[hf][:, :], scalar=ys[:, 3:4],
                                                            in1=gp_sb[:, hf * 512:(hf + 1) * 512], op0=ALU.mult, op1=ALU.mult),
             reads=[pys[hf].b, ys.b, gp_sb.b], writes=[y_.b], nowaw=(hf > 0))
    S.op("pool", lambda e: e.tensor_tensor(out=y_[:, :], in0=y_[:, :], in1=xs[:, :], op=ALU.add), reads=[y_.b, xs.b], writes=[y_.b])
    S.dma("poolq", x_out[row0:row0 + 128, :], y_[:, :], reads=[y_.b], writes=[x_out.b], nowaw=True, is_output=True)


def build_B2():
    C = Ctx("B2"); S = C.S
    x_in = C.din("x_in", [TL, D], F32)
    mem = C.din("mem", [256, D], F32)
    gq = C.din("gq", [128, 8], F32)
    gkv = C.din("gkv", [128, 8], F32)
    gpost = C.din("gpost", [D], F32)
    wq = C.din("wq", [D, D], F32); wk = C.din("wk", [D, D], F32); wv = C.din("wv", [D, D], F32); wo = C.din("wo", [D, D], F32)
    ident = C.din("ident", [128, 128], F32)
    x_out = C.dout("x_out", [TL, D], F32)
    rot = [C.ps([128, 512], F32, f"rot{i}") for i in range(5)]
    psT = [C.ps([128, 8, 128], BF16, f"psT{i}") for i in range(2)]
    psd = C.ps([128, 512], F32, "psd")
    rc = [0]

    def nrot():
        rc[0] += 1
        return rot[rc[0] % 5]
    eps_c = C.sb([128, 1], F32, "eps_c")
    S.op("dve", lambda e: e.memset(eps_c[:, :], EPS), writes=[eps_c.b])
    id_b = C.sb([128, 128], BF16, "id_b")
    S.dma("poolq", id_b[:, :], ident[:, :], writes=[id_b.b])
    ones = C.sb([128, 128], BF16, "ones")
    S.op("dve", lambda e: e.memset(ones[:, :], 1.0), writes=[ones.b])
    gq_sb = C.sb([128, 8], F32, "gq_sb"); gkv_sb = C.sb([128, 8], F32, "gkv_sb")
    S.dma("sp", gq_sb[:, :], gq[:, :], writes=[gq_sb.b])
    S.dma("sp", gkv_sb[:, :], gkv[:, :], writes=[gkv_sb.b])
    gp_sb = C.sb([128, D], F32, "gp_sb")
    S.dma("sp", gp_sb[:, :], bcast_rows(gpost, D), writes=[gp_sb.b])
    stage = [C.sb([128, D], F32, f"stage{i}") for i in range(2)]
    Wq = C.sb([128, 8, D], BF16, "Wq"); Wk = C.sb([128, 8, D], BF16, "Wk"); Wv = C.sb([128, 8, D], BF16, "Wv"); Wo = C.sb([128, 8, D], BF16, "Wo")
    load_w_scaled(C, S, wk, D, D, gkv_sb, Wk, stage)
    load_w_scaled(C, S, wv, D, D, gkv_sb, Wv, stage)
    load_w_scaled(C, S, wq, D, D, gq_sb, Wq, stage)
    load_w_scaled(C, S, wo, D, D, None, Wo, stage)
    xt = [C.sb([128, D], F32, f"xt{i}") for i in range(2)]
    sq = C.sb([128, D], BF16, "sq")
    ss = [C.sb([128, 2], F32, f"ss{i}") for i in range(2)]
    hn = [C.sb([128, D], BF16, f"hn{i}") for i in range(2)]
    mT = C.sb([128, 8, 256], BF16, "mT")
    for mt in range(2):
        xx = xt[mt]; hh = hn[mt]; pT = psT[mt]
        S.dma("sp", xx[:, :], mem[mt * 128:(mt + 1) * 128, :], writes=[xx.b])
        norm_rows(S, xx, 128, sq, ss[mt], eps_c, hh)
        for kc in range(8):
            S.op("pe", lambda e, kc=kc, hh=hh, pT=pT: e.transpose(out=pT[:, kc, :], in_=hh[:, kc * 128:(kc + 1) * 128], identity=id_b[:, :]),
                 reads=[hh.b, id_b.b], writes=[pT.b], nowaw=(kc > 0))
        S.op("dve", lambda e, pT=pT, mt=mt: e.tensor_copy(out=mT[:, :, mt * 128:(mt + 1) * 128], in_=pT[:, :, :]), reads=[pT.b], writes=[mT.b], nowaw=True)
    kT = C.sb([128, 8, 256], BF16, "kT")
    v_sb = C.sb([128, 2, D], BF16, "v_sb")
    for oc in range(8):
        p = nrot()
        for kc in range(8):
            S.op("pe", lambda e, p=p, kc=kc, oc=oc: e.matmul(p[:, 0:256], lhsT=Wk[:, kc, oc * 128:(oc + 1) * 128], rhs=mT[:, kc, :], start=(kc == 0), stop=(kc == 7)),
                 reads=[Wk.b, mT.b], writes=[p.b])
        S.op("act", lambda e, p=p, oc=oc: e.activation(out=kT[:, oc, :], in_=p[:, 0:256], func=AF.Copy), reads=[p.b], writes=[kT.b], nowaw=True)
    for mt in range(2):
        for hf in range(2):
            p = nrot()
            for kc in range(8):
                S.op("pe", lambda e, p=p, kc=kc, mt=mt, hf=hf: e.matmul(p[:, :], lhsT=mT[:, kc, mt * 128:(mt + 1) * 128], rhs=Wv[:, kc, hf * 512:(hf + 1) * 512], start=(kc == 0), stop=(kc == 7)),
                     reads=[Wv.b, mT.b], writes=[p.b])
            S.op("act", lambda e, p=p, mt=mt, hf=hf: e.activation(out=v_sb[:, mt, hf * 512:(hf + 1) * 512], in_=p[:, :], func=AF.Copy), reads=[p.b], writes=[v_sb.b], nowaw=True)
    hT = [C.sb([128, 8, 512], BF16, f"hT{i}") for i in range(2)]
    qT = [C.sb([128, 8, 512], BF16, f"qT{i}") for i in range(2)]
    oT = [C.sb([128, 8, 512], BF16, f"oT{i}") for i in range(2)]
    Pt = [C.sb([128, 512], BF16, f"Pt{i}") for i in range(4)]
    rden = [C.sb([128, 512], F32, f"rden{i}") for i in range(2)]
    xk = [C.sb([128, D], F32, f"xk{i}") for i in range(4)]
    ysq = C.sb([128, 512], BF16, "ysq")
    yss = [C.sb([128, 4], F32, f"yss{i}") for i in range(2)]
    yt = [C.sb([128, D], F32, f"yt{i}") for i in range(2)]
    pcnt = [0]

    def group(grp):
        hTg = hT[grp % 2]; qTg = qT[grp % 2]; oTg = oT[grp % 2]
        for tt in range(4):
            t = grp * 4 + tt
            xx = xk[tt]; hh = hn[t % 2]; pT = psT[t % 2]
            S.dma("sp", xx[:, :], x_in[t * 128:(t + 1) * 128, :], writes=[xx.b])
            norm_rows(S, xx, 128, sq, ss[t % 2], eps_c, hh)
            for kc in range(8):
                S.op("pe", lambda e, kc=kc, hh=hh, pT=pT: e.transpose(out=pT[:, kc, :], in_=hh[:, kc * 128:(kc + 1) * 128], identity=id_b[:, :]),
                     reads=[hh.b, id_b.b], writes=[pT.b], nowaw=(kc > 0))
            S.op("dve", lambda e, pT=pT, tt=tt: e.tensor_copy(out=hTg[:, :, tt * 128:(tt + 1) * 128], in_=pT[:, :, :]), reads=[pT.b], writes=[hTg.b], nowaw=True)
        for oc in range(8):
            p = nrot()
            for kc in range(8):
                S.op("pe", lambda e, p=p, kc=kc, oc=oc: e.matmul(p[:, :], lhsT=Wq[:, kc, oc * 128:(oc + 1) * 128], rhs=hTg[:, kc, :], start=(kc == 0), stop=(kc == 7)),
                     reads=[Wq.b, hTg.b], writes=[p.b])
            S.op("act", lambda e, p=p, oc=oc: e.activation(out=qTg[:, oc, :], in_=p[:, :], func=AF.Copy, scale=1.0 / 16), reads=[p.b], writes=[qTg.b], nowaw=True)
        for hd in range(4):
            pts = []
            for mt in range(2):
                p = nrot()
                for dc in range(2):
                    S.op("pe", lambda e, p=p, dc=dc, mt=mt, hd=hd: e.matmul(p[:, :], lhsT=kT[:, hd * 2 + dc, mt * 128:(mt + 1) * 128], rhs=qTg[:, hd * 2 + dc, :], start=(dc == 0), stop=(dc == 1)),
                         reads=[kT.b, qTg.b], writes=[p.b])
                pt = Pt[pcnt[0] % 4]; pcnt[0] += 1
                S.op("act", lambda e, p=p, pt=pt: e.activation(out=pt[:, :], in_=p[:, :], func=AF.Exp), reads=[p.b], writes=[pt.b])
                pts.append(pt)
            for mt in range(2):
                S.op("pe", lambda e, mt=mt, pts=pts: e.matmul(psd[:, :], lhsT=ones[:, :], rhs=pts[mt][:, :], start=(mt == 0), stop=(mt == 1)),
                     reads=[ones.b, pts[mt].b], writes=[psd.b])
            rd = rden[hd % 2]
            S.op("dve", lambda e, rd=rd: e.reciprocal(out=rd[:, :], in_=psd[:, :]), reads=[psd.b], writes=[rd.b])
            for dvc in range(2):
                p = nrot()
                for mt in range(2):
                    S.op("pe", lambda e, p=p, mt=mt, hd=hd, dvc=dvc, pts=pts: e.matmul(p[:, :], lhsT=v_sb[:, mt, hd * 256 + dvc * 128:hd * 256 + dvc * 128 + 128], rhs=pts[mt][:, :],
                                                                                  start=(mt == 0), stop=(mt == 1)),
                         reads=[v_sb.b, pts[mt].b], writes=[p.b])
                S.op("dve", lambda e, p=p, hd=hd, dvc=dvc, rd=rd: e.tensor_tensor(out=oTg[:, hd * 2 + dvc, :], in0=p[:, :], in1=rd[:, :], op=ALU.mult),
                     reads=[p.b, rd.b], writes=[oTg.b], nowaw=True)
        for tt in range(4):
            t = grp * 4 + tt
            pys = [nrot(), nrot()]
            for hf in range(2):
                for kc in range(8):
                    S.op("pe", lambda e, hf=hf, kc=kc, tt=tt, pys=pys: e.matmul(pys[hf][:, :], lhsT=oTg[:, kc, tt * 128:(tt + 1) * 128], rhs=Wo[:, kc, hf * 512:(hf + 1) * 512], start=(kc == 0), stop=(kc == 7)),
                         reads=[oTg.b, Wo.b], writes=[pys[hf].b])
            post_norm_residual(S, pys, yss[t % 2], ysq, eps_c, gp_sb, yt[t % 2], xk[tt], x_out, t * 128)
    for grp in range(NT // 4):
        group(grp)
    return C.finish()


def prep_B2(inp, l, core, x1_local):
    b = core // 4
    return {"x_in": x1_local, "mem": np.ascontiguousarray(inp["mem"][b]), "gq": gcol(inp["mem_norm_pre"][l]), "gkv": gcol(inp["mem_norm_kv"][l]),
            "gpost": np.ascontiguousarray(inp["mem_norm_post"][l]), "wq": np.ascontiguousarray(inp["w_mq"][l]), "wk": np.ascontiguousarray(inp["w_mk"][l]),
            "wv": np.ascontiguousarray(inp["w_mv"][l]), "wo": np.ascontiguousarray(inp["w_mo"][l]), "ident": _IDENT}


FH = 2816
NCH = 44


def build_C():
    C = Ctx("C"); S = C.S
    x_in = C.din("x_in", [TL, D], F32)
    xh = C.din("xh", [NT * 2, D], F32)
    gpre = C.din("gpre", [128, 8], F32)
    gpost = C.din("gpost", [D], F32)
    wu = C.din("wu", [D, 2 * FH], F32)
    wd = C.din("wd", [FH, D], F32)
    cwc = C.din("cwc", [128, NCH, 3], F32)
    cbc = C.din("cbc", [128, NCH], F32)
    ident = C.din("ident", [128, 128], F32)
    x_out = C.dout("x_out", [TL, D], F32)
    rot = [C.ps([128, 512], F32, f"rot{i}") for i in range(4)]
    psh = [C.ps([128, 512], F32, f"psh{i}") for i in range(2)]
    psT = [C.ps([128, 8, 128], BF16, f"psT{i}") for i in range(2)]
    rc = [0]

    def nrot():
        rc[0] += 1
        return rot[rc[0] % 4]
    eps_c = C.sb([128, 1], F32, "eps_c")
    S.op("dve", lambda e: e.memset(eps_c[:, :], EPS), writes=[eps_c.b])
    id_b = C.sb([128, 128], BF16, "id_b")
    S.dma("poolq", id_b[:, :], ident[:, :], writes=[id_b.b])
    g_sb = C.sb([128, 8], F32, "g_sb")
    S.dma("sp", g_sb[:, :], gpre[:, :], writes=[g_sb.b])
    gp_sb = C.sb([128, D], F32, "gp_sb")
    S.dma("sp", gp_sb[:, :], bcast_rows(gpost, D), writes=[gp_sb.b])
    cw_sb = C.sb([128, NCH, 3], F32, "cw_sb"); cb_sb = C.sb([128, NCH], F32, "cb_sb")
    S.dma("sp", cw_sb[:, :, :], cwc[:, :, :], writes=[cw_sb.b])
    S.dma("sp", cb_sb[:, :], cbc[:, :], writes=[cb_sb.b])
    Wu = C.sb([128, 8, 2 * FH], BF16, "Wu")
    Wd = C.sb([128, 22, D], BF16, "Wd")
    stage = [C.sb([128, 704], F32, f"stage{i}") for i in range(2)]
    si = 0
    for kc in range(8):
        for cbk in range(8):
            st = stage[si % 2]; si += 1
            S.dma("sp", st[:, :], wu[kc * 128:(kc + 1) * 128, cbk * 704:(cbk + 1) * 704], writes=[st.b])
            S.op("dve", lambda e, kc=kc, cbk=cbk, st=st: e.tensor_scalar(out=Wu[:, kc, cbk * 704:(cbk + 1) * 704], in0=st[:, :], scalar1=g_sb[:, kc:kc + 1], scalar2=None, op0=ALU.mult),
                 reads=[st.b, g_sb.b], writes=[Wu.b], nowaw=True)
    for j in range(22):
        S.dma("poolq", Wd[:, j, :], wd[j * 128:(j + 1) * 128, :], writes=[Wd.b], nowaw=True)
    xk = [C.sb([128, D], F32, f"xk{i}") for i in range(2)]
    xhs = C.sb([8, D], F32, "xhs")
    sq = C.sb([128, D], BF16, "sq")
    ss = [C.sb([128, 2], F32, f"ss{i}") for i in range(2)]
    hn = [C.sb([128, D], BF16, f"hn{i}") for i in range(2)]
    hT = [C.sb([128, 8, 520], BF16, "hT0")] * 2
    ab = [C.sb([128, 4, 130], F32, f"ab{i}") for i in range(2)]
    yy = [C.sb([128, 4, 128], F32, f"yy{i}") for i in range(2)]
    gg = [C.sb([128, 4, 128], F32, f"gg{i}") for i in range(2)]
    uT = C.sb([128, 22, 512], BF16, "uT")
    ysq = C.sb([128, 512], BF16, "ysq")
    yss = [C.sb([128, 4], F32, f"yss{i}") for i in range(2)]
    yt = [C.sb([128, D], F32, "yt0")] * 2
    cnt = [0]

    def group(grp):
        hTg = hT[grp % 2]
        for tt in range(4):
            t = grp * 4 + tt
            xx = xk[tt % 2]; hh = hn[t % 2]; pT = psT[t % 2]
            S.dma("sp", xx[:, :], x_in[t * 128:(t + 1) * 128, :], writes=[xx.b])
            norm_rows(S, xx, 128, sq, ss[t % 2], eps_c, hh)
            for kc in range(8):
                S.op("pe", lambda e, kc=kc, hh=hh, pT=pT: e.transpose(out=pT[:, kc, :], in_=hh[:, kc * 128:(kc + 1) * 128], identity=id_b[:, :]),
                     reads=[hh.b, id_b.b], writes=[pT.b], nowaw=(kc > 0))
            S.op("dve", lambda e, pT=pT, tt=tt: e.tensor_copy(out=hTg[:, :, tt * 128:(tt + 1) * 128], in_=pT[:, :, :]), reads=[pT.b], writes=[hTg.b], nowaw=True)
        hh = hn[0]; pT = psT[0]
        S.dma("sp", xhs[:, :], xh[grp * 8:(grp + 1) * 8, :], writes=[xhs.b])
        norm_rows(S, xhs, 8, sq, ss[0], eps_c, hh)
        for kc in range(8):
            S.op("pe", lambda e, kc=kc: e.transpose(out=pT[:, kc, 0:8], in_=hh[0:8, kc * 128:(kc + 1) * 128], identity=id_b[0:8, 0:8]),
                 reads=[hh.b, id_b.b], writes=[pT.b], nowaw=(kc > 0))
        S.op("dve", lambda e: e.tensor_copy(out=hTg[:, :, 512:520], in_=pT[:, :, 0:8]), reads=[pT.b], writes=[hTg.b], nowaw=True)

        def chunk(oc):
            k = cnt[0]; cnt[0] += 1
            a_ = ab[k % 2]; y_ = yy[k % 2]
            pm = nrot(); ph = psh[k % 2]
            for kc in range(8):
                S.op("pe", lambda e, kc=kc: e.matmul(pm[:, :], lhsT=Wu[:, kc, oc * 128:(oc + 1) * 128], rhs=hTg[:, kc, 0:512], start=(kc == 0), stop=(kc == 7)),
                     reads=[Wu.b, hTg.b], writes=[pm.b])
            for kc in range(8):
                S.op("pe", lambda e, kc=kc: e.matmul(ph[:, 0:8], lhsT=Wu[:, kc, oc * 128:(oc + 1) * 128], rhs=hTg[:, kc, 512:520], start=(kc == 0), stop=(kc == 7)),
                     reads=[Wu.b, hTg.b], writes=[ph.b])
            S.op("act", lambda e: e.activation(out=a_[:, :, 2:130], in_=pm[:, :].rearrange("p (t q) -> p t q", q=128), func=AF.Copy), reads=[pm.b], writes=[a_.b])
            S.op("act", lambda e: e.activation(out=a_[:, :, 0:2], in_=ph[:, 0:8].rearrange("p (t q) -> p t q", q=2), func=AF.Copy), reads=[ph.b], writes=[a_.b], nowaw=True)
            S.op("act", lambda e: e.activation(out=y_[:, :, :], in_=pm[:, :].rearrange("p (t q) -> p t q", q=128), func=AF.Identity,
                                               scale=cw_sb[:, oc, 2:3], bias=cb_sb[:, oc:oc + 1]), reads=[pm.b, cw_sb.b, cb_sb.b], writes=[y_.b])
            S.op("dve", lambda e: e.scalar_tensor_tensor(out=y_[:, :, :], in0=a_[:, :, 1:129], scalar=cw_sb[:, oc, 1:2], in1=y_[:, :, :], op0=ALU.mult, op1=ALU.add),
                 reads=[a_.b, y_.b, cw_sb.b], writes=[y_.b])
            S.op("dve", lambda e: e.scalar_tensor_tensor(out=y_[:, :, :], in0=a_[:, :, 0:128], scalar=cw_sb[:, oc, 0:1], in1=y_[:, :, :], op0=ALU.mult, op1=ALU.add),
                 reads=[a_.b, y_.b, cw_sb.b], writes=[y_.b])
            return y_
        for j in range(22):
            yg = chunk(j)
            g_ = gg[j % 2]
            S.op("act", lambda e, yg=yg, g_=g_: e.activation(out=g_[:, :, :], in_=yg[:, :, :], func=AF.Gelu_apprx_tanh), reads=[yg.b], writes=[g_.b])
            yv = chunk(22 + j)
            S.op("dve", lambda e, yv=yv, g_=g_, j=j: e.tensor_tensor(out=uT[:, j, :].rearrange("p (t q) -> p t q", q=128), in0=yv[:, :, :], in1=g_[:, :, :], op=ALU.mult),
                 reads=[yv.b, g_.b], writes=[uT.b], nowaw=True)
        for tt in range(4):
            t = grp * 4 + tt
            pys = [nrot(), nrot()]
            for hf in range(2):
                for j in range(22):
                    S.op("pe", lambda e, hf=hf, j=j, tt=tt, pys=pys: e.matmul(pys[hf][:, :], lhsT=uT[:, j, tt * 128:(tt + 1) * 128], rhs=Wd[:, j, hf * 512:(hf + 1) * 512], start=(j == 0), stop=(j == 21)),
                         reads=[uT.b, Wd.b], writes=[pys[hf].b])
            xx = xk[tt % 2]
            S.dma("sp", xx[:, :], x_in[t * 128:(t + 1) * 128, :], writes=[xx.b])
            post_norm_residual(S, pys, yss[t % 2], ysq, eps_c, gp_sb, yt[t % 2], xx, x_out, t * 128)
    for grp in range(NT // 4):
        group(grp)
    return C.finish()


def halo_rows(x_full, core, nh):
    b, cp = divmod(core, 4)
    out = np.zeros((NT, nh, x_full.shape[2]), x_full.dtype)
    for n in range(NT):
        i = 4 * n + cp
        if i > 0:
            out[n] = x_full[b, 128 * i - nh:128 * i]
    return out.reshape(NT * nh, -1)


def prep_C(inp, l, core, x2_local, x2_full):
    return {"x_in": x2_local, "xh": halo_rows(x2_full, core, 2), "gpre": gcol(inp["ffn_norm_pre"][l]), "gpost": np.ascontiguousarray(inp["ffn_norm_post"][l]),
            "wu": np.ascontiguousarray(inp["w_up"][l]), "wd": np.ascontiguousarray(inp["w_down"][l]),
            "cwc": np.ascontiguousarray(inp["conv_w"][l].reshape(3, NCH, 128).transpose(2, 1, 0)),
            "cbc": np.ascontiguousarray(inp["conv_b"][l].reshape(NCH, 128).T), "ident": _IDENT}


def scatter_local(outs, shape):
    full = np.empty(shape, np.float32)
    v = full.reshape(shape[0], 64, 128, shape[2])
    for core in range(8):
        b, cp = divmod(core, 4)
        v[b, cp::4] = outs[core].reshape(NT, 128, shape[2])
    return full


from concourse.bass_utils import run_bass_kernel_spmd

_PROGS = {}


def _prog(name):
    if name not in _PROGS:
        _PROGS[name] = {"A": build_A, "B1": build_B1, "B2": build_B2, "C": build_C}[name]()
    return _PROGS[name]


def _run(name, maps):
    res = run_bass_kernel_spmd(_prog(name), maps, core_ids=list(range(8)))
    return [{k: np.asarray(v) for k, v in r.items()} for r in res.results]


def kernel(**inputs):
    inp = {k: np.asarray(v) for k, v in inputs.items()}
    x = np.ascontiguousarray(inp["x"], dtype=np.float32)
    shape = x.shape
    for l in range(4):
        A = _run("A", [prep_A(inp, l, x, c) for c in range(8)])
        bbs = [prep_B1_batch(A, b) for b in range(2)]
        r1 = _run("B1", [prep_B1(inp, l, x, c, A, bbs[c // 4]) for c in range(8)])
        del A, bbs
        r2 = _run("B2", [prep_B2(inp, l, c, r1[c]["x_out"]) for c in range(8)])
        x2_full = scatter_local([r2[c]["x_out"] for c in range(8)], shape)
        r3 = _run("C", [prep_C(inp, l, c, r2[c]["x_out"], x2_full) for c in range(8)])
        x = scatter_local([r3[c]["x_out"] for c in range(8)], shape)
    return x
```

```python
from contextlib import ExitStack
import numpy as np
import concourse.bass as bass
import concourse.mybir as mybir

F32 = mybir.dt.float32
BF16 = mybir.dt.bfloat16
AF = mybir.ActivationFunctionType
ALU = mybir.AluOpType
AX = mybir.AxisListType

COMPUTE = ("pe", "act", "dve", "pool")
DMAQ = ("sp", "actq", "poolq")
QENG = {"sp": "sp", "actq": "act", "poolq": "pool"}
NSEM_DMA = 12
SAME_ENGINE_SYNC = True


class Buf:
    __slots__ = ("name", "writers", "readers")

    def __init__(self, name):
        self.name = name
        self.writers = []
        self.readers = []


class Op:
    __slots__ = ("stream", "emit", "deps", "signal", "dma", "idx")


class Sched:
    def __init__(self, nc, es):
        self.nc = nc
        self.es = es
        self.streams = {s: [] for s in ("pe", "act", "dve", "pool", "sp")}
        self.sems = {}
        for s in COMPUTE:
            self.sems[s] = es.enter_context(nc.semaphore("c_" + s))
        self.dsems = {}
        self.dcount = {}
        self.dnext = {}
        for q in DMAQ:
            self.dsems[q] = [es.enter_context(nc.semaphore(f"d_{q}_{i}")) for i in range(NSEM_DMA)]
            self.dcount[q] = [0] * NSEM_DMA
            self.dnext[q] = 0
        self.out_events = []

    def _collect(self, reads, writes, nowaw):
        deps = []
        for b in reads:
            deps.extend(b.writers)
        for b in writes:
            if not nowaw:
                deps.extend(b.writers)
            deps.extend(b.readers)
        return deps

    def _commit(self, ev, reads, writes, nowaw):
        for b in reads:
            if ev[0] == "c":
                b.readers = [r for r in b.readers if not (r[0] == "c" and r[1] == ev[1])]
            b.readers.append(ev)
        for b in writes:
            if nowaw:
                if ev[0] == "c":
                    b.writers = [w for w in b.writers if not (w[0] == "c" and w[1] == ev[1])]
                b.writers.append(ev)
            else:
                b.writers = [ev]
            b.readers = []

    def op(self, eng, emit, reads=(), writes=(), nowaw=False):
        o = Op()
        o.stream = eng
        o.emit = emit
        o.deps = self._collect(reads, writes, nowaw)
        o.signal = False
        o.dma = None
        o.idx = len([x for x in self.streams[eng] if x.dma is None and x.emit is not None]) if False else None
        lst = self.streams[eng]
        o.idx = len(lst)
        lst.append(o)
        ev = ("c", eng, o.idx)
        self._commit(ev, reads, writes, nowaw)
        return ev

    def dma(self, q, out, in_, reads=(), writes=(), nowaw=False, is_output=False, _emit=None, **kw):
        o = Op()
        stream = QENG[q]
        o.stream = stream
        k = self.dnext[q]
        self.dnext[q] = (k + 1) % NSEM_DMA
        self.dcount[q][k] += 1
        val = 16 * self.dcount[q][k]
        ev = ("d", q, k, val)
        o.deps = self._collect(reads, writes, nowaw)
        if self.dcount[q][k] > 1:
            o.deps.append(("d", q, k, val - 16))
        o.dma = (q, k, val)
        o.signal = False
        o.emit = _emit if _emit is not None else (lambda e: e.dma_start(out=out, in_=in_, **kw))
        lst = self.streams[stream]
        o.idx = len(lst)
        lst.append(o)
        self._commit(ev, reads, writes, nowaw)
        if is_output:
            self.out_events.append(ev)
        return ev

    def coll(self, kind, op, groups, src, dst):
        return self.dma("poolq", None, None, reads=[src.b], writes=[dst.b],
                        _emit=lambda e: e.collective_compute(kind, op, replica_groups=groups, ins=[src.t], outs=[dst.t]))

    def barrier(self):
        evs = []
        for s in COMPUTE:
            if self.streams[s]:
                idx = None
                for o in reversed(self.streams[s]):
                    if o.emit is not None and o.dma is None:
                        idx = o.idx
                        break
                if idx is not None:
                    evs.append(("c", s, idx))
        for q in DMAQ:
            for k in range(NSEM_DMA):
                if self.dcount[q][k] > 0:
                    evs.append(("d", q, k, 16 * self.dcount[q][k]))
        for s in ("pe", "act", "dve", "pool", "sp"):
            o = Op()
            o.stream = s; o.emit = None; o.deps = list(evs); o.signal = False; o.dma = None
            o.idx = len(self.streams[s])
            self.streams[s].append(o)

    def finalize(self):
        nc = self.nc
        fin = Op()
        fin.stream = "sp"; fin.emit = None; fin.deps = list(self.out_events); fin.signal = False; fin.dma = None
        fin.idx = len(self.streams["sp"])
        self.streams["sp"].append(fin)
        for s, lst in self.streams.items():
            for o in lst:
                for d in o.deps:
                    if d[0] == "c":
                        if d[1] == s and (s == "pe" or not SAME_ENGINE_SYNC):
                            continue
                        self.streams[d[1]][d[2]].signal = True
        sigcount = {}
        for s in COMPUTE:
            c = 0
            arr = []
            for o in self.streams[s]:
                if o.signal:
                    c += 1
                arr.append(c)
            sigcount[s] = arr
        block = self.es.enter_context(nc.Block())
        engs = {"pe": block.tensor, "act": block.scalar, "dve": block.vector, "pool": block.gpsimd, "sp": block.sync}

        def make(sname):
            lst = self.streams[sname]

            def body(e):
                waited_c = {s: 0 for s in COMPUTE}
                waited_d = {}
                for o in lst:
                    for d in o.deps:
                        if d[0] == "c":
                            if d[1] == sname and (sname == "pe" or not SAME_ENGINE_SYNC):
                                continue
                            v = sigcount[d[1]][d[2]]
                            if v > waited_c[d[1]]:
                                e.wait_ge(self.sems[d[1]], v)
                                waited_c[d[1]] = v
                        else:
                            key = (d[1], d[2])
                            if d[3] > waited_d.get(key, 0):
                                e.wait_ge(self.dsems[d[1]][d[2]], d[3])
                                waited_d[key] = d[3]
                    if o.emit is None:
                        continue
                    ins = o.emit(e)
                    if o.dma is not None:
                        q, k, val = o.dma
                        ins.then_inc(self.dsems[q][k], 16)
                    elif o.signal:
                        ins.then_inc(self.sems[sname], 1)
            return body

        for sname in ("sp", "pe", "act", "dve", "pool"):
            engs[sname](make(sname))


class TB:
    def __init__(self, t, name):
        self.t = t
        self.b = Buf(name)

    def __getitem__(self, k):
        return self.t[k]


class Ctx:
    def __init__(self, name):
        self.es = ExitStack()
        self.nc = bass.Bass("TRN2", target_bir_lowering=False)
        self.S = Sched(self.nc, self.es)
        self.n = 0

    def sb(self, shape, dt, name=None):
        self.n += 1
        name = name or f"sb{self.n}"
        return TB(self.es.enter_context(self.nc.sbuf_tensor(name, list(shape), dt)), name)

    def ps(self, shape, dt, name=None):
        self.n += 1
        name = name or f"ps{self.n}"
        return TB(self.es.enter_context(self.nc.psum_tensor(name, list(shape), dt)), name)

    def din(self, name, shape, dt):
        return TB(self.nc.dram_tensor(name, list(shape), dt, kind="ExternalInput").ap(), name)

    def dout(self, name, shape, dt):
        return TB(self.nc.dram_tensor(name, list(shape), dt, kind="ExternalOutput").ap(), name)

    def dscr(self, name, shape, dt):
        return TB(self.nc.dram_tensor(name, list(shape), dt, kind="Internal").ap(), name)

    def finish(self):
        self.S.finalize()
        self.es.close()
        return self.nc


def bcast_rows(ap1d_tb, n, parts=128):
    a = ap1d_tb.t
    return bass.AP(tensor=a.tensor, offset=a.offset, ap=[[0, parts], [1, n]])


D = 1024
NT = 16
TL = NT * 128
EPS = 1e-6
INW = 2072
FMW = 1536
TMW = 536


def build_A():
    C = Ctx("A"); S = C.S
    x = C.din("x", [TL, D], F32)
    gpre = C.din("gpre", [128, 8], F32)
    w = C.din("w", [D, INW], F32)
    ident = C.din("ident", [128, 128], F32)
    sgug = C.din("sgug", [256], F32)
    sgub = C.din("sgub", [256], F32)
    sguwT = C.din("sguwT", [4, 128, 128], F32)
    tril = C.din("tril", [128, 128], F32)
    sgubs = C.din("sgubs", [512], F32)
    qT = C.dout("qT", [512, TL], BF16)
    kvT = C.dout("kvT", [512, TL], BF16)
    pinT = C.dout("pinT", [256, TL], F32)
    vtm = C.dout("vtm", [TL, 256], BF16)
    gate = C.dout("gate", [TL, 24], F32)
    ocT = C.dout("ocT", [256, TL], BF16)

    Wb = C.sb([128, 8, INW], BF16, "Wb")
    stage = [C.sb([128, INW], F32, f"stage{i}") for i in range(2)]
    g_sb = C.sb([128, 8], F32, "g_sb")
    id_b = C.sb([128, 128], BF16, "id_b")
    sg_g = C.sb([128, 256], F32, "sg_g")
    sg_b = C.sb([128, 256], F32, "sg_b")
    bsb = C.sb([128, 4, 128], F32, "bsb")
    wst_f = C.sb([128, 4, 128], F32, "wst_f")
    tril_s = C.sb([128, 128], F32, "tril_s")
    wsT = C.sb([128, 4, 128], BF16, "wsT")

    eps_c = C.sb([128, 1], F32, "eps_c")
    S.op("dve", lambda e: e.memset(eps_c[:, :], EPS), writes=[eps_c.b])
    S.dma("sp", g_sb[:, :], gpre[:, :], writes=[g_sb.b])
    S.dma("poolq", id_b[:, :], ident[:, :], writes=[id_b.b])
    S.dma("sp", sg_g[:, :], bcast_rows(sgug, 256), writes=[sg_g.b])
    S.dma("sp", sg_b[:, :], bcast_rows(sgub, 256), writes=[sg_b.b])
    S.dma("sp", bsb[:, :, :].rearrange("p h t -> p (h t)"), bcast_rows(sgubs, 512), writes=[bsb.b])
    S.dma("sp", wst_f[:, :, :], sguwT[:, :, :].rearrange("h s t -> s h t"), writes=[wst_f.b])
    S.dma("sp", tril_s[:, :], tril[:, :], writes=[tril_s.b])
    for h in range(4):
        S.op("dve", lambda e, h=h: e.tensor_tensor(out=wsT[:, h, :], in0=wst_f[:, h, :], in1=tril_s[:, :], op=ALU.mult),
             reads=[wst_f.b, tril_s.b], writes=[wsT.b], nowaw=True)
    for kc in range(8):
        st = stage[kc % 2]
        S.dma("sp", st[:, :], w[kc * 128:(kc + 1) * 128, :], writes=[st.b])
        S.op("dve", lambda e, kc=kc, st=st: e.tensor_scalar(out=Wb[:, kc, :], in0=st[:, :], scalar1=g_sb[:, kc:kc + 1],
                                                           scalar2=None, op0=ALU.mult),
             reads=[st.b, g_sb.b], writes=[Wb.b], nowaw=True)

    xt = [C.sb([128, D], F32, f"xt{i}") for i in range(2)]
    sq = C.sb([128, D], BF16, "sq")
    ss = [C.sb([128, 1], F32, f"ss{i}") for i in range(2)]
    rs = [C.sb([128, 1], F32, f"rs{i}") for i in range(2)]
    hn = [C.sb([128, D], BF16, f"hn{i}") for i in range(2)]
    hT = [C.sb([128, 8, 512], BF16, f"hT{i}") for i in range(2)]
    psT = [C.ps([128, 8, 128], BF16, f"psT{i}") for i in range(2)]
    psF = [C.ps([128, 512], F32, f"psF{i}") for i in range(3)]
    psA = C.ps([128, 512], F32, "psA")
    psB = C.ps([128, 512], F32, "psB")
    psC = C.ps([128, 512], F32, "psC")
    fm_b = [C.sb([128, 512], BF16, f"fm_b{i}") for i in range(3)]
    fm_f = [C.sb([128, 512], F32, f"fm_f{i}") for i in range(2)]
    gu = [C.sb([128, 2, 512], F32, f"gu{i}") for i in range(2)]
    vt_b = [C.sb([128, 256], BF16, f"vt_b{i}") for i in range(2)]
    gt_f = [C.sb([128, 24], F32, f"gt_f{i}") for i in range(2)]
    vg = [C.sb([128, 256], F32, f"vg{i}") for i in range(2)]
    vc_ = [C.sb([128, 256], F32, f"vc{i}") for i in range(2)]
    vsq = C.sb([128, 256], F32, "vsq")
    st1 = [C.sb([128, 4], F32, f"st1{i}") for i in range(2)]
    vln = [C.sb([128, 256], BF16, f"vln{i}") for i in range(2)]
    tmp = [C.sb([128, 2, 128], F32, f"tmp{i}") for i in range(2)]
    oc = [C.sb([128, 2, 512], BF16, f"oc{i}") for i in range(2)]

    nfm = 0
    for grp in range(NT // 4):
        hTg = hT[grp % 2]
        gug = gu[grp % 2]
        ocg = oc[grp % 2]
        for tt in range(4):
            t = grp * 4 + tt
            xx = xt[t % 2]; s_ = ss[t % 2]; r_ = rs[t % 2]; hh = hn[t % 2]; pT = psT[t % 2]
            S.dma("sp", xx[:, :], x[t * 128:(t + 1) * 128, :], writes=[xx.b])
            S.op("act", lambda e, xx=xx, s_=s_: e.activation(out=sq[:, :], in_=xx[:, :], func=AF.Square, accum_out=s_[:, :]),
                 reads=[xx.b], writes=[sq.b, s_.b])
            S.op("act", lambda e, s_=s_, r_=r_: e.activation(out=r_[:, :], in_=s_[:, :], func=AF.Sqrt, scale=1.0 / D, bias=eps_c[:, :]),
                 reads=[s_.b, eps_c.b], writes=[r_.b])
            S.op("dve", lambda e, r_=r_: e.reciprocal(out=r_[:, :], in_=r_[:, :]), reads=[r_.b], writes=[r_.b])
            S.op("dve", lambda e, xx=xx, r_=r_, hh=hh: e.tensor_scalar(out=hh[:, :], in0=xx[:, :], scalar1=r_[:, :], scalar2=None,
                                                                     op0=ALU.mult), reads=[xx.b, r_.b], writes=[hh.b])
            for kc in range(8):
                S.op("pe", lambda e, kc=kc, hh=hh, pT=pT: e.transpose(out=pT[:, kc, :], in_=hh[:, kc * 128:(kc + 1) * 128], identity=id_b[:, :]),
                     reads=[hh.b, id_b.b], writes=[pT.b], nowaw=(kc > 0))
            S.op("dve", lambda e, pT=pT, tt=tt, hTg=hTg: e.tensor_copy(out=hTg[:, :, tt * 128:(tt + 1) * 128], in_=pT[:, :, :]),
                 reads=[pT.b], writes=[hTg.b], nowaw=True)
        for ocn in range(12):
            pf = psF[nfm % 3]
            for kc in range(8):
                S.op("pe", lambda e, pf=pf, kc=kc, ocn=ocn, hTg=hTg: e.matmul(pf[:, :], lhsT=Wb[:, kc, ocn * 128:(ocn + 1) * 128], rhs=hTg[:, kc, :],
                                                                          start=(kc == 0), stop=(kc == 7)),
                     reads=[Wb.b, hTg.b], writes=[pf.b])
            cols = slice(grp * 512, (grp + 1) * 512)
            if ocn < 4:
                fb = fm_b[nfm % 3]
                S.op("act", lambda e, pf=pf, fb=fb: e.activation(out=fb[:, :], in_=pf[:, :], func=AF.Copy, scale=0.125),
                     reads=[pf.b], writes=[fb.b])
                S.dma("poolq", qT[ocn * 128:(ocn + 1) * 128, cols], fb[:, :], reads=[fb.b], writes=[qT.b], nowaw=True, is_output=True)
            elif ocn < 8:
                fb = fm_b[nfm % 3]
                S.op("dve", lambda e, pf=pf, fb=fb: e.tensor_copy(out=fb[:, :], in_=pf[:, :]), reads=[pf.b], writes=[fb.b])
                S.dma("poolq", kvT[(ocn - 4) * 128:(ocn - 3) * 128, cols], fb[:, :], reads=[fb.b], writes=[kvT.b], nowaw=True, is_output=True)
            elif ocn < 10:
                ff = fm_f[nfm % 2]
                S.op("act", lambda e, pf=pf, ff=ff: e.activation(out=ff[:, :], in_=pf[:, :], func=AF.Copy),
                     reads=[pf.b], writes=[ff.b])
                S.dma("poolq", pinT[(ocn - 8) * 128:(ocn - 7) * 128, cols], ff[:, :], reads=[ff.b], writes=[pinT.b], nowaw=True, is_output=True)
            else:
                j = ocn - 10
                S.op("act", lambda e, pf=pf, j=j, gug=gug: e.activation(out=gug[:, j, :], in_=pf[:, :], func=AF.Gelu_apprx_tanh),
                     reads=[pf.b], writes=[gug.b], nowaw=(j > 0))
            nfm += 1
        for tt in range(4):
            t = grp * 4 + tt
            for kc in range(8):
                S.op("pe", lambda e, kc=kc, tt=tt, hTg=hTg: e.matmul(psA[:, :], lhsT=hTg[:, kc, tt * 128:(tt + 1) * 128], rhs=Wb[:, kc, FMW:FMW + 512],
                                                                  start=(kc == 0), stop=(kc == 7)),
                     reads=[Wb.b, hTg.b], writes=[psA.b])
            for kc in range(8):
                S.op("pe", lambda e, kc=kc, tt=tt, hTg=hTg: e.matmul(psB[:, 0:24], lhsT=hTg[:, kc, tt * 128:(tt + 1) * 128], rhs=Wb[:, kc, FMW + 512:FMW + 536],
                                                                  start=(kc == 0), stop=(kc == 7)),
                     reads=[Wb.b, hTg.b], writes=[psB.b])
            vb = vt_b[t % 2]; gf = gt_f[t % 2]; vg_ = vg[t % 2]; vcc = vc_[t % 2]; s1 = st1[t % 2]; vl = vln[t % 2]; tm = tmp[t % 2]
            S.op("dve", lambda e, vb=vb: e.tensor_copy(out=vb[:, :], in_=psA[:, 0:256]), reads=[psA.b], writes=[vb.b])
            S.dma("poolq", vtm[t * 128:(t + 1) * 128, :], vb[:, :], reads=[vb.b], writes=[vtm.b], nowaw=True, is_output=True)
            S.op("act", lambda e, gf=gf: e.activation(out=gf[:, :], in_=psB[:, 0:24], func=AF.Sigmoid), reads=[psB.b], writes=[gf.b])
            S.dma("poolq", gate[t * 128:(t + 1) * 128, :], gf[:, :], reads=[gf.b], writes=[gate.b], nowaw=True, is_output=True)
            S.op("act", lambda e, vg_=vg_, s1=s1: e.activation(out=vg_[:, :], in_=psA[:, 256:512], func=AF.Gelu_apprx_tanh, accum_out=s1[:, 0:1]),
                 reads=[psA.b], writes=[vg_.b, s1.b])
            S.op("dve", lambda e, s1=s1: e.tensor_scalar(out=s1[:, 1:2], in0=s1[:, 0:1], scalar1=-1.0 / 256, scalar2=None, op0=ALU.mult),
                 reads=[s1.b], writes=[s1.b])
            S.op("dve", lambda e, vg_=vg_, vcc=vcc, s1=s1: e.tensor_scalar(out=vcc[:, :], in0=vg_[:, :], scalar1=s1[:, 1:2], scalar2=None, op0=ALU.add),
                 reads=[vg_.b, s1.b], writes=[vcc.b])
            S.op("act", lambda e, vcc=vcc, s1=s1: e.activation(out=vsq[:, :], in_=vcc[:, :], func=AF.Square, accum_out=s1[:, 2:3]),
                 reads=[vcc.b], writes=[vsq.b, s1.b])
            S.op("act", lambda e, s1=s1: e.activation(out=s1[:, 3:4], in_=s1[:, 2:3], func=AF.Sqrt, scale=1.0 / 256, bias=eps_c[:, :]),
                 reads=[s1.b, eps_c.b], writes=[s1.b])
            S.op("dve", lambda e, s1=s1: e.reciprocal(out=s1[:, 3:4], in_=s1[:, 3:4]), reads=[s1.b], writes=[s1.b])
            S.op("dve", lambda e, vcc=vcc, s1=s1: e.tensor_scalar(out=vcc[:, :], in0=vcc[:, :], scalar1=s1[:, 3:4], scalar2=None, op0=ALU.mult),
                 reads=[vcc.b, s1.b], writes=[vcc.b])
            S.op("dve", lambda e, vcc=vcc: e.tensor_tensor(out=vcc[:, :], in0=vcc[:, :], in1=sg_g[:, :], op=ALU.mult),
                 reads=[vcc.b, sg_g.b], writes=[vcc.b])
            S.op("dve", lambda e, vcc=vcc, vl=vl: e.tensor_tensor(out=vl[:, :], in0=vcc[:, :], in1=sg_b[:, :], op=ALU.add),
                 reads=[vcc.b, sg_b.b], writes=[vl.b])
            for h in range(4):
                j = h // 2
                S.op("pe", lambda e, h=h, j=j, vl=vl: e.matmul(psC[:, h * 128:(h + 1) * 128], lhsT=vl[:, j * 128:(j + 1) * 128], rhs=wsT[:, h, :],
                                                            start=True, stop=True),
                     reads=[vl.b, wsT.b], writes=[psC.b], nowaw=(h > 0))
            for h in range(4):
                j = h // 2; r = h % 2
                P = slice(r * 64, r * 64 + 64)
                S.op("dve", lambda e, h=h, j=j, P=P, tm=tm: e.tensor_tensor(out=tm[P, j, :], in0=psC[P, h * 128:(h + 1) * 128], in1=bsb[P, h, :], op=ALU.add),
                     reads=[psC.b, bsb.b], writes=[tm.b], nowaw=True)
                S.op("dve", lambda e, j=j, P=P, tm=tm, tt=tt, gug=gug, ocg=ocg: e.tensor_tensor(out=ocg[P, j, tt * 128:(tt + 1) * 128], in0=tm[P, j, :],
                                                                                           in1=gug[P, j, tt * 128:(tt + 1) * 128], op=ALU.mult),
                     reads=[tm.b, gug.b], writes=[ocg.b], nowaw=True)
        for j in range(2):
            S.dma("poolq", ocT[j * 128:(j + 1) * 128, grp * 512:(grp + 1) * 512], ocg[:, j, :], reads=[ocg.b], writes=[ocT.b], nowaw=True, is_output=True)
    return C.finish()


_SPL = np.cumsum([0, 512, 128, 128, 128, 128, 128, 128, 24, 256, 256, 256])
_NAMES = ["q", "kc", "vc", "ksl", "vsl", "kw", "vw", "gt", "pin", "u", "v"]
_COL = {n: np.arange(_SPL[i], _SPL[i + 1]) for i, n in enumerate(_NAMES)}
_PERM = np.concatenate([_COL[n] for n in ["q", "kc", "vc", "ksl", "kw", "pin", "u", "vsl", "vw", "v", "gt"]])
_IDENT = np.eye(128, dtype=np.float32)
_TRIL_ST = np.triu(np.ones((128, 128), np.float32))


def local_rows(a, core):
    b, cp = divmod(core, 4)
    t = a[b].reshape((64, 128) + a.shape[2:])[cp::4]
    return np.ascontiguousarray(t.reshape((TL,) + a.shape[2:]))


def gcol(g):
    return np.ascontiguousarray(g.reshape(-1, 128).T)


def prep_A(inp, l, x, core):
    return {
        "x": local_rows(x, core),
        "gpre": gcol(inp["mix_norm_pre"][l]),
        "w": np.ascontiguousarray(inp["w_in"][l][:, _PERM]),
        "ident": _IDENT,
        "sgug": np.ascontiguousarray(inp["sgu_norm_g"][l]),
        "sgub": np.ascontiguousarray(inp["sgu_norm_b"][l]),
        "sguwT": np.ascontiguousarray(np.transpose(inp["sgu_w"][l], (0, 2, 1))),
        "tril": _TRIL_ST,
        "sgubs": np.ascontiguousarray(inp["sgu_b"][l].reshape(-1)),
    }


NEG = 30000.0


def build_B1(ST=4, NTL=NT):
    C = Ctx("B1"); S = C.S
    qT_in = C.din("qT_in", [128, 4, NTL * 128], BF16)
    gate_in = C.din("gate_in", [NTL * 128, 24], F32)
    kc_in = C.din("kc_in", [64, 2, 8192], BF16)
    vc_in = C.din("vc_in", [64, 2, 8192], BF16)
    ksl_in = C.din("ksl_in", [128, 8192], BF16)
    vsl_in = C.din("vsl_in", [128, 64, 2, 64], BF16)
    kw_in = C.din("kw_in", [128, NTL, 640], BF16)
    vw_in = C.din("vw_in", [128, NTL, 5, 2, 64], BF16)
    pin_in = C.din("pin_in", [128, 2, NTL, 144], F32)
    oc_in = C.din("oc_in", [128, 2, NTL * 128], BF16)
    x_in = C.din("x_in", [NTL * 128, D], F32)
    relb = C.din("relb", [32, 8], F32)
    ohs = C.din("ohs", [33, 1536], F32)
    ohw = C.din("ohw", [33, 1280], F32)
    ohw0 = C.din("ohw0", [33, 1280], F32)
    ohc = C.din("ohc", [33, 88, 128], F32)
    expand = C.din("expand", [128, 8192], F32)
    vfc = C.din("vfc", [128, 2, 9], F32)
    ident = C.din("ident", [128, 128], F32)
    invc0 = C.din("invc0", [128, 2, 128], F32)
    cpos = C.din("cpos", [64, 2, 32], F32)
    cw1 = C.din("cw1", [64, 2, 32, 128], F32)
    cb1 = C.din("cb1", [128, 2], F32)
    cw2 = C.din("cw2", [128, 2, 64], F32)
    cb2 = C.din("cb2", [2, 64], F32)
    poolw = C.din("poolw", [4, 64, 64], F32)
    pools = C.din("pools", [128, 2], F32)
    wo = C.din("wo", [D, D], F32)
    gpost = C.din("gpost", [D], F32)
    x_out = C.dout("x_out", [NTL * 128, D], F32)
    ebs_d = C.dscr("ebs_d", [8, 1536], BF16)
    ebw_d = C.dscr("ebw_d", [8, 1280], BF16)
    ebw0_d = C.dscr("ebw0_d", [8, 1280], BF16)

    rot = [C.ps([128, 512], F32, f"rot{i}") for i in range(4)]
    acc = [C.ps([128, 512], F32, f"acc{i}") for i in range(2)]
    psT = C.ps([128, 8, 128], BF16, "psT")
    psM = C.ps([128, 512], F32, "psM")
    rc = [0]

    def nrot():
        rc[0] += 1
        return rot[rc[0] % 4]

    eps_c = C.sb([128, 1], F32, "eps_c")
    S.op("dve", lambda e: e.memset(eps_c[:, :], EPS), writes=[eps_c.b])
    id_f = C.sb([128, 128], F32, "id_f")
    id_b = C.sb([128, 128], BF16, "id_b")
    S.dma("sp", id_f[:, :], ident[:, :], writes=[id_f.b])
    S.dma("poolq", id_b[:, :], ident[:, :], writes=[id_b.b])
    rb = C.sb([33, 8], F32, "rb")
    rbx = C.sb([33, 8], F32, "rbx")
    b31 = C.sb([32, 8], F32, "b31")
    S.dma("sp", rb[0:32, :], relb[:, :], writes=[rb.b])
    S.op("dve", lambda e: e.memset(rb[32:33, :], -NEG), writes=[rb.b], nowaw=True)
    S.dma("sp", b31[:, :], bass.AP(tensor=relb.t.tensor, offset=31 * 8, ap=[[0, 32], [1, 8]]), writes=[b31.b])
    S.op("dve", lambda e: e.tensor_tensor(out=rbx[0:32, :], in0=rb[0:32, :], in1=b31[:, :], op=ALU.subtract), reads=[rb.b, b31.b], writes=[rbx.b])
    S.op("dve", lambda e: e.memset(rbx[32:33, :], -NEG), writes=[rbx.b], nowaw=True)

    Tsel = C.sb([128, 11, 8, 128], BF16, "Tsel")
    Twin = C.sb([128, 5, 8, 128], BF16, "Twin")
    Tband = C.sb([128, 88, 8], F32, "Tband")
    oh_sb = C.sb([33, 512], F32, "oh_sb")
    eb_sb = C.sb([8, 1536], BF16, "eb_sb")

    def build_row(oh_d, L, lhs, scr):
        for j in range(0, L, 512):
            wd = min(512, L - j)
            S.dma("sp", oh_sb[:, 0:wd], oh_d[:, j:j + wd], writes=[oh_sb.b])
            p = nrot()
            S.op("pe", lambda e, p=p, j=j, wd=wd: e.matmul(p[0:8, 0:wd], lhsT=lhs[:, :], rhs=oh_sb[:, 0:wd], start=True, stop=True),
                 reads=[lhs.b, oh_sb.b], writes=[p.b])
            S.op("act", lambda e, p=p, j=j, wd=wd: e.activation(out=eb_sb[:, j:j + wd], in_=p[0:8, 0:wd], func=AF.Exp),
                 reads=[p.b], writes=[eb_sb.b], nowaw=True)
        S.dma("sp", scr[:, :], eb_sb[:, 0:L], reads=[eb_sb.b], writes=[scr.b])

    def toep(dst, ntile, scr, L, step):
        for i in range(ntile):
            S.dma("sp", dst[:, i, :, :], bass.AP(tensor=scr.t.tensor, offset=step * i, ap=[[1, 128], [L, 8], [1, 128]]),
                  reads=[scr.b], writes=[dst.b], nowaw=True)

    build_row(ohs, 1536, rbx, ebs_d)
    toep(Tsel, 11, ebs_d, 1536, 128)
    build_row(ohw0, 1280, rbx, ebw0_d)
    toep(Twin, 5, ebw0_d, 1280, 256)
    build_row(ohw, 1280, rbx, ebw_d)
    ohc_sb = [C.sb([33, 4, 128], F32, f"ohc_sb{i}") for i in range(2)]
    pb = [nrot(), nrot()]
    for ch in range(22):
        o_ = ohc_sb[ch % 2]
        S.dma("sp", o_[:, :, :], ohc[:, ch * 4:(ch + 1) * 4, :], writes=[o_.b])
        for k in range(4):
            c3 = ch * 4 + k
            p = pb[c3 // 64]
            S.op("pe", lambda e, p=p, o_=o_, k=k, c3=c3: e.matmul(p[:, (c3 % 64) * 8:(c3 % 64) * 8 + 8], lhsT=o_[:, k, :], rhs=rbx[:, :], start=True, stop=True),
                 reads=[o_.b, rbx.b], writes=[p.b], nowaw=True)
    S.op("act", lambda e: e.activation(out=Tband[:, 0:64, :].rearrange("p c h -> p (c h)"), in_=pb[0][:, 0:512], func=AF.Exp),
         reads=[pb[0].b], writes=[Tband.b])
    S.op("act", lambda e: e.activation(out=Tband[:, 64:88, :].rearrange("p c h -> p (c h)"), in_=pb[1][:, 0:192], func=AF.Exp),
         reads=[pb[1].b], writes=[Tband.b], nowaw=True)

    k_sb = C.sb([128, 8192], BF16, "k_sb")
    v_sb = C.sb([128, 64, 2, 65], BF16, "v_sb")
    S.op("pool", lambda e: e.memset(v_sb[:, :, :, 64:65], 1.0), writes=[v_sb.b])
    for hf in range(4):
        S.dma("sp", k_sb[:, hf * 2048:(hf + 1) * 2048], ksl_in[:, hf * 2048:(hf + 1) * 2048], writes=[k_sb.b], nowaw=True)
    for q4 in range(4):
        S.dma("sp", v_sb[:, q4 * 16:(q4 + 1) * 16, :, 0:64], vsl_in[:, q4 * 16:(q4 + 1) * 16, :, :], writes=[v_sb.b], nowaw=True)
    exp_sb = C.sb([128, 64, 128], BF16, "exp_sb")
    for q4 in range(4):
        S.dma("poolq", exp_sb[:, q4 * 16:(q4 + 1) * 16, :].rearrange("p a b -> p (a b)"), expand[:, q4 * 2048:(q4 + 1) * 2048], writes=[exp_sb.b], nowaw=True)
    vf_sb = C.sb([128, 2, 9], F32, "vf_sb")
    S.dma("sp", vf_sb[:, :, :], vfc[:, :, :], writes=[vf_sb.b])
    invc_sb = C.sb([128, 2, 128], F32, "invc_sb")
    S.dma("sp", invc_sb[:, :, :], invc0[:, :, :], writes=[invc_sb.b])
    ps_sb = C.sb([128, 2], F32, "ps_sb")
    S.dma("sp", ps_sb[:, :], pools[:, :], writes=[ps_sb.b])
    gp_sb = C.sb([128, D], F32, "gp_sb")
    S.dma("sp", gp_sb[:, :], bcast_rows(gpost, D), writes=[gp_sb.b])
    pwf = C.sb([128, 2, 128], F32, "pwf")
    pwb = C.sb([128, 2, 128], BF16, "pwb")
    S.op("dve", lambda e: e.memset(pwf[:, :, :], 0.0), writes=[pwf.b])
    for gi in range(4):
        r = gi % 2
        S.dma("sp", pwf[r * 64:(r + 1) * 64, gi // 2, r * 64:(r + 1) * 64], poolw[gi, :, :], writes=[pwf.b], nowaw=True)
    S.op("dve", lambda e: e.tensor_copy(out=pwb[:, :, :], in_=pwf[:, :, :]), reads=[pwf.b], writes=[pwb.b])
    wo_sb = C.sb([128, 8, D], BF16, "wo_sb")
    for kc in range(8):
        S.dma("poolq", wo_sb[:, kc, :], wo[kc * 128:(kc + 1) * 128, :], writes=[wo_sb.b], nowaw=True)

    w1_sb = C.sb([64, 32, 128], BF16, "w1_sb")
    pos_sb = C.sb([64, 2, 32], BF16, "pos_sb")
    w2_sb = C.sb([128, 2, 64], BF16, "w2_sb")
    b1_sb = C.sb([128, 2], F32, "b1_sb")
    b2c = C.sb([128, 1], F32, "b2c")
    w2f = C.sb([128, 64], F32, "w2f")
    w2pad = C.sb([128, 2, 128], BF16, "w2pad")
    b2bc = C.sb([128, 64], F32, "b2bc")
    bias_c = C.sb([128, 2], F32, "bias_c")
    S.dma("poolq", pos_sb[:, :, :], cpos[:, :, :], writes=[pos_sb.b])
    S.dma("poolq", w2_sb[:, :, :], cw2[:, :, :], writes=[w2_sb.b])
    S.dma("sp", b1_sb[:, :], cb1[:, :], writes=[b1_sb.b])
    for g in range(2):
        S.dma("sp", b2c[g * 64:(g + 1) * 64, :], bass.AP(tensor=cb2.t.tensor, offset=0, ap=[[1, 64], [1, 1]]), writes=[b2c.b], nowaw=True)
    S.dma("sp", w2f[:, :], cw2[:, 0, :], writes=[w2f.b])
    S.op("dve", lambda e: e.memset(w2pad[:, :, :], 0.0), writes=[w2pad.b])
    for g in range(2):
        S.op("dve", lambda e, g=g: e.tensor_copy(out=w2pad[:, g, g * 64:(g + 1) * 64], in_=w2f[:, :]), reads=[w2f.b], writes=[w2pad.b], nowaw=True)
    S.dma("sp", b2bc[:, :], bass.AP(tensor=cb2.t.tensor, offset=64, ap=[[0, 128], [1, 64]]), writes=[b2bc.b])
    kcmp = C.sb([128, 512], BF16, "kcmp")
    vcmp = C.sb([128, 4, 2, 64], BF16, "vcmp")
    cst = [C.sb([64, 4112], BF16, "cst0")]
    hTs = [C.sb([128, 512], BF16, f"hTs{i}") for i in range(2)]
    for i in range(2):
        S.op("dve", lambda e, i=i: e.memset(hTs[i][:, :], 0.0), writes=[hTs[i].b])
    ci = 0
    for a in range(2):
        S.dma("poolq", w1_sb[:, :, :], cw1[:, a, :, :], writes=[w1_sb.b])
        p = nrot()
        for l in range(32):
            S.op("pe", lambda e, p=p, a=a, l=l: e.matmul(p[:, 0:1], lhsT=w1_sb[:, l, :], rhs=pos_sb[:, a, l:l + 1], start=(l == 0), stop=(l == 31)),
                 reads=[w1_sb.b, pos_sb.b], writes=[p.b])
        S.op("dve", lambda e, p=p, a=a: e.tensor_tensor(out=bias_c[:, a:a + 1], in0=p[:, 0:1], in1=b1_sb[:, a:a + 1], op=ALU.add),
             reads=[p.b, b1_sb.b], writes=[bias_c.b], nowaw=True)
        src = kc_in if a == 0 else vc_in
        for g in range(2):
            st = cst[0]; hT_ = hTs[ci % 2]; ci += 1
            p = nrot()
            for hf in range(2):
                ntok = 4112 if hf == 0 else 4096
                ncb = 256 if hf == 0 else 255
                S.dma("sp", st[:, 0:ntok], src[:, g, hf * 4096:hf * 4096 + ntok], writes=[st.b])
                for l in range(32):
                    S.op("pe", lambda e, p=p, a=a, l=l, st=st, hf=hf, ncb=ncb: e.matmul(p[:, hf * 256:hf * 256 + ncb], lhsT=w1_sb[:, l, :], rhs=st[:, l:l + 16 * (ncb - 1) + 1:16],
                                                                                 start=(l == 0), stop=(l == 31)),
                         reads=[w1_sb.b, st.b], writes=[p.b], nowaw=(hf > 0))
            S.op("act", lambda e, p=p, a=a, hT_=hT_: e.activation(out=hT_[:, 0:511], in_=p[:, 0:511], func=AF.Gelu_apprx_tanh, bias=bias_c[:, a:a + 1]),
                 reads=[p.b, bias_c.b], writes=[hT_.b])
            if a == 0:
                p2 = nrot()
                S.op("pe", lambda e, p2=p2, hT_=hT_, g=g: e.matmul(p2[:, 0:512], lhsT=w2pad[:, g, :], rhs=hT_[:, :], start=True, stop=True),
                     reads=[w2pad.b, hT_.b], writes=[p2.b])
                S.op("act", lambda e, p2=p2, g=g: e.activation(out=kcmp[g * 64:(g + 1) * 64, :], in_=p2[g * 64:(g + 1) * 64, 0:512], func=AF.Identity, bias=b2c[g * 64:(g + 1) * 64, 0:1]),
                     reads=[p2.b, b2c.b], writes=[kcmp.b], nowaw=True)
            else:
                p2 = nrot()
                for ct in range(4):
                    S.op("pe", lambda e, p2=p2, hT_=hT_, ct=ct: e.matmul(p2[:, ct * 64:(ct + 1) * 64], lhsT=hT_[:, ct * 128:(ct + 1) * 128], rhs=w2_sb[:, 1, :], start=True, stop=True),
                         reads=[w2_sb.b, hT_.b], writes=[p2.b], nowaw=(ct > 0))
                for ct in range(4):
                    S.op("dve", lambda e, p2=p2, ct=ct, g=g: e.tensor_tensor(out=vcmp[:, ct, g, :], in0=p2[:, ct * 64:(ct + 1) * 64], in1=b2bc[:, :], op=ALU.add),
                         reads=[p2.b, b2bc.b], writes=[vcmp.b], nowaw=True)

    q_sb = [C.sb([128, 4, 128], BF16, f"q_sb{i}") for i in range(2)]
    q64 = Buf("q64")
    kw_sb = [C.sb([128, 640], BF16, f"kw_sb{i}") for i in range(2)]
    vw_sb = [C.sb([128, 5, 2, 65], BF16, f"vw_sb{i}") for i in range(2)]
    kvw64 = Buf("kvw64")
    for i in range(2):
        S.op("pool", lambda e, i=i: e.memset(vw_sb[i][:, :, :, 64:65], 1.0), writes=[kvw64], nowaw=True)
    g_sb = [C.sb([128, 24], F32, f"g_sb{i}") for i in range(2)]
    pin_sb = [C.sb([128, 2, 144], F32, f"pin_sb{i}") for i in range(2)]
    x_sb = [C.sb([128, D], F32, "x_sb0")] * 2
    mfT = [C.sb([128, 8, 128], BF16, f"mfT{i}") for i in range(2)]
    mfT_oc = [Buf(f"mfT_oc{i}") for i in range(2)]
    Ec = [C.sb([128, 4, 512], F32, "Ec0")] * 2
    pnb = [C.sb([128, 4, 512], BF16, "pnb0")] * 2
    impad = [C.sb([128, 520], F32, f"impad{i}") for i in range(2)]
    S.op("pool", lambda e: e.memset(pnb[0][:, :, :], 0.0), writes=[pnb[0].b])
    for i in range(2):
        S.op("pool", lambda e, i=i: e.memset(impad[i][:, :], 0.0), writes=[impad[i].b])
    dc = [C.sb([128, 8], F32, f"dc{i}") for i in range(2)]
    pT_sb = [C.sb([128, 4, 4, 128], BF16, "pT_sb0")] * 2
    sc = [C.sb([128, 128], F32, f"sc{i}") for i in range(2)]
    sc2 = C.sb([128, 128], F32, "sc2")
    t1 = C.sb([128, 128], F32, "t1")
    m8 = [C.sb([128, 16], F32, f"m8{i}") for i in range(2)]
    nsb = [C.sb([128, 128], BF16, f"nsb{i}") for i in range(2)]
    negT = [C.sb([128, 4, 128], BF16, f"negT{i}") for i in range(2)]
    Et = [C.sb([128, 512], BF16, f"Et{i}") for i in range(4)]
    ec = [0]
    oT_sb = [C.sb([65, 512], F32, f"oT_sb{i}") for i in range(2)]
    mix = [C.sb([128, 512], F32, f"mix{i}") for i in range(2)]
    mixb = [C.sb([128, 512], BF16, f"mixb{i}") for i in range(2)]
    tmpm = [C.sb([128, 4, 64], F32, f"tmpm{i}") for i in range(2)]
    fsc = [C.sb([128, 8], F32, f"fsc{i}") for i in range(2)]
    pl = [C.sb([128, 2, 144], F32, f"pl{i}") for i in range(4)]
    dd = [C.sb([128, 2, 128], BF16, f"dd{i}") for i in range(2)]
    ysq = C.sb([128, 512], BF16, "ysq")
    yss = [C.sb([128, 4], F32, f"yss{i}") for i in range(2)]
    yt = [C.sb([128, D], F32, "yt0")] * 2
    acn = [0]
    pc_ = [0]

    def tile(n):
        k2 = n % 2
        qs = q_sb[k2]; kws = kw_sb[k2]; vws = vw_sb[k2]; gs = g_sb[k2]; pins = pin_sb[k2]; xs = x_sb[k2]; mf = mfT[k2]
        mx = mix[k2]
        S.dma("sp", qs[:, :, :], qT_in[:, :, n * 128:(n + 1) * 128], writes=[qs.b])
        S.dma("sp", kws[:, :], kw_in[:, n, :], writes=[kws.b])
        S.dma("sp", vws[:, :, :, 0:64], vw_in[:, n, :, :, :], writes=[vws.b])
        S.dma("sp", gs[:, :], gate_in[n * 128:(n + 1) * 128, :], writes=[gs.b])
        S.dma("sp", pins[:, :, :], pin_in[:, :, n, :], writes=[pins.b])
        S.dma("sp", xs[:, :], x_in[n * 128:(n + 1) * 128, :], writes=[xs.b])
        S.dma("sp", mf[:, 6:8, :], oc_in[:, :, n * 128:(n + 1) * 128], writes=[mfT_oc[k2]])
        if n == 1:
            toep(Twin, 5, ebw_d, 1280, 256)
        Nc = 8 * ST * (n + 1)
        nct = (Nc + 127) // 128
        lo = max(0, 8 * ST * n - 56)
        c30 = lo - (8 * ST * n - 56)
        ctxg = {}

        def grp(g):
            E_ = Ec[g]; pb_ = pnb[g]; im = impad[g]; d_ = dc[g]; pTs = pT_sb[g]
            gv = gs[:, g * 12:(g + 1) * 12].rearrange("p (h r) -> p h r", r=3)
            for hh in range(4):
                h = 4 * g + hh
                p = nrot()
                S.op("pe", lambda e, p=p, hh=hh, g=g, qs=qs: e.matmul(p[:, 0:Nc], lhsT=qs[g * 64:(g + 1) * 64, hh, :], rhs=kcmp[g * 64:(g + 1) * 64, 0:Nc], start=True, stop=True),
                     reads=[qs.b, q64, kcmp.b], writes=[p.b])
                S.op("act", lambda e, p=p, hh=hh, E_=E_: e.activation(out=E_[:, hh, 0:Nc], in_=p[:, 0:Nc], func=AF.Exp),
                     reads=[p.b], writes=[E_.b], nowaw=(hh > 0))
            S.op("dve", lambda e, E_=E_, g=g: e.tensor_tensor(out=E_[:, :, lo:Nc], in0=E_[:, :, lo:Nc],
                                                           in1=Tband[:, c30:c30 + Nc - lo, 4 * g:4 * g + 4].rearrange("p c h -> p h c"), op=ALU.mult),
                 reads=[E_.b, Tband.b], writes=[E_.b])
            S.op("dve", lambda e, E_=E_, d_=d_: e.tensor_reduce(out=d_[:, 0:4], in_=E_[:, :, 0:Nc], axis=AX.X, op=ALU.add),
                 reads=[E_.b], writes=[d_.b])
            S.op("dve", lambda e, d_=d_: e.tensor_scalar(out=d_[:, 0:4], in0=d_[:, 0:4], scalar1=1e-30, scalar2=None, op0=ALU.max), reads=[d_.b], writes=[d_.b])
            S.op("dve", lambda e, d_=d_: e.reciprocal(out=d_[:, 4:8], in_=d_[:, 0:4]), reads=[d_.b], writes=[d_.b])
            S.op("dve", lambda e, E_=E_, d_=d_: e.tensor_tensor(out=E_[:, :, 0:Nc], in0=E_[:, :, 0:Nc],
                                                             in1=d_[:, 4:8].unsqueeze(2).to_broadcast([128, 4, Nc]), op=ALU.mult),
                 reads=[E_.b, d_.b], writes=[E_.b])
            S.op("dve", lambda e, E_=E_, im=im: e.tensor_reduce(out=im[:, 1:1 + Nc], in_=E_[:, :, 0:Nc].rearrange("p h c -> p c h"), axis=AX.X, op=ALU.add),
                 reads=[E_.b], writes=[im.b])
            S.op("act", lambda e, E_=E_, pb_=pb_: e.activation(out=pb_[:, :, 0:Nc], in_=E_[:, :, 0:Nc], func=AF.Copy),
                 reads=[E_.b], writes=[pb_.b])
            for hh in range(4):
                for ct in range(nct):
                    S.op("pe", lambda e, hh=hh, ct=ct, pb_=pb_: e.transpose(out=psT[:, hh * 2 + ct % 2, :], in_=pb_[:, hh, ct * 128:(ct + 1) * 128], identity=id_b[:, :]),
                         reads=[pb_.b, id_b.b], writes=[psT.b], nowaw=not (hh == 0 and ct % 2 == 0 and ct == 0))
                    if ct % 2 == 1 or ct == nct - 1:
                        c0 = ct - (ct % 2)
                        w_ = ct - c0 + 1
                        S.op("act", lambda e, hh=hh, c0=c0, w_=w_, pTs=pTs: e.activation(out=pTs[:, hh, c0:c0 + w_, :], in_=psT[:, hh * 2:hh * 2 + w_, :], func=AF.Copy),
                             reads=[psT.b], writes=[pTs.b], nowaw=True)
            po = nrot()
            for hh in range(4):
                for ct in range(nct):
                    S.op("pe", lambda e, po=po, hh=hh, ct=ct, pTs=pTs, g=g: e.matmul(po[:, hh * 64:(hh + 1) * 64], lhsT=pTs[:, hh, ct, :], rhs=vcmp[:, ct, g, :],
                                                                              start=(ct == 0), stop=(ct == nct - 1)),
                         reads=[pTs.b, vcmp.b], writes=[po.b], nowaw=not (hh == 0 and ct == 0))
            S.op("dve", lambda e, po=po, mx=mx, g=g, gv=gv: e.tensor_tensor(out=mx[:, g * 256:(g + 1) * 256].rearrange("p (h d) -> p h d", d=64),
                                                                         in0=po[:, 0:256].rearrange("p (h d) -> p h d", d=64),
                                                                         in1=gv[:, :, 0:1].to_broadcast([128, 4, 64]), op=ALU.mult),
                 reads=[po.b, gs.b], writes=[mx.b], nowaw=(g > 0))
            s_ = sc[g]; m_ = m8[g]; ns_ = nsb[g]; nT = negT[g]
            A = im[:, 0:512].rearrange("p (j m) -> p j m", m=4)
            Bv = im[:, 4:516].rearrange("p (j m) -> p j m", m=4)
            S.op("dve", lambda e, A=A: e.tensor_tensor(out=t1[:, :], in0=A[:, :, 1], in1=A[:, :, 2], op=ALU.add), reads=[im.b], writes=[t1.b])
            S.op("dve", lambda e, A=A: e.tensor_tensor(out=t1[:, :], in0=t1[:, :], in1=A[:, :, 3], op=ALU.add), reads=[im.b, t1.b], writes=[t1.b])
            S.op("dve", lambda e, A=A: e.scalar_tensor_tensor(out=t1[:, :], in0=t1[:, :], scalar=2.0, in1=A[:, :, 0], op0=ALU.mult, op1=ALU.add),
                 reads=[im.b, t1.b], writes=[t1.b])
            S.op("dve", lambda e, Bv=Bv, s_=s_: e.tensor_tensor(out=s_[:, :], in0=t1[:, :], in1=Bv[:, :, 0], op=ALU.add), reads=[im.b, t1.b], writes=[s_.b])
            j0 = max(0, 2 * ST * n - 1)
            f0 = j0 - (2 * ST * n - 1)
            wloc = 2 * ST * (n + 1) - j0
            S.op("dve", lambda e, s_=s_: e.tensor_tensor(out=s_[:, j0:j0 + wloc], in0=s_[:, j0:j0 + wloc], in1=vf_sb[:, 0, f0:f0 + wloc], op=ALU.mult),
                 reads=[s_.b, vf_sb.b], writes=[s_.b])
            S.op("dve", lambda e, s_=s_: e.tensor_tensor(out=s_[:, j0:j0 + wloc], in0=s_[:, j0:j0 + wloc], in1=vf_sb[:, 1, f0:f0 + wloc], op=ALU.add),
                 reads=[s_.b, vf_sb.b], writes=[s_.b])
            if 2 * ST * (n + 1) < 128:
                S.op("dve", lambda e, s_=s_: e.memset(s_[:, 2 * ST * (n + 1):128], -1.0), writes=[s_.b], reads=[s_.b])
            S.op("dve", lambda e, s_=s_: e.memset(s_[:, 0:1], 1e9), writes=[s_.b], reads=[s_.b])
            S.op("dve", lambda e, s_=s_, m_=m_: e.max(out=m_[:, 0:8], in_=s_[:, :]), reads=[s_.b], writes=[m_.b])
            S.op("dve", lambda e, s_=s_, m_=m_: e.match_replace(out=sc2[:, :], in_to_replace=m_[:, 0:8], in_values=s_[:, :], imm_value=-2.0),
                 reads=[s_.b, m_.b], writes=[sc2.b])
            S.op("dve", lambda e, m_=m_: e.max(out=m_[:, 8:16], in_=sc2[:, :]), reads=[sc2.b], writes=[m_.b])
            S.op("dve", lambda e, s_=s_, m_=m_, ns_=ns_: e.tensor_scalar(out=ns_[:, :], in0=s_[:, :], scalar1=m_[:, 15:16], scalar2=1.0, op0=ALU.is_ge, op1=ALU.subtract),
                 reads=[s_.b, m_.b], writes=[ns_.b])
            S.op("pe", lambda e, ns_=ns_: e.transpose(out=psT[:, 0, :], in_=ns_[:, :], identity=id_b[:, :]), reads=[ns_.b, id_b.b], writes=[psT.b])
            S.op("dve", lambda e, nT=nT: e.tensor_scalar(out=nT[:, :, :], in0=psT[:, 0:1, :].to_broadcast([128, 4, 128]), scalar1=NEG, scalar2=None, op0=ALU.mult),
                 reads=[psT.b], writes=[nT.b])
            ctxg[g] = (gv, nT)

        def grp_tail(g):
            gv, nT = ctxg[g]
            def branch(br):
                a_ = acc[acn[0] % 2]; acn[0] += 1
                nk = ST * (n + 1) if br == 1 else 5
                LA = 2
                staged = {}

                def stage1(kt):
                    p = nrot()
                    if br == 1:
                        S.op("pe", lambda e, p=p, kt=kt: e.matmul(p[:, :], lhsT=k_sb[g * 64:(g + 1) * 64, kt * 128:(kt + 1) * 128],
                                                                   rhs=qs[g * 64:(g + 1) * 64, :, :].rearrange("p h q -> p (h q)"), start=True, stop=False),
                             reads=[k_sb.b, qs.b, q64], writes=[p.b])
                        S.op("pe", lambda e, p=p, kt=kt: e.matmul(p[:, :], lhsT=exp_sb[:, kt, :], rhs=nT[:, :, :].rearrange("p h q -> p (h q)"), start=False, stop=True),
                             reads=[exp_sb.b, nT.b], writes=[p.b])
                        di = ST * n - kt + 3
                        tbl = Tsel[:, di, 4 * g:4 * g + 4, :] if di <= 10 else None
                        tb_b = Tsel.b
                        vl = v_sb[:, kt, g, :]
                        vb = v_sb.b
                    else:
                        S.op("pe", lambda e, p=p, kt=kt: e.matmul(p[:, :], lhsT=kws[g * 64:(g + 1) * 64, kt * 128:(kt + 1) * 128],
                                                                   rhs=qs[g * 64:(g + 1) * 64, :, :].rearrange("p h q -> p (h q)"), start=True, stop=True),
                             reads=[kws.b, kvw64, qs.b, q64], writes=[p.b])
                        tbl = Twin[:, kt, 4 * g:4 * g + 4, :]
                        tb_b = Twin.b
                        vl = vws[:, kt, g, :]
                        vb = vws.b
                    et = Et[ec[0] % 4]; ec[0] += 1
                    S.op("act", lambda e, p=p, et=et: e.activation(out=et[:, :], in_=p[:, :], func=AF.Exp), reads=[p.b], writes=[et.b])
                    if tbl is not None:
                        S.op("dve", lambda e, et=et, tbl=tbl: e.tensor_tensor(out=et[:, :].rearrange("p (h q) -> p h q", h=4), in0=et[:, :].rearrange("p (h q) -> p h q", h=4),
                                                                          in1=tbl, op=ALU.mult), reads=[et.b, tb_b], writes=[et.b])
                    staged[kt] = (et, vl, vb)

                def stage2(kt):
                    et, vl, vb = staged.pop(kt)
                    S.op("pe", lambda e, vl=vl, et=et, kt=kt: e.matmul(a_[0:65, :], lhsT=vl, rhs=et[:, :], start=(kt == 0), stop=(kt == nk - 1)),
                         reads=[vb, kvw64, et.b], writes=[a_.b])
                for kt in range(min(LA, nk)):
                    stage1(kt)
                for kt in range(nk):
                    if kt + LA < nk:
                        stage1(kt + LA)
                    stage2(kt)
                ot = oT_sb[pc_[0] % 2]; tm_ = tmpm[pc_[0] % 2]; fs = fsc[pc_[0] % 2]; pc_[0] += 1
                S.op("act", lambda e, a_=a_, ot=ot: e.activation(out=ot[:, :], in_=a_[0:65, :], func=AF.Copy), reads=[a_.b], writes=[ot.b])
                for hh in range(4):
                    S.op("pe", lambda e, hh=hh, ot=ot: e.transpose(out=psM[:, hh * 65:(hh + 1) * 65], in_=ot[0:65, hh * 128:(hh + 1) * 128], identity=id_f[0:65, 0:65]),
                         reads=[ot.b, id_f.b], writes=[psM.b], nowaw=(hh > 0))
                pv = psM[:, 0:260].rearrange("p (h d) -> p h d", d=65)
                S.op("dve", lambda e, pv=pv, fs=fs: e.tensor_scalar(out=fs[:, 0:4], in0=pv[:, :, 64], scalar1=1e-30, scalar2=None, op0=ALU.max), reads=[psM.b], writes=[fs.b])
                S.op("dve", lambda e, fs=fs: e.reciprocal(out=fs[:, 4:8], in_=fs[:, 0:4]), reads=[fs.b], writes=[fs.b])
                S.op("dve", lambda e, fs=fs, gv=gv, br=br: e.tensor_tensor(out=fs[:, 0:4], in0=fs[:, 4:8], in1=gv[:, :, br], op=ALU.mult), reads=[fs.b, gs.b], writes=[fs.b])
                S.op("dve", lambda e, pv=pv, fs=fs, tm_=tm_: e.tensor_tensor(out=tm_[:, :, :], in0=pv[:, :, 0:64], in1=fs[:, 0:4].unsqueeze(2).to_broadcast([128, 4, 64]), op=ALU.mult),
                     reads=[psM.b, fs.b], writes=[tm_.b])
                S.op("dve", lambda e, mx=mx, tm_=tm_, g=g: e.tensor_tensor(out=mx[:, g * 256:(g + 1) * 256].rearrange("p (h d) -> p h d", d=64),
                                                                        in0=mx[:, g * 256:(g + 1) * 256].rearrange("p (h d) -> p h d", d=64), in1=tm_[:, :, :], op=ALU.add),
                     reads=[mx.b, tm_.b], writes=[mx.b])
            for br in (1, 2):
                branch(br)
        for g in range(2):
            grp(g)
        for g in range(2):
            grp_tail(g)
        mb = mixb[k2]
        S.op("act", lambda e, mb=mb, mx=mx: e.activation(out=mb[:, :], in_=mx[:, :], func=AF.Copy), reads=[mx.b], writes=[mb.b])
        for j in range(4):
            S.op("pe", lambda e, j=j, mb=mb: e.transpose(out=psT[:, j, :], in_=mb[:, j * 128:(j + 1) * 128], identity=id_b[:, :]),
                 reads=[mb.b, id_b.b], writes=[psT.b], nowaw=(j > 0))
        S.op("act", lambda e, mf=mf: e.activation(out=mf[:, 0:4, :], in_=psT[:, 0:4, :], func=AF.Copy), reads=[psT.b], writes=[mf.b])
        p2_, p4_, p8_, p16_ = pl
        S.op("pool", lambda e, pins=pins: e.tensor_tensor(out=p2_[:, :, 1:144], in0=pins[:, :, 1:144], in1=pins[:, :, 0:143], op=ALU.add), reads=[pins.b], writes=[p2_.b])
        S.op("pool", lambda e: e.tensor_tensor(out=p4_[:, :, 3:144], in0=p2_[:, :, 3:144], in1=p2_[:, :, 1:142], op=ALU.add), reads=[p2_.b], writes=[p4_.b])
        S.op("pool", lambda e: e.tensor_tensor(out=p8_[:, :, 7:144], in0=p4_[:, :, 7:144], in1=p4_[:, :, 3:140], op=ALU.add), reads=[p4_.b], writes=[p8_.b])
        S.op("pool", lambda e: e.tensor_tensor(out=p16_[:, :, 15:144], in0=p8_[:, :, 15:144], in1=p8_[:, :, 7:136], op=ALU.add), reads=[p8_.b], writes=[p16_.b])
        d_t = dd[k2]
        for gi, (srcw, win) in enumerate(((p2_, 2), (p4_, 4), (p8_, 8), (p16_, 16))):
            ch = gi // 2; P = slice((gi % 2) * 64, (gi % 2) * 64 + 64)
            if n == 0:
                S.op("pool", lambda e, srcw=srcw, ch=ch, P=P: e.tensor_tensor(out=srcw[P, ch, 16:144], in0=srcw[P, ch, 16:144], in1=invc_sb[P, ch, :], op=ALU.mult),
                     reads=[srcw.b, invc_sb.b], writes=[srcw.b])
                S.op("pool", lambda e, srcw=srcw, ch=ch, P=P, d_t=d_t, pins=pins: e.tensor_tensor(out=d_t[P, ch, :], in0=srcw[P, ch, 16:144], in1=pins[P, ch, 16:144], op=ALU.subtract),
                     reads=[srcw.b, pins.b], writes=[d_t.b], nowaw=True)
            else:
                S.op("dve", lambda e, srcw=srcw, ch=ch, P=P, d_t=d_t, pins=pins, win=win: e.scalar_tensor_tensor(out=d_t[P, ch, :], in0=srcw[P, ch, 16:144], scalar=1.0 / win,
                                                                                                         in1=pins[P, ch, 16:144], op0=ALU.mult, op1=ALU.subtract),
                     reads=[srcw.b, pins.b], writes=[d_t.b], nowaw=True)
        pp = nrot()
        for ch in range(2):
            S.op("pe", lambda e, pp=pp, ch=ch, d_t=d_t: e.matmul(pp[:, ch * 128:(ch + 1) * 128], lhsT=pwb[:, ch, :], rhs=d_t[:, ch, :], start=True, stop=True),
                 reads=[pwb.b, d_t.b], writes=[pp.b], nowaw=(ch > 0))
        for ch in range(2):
            S.op("act", lambda e, pp=pp, ch=ch, mf=mf: e.activation(out=mf[:, 4 + ch, :], in_=pp[:, ch * 128:(ch + 1) * 128], func=AF.Copy, scale=ps_sb[:, ch:ch + 1]),
                 reads=[pp.b, ps_sb.b], writes=[mf.b], nowaw=True)
        ys = yss[k2]; y_ = yt[k2]; xo_ = y_
        pys = [nrot(), nrot()]
        for hf in range(2):
            for kc in range(8):
                S.op("pe", lambda e, hf=hf, kc=kc, mf=mf: e.matmul(pys[hf][:, :], lhsT=mf[:, kc, :], rhs=wo_sb[:, kc, hf * 512:(hf + 1) * 512], start=(kc == 0), stop=(kc == 7)),
                     reads=[mf.b, mfT_oc[k2], wo_sb.b], writes=[pys[hf].b])
            S.op("act", lambda e, hf=hf, ys=ys: e.activation(out=ysq[:, :], in_=pys[hf][:, :], func=AF.Square, accum_out=ys[:, hf:hf + 1]),
                 reads=[pys[hf].b], writes=[ysq.b, ys.b])
        S.op("dve", lambda e, ys=ys: e.tensor_tensor(out=ys[:, 2:3], in0=ys[:, 0:1], in1=ys[:, 1:2], op=ALU.add), reads=[ys.b], writes=[ys.b])
        S.op("act", lambda e, ys=ys: e.activation(out=ys[:, 3:4], in_=ys[:, 2:3], func=AF.Sqrt, scale=1.0 / D, bias=eps_c[:, :]), reads=[ys.b, eps_c.b], writes=[ys.b])
        S.op("dve", lambda e, ys=ys: e.reciprocal(out=ys[:, 3:4], in_=ys[:, 3:4]), reads=[ys.b], writes=[ys.b])
        for hf in range(2):
            S.op("dve", lambda e, hf=hf, ys=ys, y_=y_: e.scalar_tensor_tensor(out=y_[:, hf * 512:(hf + 1) * 512], in0=pys[hf][:, :], scalar=ys[:, 3:4],
                                                                        in1=gp_sb[:, hf * 512:(hf + 1) * 512], op0=ALU.mult, op1=ALU.mult),
                 reads=[pys[hf].b, ys.b, gp_sb.b], writes=[y_.b], nowaw=(hf > 0))
        S.op("pool", lambda e, y_=y_, xs=xs, xo_=xo_: e.tensor_tensor(out=xo_[:, :], in0=y_[:, :], in1=xs[:, :], op=ALU.add), reads=[y_.b, xs.b], writes=[xo_.b])
        S.dma("poolq", x_out[n * 128:(n + 1) * 128, :], xo_[:, :], reads=[xo_.b], writes=[x_out.b], nowaw=True, is_output=True)
    for n in range(NTL):
        tile(n)
    return C.finish()


def _t5_bucket(n):
    n = np.maximum(np.asarray(n, np.int64), 0)
    nf = np.maximum(n, 1).astype(np.float32)
    lg = (np.log(nf / np.float32(16.0)) / np.float32(np.log(64.0)) * np.float32(16.0)).astype(np.float32)
    large = 16 + lg.astype(np.int32)
    large = np.minimum(large, 31)
    return np.where(n < 16, n, large).astype(np.int64)


def _onehot(dist, valid):
    dist = np.asarray(dist); valid = np.asarray(valid, bool)
    b = _t5_bucket(np.where(valid, dist, 0))
    rows = np.where(valid, b, 32)
    oh = np.zeros((33, dist.size), np.float32)
    oh[rows.reshape(-1), np.arange(dist.size)] = 1.0
    return oh


_CONST_CACHE = {}


def consts_B1(cp):
    if cp in _CONST_CACHE:
        return _CONST_CACHE[cp]
    m = np.arange(1536)
    d = m - 511 + 128 * cp
    ohs = _onehot(d, d >= 0)
    w = np.repeat(np.arange(5), 256); mp = np.tile(np.arange(256), 5)
    d = 128 * (4 - w) + mp - 127
    okw = (d >= 0) & (d < 512) & (mp <= 254)
    ohw = _onehot(d, okw)
    ohw0 = _onehot(d, okw & (w >= 4 - cp))
    c3 = np.repeat(np.arange(88), 128); tq = np.tile(np.arange(128), 88)
    d = 128 * cp + tq + 865 - 16 * c3
    ohc = _onehot(d, d >= 0).reshape(33, 88, 128)
    expand = np.zeros((128, 64, 128), np.float32)
    for kt in range(64):
        expand[2 * kt + 1, kt, 0:64] = 1.0
        expand[2 * kt, kt, 64:128] = 1.0
    vfc = np.zeros((128, 2, 9), np.float32)
    for t in range(128):
        bq = 2 * cp + (1 if t >= 64 else 0)
        for jj in range(9):
            jp = jj - 1
            valid = jp <= bq
            forced = (jp == bq) or (jp == bq - 1)
            vfc[t, 0, jj] = 1.0 if (valid and not forced) else 0.0
            vfc[t, 1, jj] = 1e9 if forced else (0.0 if valid else -1.0)
    invc0 = np.zeros((128, 2, 128), np.float32)
    for p in range(128):
        for ch in range(2):
            win = (2, 4, 8, 16)[2 * ch + (1 if p >= 64 else 0)]
            t = 128 * cp + np.arange(128)
            invc0[p, ch] = 1.0 / np.minimum(t + 1, win)
    r = dict(ohs=ohs, ohw=ohw, ohw0=ohw0, ohc=ohc, expand=expand.reshape(128, 8192), vfc=vfc, invc0=invc0, ident=_IDENT)
    _CONST_CACHE[cp] = r
    return r


def gather_full(A, b, key, rows):
    r0, r1 = rows
    out = np.empty((r1 - r0, 64, 128), A[0][key].dtype)
    for cp in range(4):
        out[:, cp::4, :] = A[b * 4 + cp][key][r0:r1].reshape(r1 - r0, NT, 128)
    return out.reshape(r1 - r0, 8192)


def gather_full_tm(A, b, key, cols):
    c0, c1 = cols
    out = np.empty((64, 128, c1 - c0), A[0][key].dtype)
    for cp in range(4):
        out[cp::4] = A[b * 4 + cp][key][:, c0:c1].reshape(NT, 128, c1 - c0)
    return out.reshape(8192, c1 - c0)


def prep_B1_batch(A, b):
    kc = gather_full(A, b, "kvT", (0, 128)); vc = gather_full(A, b, "kvT", (128, 256))
    ksl = gather_full(A, b, "kvT", (256, 384)); kw = gather_full(A, b, "kvT", (384, 512))
    vsl = gather_full_tm(A, b, "vtm", (0, 128)); vw = gather_full_tm(A, b, "vtm", (128, 256))
    pin = gather_full(A, b, "pinT", (0, 256))
    r = {}
    r["kc_in"] = np.ascontiguousarray(kc.reshape(2, 64, 8192).transpose(1, 0, 2))
    r["vc_in"] = np.ascontiguousarray(vc.reshape(2, 64, 8192).transpose(1, 0, 2))
    r["ksl_in"] = np.ascontiguousarray(ksl.reshape(128, 64, 128)[:, :, ::-1].reshape(128, 8192))
    r["vsl_in"] = np.ascontiguousarray(vsl.reshape(64, 128, 2, 64)[:, ::-1].transpose(1, 0, 2, 3))
    r["kwp"] = np.concatenate([np.zeros((128, 512), kw.dtype), kw], axis=1)
    r["vwp"] = np.concatenate([np.zeros((512, 128), vw.dtype), vw], axis=0)
    r["pinp"] = np.concatenate([np.zeros((256, 16), pin.dtype), pin], axis=1)
    return r


def prep_B1(inp, l, x, core, A, bb):
    b, cp = divmod(core, 4)
    r = dict(consts_B1(cp))
    a = A[core]
    r["qT_in"] = np.ascontiguousarray(a["qT"].reshape(2, 4, 64, TL).transpose(0, 2, 1, 3).reshape(128, 4, TL))
    r["gate_in"] = a["gate"]
    for k in ("kc_in", "vc_in", "ksl_in", "vsl_in"):
        r[k] = bb[k]
    kw_in = np.empty((128, NT, 640), bb["kwp"].dtype)
    vw_in = np.empty((128, NT, 5, 2, 64), bb["vwp"].dtype)
    pin_in = np.empty((128, 2, NT, 144), np.float32)
    for n in range(NT):
        i = 4 * n + cp
        kw_in[:, n, :] = bb["kwp"][:, 128 * i:128 * i + 640].reshape(128, 5, 128)[:, :, ::-1].reshape(128, 640)
        vw_in[:, n] = bb["vwp"][128 * i:128 * i + 640].reshape(5, 128, 2, 64)[:, ::-1].transpose(1, 0, 2, 3)
        pin_in[:, :, n, :] = bb["pinp"][:, 128 * i:128 * i + 144].reshape(2, 128, 144).transpose(1, 0, 2)
    r["kw_in"] = kw_in; r["vw_in"] = vw_in; r["pin_in"] = pin_in
    r["oc_in"] = np.ascontiguousarray(a["ocT"].reshape(2, 128, TL).transpose(1, 0, 2))
    r["x_in"] = local_rows(x, core)
    r["relb"] = np.ascontiguousarray(inp["rel_bias"])
    r["cpos"] = np.ascontiguousarray(inp["cmp_pos"][l].transpose(2, 0, 1))
    r["cw1"] = np.ascontiguousarray(inp["cmp_w1"][l].transpose(2, 0, 1, 3))
    r["cb1"] = np.ascontiguousarray(inp["cmp_b1"][l].T)
    r["cw2"] = np.ascontiguousarray(inp["cmp_w2"][l].transpose(1, 0, 2))
    r["cb2"] = np.ascontiguousarray(inp["cmp_b2"][l])
    r["poolw"] = np.ascontiguousarray(inp["pool_w"][l])
    r["pools"] = gcol(inp["pool_scale"][l])
    r["wo"] = np.ascontiguousarray(inp["w_out"][l])
    r["gpost"] = np.ascontiguousarray(inp["mix_norm_post"][l])
    return r


def load_w_scaled(C, S, w_d, rows, cols, g_sb, Wb, stage, q="sp"):
    for kc in range(rows // 128):
        st = stage[kc % 2]
        S.dma(q, st[:, 0:cols], w_d[kc * 128:(kc + 1) * 128, :], writes=[st.b])
        if g_sb is None:
            S.op("dve", lambda e, kc=kc, st=st: e.tensor_copy(out=Wb[:, kc, :], in_=st[:, 0:cols]), reads=[st.b], writes=[Wb.b], nowaw=True)
        else:
            S.op("dve", lambda e, kc=kc, st=st: e.tensor_scalar(out=Wb[:, kc, :], in0=st[:, 0:cols], scalar1=g_sb[:, kc:kc + 1], scalar2=None, op0=ALU.mult),
                 reads=[st.b, g_sb.b], writes=[Wb.b], nowaw=True)


def norm_rows(S, xin, nrows, sq, ss, eps_c, out_b):
    S.op("act", lambda e: e.activation(out=sq[0:nrows, :], in_=xin[0:nrows, :], func=AF.Square, accum_out=ss[0:nrows, 0:1]),
         reads=[xin.b], writes=[sq.b, ss.b])
    S.op("act", lambda e: e.activation(out=ss[0:nrows, 1:2], in_=ss[0:nrows, 0:1], func=AF.Sqrt, scale=1.0 / D, bias=eps_c[0:nrows, :]),
         reads=[ss.b, eps_c.b], writes=[ss.b])
    S.op("dve", lambda e: e.reciprocal(out=ss[0:nrows, 1:2], in_=ss[0:nrows, 1:2]), reads=[ss.b], writes=[ss.b])
    S.op("dve", lambda e: e.tensor_scalar(out=out_b[0:nrows, :], in0=xin[0:nrows, :], scalar1=ss[0:nrows, 1:2], scalar2=None, op0=ALU.mult),
         reads=[xin.b, ss.b], writes=[out_b.b])


def post_norm_residual(S, pys, ys, ysq, eps_c, gp_sb, y_, xs, x_out, row0):
    for hf in range(2):
        S.op("act", lambda e, hf=hf: e.activation(out=ysq[:, :], in_=pys[hf][:, :], func=AF.Square, accum_out=ys[:, hf:hf + 1]),
             reads=[pys[hf].b], writes=[ysq.b, ys.b])
    S.op("dve", lambda e: e.tensor_tensor(out=ys[:, 2:3], in0=ys[:, 0:1], in1=ys[:, 1:2], op=ALU.add), reads=[ys.b], writes=[ys.b])
    S.op("act", lambda e: e.activation(out=ys[:, 3:4], in_=ys[:, 2:3], func=AF.Sqrt, scale=1.0 / D, bias=eps_c[:, :]), reads=[ys.b, eps_c.b], writes=[ys.b])
    S.op("dve", lambda e: e.reciprocal(out=ys[:, 3:4], in_=ys[:, 3:4]), reads=[ys.b], writes=[ys.b])
    for hf in range(2):
        S.op("dve", lambda e, hf=hf: e.scalar_tensor_tensor(out=y_[:, hf * 512:(hf + 1) * 512], in0=pys[hf][:, :], scalar=ys[:, 3:4],
                                                            in1=gp_sb[:, hf * 512:(hf + 1) * 512], op0=ALU.mult, op1=ALU.mult),
             reads=[pys[hf].b, ys.b, gp_sb.b], writes=[y_.b], nowaw=(hf > 0))
    S.op("pool", lambda e: e.tensor_tensor(out=y_[:, :], in0=y_[:, :], in1=xs[:, :], op=ALU.add), reads=[y_.b, xs.b], writes=[y_.b])
    S.dma("poolq", x_out[row0:row0 + 128, :], y_[:, :], reads=[y_.b], writes=[x_out.b], nowaw=True, is_output=True)


def build_B2():
    C = Ctx("B2"); S = C.S
    x_in = C.din("x_in", [TL, D], F32)
    mem = C.din("mem", [256, D], F32)
    gq = C.din("gq", [128, 8], F32)
    gkv = C.din("gkv", [128, 8], F32)
    gpost = C.din("gpost", [D], F32)
    wq = C.din("wq", [D, D], F32); wk = C.din("wk", [D, D], F32); wv = C.din("wv", [D, D], F32); wo = C.din("wo", [D, D], F32)
    ident = C.din("ident", [128, 128], F32)
    x_out = C.dout("x_out", [TL, D], F32)
    rot = [C.ps([128, 512], F32, f"rot{i}") for i in range(5)]
    psT = [C.ps([128, 8, 128], BF16, f"psT{i}") for i in range(2)]
    psd = C.ps([128, 512], F32, "psd")
    rc = [0]

    def nrot():
        rc[0] += 1
        return rot[rc[0] % 5]
    eps_c = C.sb([128, 1], F32, "eps_c")
    S.op("dve", lambda e: e.memset(eps_c[:, :], EPS), writes=[eps_c.b])
    id_b = C.sb([128, 128], BF16, "id_b")
    S.dma("poolq", id_b[:, :], ident[:, :], writes=[id_b.b])
    ones = C.sb([128, 128], BF16, "ones")
    S.op("dve", lambda e: e.memset(ones[:, :], 1.0), writes=[ones.b])
    gq_sb = C.sb([128, 8], F32, "gq_sb"); gkv_sb = C.sb([128, 8], F32, "gkv_sb")
    S.dma("sp", gq_sb[:, :], gq[:, :], writes=[gq_sb.b])
    S.dma("sp", gkv_sb[:, :], gkv[:, :], writes=[gkv_sb.b])
    gp_sb = C.sb([128, D], F32, "gp_sb")
    S.dma("sp", gp_sb[:, :], bcast_rows(gpost, D), writes=[gp_sb.b])
    stage = [C.sb([128, D], F32, f"stage{i}") for i in range(2)]
    Wq = C.sb([128, 8, D], BF16, "Wq"); Wk = C.sb([128, 8, D], BF16, "Wk"); Wv = C.sb([128, 8, D], BF16, "Wv"); Wo = C.sb([128, 8, D], BF16, "Wo")
    load_w_scaled(C, S, wk, D, D, gkv_sb, Wk, stage)
    load_w_scaled(C, S, wv, D, D, gkv_sb, Wv, stage)
    load_w_scaled(C, S, wq, D, D, gq_sb, Wq, stage)
    load_w_scaled(C, S, wo, D, D, None, Wo, stage)
    xt = [C.sb([128, D], F32, f"xt{i}") for i in range(2)]
    sq = C.sb([128, D], BF16, "sq")
    ss = [C.sb([128, 2], F32, f"ss{i}") for i in range(2)]
    hn = [C.sb([128, D], BF16, f"hn{i}") for i in range(2)]
    mT = C.sb([128, 8, 256], BF16, "mT")
    for mt in range(2):
        xx = xt[mt]; hh = hn[mt]; pT = psT[mt]
        S.dma("sp", xx[:, :], mem[mt * 128:(mt + 1) * 128, :], writes=[xx.b])
        norm_rows(S, xx, 128, sq, ss[mt], eps_c, hh)
        for kc in range(8):
            S.op("pe", lambda e, kc=kc, hh=hh, pT=pT: e.transpose(out=pT[:, kc, :], in_=hh[:, kc * 128:(kc + 1) * 128], identity=id_b[:, :]),
                 reads=[hh.b, id_b.b], writes=[pT.b], nowaw=(kc > 0))
        S.op("dve", lambda e, pT=pT, mt=mt: e.tensor_copy(out=mT[:, :, mt * 128:(mt + 1) * 128], in_=pT[:, :, :]), reads=[pT.b], writes=[mT.b], nowaw=True)
    kT = C.sb([128, 8, 256], BF16, "kT")
    v_sb = C.sb([128, 2, D], BF16, "v_sb")
    for oc in range(8):
        p = nrot()
        for kc in range(8):
            S.op("pe", lambda e, p=p, kc=kc, oc=oc: e.matmul(p[:, 0:256], lhsT=Wk[:, kc, oc * 128:(oc + 1) * 128], rhs=mT[:, kc, :], start=(kc == 0), stop=(kc == 7)),
                 reads=[Wk.b, mT.b], writes=[p.b])
        S.op("act", lambda e, p=p, oc=oc: e.activation(out=kT[:, oc, :], in_=p[:, 0:256], func=AF.Copy), reads=[p.b], writes=[kT.b], nowaw=True)
    for mt in range(2):
        for hf in range(2):
            p = nrot()
            for kc in range(8):
                S.op("pe", lambda e, p=p, kc=kc, mt=mt, hf=hf: e.matmul(p[:, :], lhsT=mT[:, kc, mt * 128:(mt + 1) * 128], rhs=Wv[:, kc, hf * 512:(hf + 1) * 512], start=(kc == 0), stop=(kc == 7)),
                     reads=[Wv.b, mT.b], writes=[p.b])
            S.op("act", lambda e, p=p, mt=mt, hf=hf: e.activation(out=v_sb[:, mt, hf * 512:(hf + 1) * 512], in_=p[:, :], func=AF.Copy), reads=[p.b], writes=[v_sb.b], nowaw=True)
    hT = [C.sb([128, 8, 512], BF16, f"hT{i}") for i in range(2)]
    qT = [C.sb([128, 8, 512], BF16, f"qT{i}") for i in range(2)]
    oT = [C.sb([128, 8, 512], BF16, f"oT{i}") for i in range(2)]
    Pt = [C.sb([128, 512], BF16, f"Pt{i}") for i in range(4)]
    rden = [C.sb([128, 512], F32, f"rden{i}") for i in range(2)]
    xk = [C.sb([128, D], F32, f"xk{i}") for i in range(4)]
    ysq = C.sb([128, 512], BF16, "ysq")
    yss = [C.sb([128, 4], F32, f"yss{i}") for i in range(2)]
    yt = [C.sb([128, D], F32, f"yt{i}") for i in range(2)]
    pcnt = [0]

    def group(grp):
        hTg = hT[grp % 2]; qTg = qT[grp % 2]; oTg = oT[grp % 2]
        for tt in range(4):
            t = grp * 4 + tt
            xx = xk[tt]; hh = hn[t % 2]; pT = psT[t % 2]
            S.dma("sp", xx[:, :], x_in[t * 128:(t + 1) * 128, :], writes=[xx.b])
            norm_rows(S, xx, 128, sq, ss[t % 2], eps_c, hh)
            for kc in range(8):
                S.op("pe", lambda e, kc=kc, hh=hh, pT=pT: e.transpose(out=pT[:, kc, :], in_=hh[:, kc * 128:(kc + 1) * 128], identity=id_b[:, :]),
                     reads=[hh.b, id_b.b], writes=[pT.b], nowaw=(kc > 0))
            S.op("dve", lambda e, pT=pT, tt=tt: e.tensor_copy(out=hTg[:, :, tt * 128:(tt + 1) * 128], in_=pT[:, :, :]), reads=[pT.b], writes=[hTg.b], nowaw=True)
        for oc in range(8):
            p = nrot()
            for kc in range(8):
                S.op("pe", lambda e, p=p, kc=kc, oc=oc: e.matmul(p[:, :], lhsT=Wq[:, kc, oc * 128:(oc + 1) * 128], rhs=hTg[:, kc, :], start=(kc == 0), stop=(kc == 7)),
                     reads=[Wq.b, hTg.b], writes=[p.b])
            S.op("act", lambda e, p=p, oc=oc: e.activation(out=qTg[:, oc, :], in_=p[:, :], func=AF.Copy, scale=1.0 / 16), reads=[p.b], writes=[qTg.b], nowaw=True)
        for hd in range(4):
            pts = []
            for mt in range(2):
                p = nrot()
                for dc in range(2):
                    S.op("pe", lambda e, p=p, dc=dc, mt=mt, hd=hd: e.matmul(p[:, :], lhsT=kT[:, hd * 2 + dc, mt * 128:(mt + 1) * 128], rhs=qTg[:, hd * 2 + dc, :], start=(dc == 0), stop=(dc == 1)),
                         reads=[kT.b, qTg.b], writes=[p.b])
                pt = Pt[pcnt[0] % 4]; pcnt[0] += 1
                S.op("act", lambda e, p=p, pt=pt: e.activation(out=pt[:, :], in_=p[:, :], func=AF.Exp), reads=[p.b], writes=[pt.b])
                pts.append(pt)
            for mt in range(2):
                S.op("pe", lambda e, mt=mt, pts=pts: e.matmul(psd[:, :], lhsT=ones[:, :], rhs=pts[mt][:, :], start=(mt == 0), stop=(mt == 1)),
                     reads=[ones.b, pts[mt].b], writes=[psd.b])
            rd = rden[hd % 2]
            S.op("dve", lambda e, rd=rd: e.reciprocal(out=rd[:, :], in_=psd[:, :]), reads=[psd.b], writes=[rd.b])
            for dvc in range(2):
                p = nrot()
                for mt in range(2):
                    S.op("pe", lambda e, p=p, mt=mt, hd=hd, dvc=dvc, pts=pts: e.matmul(p[:, :], lhsT=v_sb[:, mt, hd * 256 + dvc * 128:hd * 256 + dvc * 128 + 128], rhs=pts[mt][:, :],
                                                                                  start=(mt == 0), stop=(mt == 1)),
                         reads=[v_sb.b, pts[mt].b], writes=[p.b])
                S.op("dve", lambda e, p=p, hd=hd, dvc=dvc, rd=rd: e.tensor_tensor(out=oTg[:, hd * 2 + dvc, :], in0=p[:, :], in1=rd[:, :], op=ALU.mult),
                     reads=[p.b, rd.b], writes=[oTg.b], nowaw=True)
        for tt in range(4):
            t = grp * 4 + tt
            pys = [nrot(), nrot()]
            for hf in range(2):
                for kc in range(8):
                    S.op("pe", lambda e, hf=hf, kc=kc, tt=tt, pys=pys: e.matmul(pys[hf][:, :], lhsT=oTg[:, kc, tt * 128:(tt + 1) * 128], rhs=Wo[:, kc, hf * 512:(hf + 1) * 512], start=(kc == 0), stop=(kc == 7)),
                         reads=[oTg.b, Wo.b], writes=[pys[hf].b])
            post_norm_residual(S, pys, yss[t % 2], ysq, eps_c, gp_sb, yt[t % 2], xk[tt], x_out, t * 128)
    for grp in range(NT // 4):
        group(grp)
    return C.finish()


def prep_B2(inp, l, core, x1_local):
    b = core // 4
    return {"x_in": x1_local, "mem": np.ascontiguousarray(inp["mem"][b]), "gq": gcol(inp["mem_norm_pre"][l]), "gkv": gcol(inp["mem_norm_kv"][l]),
            "gpost": np.ascontiguousarray(inp["mem_norm_post"][l]), "wq": np.ascontiguousarray(inp["w_mq"][l]), "wk": np.ascontiguousarray(inp["w_mk"][l]),
            "wv": np.ascontiguousarray(inp["w_mv"][l]), "wo": np.ascontiguousarray(inp["w_mo"][l]), "ident": _IDENT}


FH = 2816
NCH = 44


def build_C():
    C = Ctx("C"); S = C.S
    x_in = C.din("x_in", [TL, D], F32)
    xh = C.din("xh", [NT * 2, D], F32)
    gpre = C.din("gpre", [128, 8], F32)
    gpost = C.din("gpost", [D], F32)
    wu = C.din("wu", [D, 2 * FH], F32)
    wd = C.din("wd", [FH, D], F32)
    cwc = C.din("cwc", [128, NCH, 3], F32)
    cbc = C.din("cbc", [128, NCH], F32)
    ident = C.din("ident", [128, 128], F32)
    x_out = C.dout("x_out", [TL, D], F32)
    rot = [C.ps([128, 512], F32, f"rot{i}") for i in range(4)]
    psh = [C.ps([128, 512], F32, f"psh{i}") for i in range(2)]
    psT = [C.ps([128, 8, 128], BF16, f"psT{i}") for i in range(2)]
    rc = [0]

    def nrot():
        rc[0] += 1
        return rot[rc[0] % 4]
    eps_c = C.sb([128, 1], F32, "eps_c")
    S.op("dve", lambda e: e.memset(eps_c[:, :], EPS), writes=[eps_c.b])
    id_b = C.sb([128, 128], BF16, "id_b")
    S.dma("poolq", id_b[:, :], ident[:, :], writes=[id_b.b])
    g_sb = C.sb([128, 8], F32, "g_sb")
    S.dma("sp", g_sb[:, :], gpre[:, :], writes=[g_sb.b])
    gp_sb = C.sb([128, D], F32, "gp_sb")
    S.dma("sp", gp_sb[:, :], bcast_rows(gpost, D), writes=[gp_sb.b])
    cw_sb = C.sb([128, NCH, 3], F32, "cw_sb"); cb_sb = C.sb([128, NCH], F32, "cb_sb")
    S.dma("sp", cw_sb[:, :, :], cwc[:, :, :], writes=[cw_sb.b])
    S.dma("sp", cb_sb[:, :], cbc[:, :], writes=[cb_sb.b])
    Wu = C.sb([128, 8, 2 * FH], BF16, "Wu")
    Wd = C.sb([128, 22, D], BF16, "Wd")
    stage = [C.sb([128, 704], F32, f"stage{i}") for i in range(2)]
    si = 0
    for kc in range(8):
        for cbk in range(8):
            st = stage[si % 2]; si += 1
            S.dma("sp", st[:, :], wu[kc * 128:(kc + 1) * 128, cbk * 704:(cbk + 1) * 704], writes=[st.b])
            S.op("dve", lambda e, kc=kc, cbk=cbk, st=st: e.tensor_scalar(out=Wu[:, kc, cbk * 704:(cbk + 1) * 704], in0=st[:, :], scalar1=g_sb[:, kc:kc + 1], scalar2=None, op0=ALU.mult),
                 reads=[st.b, g_sb.b], writes=[Wu.b], nowaw=True)
    for j in range(22):
        S.dma("poolq", Wd[:, j, :], wd[j * 128:(j + 1) * 128, :], writes=[Wd.b], nowaw=True)
    xk = [C.sb([128, D], F32, f"xk{i}") for i in range(2)]
    xhs = C.sb([8, D], F32, "xhs")
    sq = C.sb([128, D], BF16, "sq")
    ss = [C.sb([128, 2], F32, f"ss{i}") for i in range(2)]
    hn = [C.sb([128, D], BF16, f"hn{i}") for i in range(2)]
    hT = [C.sb([128, 8, 520], BF16, "hT0")] * 2
    ab = [C.sb([128, 4, 130], F32, f"ab{i}") for i in range(2)]
    yy = [C.sb([128, 4, 128], F32, f"yy{i}") for i in range(2)]
    gg = [C.sb([128, 4, 128], F32, f"gg{i}") for i in range(2)]
    uT = C.sb([128, 22, 512], BF16, "uT")
    ysq = C.sb([128, 512], BF16, "ysq")
    yss = [C.sb([128, 4], F32, f"yss{i}") for i in range(2)]
    yt = [C.sb([128, D], F32, "yt0")] * 2
    cnt = [0]

    def group(grp):
        hTg = hT[grp % 2]
        for tt in range(4):
            t = grp * 4 + tt
            xx = xk[tt % 2]; hh = hn[t % 2]; pT = psT[t % 2]
            S.dma("sp", xx[:, :], x_in[t * 128:(t + 1) * 128, :], writes=[xx.b])
            norm_rows(S, xx, 128, sq, ss[t % 2], eps_c, hh)
            for kc in range(8):
                S.op("pe", lambda e, kc=kc, hh=hh, pT=pT: e.transpose(out=pT[:, kc, :], in_=hh[:, kc * 128:(kc + 1) * 128], identity=id_b[:, :]),
                     reads=[hh.b, id_b.b], writes=[pT.b], nowaw=(kc > 0))
            S.op("dve", lambda e, pT=pT, tt=tt: e.tensor_copy(out=hTg[:, :, tt * 128:(tt + 1) * 128], in_=pT[:, :, :]), reads=[pT.b], writes=[hTg.b], nowaw=True)
        hh = hn[0]; pT = psT[0]
        S.dma("sp", xhs[:, :], xh[grp * 8:(grp + 1) * 8, :], writes=[xhs.b])
        norm_rows(S, xhs, 8, sq, ss[0], eps_c, hh)
        for kc in range(8):
            S.op("pe", lambda e, kc=kc: e.transpose(out=pT[:, kc, 0:8], in_=hh[0:8, kc * 128:(kc + 1) * 128], identity=id_b[0:8, 0:8]),
                 reads=[hh.b, id_b.b], writes=[pT.b], nowaw=(kc > 0))
        S.op("dve", lambda e: e.tensor_copy(out=hTg[:, :, 512:520], in_=pT[:, :, 0:8]), reads=[pT.b], writes=[hTg.b], nowaw=True)

        def chunk(oc):
            k = cnt[0]; cnt[0] += 1
            a_ = ab[k % 2]; y_ = yy[k % 2]
            pm = nrot(); ph = psh[k % 2]
            for kc in range(8):
                S.op("pe", lambda e, kc=kc: e.matmul(pm[:, :], lhsT=Wu[:, kc, oc * 128:(oc + 1) * 128], rhs=hTg[:, kc, 0:512], start=(kc == 0), stop=(kc == 7)),
                     reads=[Wu.b, hTg.b], writes=[pm.b])
            for kc in range(8):
                S.op("pe", lambda e, kc=kc: e.matmul(ph[:, 0:8], lhsT=Wu[:, kc, oc * 128:(oc + 1) * 128], rhs=hTg[:, kc, 512:520], start=(kc == 0), stop=(kc == 7)),
                     reads=[Wu.b, hTg.b], writes=[ph.b])
            S.op("act", lambda e: e.activation(out=a_[:, :, 2:130], in_=pm[:, :].rearrange("p (t q) -> p t q", q=128), func=AF.Copy), reads=[pm.b], writes=[a_.b])
            S.op("act", lambda e: e.activation(out=a_[:, :, 0:2], in_=ph[:, 0:8].rearrange("p (t q) -> p t q", q=2), func=AF.Copy), reads=[ph.b], writes=[a_.b], nowaw=True)
            S.op("act", lambda e: e.activation(out=y_[:, :, :], in_=pm[:, :].rearrange("p (t q) -> p t q", q=128), func=AF.Identity,
                                               scale=cw_sb[:, oc, 2:3], bias=cb_sb[:, oc:oc + 1]), reads=[pm.b, cw_sb.b, cb_sb.b], writes=[y_.b])
            S.op("dve", lambda e: e.scalar_tensor_tensor(out=y_[:, :, :], in0=a_[:, :, 1:129], scalar=cw_sb[:, oc, 1:2], in1=y_[:, :, :], op0=ALU.mult, op1=ALU.add),
                 reads=[a_.b, y_.b, cw_sb.b], writes=[y_.b])
            S.op("dve", lambda e: e.scalar_tensor_tensor(out=y_[:, :, :], in0=a_[:, :, 0:128], scalar=cw_sb[:, oc, 0:1], in1=y_[:, :, :], op0=ALU.mult, op1=ALU.add),
                 reads=[a_.b, y_.b, cw_sb.b], writes=[y_.b])
            return y_
        for j in range(22):
            yg = chunk(j)
            g_ = gg[j % 2]
            S.op("act", lambda e, yg=yg, g_=g_: e.activation(out=g_[:, :, :], in_=yg[:, :, :], func=AF.Gelu_apprx_tanh), reads=[yg.b], writes=[g_.b])
            yv = chunk(22 + j)
            S.op("dve", lambda e, yv=yv, g_=g_, j=j: e.tensor_tensor(out=uT[:, j, :].rearrange("p (t q) -> p t q", q=128), in0=yv[:, :, :], in1=g_[:, :, :], op=ALU.mult),
                 reads=[yv.b, g_.b], writes=[uT.b], nowaw=True)
        for tt in range(4):
            t = grp * 4 + tt
            pys = [nrot(), nrot()]
            for hf in range(2):
                for j in range(22):
                    S.op("pe", lambda e, hf=hf, j=j, tt=tt, pys=pys: e.matmul(pys[hf][:, :], lhsT=uT[:, j, tt * 128:(tt + 1) * 128], rhs=Wd[:, j, hf * 512:(hf + 1) * 512], start=(j == 0), stop=(j == 21)),
                         reads=[uT.b, Wd.b], writes=[pys[hf].b])
            xx = xk[tt % 2]
            S.dma("sp", xx[:, :], x_in[t * 128:(t + 1) * 128, :], writes=[xx.b])
            post_norm_residual(S, pys, yss[t % 2], ysq, eps_c, gp_sb, yt[t % 2], xx, x_out, t * 128)
    for grp in range(NT // 4):
        group(grp)
    return C.finish()


def halo_rows(x_full, core, nh):
    b, cp = divmod(core, 4)
    out = np.zeros((NT, nh, x_full.shape[2]), x_full.dtype)
    for n in range(NT):
        i = 4 * n + cp
        if i > 0:
            out[n] = x_full[b, 128 * i - nh:128 * i]
    return out.reshape(NT * nh, -1)


def prep_C(inp, l, core, x2_local, x2_full):
    return {"x_in": x2_local, "xh": halo_rows(x2_full, core, 2), "gpre": gcol(inp["ffn_norm_pre"][l]), "gpost": np.ascontiguousarray(inp["ffn_norm_post"][l]),
            "wu": np.ascontiguousarray(inp["w_up"][l]), "wd": np.ascontiguousarray(inp["w_down"][l]),
            "cwc": np.ascontiguousarray(inp["conv_w"][l].reshape(3, NCH, 128).transpose(2, 1, 0)),
            "cbc": np.ascontiguousarray(inp["conv_b"][l].reshape(NCH, 128).T), "ident": _IDENT}


def scatter_local(outs, shape):
    full = np.empty(shape, np.float32)
    v = full.reshape(shape[0], 64, 128, shape[2])
    for core in range(8):
        b, cp = divmod(core, 4)
        v[b, cp::4] = outs[core].reshape(NT, 128, shape[2])
    return full


from concourse.bass_utils import run_bass_kernel_spmd

_PROGS = {}


def _prog(name):
    if name not in _PROGS:
        _PROGS[name] = {"A": build_A, "B1": build_B1, "B2": build_B2, "C": build_C}[name]()
    return _PROGS[name]


def _run(name, maps):
    res = run_bass_kernel_spmd(_prog(name), maps, core_ids=list(range(8)))
    return [{k: np.asarray(v) for k, v in r.items()} for r in res.results]


def kernel(**inputs):
    inp = {k: np.asarray(v) for k, v in inputs.items()}
    x = np.ascontiguousarray(inp["x"], dtype=np.float32)
    shape = x.shape
    for l in range(4):
        A = _run("A", [prep_A(inp, l, x, c) for c in range(8)])
        bbs = [prep_B1_batch(A, b) for b in range(2)]
        r1 = _run("B1", [prep_B1(inp, l, x, c, A, bbs[c // 4]) for c in range(8)])
        del A, bbs
        r2 = _run("B2", [prep_B2(inp, l, c, r1[c]["x_out"]) for c in range(8)])
        x2_full = scatter_local([r2[c]["x_out"] for c in range(8)], shape)
        r3 = _run("C", [prep_C(inp, l, c, r2[c]["x_out"], x2_full) for c in range(8)])
        x = scatter_local([r3[c]["x_out"] for c in range(8)], shape)
    return x
```

```python
from contextlib import ExitStack
import numpy as np
import concourse.bass as bass
import concourse.mybir as mybir

F32 = mybir.dt.float32
BF16 = mybir.dt.bfloat16
AF = mybir.ActivationFunctionType
ALU = mybir.AluOpType
AX = mybir.AxisListType

COMPUTE = ("pe", "act", "dve", "pool")
DMAQ = ("sp", "actq", "poolq")
QENG = {"sp": "sp", "actq": "act", "poolq": "pool"}
NSEM_DMA = 12
SAME_ENGINE_SYNC = True


class Buf:
    __slots__ = ("name", "writers", "readers")

    def __init__(self, name):
        self.name = name
        self.writers = []
        self.readers = []


class Op:
    __slots__ = ("stream", "emit", "deps", "signal", "dma", "idx")


class Sched:
    def __init__(self, nc, es):
        self.nc = nc
        self.es = es
        self.streams = {s: [] for s in ("pe", "act", "dve", "pool", "sp")}
        self.sems = {}
        for s in COMPUTE:
            self.sems[s] = es.enter_context(nc.semaphore("c_" + s))
        self.dsems = {}
        self.dcount = {}
        self.dnext = {}
        for q in DMAQ:
            self.dsems[q] = [es.enter_context(nc.semaphore(f"d_{q}_{i}")) for i in range(NSEM_DMA)]
            self.dcount[q] = [0] * NSEM_DMA
            self.dnext[q] = 0
        self.out_events = []

    def _collect(self, reads, writes, nowaw):
        deps = []
        for b in reads:
            deps.extend(b.writers)
        for b in writes:
            if not nowaw:
                deps.extend(b.writers)
            deps.extend(b.readers)
        return deps

    def _commit(self, ev, reads, writes, nowaw):
        for b in reads:
            if ev[0] == "c":
                b.readers = [r for r in b.readers if not (r[0] == "c" and r[1] == ev[1])]
            b.readers.append(ev)
        for b in writes:
            if nowaw:
                if ev[0] == "c":
                    b.writers = [w for w in b.writers if not (w[0] == "c" and w[1] == ev[1])]
                b.writers.append(ev)
            else:
                b.writers = [ev]
            b.readers = []

    def op(self, eng, emit, reads=(), writes=(), nowaw=False):
        o = Op()
        o.stream = eng
        o.emit = emit
        o.deps = self._collect(reads, writes, nowaw)
        o.signal = False
        o.dma = None
        o.idx = len([x for x in self.streams[eng] if x.dma is None and x.emit is not None]) if False else None
        lst = self.streams[eng]
        o.idx = len(lst)
        lst.append(o)
        ev = ("c", eng, o.idx)
        self._commit(ev, reads, writes, nowaw)
        return ev

    def dma(self, q, out, in_, reads=(), writes=(), nowaw=False, is_output=False, _emit=None, **kw):
        o = Op()
        stream = QENG[q]
        o.stream = stream
        k = self.dnext[q]
        self.dnext[q] = (k + 1) % NSEM_DMA
        self.dcount[q][k] += 1
        val = 16 * self.dcount[q][k]
        ev = ("d", q, k, val)
        o.deps = self._collect(reads, writes, nowaw)
        if self.dcount[q][k] > 1:
            o.deps.append(("d", q, k, val - 16))
        o.dma = (q, k, val)
        o.signal = False
        o.emit = _emit if _emit is not None else (lambda e: e.dma_start(out=out, in_=in_, **kw))
        lst = self.streams[stream]
        o.idx = len(lst)
        lst.append(o)
        self._commit(ev, reads, writes, nowaw)
        if is_output:
            self.out_events.append(ev)
        return ev

    def coll(self, kind, op, groups, src, dst):
        return self.dma("poolq", None, None, reads=[src.b], writes=[dst.b],
                        _emit=lambda e: e.collective_compute(kind, op, replica_groups=groups, ins=[src.t], outs=[dst.t]))

    def barrier(self):
        evs = []
        for s in COMPUTE:
            if self.streams[s]:
                idx = None
                for o in reversed(self.streams[s]):
                    if o.emit is not None and o.dma is None:
                        idx = o.idx
                        break
                if idx is not None:
                    evs.append(("c", s, idx))
        for q in DMAQ:
            for k in range(NSEM_DMA):
                if self.dcount[q][k] > 0:
                    evs.append(("d", q, k, 16 * self.dcount[q][k]))
        for s in ("pe", "act", "dve", "pool", "sp"):
            o = Op()
            o.stream = s; o.emit = None; o.deps = list(evs); o.signal = False; o.dma = None
            o.idx = len(self.streams[s])
            self.streams[s].append(o)

    def finalize(self):
        nc = self.nc
        fin = Op()
        fin.stream = "sp"; fin.emit = None; fin.deps = list(self.out_events); fin.signal = False; fin.dma = None
        fin.idx = len(self.streams["sp"])
        self.streams["sp"].append(fin)
        for s, lst in self.streams.items():
            for o in lst:
                for d in o.deps:
                    if d[0] == "c":
                        if d[1] == s and (s == "pe" or not SAME_ENGINE_SYNC):
                            continue
                        self.streams[d[1]][d[2]].signal = True
        sigcount = {}
        for s in COMPUTE:
            c = 0
            arr = []
            for o in self.streams[s]:
                if o.signal:
                    c += 1
                arr.append(c)
            sigcount[s] = arr
        block = self.es.enter_context(nc.Block())
        engs = {"pe": block.tensor, "act": block.scalar, "dve": block.vector, "pool": block.gpsimd, "sp": block.sync}

        def make(sname):
            lst = self.streams[sname]

            def body(e):
                waited_c = {s: 0 for s in COMPUTE}
                waited_d = {}
                for o in lst:
                    for d in o.deps:
                        if d[0] == "c":
                            if d[1] == sname and (sname == "pe" or not SAME_ENGINE_SYNC):
                                continue
                            v = sigcount[d[1]][d[2]]
                            if v > waited_c[d[1]]:
                                e.wait_ge(self.sems[d[1]], v)
                                waited_c[d[1]] = v
                        else:
                            key = (d[1], d[2])
                            if d[3] > waited_d.get(key, 0):
                                e.wait_ge(self.dsems[d[1]][d[2]], d[3])
                                waited_d[key] = d[3]
                    if o.emit is None:
                        continue
                    ins = o.emit(e)
                    if o.dma is not None:
                        q, k, val = o.dma
                        ins.then_inc(self.dsems[q][k], 16)
                    elif o.signal:
                        ins.then_inc(self.sems[sname], 1)
            return body

        for sname in ("sp", "pe", "act", "dve", "pool"):
            engs[sname](make(sname))


class TB:
    def __init__(self, t, name):
        self.t = t
        self.b = Buf(name)

    def __getitem__(self, k):
        return self.t[k]


class Ctx:
    def __init__(self, name):
        self.es = ExitStack()
        self.nc = bass.Bass("TRN2", target_bir_lowering=False)
        self.S = Sched(self.nc, self.es)
        self.n = 0
        self.cur = self.es
        self.prefix = ""
        self.alias = {}

    def phase(self, prefix, alias=None):
        ctx = self

        class _P:
            def __enter__(self_p):
                self_p.old = (ctx.cur, ctx.prefix, ctx.alias)
                self_p.st = ExitStack()
                ctx.cur, ctx.prefix, ctx.alias = self_p.st, prefix, dict(alias or {})
                return ctx

            def __exit__(self_p, *a):
                self_p.st.close()
                ctx.cur, ctx.prefix, ctx.alias = self_p.old
                return False
        return _P()

    def sb(self, shape, dt, name=None):
        self.n += 1
        name = self.prefix + (name or f"sb{self.n}")
        return TB(self.cur.enter_context(self.nc.sbuf_tensor(name, list(shape), dt)), name)

    def ps(self, shape, dt, name=None):
        self.n += 1
        name = self.prefix + (name or f"ps{self.n}")
        return TB(self.cur.enter_context(self.nc.psum_tensor(name, list(shape), dt)), name)

    def din(self, name, shape, dt):
        if name in self.alias:
            return self.alias[name]
        name = self.prefix + name
        return TB(self.nc.dram_tensor(name, list(shape), dt, kind="ExternalInput").ap(), name)

    def dout(self, name, shape, dt):
        if name in self.alias:
            return self.alias[name]
        name = self.prefix + name
        return TB(self.nc.dram_tensor(name, list(shape), dt, kind="ExternalOutput").ap(), name)

    def dscr(self, name, shape, dt):
        name = self.prefix + name
        return TB(self.nc.dram_tensor(name, list(shape), dt, kind="Internal").ap(), name)

    def finish(self):
        self.S.finalize()
        self.es.close()
        return self.nc


def bcast_rows(ap1d_tb, n, parts=128):
    a = ap1d_tb.t
    return bass.AP(tensor=a.tensor, offset=a.offset, ap=[[0, parts], [1, n]])


D = 1024
NT = 16
TL = NT * 128
EPS = 1e-6
INW = 2072
FMW = 1536
TMW = 536


def build_A(C=None):
    own = C is None
    C = C or Ctx("A"); S = C.S
    x = C.din("x", [TL, D], F32)
    gpre = C.din("gpre", [128, 8], F32)
    w = C.din("w", [D, INW], F32)
    ident = C.din("ident", [128, 128], F32)
    sgug = C.din("sgug", [256], F32)
    sgub = C.din("sgub", [256], F32)
    sguwT = C.din("sguwT", [4, 128, 128], F32)
    tril = C.din("tril", [128, 128], F32)
    sgubs = C.din("sgubs", [512], F32)
    qT = C.dout("qT", [512, TL], BF16)
    kvT = C.dout("kvT", [512, TL], BF16)
    pinT = C.dout("pinT", [256, TL], F32)
    vtm = C.dout("vtm", [TL, 256], BF16)
    gate = C.dout("gate", [TL, 24], F32)
    ocT = C.dout("ocT", [256, TL], BF16)

    Wb = C.sb([128, 8, INW], BF16, "Wb")
    stage = [C.sb([128, INW], F32, f"stage{i}") for i in range(2)]
    g_sb = C.sb([128, 8], F32, "g_sb")
    id_b = C.sb([128, 128], BF16, "id_b")
    sg_g = C.sb([128, 256], F32, "sg_g")
    sg_b = C.sb([128, 256], F32, "sg_b")
    bsb = C.sb([128, 4, 128], F32, "bsb")
    wst_f = C.sb([128, 4, 128], F32, "wst_f")
    tril_s = C.sb([128, 128], F32, "tril_s")
    wsT = C.sb([128, 4, 128], BF16, "wsT")

    eps_c = C.sb([128, 1], F32, "eps_c")
    S.op("dve", lambda e: e.memset(eps_c[:, :], EPS), writes=[eps_c.b])
    S.dma("sp", g_sb[:, :], gpre[:, :], writes=[g_sb.b])
    S.dma("poolq", id_b[:, :], ident[:, :], writes=[id_b.b])
    S.dma("sp", sg_g[:, :], bcast_rows(sgug, 256), writes=[sg_g.b])
    S.dma("sp", sg_b[:, :], bcast_rows(sgub, 256), writes=[sg_b.b])
    S.dma("sp", bsb[:, :, :].rearrange("p h t -> p (h t)"), bcast_rows(sgubs, 512), writes=[bsb.b])
    S.dma("sp", wst_f[:, :, :], sguwT[:, :, :].rearrange("h s t -> s h t"), writes=[wst_f.b])
    S.dma("sp", tril_s[:, :], tril[:, :], writes=[tril_s.b])
    for h in range(4):
        S.op("dve", lambda e, h=h: e.tensor_tensor(out=wsT[:, h, :], in0=wst_f[:, h, :], in1=tril_s[:, :], op=ALU.mult),
             reads=[wst_f.b, tril_s.b], writes=[wsT.b], nowaw=True)
    for kc in range(8):
        st = stage[kc % 2]
        S.dma("sp", st[:, :], w[kc * 128:(kc + 1) * 128, :], writes=[st.b])
        S.op("dve", lambda e, kc=kc, st=st: e.tensor_scalar(out=Wb[:, kc, :], in0=st[:, :], scalar1=g_sb[:, kc:kc + 1],
                                                           scalar2=None, op0=ALU.mult),
             reads=[st.b, g_sb.b], writes=[Wb.b], nowaw=True)

    xt = [C.sb([128, D], F32, f"xt{i}") for i in range(2)]
    sq = C.sb([128, D], BF16, "sq")
    ss = [C.sb([128, 1], F32, f"ss{i}") for i in range(2)]
    rs = [C.sb([128, 1], F32, f"rs{i}") for i in range(2)]
    hn = [C.sb([128, D], BF16, f"hn{i}") for i in range(2)]
    hT = [C.sb([128, 8, 512], BF16, f"hT{i}") for i in range(2)]
    psT = [C.ps([128, 8, 128], BF16, f"psT{i}") for i in range(2)]
    psF = [C.ps([128, 512], F32, f"psF{i}") for i in range(3)]
    psA = C.ps([128, 512], F32, "psA")
    psB = C.ps([128, 512], F32, "psB")
    psC = C.ps([128, 512], F32, "psC")
    fm_b = [C.sb([128, 512], BF16, f"fm_b{i}") for i in range(3)]
    fm_f = [C.sb([128, 512], F32, f"fm_f{i}") for i in range(2)]
    gu = [C.sb([128, 2, 512], F32, f"gu{i}") for i in range(2)]
    vt_b = [C.sb([128, 256], BF16, f"vt_b{i}") for i in range(2)]
    gt_f = [C.sb([128, 24], F32, f"gt_f{i}") for i in range(2)]
    vg = [C.sb([128, 256], F32, f"vg{i}") for i in range(2)]
    vc_ = [C.sb([128, 256], F32, f"vc{i}") for i in range(2)]
    vsq = C.sb([128, 256], F32, "vsq")
    st1 = [C.sb([128, 4], F32, f"st1{i}") for i in range(2)]
    vln = [C.sb([128, 256], BF16, f"vln{i}") for i in range(2)]
    tmp = [C.sb([128, 2, 128], F32, f"tmp{i}") for i in range(2)]
    oc = [C.sb([128, 2, 512], BF16, f"oc{i}") for i in range(2)]

    nfm = 0
    for grp in range(NT // 4):
        hTg = hT[grp % 2]
        gug = gu[grp % 2]
        ocg = oc[grp % 2]
        for tt in range(4):
            t = grp * 4 + tt
            xx = xt[t % 2]; s_ = ss[t % 2]; r_ = rs[t % 2]; hh = hn[t % 2]; pT = psT[t % 2]
            S.dma("sp", xx[:, :], x[t * 128:(t + 1) * 128, :], writes=[xx.b])
            S.op("act", lambda e, xx=xx, s_=s_: e.activation(out=sq[:, :], in_=xx[:, :], func=AF.Square, accum_out=s_[:, :]),
                 reads=[xx.b], writes=[sq.b, s_.b])
            S.op("act", lambda e, s_=s_, r_=r_: e.activation(out=r_[:, :], in_=s_[:, :], func=AF.Sqrt, scale=1.0 / D, bias=eps_c[:, :]),
                 reads=[s_.b, eps_c.b], writes=[r_.b])
            S.op("dve", lambda e, r_=r_: e.reciprocal(out=r_[:, :], in_=r_[:, :]), reads=[r_.b], writes=[r_.b])
            S.op("dve", lambda e, xx=xx, r_=r_, hh=hh: e.tensor_scalar(out=hh[:, :], in0=xx[:, :], scalar1=r_[:, :], scalar2=None,
                                                                     op0=ALU.mult), reads=[xx.b, r_.b], writes=[hh.b])
            for kc in range(8):
                S.op("pe", lambda e, kc=kc, hh=hh, pT=pT: e.transpose(out=pT[:, kc, :], in_=hh[:, kc * 128:(kc + 1) * 128], identity=id_b[:, :]),
                     reads=[hh.b, id_b.b], writes=[pT.b], nowaw=(kc > 0))
            S.op("dve", lambda e, pT=pT, tt=tt, hTg=hTg: e.tensor_copy(out=hTg[:, :, tt * 128:(tt + 1) * 128], in_=pT[:, :, :]),
                 reads=[pT.b], writes=[hTg.b], nowaw=True)
        for ocn in range(12):
            pf = psF[nfm % 3]
            for kc in range(8):
                S.op("pe", lambda e, pf=pf, kc=kc, ocn=ocn, hTg=hTg: e.matmul(pf[:, :], lhsT=Wb[:, kc, ocn * 128:(ocn + 1) * 128], rhs=hTg[:, kc, :],
                                                                          start=(kc == 0), stop=(kc == 7)),
                     reads=[Wb.b, hTg.b], writes=[pf.b])
            cols = slice(grp * 512, (grp + 1) * 512)
            if ocn < 4:
                fb = fm_b[nfm % 3]
                S.op("act", lambda e, pf=pf, fb=fb: e.activation(out=fb[:, :], in_=pf[:, :], func=AF.Copy, scale=0.125),
                     reads=[pf.b], writes=[fb.b])
                S.dma("poolq", qT[ocn * 128:(ocn + 1) * 128, cols], fb[:, :], reads=[fb.b], writes=[qT.b], nowaw=True, is_output=True)
            elif ocn < 8:
                fb = fm_b[nfm % 3]
                S.op("dve", lambda e, pf=pf, fb=fb: e.tensor_copy(out=fb[:, :], in_=pf[:, :]), reads=[pf.b], writes=[fb.b])
                S.dma("poolq", kvT[(ocn - 4) * 128:(ocn - 3) * 128, cols], fb[:, :], reads=[fb.b], writes=[kvT.b], nowaw=True, is_output=True)
            elif ocn < 10:
                ff = fm_f[nfm % 2]
                S.op("act", lambda e, pf=pf, ff=ff: e.activation(out=ff[:, :], in_=pf[:, :], func=AF.Copy),
                     reads=[pf.b], writes=[ff.b])
                S.dma("poolq", pinT[(ocn - 8) * 128:(ocn - 7) * 128, cols], ff[:, :], reads=[ff.b], writes=[pinT.b], nowaw=True, is_output=True)
            else:
                j = ocn - 10
                S.op("act", lambda e, pf=pf, j=j, gug=gug: e.activation(out=gug[:, j, :], in_=pf[:, :], func=AF.Gelu_apprx_tanh),
                     reads=[pf.b], writes=[gug.b], nowaw=(j > 0))
            nfm += 1
        for tt in range(4):
            t = grp * 4 + tt
            for kc in range(8):
                S.op("pe", lambda e, kc=kc, tt=tt, hTg=hTg: e.matmul(psA[:, :], lhsT=hTg[:, kc, tt * 128:(tt + 1) * 128], rhs=Wb[:, kc, FMW:FMW + 512],
                                                                  start=(kc == 0), stop=(kc == 7)),
                     reads=[Wb.b, hTg.b], writes=[psA.b])
            for kc in range(8):
                S.op("pe", lambda e, kc=kc, tt=tt, hTg=hTg: e.matmul(psB[:, 0:24], lhsT=hTg[:, kc, tt * 128:(tt + 1) * 128], rhs=Wb[:, kc, FMW + 512:FMW + 536],
                                                                  start=(kc == 0), stop=(kc == 7)),
                     reads=[Wb.b, hTg.b], writes=[psB.b])
            vb = vt_b[t % 2]; gf = gt_f[t % 2]; vg_ = vg[t % 2]; vcc = vc_[t % 2]; s1 = st1[t % 2]; vl = vln[t % 2]; tm = tmp[t % 2]
            S.op("dve", lambda e, vb=vb: e.tensor_copy(out=vb[:, :], in_=psA[:, 0:256]), reads=[psA.b], writes=[vb.b])
            S.dma("poolq", vtm[t * 128:(t + 1) * 128, :], vb[:, :], reads=[vb.b], writes=[vtm.b], nowaw=True, is_output=True)
            S.op("act", lambda e, gf=gf: e.activation(out=gf[:, :], in_=psB[:, 0:24], func=AF.Sigmoid), reads=[psB.b], writes=[gf.b])
            S.dma("poolq", gate[t * 128:(t + 1) * 128, :], gf[:, :], reads=[gf.b], writes=[gate.b], nowaw=True, is_output=True)
            S.op("act", lambda e, vg_=vg_, s1=s1: e.activation(out=vg_[:, :], in_=psA[:, 256:512], func=AF.Gelu_apprx_tanh, accum_out=s1[:, 0:1]),
                 reads=[psA.b], writes=[vg_.b, s1.b])
            S.op("dve", lambda e, s1=s1: e.tensor_scalar(out=s1[:, 1:2], in0=s1[:, 0:1], scalar1=-1.0 / 256, scalar2=None, op0=ALU.mult),
                 reads=[s1.b], writes=[s1.b])
            S.op("dve", lambda e, vg_=vg_, vcc=vcc, s1=s1: e.tensor_scalar(out=vcc[:, :], in0=vg_[:, :], scalar1=s1[:, 1:2], scalar2=None, op0=ALU.add),
                 reads=[vg_.b, s1.b], writes=[vcc.b])
            S.op("act", lambda e, vcc=vcc, s1=s1: e.activation(out=vsq[:, :], in_=vcc[:, :], func=AF.Square, accum_out=s1[:, 2:3]),
                 reads=[vcc.b], writes=[vsq.b, s1.b])
            S.op("act", lambda e, s1=s1: e.activation(out=s1[:, 3:4], in_=s1[:, 2:3], func=AF.Sqrt, scale=1.0 / 256, bias=eps_c[:, :]),
                 reads=[s1.b, eps_c.b], writes=[s1.b])
            S.op("dve", lambda e, s1=s1: e.reciprocal(out=s1[:, 3:4], in_=s1[:, 3:4]), reads=[s1.b], writes=[s1.b])
            S.op("dve", lambda e, vcc=vcc, s1=s1: e.tensor_scalar(out=vcc[:, :], in0=vcc[:, :], scalar1=s1[:, 3:4], scalar2=None, op0=ALU.mult),
                 reads=[vcc.b, s1.b], writes=[vcc.b])
            S.op("dve", lambda e, vcc=vcc: e.tensor_tensor(out=vcc[:, :], in0=vcc[:, :], in1=sg_g[:, :], op=ALU.mult),
                 reads=[vcc.b, sg_g.b], writes=[vcc.b])
            S.op("dve", lambda e, vcc=vcc, vl=vl: e.tensor_tensor(out=vl[:, :], in0=vcc[:, :], in1=sg_b[:, :], op=ALU.add),
                 reads=[vcc.b, sg_b.b], writes=[vl.b])
            for h in range(4):
                j = h // 2
                S.op("pe", lambda e, h=h, j=j, vl=vl: e.matmul(psC[:, h * 128:(h + 1) * 128], lhsT=vl[:, j * 128:(j + 1) * 128], rhs=wsT[:, h, :],
                                                            start=True, stop=True),
                     reads=[vl.b, wsT.b], writes=[psC.b], nowaw=(h > 0))
            for h in range(4):
                j = h // 2; r = h % 2
                P = slice(r * 64, r * 64 + 64)
                S.op("dve", lambda e, h=h, j=j, P=P, tm=tm: e.tensor_tensor(out=tm[P, j, :], in0=psC[P, h * 128:(h + 1) * 128], in1=bsb[P, h, :], op=ALU.add),
                     reads=[psC.b, bsb.b], writes=[tm.b], nowaw=True)
                S.op("dve", lambda e, j=j, P=P, tm=tm, tt=tt, gug=gug, ocg=ocg: e.tensor_tensor(out=ocg[P, j, tt * 128:(tt + 1) * 128], in0=tm[P, j, :],
                                                                                           in1=gug[P, j, tt * 128:(tt + 1) * 128], op=ALU.mult),
                     reads=[tm.b, gug.b], writes=[ocg.b], nowaw=True)
        for j in range(2):
            S.dma("poolq", ocT[j * 128:(j + 1) * 128, grp * 512:(grp + 1) * 512], ocg[:, j, :], reads=[ocg.b], writes=[ocT.b], nowaw=True, is_output=True)
    return C.finish() if own else None


_SPL = np.cumsum([0, 512, 128, 128, 128, 128, 128, 128, 24, 256, 256, 256])
_NAMES = ["q", "kc", "vc", "ksl", "vsl", "kw", "vw", "gt", "pin", "u", "v"]
_COL = {n: np.arange(_SPL[i], _SPL[i + 1]) for i, n in enumerate(_NAMES)}
_PERM = np.concatenate([_COL[n] for n in ["q", "kc", "vc", "ksl", "kw", "pin", "u", "vsl", "vw", "v", "gt"]])
_IDENT = np.eye(128, dtype=np.float32)
_TRIL_ST = np.triu(np.ones((128, 128), np.float32))


def local_rows(a, core):
    b, cp = divmod(core, 4)
    t = a[b].reshape((64, 128) + a.shape[2:])[cp::4]
    return np.ascontiguousarray(t.reshape((TL,) + a.shape[2:]))


def gcol(g):
    return np.ascontiguousarray(g.reshape(-1, 128).T)


def prep_A(inp, l, x, core):
    return {
        "x": local_rows(x, core),
        "gpre": gcol(inp["mix_norm_pre"][l]),
        "w": np.ascontiguousarray(inp["w_in"][l][:, _PERM]),
        "ident": _IDENT,
        "sgug": np.ascontiguousarray(inp["sgu_norm_g"][l]),
        "sgub": np.ascontiguousarray(inp["sgu_norm_b"][l]),
        "sguwT": np.ascontiguousarray(np.transpose(inp["sgu_w"][l], (0, 2, 1))),
        "tril": _TRIL_ST,
        "sgubs": np.ascontiguousarray(inp["sgu_b"][l].reshape(-1)),
    }


NEG = 30000.0


def build_B1(ST=4, NTL=NT, C=None):
    own = C is None
    C = C or Ctx("B1"); S = C.S
    qT_in = C.din("qT_in", [128, 4, NTL * 128], BF16)
    gate_in = C.din("gate_in", [NTL * 128, 24], F32)
    kc_in = C.din("kc_in", [64, 2, 8192], BF16)
    vc_in = C.din("vc_in", [64, 2, 8192], BF16)
    ksl_in = C.din("ksl_in", [128, 8192], BF16)
    vsl_in = C.din("vsl_in", [128, 64, 2, 64], BF16)
    kw_in = C.din("kw_in", [128, NTL, 640], BF16)
    vw_in = C.din("vw_in", [128, NTL, 5, 2, 64], BF16)
    pin_in = C.din("pin_in", [128, 2, NTL, 144], F32)
    oc_in = C.din("oc_in", [128, 2, NTL * 128], BF16)
    x_in = C.din("x_in", [NTL * 128, D], F32)
    relb = C.din("relb", [32, 8], F32)
    ohs = C.din("ohs", [33, 1536], F32)
    ohw = C.din("ohw", [33, 1280], F32)
    ohw0 = C.din("ohw0", [33, 1280], F32)
    ohc = C.din("ohc", [33, 88, 128], F32)
    expand = C.din("expand", [128, 8192], F32)
    vfc = C.din("vfc", [128, 2, 9], F32)
    ident = C.din("ident", [128, 128], F32)
    invc0 = C.din("invc0", [128, 2, 128], F32)
    cpos = C.din("cpos", [64, 2, 32], F32)
    cw1 = C.din("cw1", [64, 2, 32, 128], F32)
    cb1 = C.din("cb1", [128, 2], F32)
    cw2 = C.din("cw2", [128, 2, 64], F32)
    cb2 = C.din("cb2", [2, 64], F32)
    poolw = C.din("poolw", [4, 64, 64], F32)
    pools = C.din("pools", [128, 2], F32)
    wo = C.din("wo", [D, D], F32)
    gpost = C.din("gpost", [D], F32)
    x_out = C.dout("x_out", [NTL * 128, D], F32)
    ebs_d = C.dscr("ebs_d", [8, 1536], BF16)
    ebw_d = C.dscr("ebw_d", [8, 1280], BF16)
    ebw0_d = C.dscr("ebw0_d", [8, 1280], BF16)

    rot = [C.ps([128, 512], F32, f"rot{i}") for i in range(4)]
    acc = [C.ps([128, 512], F32, f"acc{i}") for i in range(2)]
    psT = C.ps([128, 8, 128], BF16, "psT")
    psM = C.ps([128, 512], F32, "psM")
    rc = [0]

    def nrot():
        rc[0] += 1
        return rot[rc[0] % 4]

    eps_c = C.sb([128, 1], F32, "eps_c")
    S.op("dve", lambda e: e.memset(eps_c[:, :], EPS), writes=[eps_c.b])
    id_f = C.sb([128, 128], F32, "id_f")
    id_b = C.sb([128, 128], BF16, "id_b")
    S.dma("sp", id_f[:, :], ident[:, :], writes=[id_f.b])
    S.dma("poolq", id_b[:, :], ident[:, :], writes=[id_b.b])
    rb = C.sb([33, 8], F32, "rb")
    rbx = C.sb([33, 8], F32, "rbx")
    b31 = C.sb([32, 8], F32, "b31")
    S.dma("sp", rb[0:32, :], relb[:, :], writes=[rb.b])
    S.op("dve", lambda e: e.memset(rb[32:33, :], -NEG), writes=[rb.b], nowaw=True)
    S.dma("sp", b31[:, :], bass.AP(tensor=relb.t.tensor, offset=31 * 8, ap=[[0, 32], [1, 8]]), writes=[b31.b])
    S.op("dve", lambda e: e.tensor_tensor(out=rbx[0:32, :], in0=rb[0:32, :], in1=b31[:, :], op=ALU.subtract), reads=[rb.b, b31.b], writes=[rbx.b])
    S.op("dve", lambda e: e.memset(rbx[32:33, :], -NEG), writes=[rbx.b], nowaw=True)

    Tsel = C.sb([128, 11, 8, 128], BF16, "Tsel")
    Twin = C.sb([128, 5, 8, 128], BF16, "Twin")
    Tband = C.sb([128, 88, 8], F32, "Tband")
    oh_sb = C.sb([33, 512], F32, "oh_sb")
    eb_sb = C.sb([8, 1536], BF16, "eb_sb")

    def build_row(oh_d, L, lhs, scr):
        for j in range(0, L, 512):
            wd = min(512, L - j)
            S.dma("sp", oh_sb[:, 0:wd], oh_d[:, j:j + wd], writes=[oh_sb.b])
            p = nrot()
            S.op("pe", lambda e, p=p, j=j, wd=wd: e.matmul(p[0:8, 0:wd], lhsT=lhs[:, :], rhs=oh_sb[:, 0:wd], start=True, stop=True),
                 reads=[lhs.b, oh_sb.b], writes=[p.b])
            S.op("act", lambda e, p=p, j=j, wd=wd: e.activation(out=eb_sb[:, j:j + wd], in_=p[0:8, 0:wd], func=AF.Exp),
                 reads=[p.b], writes=[eb_sb.b], nowaw=True)
        S.dma("sp", scr[:, :], eb_sb[:, 0:L], reads=[eb_sb.b], writes=[scr.b])

    def toep(dst, ntile, scr, L, step):
        for i in range(ntile):
            S.dma("sp", dst[:, i, :, :], bass.AP(tensor=scr.t.tensor, offset=step * i, ap=[[1, 128], [L, 8], [1, 128]]),
                  reads=[scr.b], writes=[dst.b], nowaw=True)

    build_row(ohs, 1536, rbx, ebs_d)
    toep(Tsel, 11, ebs_d, 1536, 128)
    build_row(ohw0, 1280, rbx, ebw0_d)
    toep(Twin, 5, ebw0_d, 1280, 256)
    build_row(ohw, 1280, rbx, ebw_d)
    ohc_sb = [C.sb([33, 4, 128], F32, f"ohc_sb{i}") for i in range(2)]
    pb = [nrot(), nrot()]
    for ch in range(22):
        o_ = ohc_sb[ch % 2]
        S.dma("sp", o_[:, :, :], ohc[:, ch * 4:(ch + 1) * 4, :], writes=[o_.b])
        for k in range(4):
            c3 = ch * 4 + k
            p = pb[c3 // 64]
            S.op("pe", lambda e, p=p, o_=o_, k=k, c3=c3: e.matmul(p[:, (c3 % 64) * 8:(c3 % 64) * 8 + 8], lhsT=o_[:, k, :], rhs=rbx[:, :], start=True, stop=True),
                 reads=[o_.b, rbx.b], writes=[p.b], nowaw=True)
    S.op("act", lambda e: e.activation(out=Tband[:, 0:64, :].rearrange("p c h -> p (c h)"), in_=pb[0][:, 0:512], func=AF.Exp),
         reads=[pb[0].b], writes=[Tband.b])
    S.op("act", lambda e: e.activation(out=Tband[:, 64:88, :].rearrange("p c h -> p (c h)"), in_=pb[1][:, 0:192], func=AF.Exp),
         reads=[pb[1].b], writes=[Tband.b], nowaw=True)

    k_sb = C.sb([128, 8192], BF16, "k_sb")
    v_sb = C.sb([128, 64, 2, 65], BF16, "v_sb")
    S.op("pool", lambda e: e.memset(v_sb[:, :, :, 64:65], 1.0), writes=[v_sb.b])
    for hf in range(4):
        S.dma("sp", k_sb[:, hf * 2048:(hf + 1) * 2048], ksl_in[:, hf * 2048:(hf + 1) * 2048], writes=[k_sb.b], nowaw=True)
    for q4 in range(4):
        S.dma("sp", v_sb[:, q4 * 16:(q4 + 1) * 16, :, 0:64], vsl_in[:, q4 * 16:(q4 + 1) * 16, :, :], writes=[v_sb.b], nowaw=True)
    exp_sb = C.sb([128, 64, 128], BF16, "exp_sb")
    for q4 in range(4):
        S.dma("poolq", exp_sb[:, q4 * 16:(q4 + 1) * 16, :].rearrange("p a b -> p (a b)"), expand[:, q4 * 2048:(q4 + 1) * 2048], writes=[exp_sb.b], nowaw=True)
    vf_sb = C.sb([128, 2, 9], F32, "vf_sb")
    S.dma("sp", vf_sb[:, :, :], vfc[:, :, :], writes=[vf_sb.b])
    invc_sb = C.sb([128, 2, 128], F32, "invc_sb")
    S.dma("sp", invc_sb[:, :, :], invc0[:, :, :], writes=[invc_sb.b])
    ps_sb = C.sb([128, 2], F32, "ps_sb")
    S.dma("sp", ps_sb[:, :], pools[:, :], writes=[ps_sb.b])
    gp_sb = C.sb([128, D], F32, "gp_sb")
    S.dma("sp", gp_sb[:, :], bcast_rows(gpost, D), writes=[gp_sb.b])
    pwf = C.sb([128, 2, 128], F32, "pwf")
    pwb = C.sb([128, 2, 128], BF16, "pwb")
    S.op("dve", lambda e: e.memset(pwf[:, :, :], 0.0), writes=[pwf.b])
    for gi in range(4):
        r = gi % 2
        S.dma("sp", pwf[r * 64:(r + 1) * 64, gi // 2, r * 64:(r + 1) * 64], poolw[gi, :, :], writes=[pwf.b], nowaw=True)
    S.op("dve", lambda e: e.tensor_copy(out=pwb[:, :, :], in_=pwf[:, :, :]), reads=[pwf.b], writes=[pwb.b])
    wo_sb = C.sb([128, 8, D], BF16, "wo_sb")
    for kc in range(8):
        S.dma("poolq", wo_sb[:, kc, :], wo[kc * 128:(kc + 1) * 128, :], writes=[wo_sb.b], nowaw=True)

    w1_sb = C.sb([64, 32, 128], BF16, "w1_sb")
    pos_sb = C.sb([64, 2, 32], BF16, "pos_sb")
    w2_sb = C.sb([128, 2, 64], BF16, "w2_sb")
    b1_sb = C.sb([128, 2], F32, "b1_sb")
    b2c = C.sb([128, 1], F32, "b2c")
    w2f = C.sb([128, 64], F32, "w2f")
    w2pad = C.sb([128, 2, 128], BF16, "w2pad")
    b2bc = C.sb([128, 64], F32, "b2bc")
    bias_c = C.sb([128, 2], F32, "bias_c")
    S.dma("poolq", pos_sb[:, :, :], cpos[:, :, :], writes=[pos_sb.b])
    S.dma("poolq", w2_sb[:, :, :], cw2[:, :, :], writes=[w2_sb.b])
    S.dma("sp", b1_sb[:, :], cb1[:, :], writes=[b1_sb.b])
    for g in range(2):
        S.dma("sp", b2c[g * 64:(g + 1) * 64, :], bass.AP(tensor=cb2.t.tensor, offset=0, ap=[[1, 64], [1, 1]]), writes=[b2c.b], nowaw=True)
    S.dma("sp", w2f[:, :], cw2[:, 0, :], writes=[w2f.b])
    S.op("dve", lambda e: e.memset(w2pad[:, :, :], 0.0), writes=[w2pad.b])
    for g in range(2):
        S.op("dve", lambda e, g=g: e.tensor_copy(out=w2pad[:, g, g * 64:(g + 1) * 64], in_=w2f[:, :]), reads=[w2f.b], writes=[w2pad.b], nowaw=True)
    S.dma("sp", b2bc[:, :], bass.AP(tensor=cb2.t.tensor, offset=64, ap=[[0, 128], [1, 64]]), writes=[b2bc.b])
    kcmp = C.sb([128, 512], BF16, "kcmp")
    vcmp = C.sb([128, 4, 2, 64], BF16, "vcmp")
    cst = [C.sb([64, 4112], BF16, "cst0")]
    hTs = [C.sb([128, 512], BF16, f"hTs{i}") for i in range(2)]
    for i in range(2):
        S.op("dve", lambda e, i=i: e.memset(hTs[i][:, :], 0.0), writes=[hTs[i].b])
    ci = 0
    for a in range(2):
        S.dma("poolq", w1_sb[:, :, :], cw1[:, a, :, :], writes=[w1_sb.b])
        p = nrot()
        for l in range(32):
            S.op("pe", lambda e, p=p, a=a, l=l: e.matmul(p[:, 0:1], lhsT=w1_sb[:, l, :], rhs=pos_sb[:, a, l:l + 1], start=(l == 0), stop=(l == 31)),
                 reads=[w1_sb.b, pos_sb.b], writes=[p.b])
        S.op("dve", lambda e, p=p, a=a: e.tensor_tensor(out=bias_c[:, a:a + 1], in0=p[:, 0:1], in1=b1_sb[:, a:a + 1], op=ALU.add),
             reads=[p.b, b1_sb.b], writes=[bias_c.b], nowaw=True)
        src = kc_in if a == 0 else vc_in
        for g in range(2):
            st = cst[0]; hT_ = hTs[ci % 2]; ci += 1
            p = nrot()
            for hf in range(2):
                ntok = 4112 if hf == 0 else 4096
                ncb = 256 if hf == 0 else 255
                S.dma("sp", st[:, 0:ntok], src[:, g, hf * 4096:hf * 4096 + ntok], writes=[st.b])
                for l in range(32):
                    S.op("pe", lambda e, p=p, a=a, l=l, st=st, hf=hf, ncb=ncb: e.matmul(p[:, hf * 256:hf * 256 + ncb], lhsT=w1_sb[:, l, :], rhs=st[:, l:l + 16 * (ncb - 1) + 1:16],
                                                                                 start=(l == 0), stop=(l == 31)),
                         reads=[w1_sb.b, st.b], writes=[p.b], nowaw=(hf > 0))
            S.op("act", lambda e, p=p, a=a, hT_=hT_: e.activation(out=hT_[:, 0:511], in_=p[:, 0:511], func=AF.Gelu_apprx_tanh, bias=bias_c[:, a:a + 1]),
                 reads=[p.b, bias_c.b], writes=[hT_.b])
            if a == 0:
                p2 = nrot()
                S.op("pe", lambda e, p2=p2, hT_=hT_, g=g: e.matmul(p2[:, 0:512], lhsT=w2pad[:, g, :], rhs=hT_[:, :], start=True, stop=True),
                     reads=[w2pad.b, hT_.b], writes=[p2.b])
                S.op("act", lambda e, p2=p2, g=g: e.activation(out=kcmp[g * 64:(g + 1) * 64, :], in_=p2[g * 64:(g + 1) * 64, 0:512], func=AF.Identity, bias=b2c[g * 64:(g + 1) * 64, 0:1]),
                     reads=[p2.b, b2c.b], writes=[kcmp.b], nowaw=True)
            else:
                p2 = nrot()
                for ct in range(4):
                    S.op("pe", lambda e, p2=p2, hT_=hT_, ct=ct: e.matmul(p2[:, ct * 64:(ct + 1) * 64], lhsT=hT_[:, ct * 128:(ct + 1) * 128], rhs=w2_sb[:, 1, :], start=True, stop=True),
                         reads=[w2_sb.b, hT_.b], writes=[p2.b], nowaw=(ct > 0))
                for ct in range(4):
                    S.op("dve", lambda e, p2=p2, ct=ct, g=g: e.tensor_tensor(out=vcmp[:, ct, g, :], in0=p2[:, ct * 64:(ct + 1) * 64], in1=b2bc[:, :], op=ALU.add),
                         reads=[p2.b, b2bc.b], writes=[vcmp.b], nowaw=True)

    q_sb = [C.sb([128, 4, 128], BF16, f"q_sb{i}") for i in range(2)]
    q64 = Buf("q64")
    kw_sb = [C.sb([128, 640], BF16, f"kw_sb{i}") for i in range(2)]
    vw_sb = [C.sb([128, 5, 2, 65], BF16, f"vw_sb{i}") for i in range(2)]
    kvw64 = Buf("kvw64")
    for i in range(2):
        S.op("pool", lambda e, i=i: e.memset(vw_sb[i][:, :, :, 64:65], 1.0), writes=[kvw64], nowaw=True)
    g_sb = [C.sb([128, 24], F32, f"g_sb{i}") for i in range(2)]
    pin_sb = [C.sb([128, 2, 144], F32, f"pin_sb{i}") for i in range(2)]
    x_sb = [C.sb([128, D], F32, "x_sb0")] * 2
    mfT = [C.sb([128, 8, 128], BF16, f"mfT{i}") for i in range(2)]
    mfT_oc = [Buf(f"mfT_oc{i}") for i in range(2)]
    Ec = [C.sb([128, 4, 512], F32, "Ec0")] * 2
    pnb = [C.sb([128, 4, 512], BF16, "pnb0")] * 2
    impad = [C.sb([128, 520], F32, f"impad{i}") for i in range(2)]
    S.op("pool", lambda e: e.memset(pnb[0][:, :, :], 0.0), writes=[pnb[0].b])
    for i in range(2):
        S.op("pool", lambda e, i=i: e.memset(impad[i][:, :], 0.0), writes=[impad[i].b])
    dc = [C.sb([128, 8], F32, f"dc{i}") for i in range(2)]
    pT_sb = [C.sb([128, 4, 4, 128], BF16, "pT_sb0")] * 2
    sc = [C.sb([128, 128], F32, f"sc{i}") for i in range(2)]
    sc2 = C.sb([128, 128], F32, "sc2")
    t1 = C.sb([128, 128], F32, "t1")
    m8 = [C.sb([128, 16], F32, f"m8{i}") for i in range(2)]
    nsb = [C.sb([128, 128], BF16, f"nsb{i}") for i in range(2)]
    negT = [C.sb([128, 4, 128], BF16, f"negT{i}") for i in range(2)]
    Et = [C.sb([128, 512], BF16, f"Et{i}") for i in range(4)]
    ec = [0]
    oT_sb = [C.sb([65, 512], F32, f"oT_sb{i}") for i in range(2)]
    mix = [C.sb([128, 512], F32, f"mix{i}") for i in range(2)]
    mixb = [C.sb([128, 512], BF16, f"mixb{i}") for i in range(2)]
    tmpm = [C.sb([128, 4, 64], F32, f"tmpm{i}") for i in range(2)]
    fsc = [C.sb([128, 8], F32, f"fsc{i}") for i in range(2)]
    pl = [C.sb([128, 2, 144], F32, f"pl{i}") for i in range(4)]
    dd = [C.sb([128, 2, 128], BF16, f"dd{i}") for i in range(2)]
    ysq = C.sb([128, 512], BF16, "ysq")
    yss = [C.sb([128, 4], F32, f"yss{i}") for i in range(2)]
    yt = [C.sb([128, D], F32, "yt0")] * 2
    acn = [0]
    pc_ = [0]

    def tile(n):
        k2 = n % 2
        qs = q_sb[k2]; kws = kw_sb[k2]; vws = vw_sb[k2]; gs = g_sb[k2]; pins = pin_sb[k2]; xs = x_sb[k2]; mf = mfT[k2]
        mx = mix[k2]
        S.dma("sp", qs[:, :, :], qT_in[:, :, n * 128:(n + 1) * 128], writes=[qs.b])
        S.dma("sp", kws[:, :], kw_in[:, n, :], writes=[kws.b])
        S.dma("sp", vws[:, :, :, 0:64], vw_in[:, n, :, :, :], writes=[vws.b])
        S.dma("sp", gs[:, :], gate_in[n * 128:(n + 1) * 128, :], writes=[gs.b])
        S.dma("sp", pins[:, :, :], pin_in[:, :, n, :], writes=[pins.b])
        S.dma("sp", xs[:, :], x_in[n * 128:(n + 1) * 128, :], writes=[xs.b])
        S.dma("sp", mf[:, 6:8, :], oc_in[:, :, n * 128:(n + 1) * 128], writes=[mfT_oc[k2]])
        if n == 1:
            toep(Twin, 5, ebw_d, 1280, 256)
        Nc = 8 * ST * (n + 1)
        nct = (Nc + 127) // 128
        lo = max(0, 8 * ST * n - 56)
        c30 = lo - (8 * ST * n - 56)
        def grp(g):
            E_ = Ec[g]; pb_ = pnb[g]; im = impad[g]; d_ = dc[g]; pTs = pT_sb[g]
            gv = gs[:, g * 12:(g + 1) * 12].rearrange("p (h r) -> p h r", r=3)
            for hh in range(4):
                h = 4 * g + hh
                p = nrot()
                S.op("pe", lambda e, p=p, hh=hh, g=g, qs=qs: e.matmul(p[:, 0:Nc], lhsT=qs[g * 64:(g + 1) * 64, hh, :], rhs=kcmp[g * 64:(g + 1) * 64, 0:Nc], start=True, stop=True),
                     reads=[qs.b, q64, kcmp.b], writes=[p.b])
                S.op("act", lambda e, p=p, hh=hh, E_=E_: e.activation(out=E_[:, hh, 0:Nc], in_=p[:, 0:Nc], func=AF.Exp),
                     reads=[p.b], writes=[E_.b], nowaw=(hh > 0))
            S.op("dve", lambda e, E_=E_, g=g: e.tensor_tensor(out=E_[:, :, lo:Nc], in0=E_[:, :, lo:Nc],
                                                           in1=Tband[:, c30:c30 + Nc - lo, 4 * g:4 * g + 4].rearrange("p c h -> p h c"), op=ALU.mult),
                 reads=[E_.b, Tband.b], writes=[E_.b])
            S.op("dve", lambda e, E_=E_, d_=d_: e.tensor_reduce(out=d_[:, 0:4], in_=E_[:, :, 0:Nc], axis=AX.X, op=ALU.add),
                 reads=[E_.b], writes=[d_.b])
            S.op("dve", lambda e, d_=d_: e.tensor_scalar(out=d_[:, 0:4], in0=d_[:, 0:4], scalar1=1e-30, scalar2=None, op0=ALU.max), reads=[d_.b], writes=[d_.b])
            S.op("dve", lambda e, d_=d_: e.reciprocal(out=d_[:, 4:8], in_=d_[:, 0:4]), reads=[d_.b], writes=[d_.b])
            S.op("dve", lambda e, E_=E_, d_=d_: e.tensor_tensor(out=E_[:, :, 0:Nc], in0=E_[:, :, 0:Nc],
                                                             in1=d_[:, 4:8].unsqueeze(2).to_broadcast([128, 4, Nc]), op=ALU.mult),
                 reads=[E_.b, d_.b], writes=[E_.b])
            S.op("dve", lambda e, E_=E_, im=im: e.tensor_reduce(out=im[:, 1:1 + Nc], in_=E_[:, :, 0:Nc].rearrange("p h c -> p c h"), axis=AX.X, op=ALU.add),
                 reads=[E_.b], writes=[im.b])
            S.op("act", lambda e, E_=E_, pb_=pb_: e.activation(out=pb_[:, :, 0:Nc], in_=E_[:, :, 0:Nc], func=AF.Copy),
                 reads=[E_.b], writes=[pb_.b])
            for hh in range(4):
                for ct in range(nct):
                    S.op("pe", lambda e, hh=hh, ct=ct, pb_=pb_: e.transpose(out=psT[:, hh * 2 + ct % 2, :], in_=pb_[:, hh, ct * 128:(ct + 1) * 128], identity=id_b[:, :]),
                         reads=[pb_.b, id_b.b], writes=[psT.b], nowaw=not (hh == 0 and ct % 2 == 0 and ct == 0))
                    if ct % 2 == 1 or ct == nct - 1:
                        c0 = ct - (ct % 2)
                        w_ = ct - c0 + 1
                        S.op("act", lambda e, hh=hh, c0=c0, w_=w_, pTs=pTs: e.activation(out=pTs[:, hh, c0:c0 + w_, :], in_=psT[:, hh * 2:hh * 2 + w_, :], func=AF.Copy),
                             reads=[psT.b], writes=[pTs.b], nowaw=True)
            po = nrot()
            for hh in range(4):
                for ct in range(nct):
                    S.op("pe", lambda e, po=po, hh=hh, ct=ct, pTs=pTs, g=g: e.matmul(po[:, hh * 64:(hh + 1) * 64], lhsT=pTs[:, hh, ct, :], rhs=vcmp[:, ct, g, :],
                                                                              start=(ct == 0), stop=(ct == nct - 1)),
                         reads=[pTs.b, vcmp.b], writes=[po.b], nowaw=not (hh == 0 and ct == 0))
            S.op("dve", lambda e, po=po, mx=mx, g=g, gv=gv: e.tensor_tensor(out=mx[:, g * 256:(g + 1) * 256].rearrange("p (h d) -> p h d", d=64),
                                                                         in0=po[:, 0:256].rearrange("p (h d) -> p h d", d=64),
                                                                         in1=gv[:, :, 0:1].to_broadcast([128, 4, 64]), op=ALU.mult),
                 reads=[po.b, gs.b], writes=[mx.b], nowaw=(g > 0))
            s_ = sc[g]; m_ = m8[g]; ns_ = nsb[g]; nT = negT[g]
            A = im[:, 0:512].rearrange("p (j m) -> p j m", m=4)
            Bv = im[:, 4:516].rearrange("p (j m) -> p j m", m=4)
            S.op("dve", lambda e, A=A: e.tensor_tensor(out=t1[:, :], in0=A[:, :, 1], in1=A[:, :, 2], op=ALU.add), reads=[im.b], writes=[t1.b])
            S.op("dve", lambda e, A=A: e.tensor_tensor(out=t1[:, :], in0=t1[:, :], in1=A[:, :, 3], op=ALU.add), reads=[im.b, t1.b], writes=[t1.b])
            S.op("dve", lambda e, A=A: e.scalar_tensor_tensor(out=t1[:, :], in0=t1[:, :], scalar=2.0, in1=A[:, :, 0], op0=ALU.mult, op1=ALU.add),
                 reads=[im.b, t1.b], writes=[t1.b])
            S.op("dve", lambda e, Bv=Bv, s_=s_: e.tensor_tensor(out=s_[:, :], in0=t1[:, :], in1=Bv[:, :, 0], op=ALU.add), reads=[im.b, t1.b], writes=[s_.b])
            j0 = max(0, 2 * ST * n - 1)
            f0 = j0 - (2 * ST * n - 1)
            wloc = 2 * ST * (n + 1) - j0
            S.op("dve", lambda e, s_=s_: e.tensor_tensor(out=s_[:, j0:j0 + wloc], in0=s_[:, j0:j0 + wloc], in1=vf_sb[:, 0, f0:f0 + wloc], op=ALU.mult),
                 reads=[s_.b, vf_sb.b], writes=[s_.b])
            S.op("dve", lambda e, s_=s_: e.tensor_tensor(out=s_[:, j0:j0 + wloc], in0=s_[:, j0:j0 + wloc], in1=vf_sb[:, 1, f0:f0 + wloc], op=ALU.add),
                 reads=[s_.b, vf_sb.b], writes=[s_.b])
            if 2 * ST * (n + 1) < 128:
                S.op("dve", lambda e, s_=s_: e.memset(s_[:, 2 * ST * (n + 1):128], -1.0), writes=[s_.b], reads=[s_.b])
            S.op("dve", lambda e, s_=s_: e.memset(s_[:, 0:1], 1e9), writes=[s_.b], reads=[s_.b])
            S.op("dve", lambda e, s_=s_, m_=m_: e.max(out=m_[:, 0:8], in_=s_[:, :]), reads=[s_.b], writes=[m_.b])
            S.op("dve", lambda e, s_=s_, m_=m_: e.match_replace(out=sc2[:, :], in_to_replace=m_[:, 0:8], in_values=s_[:, :], imm_value=-2.0),
                 reads=[s_.b, m_.b], writes=[sc2.b])
            S.op("dve", lambda e, m_=m_: e.max(out=m_[:, 8:16], in_=sc2[:, :]), reads=[sc2.b], writes=[m_.b])
            S.op("dve", lambda e, s_=s_, m_=m_, ns_=ns_: e.tensor_scalar(out=ns_[:, :], in0=s_[:, :], scalar1=m_[:, 15:16], scalar2=1.0, op0=ALU.is_ge, op1=ALU.subtract),
                 reads=[s_.b, m_.b], writes=[ns_.b])
            S.op("pe", lambda e, ns_=ns_: e.transpose(out=psT[:, 0, :], in_=ns_[:, :], identity=id_b[:, :]), reads=[ns_.b, id_b.b], writes=[psT.b])
            S.op("dve", lambda e, nT=nT: e.tensor_scalar(out=nT[:, :, :], in0=psT[:, 0:1, :].to_broadcast([128, 4, 128]), scalar1=NEG, scalar2=None, op0=ALU.mult),
                 reads=[psT.b], writes=[nT.b])
            def branch(br):
                a_ = acc[acn[0] % 2]; acn[0] += 1
                nk = ST * (n + 1) if br == 1 else 5
                LA = 2
                staged = {}

                def stage1(kt):
                    p = nrot()
                    if br == 1:
                        S.op("pe", lambda e, p=p, kt=kt: e.matmul(p[:, :], lhsT=k_sb[g * 64:(g + 1) * 64, kt * 128:(kt + 1) * 128],
                                                                   rhs=qs[g * 64:(g + 1) * 64, :, :].rearrange("p h q -> p (h q)"), start=True, stop=False),
                             reads=[k_sb.b, qs.b, q64], writes=[p.b])
                        S.op("pe", lambda e, p=p, kt=kt: e.matmul(p[:, :], lhsT=exp_sb[:, kt, :], rhs=nT[:, :, :].rearrange("p h q -> p (h q)"), start=False, stop=True),
                             reads=[exp_sb.b, nT.b], writes=[p.b])
                        di = ST * n - kt + 3
                        tbl = Tsel[:, di, 4 * g:4 * g + 4, :] if di <= 10 else None
                        tb_b = Tsel.b
                        vl = v_sb[:, kt, g, :]
                        vb = v_sb.b
                    else:
                        S.op("pe", lambda e, p=p, kt=kt: e.matmul(p[:, :], lhsT=kws[g * 64:(g + 1) * 64, kt * 128:(kt + 1) * 128],
                                                                   rhs=qs[g * 64:(g + 1) * 64, :, :].rearrange("p h q -> p (h q)"), start=True, stop=True),
                             reads=[kws.b, kvw64, qs.b, q64], writes=[p.b])
                        tbl = Twin[:, kt, 4 * g:4 * g + 4, :]
                        tb_b = Twin.b
                        vl = vws[:, kt, g, :]
                        vb = vws.b
                    et = Et[ec[0] % 4]; ec[0] += 1
                    S.op("act", lambda e, p=p, et=et: e.activation(out=et[:, :], in_=p[:, :], func=AF.Exp), reads=[p.b], writes=[et.b])
                    if tbl is not None:
                        S.op("dve", lambda e, et=et, tbl=tbl: e.tensor_tensor(out=et[:, :].rearrange("p (h q) -> p h q", h=4), in0=et[:, :].rearrange("p (h q) -> p h q", h=4),
                                                                          in1=tbl, op=ALU.mult), reads=[et.b, tb_b], writes=[et.b])
                    staged[kt] = (et, vl, vb)

                def stage2(kt):
                    et, vl, vb = staged.pop(kt)
                    S.op("pe", lambda e, vl=vl, et=et, kt=kt: e.matmul(a_[0:65, :], lhsT=vl, rhs=et[:, :], start=(kt == 0), stop=(kt == nk - 1)),
                         reads=[vb, kvw64, et.b], writes=[a_.b])
                for kt in range(min(LA, nk)):
                    stage1(kt)
                for kt in range(nk):
                    if kt + LA < nk:
                        stage1(kt + LA)
                    stage2(kt)
                ot = oT_sb[pc_[0] % 2]; tm_ = tmpm[pc_[0] % 2]; fs = fsc[pc_[0] % 2]; pc_[0] += 1
                S.op("act", lambda e, a_=a_, ot=ot: e.activation(out=ot[:, :], in_=a_[0:65, :], func=AF.Copy), reads=[a_.b], writes=[ot.b])
                for hh in range(4):
                    S.op("pe", lambda e, hh=hh, ot=ot: e.transpose(out=psM[:, hh * 65:(hh + 1) * 65], in_=ot[0:65, hh * 128:(hh + 1) * 128], identity=id_f[0:65, 0:65]),
                         reads=[ot.b, id_f.b], writes=[psM.b], nowaw=(hh > 0))
                pv = psM[:, 0:260].rearrange("p (h d) -> p h d", d=65)
                S.op("dve", lambda e, pv=pv, fs=fs: e.tensor_scalar(out=fs[:, 0:4], in0=pv[:, :, 64], scalar1=1e-30, scalar2=None, op0=ALU.max), reads=[psM.b], writes=[fs.b])
                S.op("dve", lambda e, fs=fs: e.reciprocal(out=fs[:, 4:8], in_=fs[:, 0:4]), reads=[fs.b], writes=[fs.b])
                S.op("dve", lambda e, fs=fs, gv=gv, br=br: e.tensor_tensor(out=fs[:, 0:4], in0=fs[:, 4:8], in1=gv[:, :, br], op=ALU.mult), reads=[fs.b, gs.b], writes=[fs.b])
                S.op("dve", lambda e, pv=pv, fs=fs, tm_=tm_: e.tensor_tensor(out=tm_[:, :, :], in0=pv[:, :, 0:64], in1=fs[:, 0:4].unsqueeze(2).to_broadcast([128, 4, 64]), op=ALU.mult),
                     reads=[psM.b, fs.b], writes=[tm_.b])
                S.op("dve", lambda e, mx=mx, tm_=tm_, g=g: e.tensor_tensor(out=mx[:, g * 256:(g + 1) * 256].rearrange("p (h d) -> p h d", d=64),
                                                                        in0=mx[:, g * 256:(g + 1) * 256].rearrange("p (h d) -> p h d", d=64), in1=tm_[:, :, :], op=ALU.add),
                     reads=[mx.b, tm_.b], writes=[mx.b])
            for br in (1, 2):
                branch(br)
        for g in range(2):
            grp(g)
        mb = mixb[k2]
        S.op("act", lambda e, mb=mb, mx=mx: e.activation(out=mb[:, :], in_=mx[:, :], func=AF.Copy), reads=[mx.b], writes=[mb.b])
        for j in range(4):
            S.op("pe", lambda e, j=j, mb=mb: e.transpose(out=psT[:, j, :], in_=mb[:, j * 128:(j + 1) * 128], identity=id_b[:, :]),
                 reads=[mb.b, id_b.b], writes=[psT.b], nowaw=(j > 0))
        S.op("act", lambda e, mf=mf: e.activation(out=mf[:, 0:4, :], in_=psT[:, 0:4, :], func=AF.Copy), reads=[psT.b], writes=[mf.b])
        p2_, p4_, p8_, p16_ = pl
        S.op("pool", lambda e, pins=pins: e.tensor_tensor(out=p2_[:, :, 1:144], in0=pins[:, :, 1:144], in1=pins[:, :, 0:143], op=ALU.add), reads=[pins.b], writes=[p2_.b])
        S.op("pool", lambda e: e.tensor_tensor(out=p4_[:, :, 3:144], in0=p2_[:, :, 3:144], in1=p2_[:, :, 1:142], op=ALU.add), reads=[p2_.b], writes=[p4_.b])
        S.op("pool", lambda e: e.tensor_tensor(out=p8_[:, :, 7:144], in0=p4_[:, :, 7:144], in1=p4_[:, :, 3:140], op=ALU.add), reads=[p4_.b], writes=[p8_.b])
        S.op("pool", lambda e: e.tensor_tensor(out=p16_[:, :, 15:144], in0=p8_[:, :, 15:144], in1=p8_[:, :, 7:136], op=ALU.add), reads=[p8_.b], writes=[p16_.b])
        d_t = dd[k2]
        for gi, (srcw, win) in enumerate(((p2_, 2), (p4_, 4), (p8_, 8), (p16_, 16))):
            ch = gi // 2; P = slice((gi % 2) * 64, (gi % 2) * 64 + 64)
            if n == 0:
                S.op("pool", lambda e, srcw=srcw, ch=ch, P=P: e.tensor_tensor(out=srcw[P, ch, 16:144], in0=srcw[P, ch, 16:144], in1=invc_sb[P, ch, :], op=ALU.mult),
                     reads=[srcw.b, invc_sb.b], writes=[srcw.b])
                S.op("pool", lambda e, srcw=srcw, ch=ch, P=P, d_t=d_t, pins=pins: e.tensor_tensor(out=d_t[P, ch, :], in0=srcw[P, ch, 16:144], in1=pins[P, ch, 16:144], op=ALU.subtract),
                     reads=[srcw.b, pins.b], writes=[d_t.b], nowaw=True)
            else:
                S.op("dve", lambda e, srcw=srcw, ch=ch, P=P, d_t=d_t, pins=pins, win=win: e.scalar_tensor_tensor(out=d_t[P, ch, :], in0=srcw[P, ch, 16:144], scalar=1.0 / win,
                                                                                                         in1=pins[P, ch, 16:144], op0=ALU.mult, op1=ALU.subtract),
                     reads=[srcw.b, pins.b], writes=[d_t.b], nowaw=True)
        pp = nrot()
        for ch in range(2):
            S.op("pe", lambda e, pp=pp, ch=ch, d_t=d_t: e.matmul(pp[:, ch * 128:(ch + 1) * 128], lhsT=pwb[:, ch, :], rhs=d_t[:, ch, :], start=True, stop=True),
                 reads=[pwb.b, d_t.b], writes=[pp.b], nowaw=(ch > 0))
        for ch in range(2):
            S.op("act", lambda e, pp=pp, ch=ch, mf=mf: e.activation(out=mf[:, 4 + ch, :], in_=pp[:, ch * 128:(ch + 1) * 128], func=AF.Copy, scale=ps_sb[:, ch:ch + 1]),
                 reads=[pp.b, ps_sb.b], writes=[mf.b], nowaw=True)
        ys = yss[k2]; y_ = yt[k2]; xo_ = y_
        pys = [nrot(), nrot()]
        for hf in range(2):
            for kc in range(8):
                S.op("pe", lambda e, hf=hf, kc=kc, mf=mf: e.matmul(pys[hf][:, :], lhsT=mf[:, kc, :], rhs=wo_sb[:, kc, hf * 512:(hf + 1) * 512], start=(kc == 0), stop=(kc == 7)),
                     reads=[mf.b, mfT_oc[k2], wo_sb.b], writes=[pys[hf].b])
            S.op("act", lambda e, hf=hf, ys=ys: e.activation(out=ysq[:, :], in_=pys[hf][:, :], func=AF.Square, accum_out=ys[:, hf:hf + 1]),
                 reads=[pys[hf].b], writes=[ysq.b, ys.b])
        S.op("dve", lambda e, ys=ys: e.tensor_tensor(out=ys[:, 2:3], in0=ys[:, 0:1], in1=ys[:, 1:2], op=ALU.add), reads=[ys.b], writes=[ys.b])
        S.op("act", lambda e, ys=ys: e.activation(out=ys[:, 3:4], in_=ys[:, 2:3], func=AF.Sqrt, scale=1.0 / D, bias=eps_c[:, :]), reads=[ys.b, eps_c.b], writes=[ys.b])
        S.op("dve", lambda e, ys=ys: e.reciprocal(out=ys[:, 3:4], in_=ys[:, 3:4]), reads=[ys.b], writes=[ys.b])
        for hf in range(2):
            S.op("dve", lambda e, hf=hf, ys=ys, y_=y_: e.scalar_tensor_tensor(out=y_[:, hf * 512:(hf + 1) * 512], in0=pys[hf][:, :], scalar=ys[:, 3:4],
                                                                        in1=gp_sb[:, hf * 512:(hf + 1) * 512], op0=ALU.mult, op1=ALU.mult),
                 reads=[pys[hf].b, ys.b, gp_sb.b], writes=[y_.b], nowaw=(hf > 0))
        S.op("pool", lambda e, y_=y_, xs=xs, xo_=xo_: e.tensor_tensor(out=xo_[:, :], in0=y_[:, :], in1=xs[:, :], op=ALU.add), reads=[y_.b, xs.b], writes=[xo_.b])
        S.dma("poolq", x_out[n * 128:(n + 1) * 128, :], xo_[:, :], reads=[xo_.b], writes=[x_out.b], nowaw=True, is_output=True)
    for n in range(NTL):
        tile(n)
    return C.finish() if own else None


def _t5_bucket(n):
    n = np.maximum(np.asarray(n, np.int64), 0)
    nf = np.maximum(n, 1).astype(np.float32)
    lg = (np.log(nf / np.float32(16.0)) / np.float32(np.log(64.0)) * np.float32(16.0)).astype(np.float32)
    large = 16 + lg.astype(np.int32)
    large = np.minimum(large, 31)
    return np.where(n < 16, n, large).astype(np.int64)


def _onehot(dist, valid):
    dist = np.asarray(dist); valid = np.asarray(valid, bool)
    b = _t5_bucket(np.where(valid, dist, 0))
    rows = np.where(valid, b, 32)
    oh = np.zeros((33, dist.size), np.float32)
    oh[rows.reshape(-1), np.arange(dist.size)] = 1.0
    return oh


_CONST_CACHE = {}


def consts_B1(cp):
    if cp in _CONST_CACHE:
        return _CONST_CACHE[cp]
    m = np.arange(1536)
    d = m - 511 + 128 * cp
    ohs = _onehot(d, d >= 0)
    w = np.repeat(np.arange(5), 256); mp = np.tile(np.arange(256), 5)
    d = 128 * (4 - w) + mp - 127
    okw = (d >= 0) & (d < 512) & (mp <= 254)
    ohw = _onehot(d, okw)
    ohw0 = _onehot(d, okw & (w >= 4 - cp))
    c3 = np.repeat(np.arange(88), 128); tq = np.tile(np.arange(128), 88)
    d = 128 * cp + tq + 865 - 16 * c3
    ohc = _onehot(d, d >= 0).reshape(33, 88, 128)
    expand = np.zeros((128, 64, 128), np.float32)
    for kt in range(64):
        expand[2 * kt + 1, kt, 0:64] = 1.0
        expand[2 * kt, kt, 64:128] = 1.0
    vfc = np.zeros((128, 2, 9), np.float32)
    for t in range(128):
        bq = 2 * cp + (1 if t >= 64 else 0)
        for jj in range(9):
            jp = jj - 1
            valid = jp <= bq
            forced = (jp == bq) or (jp == bq - 1)
            vfc[t, 0, jj] = 1.0 if (valid and not forced) else 0.0
            vfc[t, 1, jj] = 1e9 if forced else (0.0 if valid else -1.0)
    invc0 = np.zeros((128, 2, 128), np.float32)
    for p in range(128):
        for ch in range(2):
            win = (2, 4, 8, 16)[2 * ch + (1 if p >= 64 else 0)]
            t = 128 * cp + np.arange(128)
            invc0[p, ch] = 1.0 / np.minimum(t + 1, win)
    r = dict(ohs=ohs, ohw=ohw, ohw0=ohw0, ohc=ohc, expand=expand.reshape(128, 8192), vfc=vfc, invc0=invc0, ident=_IDENT)
    _CONST_CACHE[cp] = r
    return r


def gather_full(A, b, key, rows):
    r0, r1 = rows
    out = np.empty((r1 - r0, 64, 128), A[0][key].dtype)
    for cp in range(4):
        out[:, cp::4, :] = A[b * 4 + cp][key][r0:r1].reshape(r1 - r0, NT, 128)
    return out.reshape(r1 - r0, 8192)


def gather_full_tm(A, b, key, cols):
    c0, c1 = cols
    out = np.empty((64, 128, c1 - c0), A[0][key].dtype)
    for cp in range(4):
        out[cp::4] = A[b * 4 + cp][key][:, c0:c1].reshape(NT, 128, c1 - c0)
    return out.reshape(8192, c1 - c0)


def prep_B1_batch(A, b):
    kc = gather_full(A, b, "kvT", (0, 128)); vc = gather_full(A, b, "kvT", (128, 256))
    ksl = gather_full(A, b, "kvT", (256, 384)); kw = gather_full(A, b, "kvT", (384, 512))
    vsl = gather_full_tm(A, b, "vtm", (0, 128)); vw = gather_full_tm(A, b, "vtm", (128, 256))
    pin = gather_full(A, b, "pinT", (0, 256))
    r = {}
    r["kc_in"] = np.ascontiguousarray(kc.reshape(2, 64, 8192).transpose(1, 0, 2))
    r["vc_in"] = np.ascontiguousarray(vc.reshape(2, 64, 8192).transpose(1, 0, 2))
    r["ksl_in"] = np.ascontiguousarray(ksl.reshape(128, 64, 128)[:, :, ::-1].reshape(128, 8192))
    r["vsl_in"] = np.ascontiguousarray(vsl.reshape(64, 128, 2, 64)[:, ::-1].transpose(1, 0, 2, 3))
    r["kwp"] = np.concatenate([np.zeros((128, 512), kw.dtype), kw], axis=1)
    r["vwp"] = np.concatenate([np.zeros((512, 128), vw.dtype), vw], axis=0)
    r["pinp"] = np.concatenate([np.zeros((256, 16), pin.dtype), pin], axis=1)
    return r


def prep_B1(inp, l, x, core, A, bb):
    b, cp = divmod(core, 4)
    r = dict(consts_B1(cp))
    a = A[core]
    r["qT_in"] = np.ascontiguousarray(a["qT"].reshape(2, 4, 64, TL).transpose(0, 2, 1, 3).reshape(128, 4, TL))
    r["gate_in"] = a["gate"]
    for k in ("kc_in", "vc_in", "ksl_in", "vsl_in"):
        r[k] = bb[k]
    kw_in = np.empty((128, NT, 640), bb["kwp"].dtype)
    vw_in = np.empty((128, NT, 5, 2, 64), bb["vwp"].dtype)
    pin_in = np.empty((128, 2, NT, 144), np.float32)
    for n in range(NT):
        i = 4 * n + cp
        kw_in[:, n, :] = bb["kwp"][:, 128 * i:128 * i + 640].reshape(128, 5, 128)[:, :, ::-1].reshape(128, 640)
        vw_in[:, n] = bb["vwp"][128 * i:128 * i + 640].reshape(5, 128, 2, 64)[:, ::-1].transpose(1, 0, 2, 3)
        pin_in[:, :, n, :] = bb["pinp"][:, 128 * i:128 * i + 144].reshape(2, 128, 144).transpose(1, 0, 2)
    r["kw_in"] = kw_in; r["vw_in"] = vw_in; r["pin_in"] = pin_in
    r["oc_in"] = np.ascontiguousarray(a["ocT"].reshape(2, 128, TL).transpose(1, 0, 2))
    r["x_in"] = local_rows(x, core)
    r["relb"] = np.ascontiguousarray(inp["rel_bias"])
    r["cpos"] = np.ascontiguousarray(inp["cmp_pos"][l].transpose(2, 0, 1))
    r["cw1"] = np.ascontiguousarray(inp["cmp_w1"][l].transpose(2, 0, 1, 3))
    r["cb1"] = np.ascontiguousarray(inp["cmp_b1"][l].T)
    r["cw2"] = np.ascontiguousarray(inp["cmp_w2"][l].transpose(1, 0, 2))
    r["cb2"] = np.ascontiguousarray(inp["cmp_b2"][l])
    r["poolw"] = np.ascontiguousarray(inp["pool_w"][l])
    r["pools"] = gcol(inp["pool_scale"][l])
    r["wo"] = np.ascontiguousarray(inp["w_out"][l])
    r["gpost"] = np.ascontiguousarray(inp["mix_norm_post"][l])
    return r


def load_w_scaled(C, S, w_d, rows, cols, g_sb, Wb, stage, q="sp"):
    for kc in range(rows // 128):
        st = stage[kc % 2]
        S.dma(q, st[:, 0:cols], w_d[kc * 128:(kc + 1) * 128, :], writes=[st.b])
        if g_sb is None:
            S.op("dve", lambda e, kc=kc, st=st: e.tensor_copy(out=Wb[:, kc, :], in_=st[:, 0:cols]), reads=[st.b], writes=[Wb.b], nowaw=True)
        else:
            S.op("dve", lambda e, kc=kc, st=st: e.tensor_scalar(out=Wb[:, kc, :], in0=st[:, 0:cols], scalar1=g_sb[:, kc:kc + 1], scalar2=None, op0=ALU.mult),
                 reads=[st.b, g_sb.b], writes=[Wb.b], nowaw=True)


def norm_rows(S, xin, nrows, sq, ss, eps_c, out_b):
    S.op("act", lambda e: e.activation(out=sq[0:nrows, :], in_=xin[0:nrows, :], func=AF.Square, accum_out=ss[0:nrows, 0:1]),
         reads=[xin.b], writes=[sq.b, ss.b])
    S.op("act", lambda e: e.activation(out=ss[0:nrows, 1:2], in_=ss[0:nrows, 0:1], func=AF.Sqrt, scale=1.0 / D, bias=eps_c[0:nrows, :]),
         reads=[ss.b, eps_c.b], writes=[ss.b])
    S.op("dve", lambda e: e.reciprocal(out=ss[0:nrows, 1:2], in_=ss[0:nrows, 1:2]), reads=[ss.b], writes=[ss.b])
    S.op("dve", lambda e: e.tensor_scalar(out=out_b[0:nrows, :], in0=xin[0:nrows, :], scalar1=ss[0:nrows, 1:2], scalar2=None, op0=ALU.mult),
         reads=[xin.b, ss.b], writes=[out_b.b])


def post_norm_residual(S, pys, ys, ysq, eps_c, gp_sb, y_, xs, x_out, row0):
    for hf in range(2):
        S.op("act", lambda e, hf=hf: e.activation(out=ysq[:, :], in_=pys[hf][:, :], func=AF.Square, accum_out=ys[:, hf:hf + 1]),
             reads=[pys[hf].b], writes=[ysq.b, ys.b])
    S.op("dve", lambda e: e.tensor_tensor(out=ys[:, 2:3], in0=ys[:, 0:1], in1=ys[:, 1:2], op=ALU.add), reads=[ys.b], writes=[ys.b])
    S.op("act", lambda e: e.activation(out=ys[:, 3:4], in_=ys[:, 2:3], func=AF.Sqrt, scale=1.0 / D, bias=eps_c[:, :]), reads=[ys.b, eps_c.b], writes=[ys.b])
    S.op("dve", lambda e: e.reciprocal(out=ys[:, 3:4], in_=ys[:, 3:4]), reads=[ys.b], writes=[ys.b])
    for hf in range(2):
        S.op("dve", lambda e, hf=hf: e.scalar_tensor_tensor(out=y_[:, hf * 512:(hf + 1) * 512], in0=pys[hf][:, :], scalar=ys[:, 3:4],
                                                            in1=gp_sb[:, hf * 512:(hf + 1) * 512], op0=ALU.mult, op1=ALU.mult),
             reads=[pys[hf].b, ys.b, gp_sb.b], writes=[y_.b], nowaw=(hf > 0))
    S.op("pool", lambda e: e.tensor_tensor(out=y_[:, :], in0=y_[:, :], in1=xs[:, :], op=ALU.add), reads=[y_.b, xs.b], writes=[y_.b])
    S.dma("poolq", x_out[row0:row0 + 128, :], y_[:, :], reads=[y_.b], writes=[x_out.b], nowaw=True, is_output=True)


def build_B2(C=None):
    own = C is None
    C = C or Ctx("B2"); S = C.S
    x_in = C.din("x_in", [TL, D], F32)
    mem = C.din("mem", [256, D], F32)
    gq = C.din("gq", [128, 8], F32)
    gkv = C.din("gkv", [128, 8], F32)
    gpost = C.din("gpost", [D], F32)
    wq = C.din("wq", [D, D], F32); wk = C.din("wk", [D, D], F32); wv = C.din("wv", [D, D], F32); wo = C.din("wo", [D, D], F32)
    ident = C.din("ident", [128, 128], F32)
    x_out = C.dout("x_out", [TL, D], F32)
    rot = [C.ps([128, 512], F32, f"rot{i}") for i in range(5)]
    psT = [C.ps([128, 8, 128], BF16, f"psT{i}") for i in range(2)]
    psd = C.ps([128, 512], F32, "psd")
    rc = [0]

    def nrot():
        rc[0] += 1
        return rot[rc[0] % 5]
    eps_c = C.sb([128, 1], F32, "eps_c")
    S.op("dve", lambda e: e.memset(eps_c[:, :], EPS), writes=[eps_c.b])
    id_b = C.sb([128, 128], BF16, "id_b")
    S.dma("poolq", id_b[:, :], ident[:, :], writes=[id_b.b])
    ones = C.sb([128, 128], BF16, "ones")
    S.op("dve", lambda e: e.memset(ones[:, :], 1.0), writes=[ones.b])
    gq_sb = C.sb([128, 8], F32, "gq_sb"); gkv_sb = C.sb([128, 8], F32, "gkv_sb")
    S.dma("sp", gq_sb[:, :], gq[:, :], writes=[gq_sb.b])
    S.dma("sp", gkv_sb[:, :], gkv[:, :], writes=[gkv_sb.b])
    gp_sb = C.sb([128, D], F32, "gp_sb")
    S.dma("sp", gp_sb[:, :], bcast_rows(gpost, D), writes=[gp_sb.b])
    stage = [C.sb([128, D], F32, f"stage{i}") for i in range(2)]
    Wq = C.sb([128, 8, D], BF16, "Wq"); Wk = C.sb([128, 8, D], BF16, "Wk"); Wv = C.sb([128, 8, D], BF16, "Wv"); Wo = C.sb([128, 8, D], BF16, "Wo")
    load_w_scaled(C, S, wk, D, D, gkv_sb, Wk, stage)
    load_w_scaled(C, S, wv, D, D, gkv_sb, Wv, stage)
    load_w_scaled(C, S, wq, D, D, gq_sb, Wq, stage)
    load_w_scaled(C, S, wo, D, D, None, Wo, stage)
    xt = [C.sb([128, D], F32, f"xt{i}") for i in range(2)]
    sq = C.sb([128, D], BF16, "sq")
    ss = [C.sb([128, 2], F32, f"ss{i}") for i in range(2)]
    hn = [C.sb([128, D], BF16, f"hn{i}") for i in range(2)]
    mT = C.sb([128, 8, 256], BF16, "mT")
    for mt in range(2):
        xx = xt[mt]; hh = hn[mt]; pT = psT[mt]
        S.dma("sp", xx[:, :], mem[mt * 128:(mt + 1) * 128, :], writes=[xx.b])
        norm_rows(S, xx, 128, sq, ss[mt], eps_c, hh)
        for kc in range(8):
            S.op("pe", lambda e, kc=kc, hh=hh, pT=pT: e.transpose(out=pT[:, kc, :], in_=hh[:, kc * 128:(kc + 1) * 128], identity=id_b[:, :]),
                 reads=[hh.b, id_b.b], writes=[pT.b], nowaw=(kc > 0))
        S.op("dve", lambda e, pT=pT, mt=mt: e.tensor_copy(out=mT[:, :, mt * 128:(mt + 1) * 128], in_=pT[:, :, :]), reads=[pT.b], writes=[mT.b], nowaw=True)
    kT = C.sb([128, 8, 256], BF16, "kT")
    v_sb = C.sb([128, 2, D], BF16, "v_sb")
    for oc in range(8):
        p = nrot()
        for kc in range(8):
            S.op("pe", lambda e, p=p, kc=kc, oc=oc: e.matmul(p[:, 0:256], lhsT=Wk[:, kc, oc * 128:(oc + 1) * 128], rhs=mT[:, kc, :], start=(kc == 0), stop=(kc == 7)),
                 reads=[Wk.b, mT.b], writes=[p.b])
        S.op("act", lambda e, p=p, oc=oc: e.activation(out=kT[:, oc, :], in_=p[:, 0:256], func=AF.Copy), reads=[p.b], writes=[kT.b], nowaw=True)
    for mt in range(2):
        for hf in range(2):
            p = nrot()
            for kc in range(8):
                S.op("pe", lambda e, p=p, kc=kc, mt=mt, hf=hf: e.matmul(p[:, :], lhsT=mT[:, kc, mt * 128:(mt + 1) * 128], rhs=Wv[:, kc, hf * 512:(hf + 1) * 512], start=(kc == 0), stop=(kc == 7)),
                     reads=[Wv.b, mT.b], writes=[p.b])
            S.op("act", lambda e, p=p, mt=mt, hf=hf: e.activation(out=v_sb[:, mt, hf * 512:(hf + 1) * 512], in_=p[:, :], func=AF.Copy), reads=[p.b], writes=[v_sb.b], nowaw=True)
    hT = [C.sb([128, 8, 512], BF16, f"hT{i}") for i in range(2)]
    qT = [C.sb([128, 8, 512], BF16, f"qT{i}") for i in range(2)]
    oT = [C.sb([128, 8, 512], BF16, f"oT{i}") for i in range(2)]
    Pt = [C.sb([128, 512], BF16, f"Pt{i}") for i in range(4)]
    rden = [C.sb([128, 512], F32, f"rden{i}") for i in range(2)]
    xk = [C.sb([128, D], F32, f"xk{i}") for i in range(4)]
    ysq = C.sb([128, 512], BF16, "ysq")
    yss = [C.sb([128, 4], F32, f"yss{i}") for i in range(2)]
    yt = [C.sb([128, D], F32, f"yt{i}") for i in range(2)]
    pcnt = [0]

    def group(grp):
        hTg = hT[grp % 2]; qTg = qT[grp % 2]; oTg = oT[grp % 2]
        for tt in range(4):
            t = grp * 4 + tt
            xx = xk[tt]; hh = hn[t % 2]; pT = psT[t % 2]
            S.dma("sp", xx[:, :], x_in[t * 128:(t + 1) * 128, :], writes=[xx.b])
            norm_rows(S, xx, 128, sq, ss[t % 2], eps_c, hh)
            for kc in range(8):
                S.op("pe", lambda e, kc=kc, hh=hh, pT=pT: e.transpose(out=pT[:, kc, :], in_=hh[:, kc * 128:(kc + 1) * 128], identity=id_b[:, :]),
                     reads=[hh.b, id_b.b], writes=[pT.b], nowaw=(kc > 0))
            S.op("dve", lambda e, pT=pT, tt=tt: e.tensor_copy(out=hTg[:, :, tt * 128:(tt + 1) * 128], in_=pT[:, :, :]), reads=[pT.b], writes=[hTg.b], nowaw=True)
        for oc in range(8):
            p = nrot()
            for kc in range(8):
                S.op("pe", lambda e, p=p, kc=kc, oc=oc: e.matmul(p[:, :], lhsT=Wq[:, kc, oc * 128:(oc + 1) * 128], rhs=hTg[:, kc, :], start=(kc == 0), stop=(kc == 7)),
                     reads=[Wq.b, hTg.b], writes=[p.b])
            S.op("act", lambda e, p=p, oc=oc: e.activation(out=qTg[:, oc, :], in_=p[:, :], func=AF.Copy, scale=1.0 / 16), reads=[p.b], writes=[qTg.b], nowaw=True)
        for hd in range(4):
            pts = []
            for mt in range(2):
                p = nrot()
                for dc in range(2):
                    S.op("pe", lambda e, p=p, dc=dc, mt=mt, hd=hd: e.matmul(p[:, :], lhsT=kT[:, hd * 2 + dc, mt * 128:(mt + 1) * 128], rhs=qTg[:, hd * 2 + dc, :], start=(dc == 0), stop=(dc == 1)),
                         reads=[kT.b, qTg.b], writes=[p.b])
                pt = Pt[pcnt[0] % 4]; pcnt[0] += 1
                S.op("act", lambda e, p=p, pt=pt: e.activation(out=pt[:, :], in_=p[:, :], func=AF.Exp), reads=[p.b], writes=[pt.b])
                pts.append(pt)
            for mt in range(2):
                S.op("pe", lambda e, mt=mt, pts=pts: e.matmul(psd[:, :], lhsT=ones[:, :], rhs=pts[mt][:, :], start=(mt == 0), stop=(mt == 1)),
                     reads=[ones.b, pts[mt].b], writes=[psd.b])
            rd = rden[hd % 2]
            S.op("dve", lambda e, rd=rd: e.reciprocal(out=rd[:, :], in_=psd[:, :]), reads=[psd.b], writes=[rd.b])
            for dvc in range(2):
                p = nrot()
                for mt in range(2):
                    S.op("pe", lambda e, p=p, mt=mt, hd=hd, dvc=dvc, pts=pts: e.matmul(p[:, :], lhsT=v_sb[:, mt, hd * 256 + dvc * 128:hd * 256 + dvc * 128 + 128], rhs=pts[mt][:, :],
                                                                                  start=(mt == 0), stop=(mt == 1)),
                         reads=[v_sb.b, pts[mt].b], writes=[p.b])
                S.op("dve", lambda e, p=p, hd=hd, dvc=dvc, rd=rd: e.tensor_tensor(out=oTg[:, hd * 2 + dvc, :], in0=p[:, :], in1=rd[:, :], op=ALU.mult),
                     reads=[p.b, rd.b], writes=[oTg.b], nowaw=True)
        for tt in range(4):
            t = grp * 4 + tt
            pys = [nrot(), nrot()]
            for hf in range(2):
                for kc in range(8):
                    S.op("pe", lambda e, hf=hf, kc=kc, tt=tt, pys=pys: e.matmul(pys[hf][:, :], lhsT=oTg[:, kc, tt * 128:(tt + 1) * 128], rhs=Wo[:, kc, hf * 512:(hf + 1) * 512], start=(kc == 0), stop=(kc == 7)),
                         reads=[oTg.b, Wo.b], writes=[pys[hf].b])
            post_norm_residual(S, pys, yss[t % 2], ysq, eps_c, gp_sb, yt[t % 2], xk[tt], x_out, t * 128)
    for grp in range(NT // 4):
        group(grp)
    return C.finish() if own else None


def prep_B2(inp, l, core, x1_local):
    b = core // 4
    return {"x_in": x1_local, "mem": np.ascontiguousarray(inp["mem"][b]), "gq": gcol(inp["mem_norm_pre"][l]), "gkv": gcol(inp["mem_norm_kv"][l]),
            "gpost": np.ascontiguousarray(inp["mem_norm_post"][l]), "wq": np.ascontiguousarray(inp["w_mq"][l]), "wk": np.ascontiguousarray(inp["w_mk"][l]),
            "wv": np.ascontiguousarray(inp["w_mv"][l]), "wo": np.ascontiguousarray(inp["w_mo"][l]), "ident": _IDENT}


FH = 2816
NCH = 44


def build_C(C=None):
    own = C is None
    C = C or Ctx("C"); S = C.S
    x_in = C.din("x_in", [TL, D], F32)
    xh = C.din("xh", [NT * 2, D], F32)
    gpre = C.din("gpre", [128, 8], F32)
    gpost = C.din("gpost", [D], F32)
    wu = C.din("wu", [D, 2 * FH], F32)
    wd = C.din("wd", [FH, D], F32)
    cwc = C.din("cwc", [128, NCH, 3], F32)
    cbc = C.din("cbc", [128, NCH], F32)
    ident = C.din("ident", [128, 128], F32)
    x_out = C.dout("x_out", [TL, D], F32)
    rot = [C.ps([128, 512], F32, f"rot{i}") for i in range(4)]
    psh = [C.ps([128, 512], F32, f"psh{i}") for i in range(2)]
    psT = [C.ps([128, 8, 128], BF16, f"psT{i}") for i in range(2)]
    rc = [0]

    def nrot():
        rc[0] += 1
        return rot[rc[0] % 4]
    eps_c = C.sb([128, 1], F32, "eps_c")
    S.op("dve", lambda e: e.memset(eps_c[:, :], EPS), writes=[eps_c.b])
    id_b = C.sb([128, 128], BF16, "id_b")
    S.dma("poolq", id_b[:, :], ident[:, :], writes=[id_b.b])
    g_sb = C.sb([128, 8], F32, "g_sb")
    S.dma("sp", g_sb[:, :], gpre[:, :], writes=[g_sb.b])
    gp_sb = C.sb([128, D], F32, "gp_sb")
    S.dma("sp", gp_sb[:, :], bcast_rows(gpost, D), writes=[gp_sb.b])
    cw_sb = C.sb([128, NCH, 3], F32, "cw_sb"); cb_sb = C.sb([128, NCH], F32, "cb_sb")
    S.dma("sp", cw_sb[:, :, :], cwc[:, :, :], writes=[cw_sb.b])
    S.dma("sp", cb_sb[:, :], cbc[:, :], writes=[cb_sb.b])
    Wu = C.sb([128, 8, 2 * FH], BF16, "Wu")
    Wd = C.sb([128, 22, D], BF16, "Wd")
    stage = [C.sb([128, 704], F32, f"stage{i}") for i in range(2)]
    si = 0
    for kc in range(8):
        for cbk in range(8):
            st = stage[si % 2]; si += 1
            S.dma("sp", st[:, :], wu[kc * 128:(kc + 1) * 128, cbk * 704:(cbk + 1) * 704], writes=[st.b])
            S.op("dve", lambda e, kc=kc, cbk=cbk, st=st: e.tensor_scalar(out=Wu[:, kc, cbk * 704:(cbk + 1) * 704], in0=st[:, :], scalar1=g_sb[:, kc:kc + 1], scalar2=None, op0=ALU.mult),
                 reads=[st.b, g_sb.b], writes=[Wu.b], nowaw=True)
    for j in range(22):
        S.dma("poolq", Wd[:, j, :], wd[j * 128:(j + 1) * 128, :], writes=[Wd.b], nowaw=True)
    xk = [C.sb([128, D], F32, f"xk{i}") for i in range(2)]
    xhs = C.sb([8, D], F32, "xhs")
    sq = C.sb([128, D], BF16, "sq")
    ss = [C.sb([128, 2], F32, f"ss{i}") for i in range(2)]
    hn = [C.sb([128, D], BF16, f"hn{i}") for i in range(2)]
    hT = [C.sb([128, 8, 520], BF16, "hT0")] * 2
    ab = [C.sb([128, 4, 130], F32, f"ab{i}") for i in range(2)]
    yy = [C.sb([128, 4, 128], F32, f"yy{i}") for i in range(2)]
    gg = [C.sb([128, 4, 128], F32, f"gg{i}") for i in range(2)]
    uT = C.sb([128, 22, 512], BF16, "uT")
    ysq = C.sb([128, 512], BF16, "ysq")
    yss = [C.sb([128, 4], F32, f"yss{i}") for i in range(2)]
    yt = [C.sb([128, D], F32, "yt0")] * 2
    cnt = [0]

    def group(grp):
        hTg = hT[grp % 2]
        for tt in range(4):
            t = grp * 4 + tt
            xx = xk[tt % 2]; hh = hn[t % 2]; pT = psT[t % 2]
            S.dma("sp", xx[:, :], x_in[t * 128:(t + 1) * 128, :], writes=[xx.b])
            norm_rows(S, xx, 128, sq, ss[t % 2], eps_c, hh)
            for kc in range(8):
                S.op("pe", lambda e, kc=kc, hh=hh, pT=pT: e.transpose(out=pT[:, kc, :], in_=hh[:, kc * 128:(kc + 1) * 128], identity=id_b[:, :]),
                     reads=[hh.b, id_b.b], writes=[pT.b], nowaw=(kc > 0))
            S.op("dve", lambda e, pT=pT, tt=tt: e.tensor_copy(out=hTg[:, :, tt * 128:(tt + 1) * 128], in_=pT[:, :, :]), reads=[pT.b], writes=[hTg.b], nowaw=True)
        hh = hn[0]; pT = psT[0]
        S.dma("sp", xhs[:, :], xh[grp * 8:(grp + 1) * 8, :], writes=[xhs.b])
        norm_rows(S, xhs, 8, sq, ss[0], eps_c, hh)
        for kc in range(8):
            S.op("pe", lambda e, kc=kc: e.transpose(out=pT[:, kc, 0:8], in_=hh[0:8, kc * 128:(kc + 1) * 128], identity=id_b[0:8, 0:8]),
                 reads=[hh.b, id_b.b], writes=[pT.b], nowaw=(kc > 0))
        S.op("dve", lambda e: e.tensor_copy(out=hTg[:, :, 512:520], in_=pT[:, :, 0:8]), reads=[pT.b], writes=[hTg.b], nowaw=True)

        def chunk(oc):
            k = cnt[0]; cnt[0] += 1
            a_ = ab[k % 2]; y_ = yy[k % 2]
            pm = nrot(); ph = psh[k % 2]
            for kc in range(8):
                S.op("pe", lambda e, kc=kc: e.matmul(pm[:, :], lhsT=Wu[:, kc, oc * 128:(oc + 1) * 128], rhs=hTg[:, kc, 0:512], start=(kc == 0), stop=(kc == 7)),
                     reads=[Wu.b, hTg.b], writes=[pm.b])
            for kc in range(8):
                S.op("pe", lambda e, kc=kc: e.matmul(ph[:, 0:8], lhsT=Wu[:, kc, oc * 128:(oc + 1) * 128], rhs=hTg[:, kc, 512:520], start=(kc == 0), stop=(kc == 7)),
                     reads=[Wu.b, hTg.b], writes=[ph.b])
            S.op("act", lambda e: e.activation(out=a_[:, :, 2:130], in_=pm[:, :].rearrange("p (t q) -> p t q", q=128), func=AF.Copy), reads=[pm.b], writes=[a_.b])
            S.op("act", lambda e: e.activation(out=a_[:, :, 0:2], in_=ph[:, 0:8].rearrange("p (t q) -> p t q", q=2), func=AF.Copy), reads=[ph.b], writes=[a_.b], nowaw=True)
            S.op("act", lambda e: e.activation(out=y_[:, :, :], in_=pm[:, :].rearrange("p (t q) -> p t q", q=128), func=AF.Identity,
                                               scale=cw_sb[:, oc, 2:3], bias=cb_sb[:, oc:oc + 1]), reads=[pm.b, cw_sb.b, cb_sb.b], writes=[y_.b])
            S.op("dve", lambda e: e.scalar_tensor_tensor(out=y_[:, :, :], in0=a_[:, :, 1:129], scalar=cw_sb[:, oc, 1:2], in1=y_[:, :, :], op0=ALU.mult, op1=ALU.add),
                 reads=[a_.b, y_.b, cw_sb.b], writes=[y_.b])
            S.op("dve", lambda e: e.scalar_tensor_tensor(out=y_[:, :, :], in0=a_[:, :, 0:128], scalar=cw_sb[:, oc, 0:1], in1=y_[:, :, :], op0=ALU.mult, op1=ALU.add),
                 reads=[a_.b, y_.b, cw_sb.b], writes=[y_.b])
            return y_
        for j in range(22):
            yg = chunk(j)
            g_ = gg[j % 2]
            S.op("act", lambda e, yg=yg, g_=g_: e.activation(out=g_[:, :, :], in_=yg[:, :, :], func=AF.Gelu_apprx_tanh), reads=[yg.b], writes=[g_.b])
            yv = chunk(22 + j)
            S.op("dve", lambda e, yv=yv, g_=g_, j=j: e.tensor_tensor(out=uT[:, j, :].rearrange("p (t q) -> p t q", q=128), in0=yv[:, :, :], in1=g_[:, :, :], op=ALU.mult),
                 reads=[yv.b, g_.b], writes=[uT.b], nowaw=True)
        for tt in range(4):
            t = grp * 4 + tt
            pys = [nrot(), nrot()]
            for hf in range(2):
                for j in range(22):
                    S.op("pe", lambda e, hf=hf, j=j, tt=tt, pys=pys: e.matmul(pys[hf][:, :], lhsT=uT[:, j, tt * 128:(tt + 1) * 128], rhs=Wd[:, j, hf * 512:(hf + 1) * 512], start=(j == 0), stop=(j == 21)),
                         reads=[uT.b, Wd.b], writes=[pys[hf].b])
            xx = xk[tt % 2]
            S.dma("sp", xx[:, :], x_in[t * 128:(t + 1) * 128, :], writes=[xx.b])
            post_norm_residual(S, pys, yss[t % 2], ysq, eps_c, gp_sb, yt[t % 2], xx, x_out, t * 128)
    for grp in range(NT // 4):
        group(grp)
    return C.finish() if own else None


def halo_rows(x_full, core, nh):
    b, cp = divmod(core, 4)
    out = np.zeros((NT, nh, x_full.shape[2]), x_full.dtype)
    for n in range(NT):
        i = 4 * n + cp
        if i > 0:
            out[n] = x_full[b, 128 * i - nh:128 * i]
    return out.reshape(NT * nh, -1)


def prep_C(inp, l, core, x2_local, x2_full):
    return {"x_in": x2_local, "xh": halo_rows(x2_full, core, 2), "gpre": gcol(inp["ffn_norm_pre"][l]), "gpost": np.ascontiguousarray(inp["ffn_norm_post"][l]),
            "wu": np.ascontiguousarray(inp["w_up"][l]), "wd": np.ascontiguousarray(inp["w_down"][l]),
            "cwc": np.ascontiguousarray(inp["conv_w"][l].reshape(3, NCH, 128).transpose(2, 1, 0)),
            "cbc": np.ascontiguousarray(inp["conv_b"][l].reshape(NCH, 128).T), "ident": _IDENT}


def scatter_local(outs, shape):
    full = np.empty(shape, np.float32)
    v = full.reshape(shape[0], 64, 128, shape[2])
    for core in range(8):
        b, cp = divmod(core, 4)
        v[b, cp::4] = outs[core].reshape(NT, 128, shape[2])
    return full


from concourse.bass_utils import run_bass_kernel_spmd

def build_B12():
    C = Ctx("B12")
    x1 = C.dscr("x1_d", [TL, D], F32)
    with C.phase("b1_", {"x_out": x1}):
        build_B1(C=C)
    C.S.barrier()
    with C.phase("b2_", {"x_in": x1}):
        build_B2(C=C)
    return C.finish()


def build_CA():
    C = Ctx("CA")
    x3 = C.dout("x_out", [TL, D], F32)
    with C.phase("c_", {"x_out": x3}):
        build_C(C=C)
    C.S.barrier()
    with C.phase("a_", {"x": x3}):
        build_A(C=C)
    return C.finish()


_PROGS = {}


def _prog(name):
    if name not in _PROGS:
        _PROGS[name] = {"A": build_A, "B12": build_B12, "CA": build_CA, "C": build_C}[name]()
    return _PROGS[name]


def _run(name, maps):
    res = run_bass_kernel_spmd(_prog(name), maps, core_ids=list(range(8)))
    return [{k: np.asarray(v) for k, v in r.items()} for r in res.results]


def _pre(p, d, drop=()):
    return {p + k: v for k, v in d.items() if k not in drop}


def kernel(**inputs):
    inp = {k: np.asarray(v) for k, v in inputs.items()}
    x = np.ascontiguousarray(inp["x"], dtype=np.float32)
    shape = x.shape
    A = _run("A", [prep_A(inp, 0, x, c) for c in range(8)])
    for l in range(4):
        bbs = [prep_B1_batch(A, b) for b in range(2)]
        m = []
        for c in range(8):
            d = _pre("b1_", prep_B1(inp, l, x, c, A, bbs[c // 4]))
            d.update(_pre("b2_", prep_B2(inp, l, c, None), drop=("x_in",)))
            m.append(d)
        r2 = _run("B12", m)
        del A, bbs, m
        x2l = [r2[c]["b2_x_out"] for c in range(8)]
        x2_full = scatter_local(x2l, shape)
        if l < 3:
            m = []
            for c in range(8):
                d = _pre("c_", prep_C(inp, l, c, x2l[c], x2_full), drop=("x_out",))
                d.update(_pre("a_", prep_A(inp, l + 1, x, c), drop=("x",)))
                m.append(d)
            r3 = _run("CA", m)
            x = scatter_local([r3[c]["x_out"] for c in range(8)], shape)
            A = [{k[2:]: v for k, v in r3[c].items() if k.startswith("a_")} for c in range(8)]
        else:
            r3 = _run("C", [prep_C(inp, l, c, x2l[c], x2_full) for c in range(8)])
            x = scatter_local([r3[c]["x_out"] for c in range(8)], shape)
    return x
```
